# Optimizing a Trainium2 kernel written in Bass

```python
import math
import jax, jax.numpy as jnp
from jax import lax
import numpy as np

D_MODEL = 2048
BATCH = 8
SEQ = 2048
DEPTH = 2

HEAD_DIM = 64
BLOCK = 128
A_Q_HEADS = 16
A_KV_HEADS = 2
A_GROUP = A_Q_HEADS // A_KV_HEADS
A_WINDOW = 128
B_HEADS = 8
B_PATTERNS = ((128, 1), (512, 4), (2048, 16))
C_HEADS = 16
D_HEADS = 16
D_Q_RANK = 512
D_KV_RANK = 256
D_NOPE = 64
D_ROPE = 32
D_V = 64
ROPE_BASE = 10000.0
D_FF = 4 * D_MODEL
LN_EPS = 1e-5
RMS_EPS = 1e-6
ALPHA = (2 * DEPTH) ** 0.25
BETA = (8 * DEPTH) ** -0.25
N_EVEN = (DEPTH + 1) // 2
N_ODD = DEPTH // 2
A_Q_W = A_Q_HEADS * HEAD_DIM
A_KV_W = A_KV_HEADS * HEAD_DIM
B_W = B_HEADS * HEAD_DIM
EVEN_IN = A_Q_W + 2 * A_KV_W + 3 * B_W * len(B_PATTERNS)
EVEN_OUT = A_Q_W + B_W
C_W = C_HEADS * HEAD_DIM
ODD_IN = 3 * C_W + D_Q_RANK + D_KV_RANK + D_ROPE
ODD_OUT = C_W + D_HEADS * D_V

kernel_name = 'hybrid_swa_dilated_stickbreak_mla_deepnorm'


def layer_norm(x, g, b):
    xf = x.astype(jnp.float32)
    mu = jnp.mean(xf, axis=-1, keepdims=True)
    xc = xf - mu
    var = jnp.mean(xc * xc, axis=-1, keepdims=True)
    y = xc * lax.rsqrt(var + LN_EPS) * g.astype(jnp.float32) + b.astype(jnp.float32)
    return y.astype(x.dtype)


def rms_norm(x, g):
    xf = x.astype(jnp.float32)
    y = xf * lax.rsqrt(jnp.mean(xf * xf, axis=-1, keepdims=True) + RMS_EPS) * g.astype(jnp.float32)
    return y.astype(x.dtype)


def alibi_slopes(n):
    return 2.0 ** (-8.0 * jnp.arange(1, n + 1, dtype=jnp.float32) / n)


def banded_attention(q, k, v, n_back, dist_scale, slopes, sinks=None):
    bsz, n, kh, g, dh = q.shape
    nb = -(-n // BLOCK)
    n_prev = -(-n_back // BLOCK)
    pad = nb * BLOCK - n
    qb = jnp.pad(q, ((0, 0), (0, pad), (0, 0), (0, 0), (0, 0))).reshape(bsz, nb, BLOCK, kh, g, dh)
    kv_pad = ((0, 0), (n_prev * BLOCK, pad), (0, 0), (0, 0))
    kp = jnp.pad(k, kv_pad).reshape(bsz, nb + n_prev, BLOCK, kh, dh)
    vp = jnp.pad(v, kv_pad).reshape(bsz, nb + n_prev, BLOCK, kh, dh)
    kb = jnp.concatenate([kp[:, p:p + nb] for p in range(n_prev + 1)], axis=2)
    vb = jnp.concatenate([vp[:, p:p + nb] for p in range(n_prev + 1)], axis=2)
    scores = jnp.einsum('bnqkgd,bnskd->bnkgqs', qb, kb).astype(jnp.float32) * (1.0 / math.sqrt(dh))
    n_keys = (n_prev + 1) * BLOCK
    qpos = jnp.arange(BLOCK)
    spos = jnp.arange(n_keys)
    rel = n_prev * BLOCK + qpos[:, None] - spos[None, :]
    key_idx = (jnp.arange(nb)[:, None] - n_prev) * BLOCK + spos[None, :]
    valid = (rel >= 0)[None] & (rel <= n_back)[None] & (key_idx >= 0)[:, None, :]
    bias = -slopes.astype(jnp.float32)[:, :, None, None] * (rel * dist_scale).astype(jnp.float32)
    scores = jnp.where(valid[None, :, None, None], scores + bias[None, None], -jnp.inf)
    m = jnp.max(scores, axis=-1)
    if sinks is not None:
        sink = sinks.astype(jnp.float32)[None, None, :, :, None]
        m = jnp.maximum(m, sink)
    p = jnp.exp(scores - m[..., None])
    denom = jnp.sum(p, axis=-1)
    if sinks is not None:
        denom = denom + jnp.exp(sink - m)
    out = jnp.einsum('bnkgqs,bnskd->bnqkgd', p.astype(v.dtype), vb).astype(jnp.float32)
    out = out / jnp.moveaxis(denom, -1, 2)[..., None]
    lse = jnp.moveaxis(m + jnp.log(denom), -1, 2)
    out = out.reshape(bsz, nb * BLOCK, kh, g, dh)[:, :n].astype(q.dtype)
    lse = lse.reshape(bsz, nb * BLOCK, kh, g)[:, :n]
    return out, lse


def to_strided(t, d):
    b, s = t.shape[:2]
    rest = t.shape[2:]
    t = jnp.moveaxis(t.reshape(b, s // d, d, *rest), 2, 1)
    return t.reshape(b * d, s // d, *rest)


def from_strided(t, b):
    bd, n = t.shape[:2]
    d = bd // b
    rest = t.shape[2:]
    t = jnp.moveaxis(t.reshape(b, d, n, *rest), 1, 2)
    return t.reshape(b, n * d, *rest)


def even_mixer(x, w_in, sinks, w_out):
    bsz, seq, _ = x.shape
    h = jnp.einsum('bsd,de->bse', x, w_in)
    qa, ka, va, hb = jnp.split(h, [A_Q_W, A_Q_W + A_KV_W, A_Q_W + 2 * A_KV_W], axis=-1)
    qa = qa.reshape(bsz, seq, A_KV_HEADS, A_GROUP, HEAD_DIM)
    ka = ka.reshape(bsz, seq, A_KV_HEADS, HEAD_DIM)
    va = va.reshape(bsz, seq, A_KV_HEADS, HEAD_DIM)
    oa, _ = banded_attention(qa, ka, va, A_WINDOW - 1, 1,
                             alibi_slopes(A_Q_HEADS).reshape(A_KV_HEADS, A_GROUP),
                             sinks.reshape(A_KV_HEADS, A_GROUP))
    slopes_b = alibi_slopes(B_HEADS).reshape(B_HEADS, 1)
    outs, lses = [], []
    for gi, (window, dil) in enumerate(B_PATTERNS):
        blk = hb[..., gi * 3 * B_W:(gi + 1) * 3 * B_W].reshape(bsz, seq, 3, B_HEADS, HEAD_DIM)
        qb, kb, vb = blk[:, :, 0], blk[:, :, 1], blk[:, :, 2]
        o, lse = banded_attention(to_strided(qb[:, :, :, None], dil), to_strided(kb, dil),
                                  to_strided(vb, dil), window // dil, dil, slopes_b)
        outs.append(from_strided(o[:, :, :, 0], bsz))
        lses.append(from_strided(lse[:, :, :, 0], bsz))
    mix = jax.nn.softmax(jnp.stack(lses), axis=0)
    ob = jnp.einsum('gbsh,gbshd->bshd', mix, jnp.stack(outs).astype(jnp.float32)).astype(x.dtype)
    y = jnp.concatenate([oa.reshape(bsz, seq, A_Q_W), ob.reshape(bsz, seq, B_W)], axis=-1)
    return jnp.einsum('bse,ed->bsd', y, w_out)


def stick_breaking_attention(q, k, v):
    bsz, seq, nh, dh = q.shape
    scale = 1.0 / math.sqrt(dh)
    outs = []
    for i in range(seq // BLOCK):
        lo, hi = i * BLOCK, (i + 1) * BLOCK
        z = jnp.einsum('bqhd,bshd->bhqs', q[:, lo:hi], k[:, :hi]).astype(jnp.float32) * scale
        strict = jnp.arange(hi)[None, :] < (lo + jnp.arange(BLOCK))[:, None]
        log_beta = jax.nn.log_sigmoid(z)
        log_keep = jnp.where(strict, jax.nn.log_sigmoid(-z), 0.0)
        after = lax.cumsum(log_keep, axis=3, reverse=True) - log_keep
        w = jnp.where(strict, jnp.exp(log_beta + after), 0.0)
        outs.append(jnp.einsum('bhqs,bshd->bqhd', w.astype(v.dtype), v[:, :hi]))
    return jnp.concatenate(outs, axis=1)


def apply_rope(x, cos, sin):
    half = x.shape[-1] // 2
    shape = (1, cos.shape[0]) + (1,) * (x.ndim - 3) + (half,)
    c = cos.reshape(shape).astype(x.dtype)
    s = sin.reshape(shape).astype(x.dtype)
    x1, x2 = x[..., :half], x[..., half:]
    return jnp.concatenate([x1 * c - x2 * s, x1 * s + x2 * c], axis=-1)


def mla_attention(q_nope, q_rope, k_nope, k_rope, v):
    seq = q_nope.shape[1]
    scale = 1.0 / math.sqrt(D_NOPE + D_ROPE)
    outs = []
    for i in range(seq // BLOCK):
        lo, hi = i * BLOCK, (i + 1) * BLOCK
        s = (jnp.einsum('bqhd,bshd->bhqs', q_nope[:, lo:hi], k_nope[:, :hi]).astype(jnp.float32)
             + jnp.einsum('bqhr,bsr->bhqs', q_rope[:, lo:hi], k_rope[:, :hi]).astype(jnp.float32)) * scale
        causal = jnp.arange(hi)[None, :] <= (lo + jnp.arange(BLOCK))[:, None]
        p = jax.nn.softmax(jnp.where(causal, s, -jnp.inf), axis=-1)
        outs.append(jnp.einsum('bhqs,bshd->bqhd', p.astype(v.dtype), v[:, :hi]))
    return jnp.concatenate(outs, axis=1)


def odd_mixer(x, w_in, q_norm_g, kv_norm_g, w_uq, w_ukv, w_out):
    bsz, seq, _ = x.shape
    h = jnp.einsum('bsd,de->bse', x, w_in)
    qc, kc, vc, cq, ckv, kr = jnp.split(
        h, [C_W, 2 * C_W, 3 * C_W, 3 * C_W + D_Q_RANK, 3 * C_W + D_Q_RANK + D_KV_RANK], axis=-1)
    oc = stick_breaking_attention(qc.reshape(bsz, seq, C_HEADS, HEAD_DIM),
                                  kc.reshape(bsz, seq, C_HEADS, HEAD_DIM),
                                  vc.reshape(bsz, seq, C_HEADS, HEAD_DIM))
    q = jnp.einsum('bsr,re->bse', rms_norm(cq, q_norm_g), w_uq).reshape(bsz, seq, D_HEADS, D_NOPE + D_ROPE)
    kv = jnp.einsum('bsr,re->bse', rms_norm(ckv, kv_norm_g), w_ukv).reshape(bsz, seq, D_HEADS, D_NOPE + D_V)
    q_nope, q_rope = q[..., :D_NOPE], q[..., D_NOPE:]
    k_nope, v = kv[..., :D_NOPE], kv[..., D_NOPE:]
    inv_freq = ROPE_BASE ** (-jnp.arange(0, D_ROPE, 2, dtype=jnp.float32) / D_ROPE)
    ang = jnp.arange(seq, dtype=jnp.float32)[:, None] * inv_freq[None, :]
    cos, sin = jnp.cos(ang), jnp.sin(ang)
    od = mla_attention(q_nope, apply_rope(q_rope, cos, sin), k_nope, apply_rope(kr, cos, sin), v)
    y = jnp.concatenate([oc.reshape(bsz, seq, C_W), od.reshape(bsz, seq, D_HEADS * D_V)], axis=-1)
    return jnp.einsum('bse,ed->bsd', y, w_out)


def sqrelu_mlp(x, w1, w2):
    hid = jax.nn.relu(jnp.einsum('bsd,df->bsf', x, w1))
    return jnp.einsum('bsf,fd->bsd', hid * hid, w2)


def setup_inputs(seed: int = 0) -> dict:
    key = jax.random.key(seed)
    ks = jax.random.split(key, 16)
    f32 = jnp.float32

    def normal(k, shape, scale):
        return jax.random.normal(k, shape, f32) * scale

    x = normal(ks[0], (BATCH, SEQ, D_MODEL), 1.0)
    b_group_col = jnp.concatenate([jnp.ones(2 * B_W, f32), jnp.full((B_W,), BETA, f32)])
    even_col = jnp.concatenate([jnp.ones(A_Q_W + A_KV_W, f32), jnp.full((A_KV_W,), BETA, f32)]
                               + [b_group_col] * len(B_PATTERNS))
    even_w_in = normal(ks[1], (N_EVEN, D_MODEL, EVEN_IN), D_MODEL ** -0.5) * even_col
    even_sinks = 1.0 + normal(ks[2], (N_EVEN, A_Q_HEADS), 1.0)
    even_w_out = normal(ks[3], (N_EVEN, EVEN_OUT, D_MODEL), BETA * EVEN_OUT ** -0.5)
    odd_col = jnp.concatenate([jnp.ones(2 * C_W, f32), jnp.full((C_W,), BETA, f32),
                               jnp.ones(D_Q_RANK + D_KV_RANK + D_ROPE, f32)])
    odd_w_in = normal(ks[4], (N_ODD, D_MODEL, ODD_IN), D_MODEL ** -0.5) * odd_col
    odd_q_norm_g = 1.0 + normal(ks[5], (N_ODD, D_Q_RANK), 0.02)
    odd_kv_norm_g = 1.0 + normal(ks[6], (N_ODD, D_KV_RANK), 0.02)
    odd_w_uq = normal(ks[7], (N_ODD, D_Q_RANK, D_HEADS * (D_NOPE + D_ROPE)), D_Q_RANK ** -0.5)
    ukv_col = jnp.tile(jnp.concatenate([jnp.ones(D_NOPE, f32), jnp.full((D_V,), BETA, f32)]), D_HEADS)
    odd_w_ukv = normal(ks[8], (N_ODD, D_KV_RANK, D_HEADS * (D_NOPE + D_V)), D_KV_RANK ** -0.5) * ukv_col
    odd_w_out = normal(ks[9], (N_ODD, ODD_OUT, D_MODEL), BETA * ODD_OUT ** -0.5)
    ln1_g = 1.0 + normal(ks[10], (DEPTH, D_MODEL), 0.02)
    ln1_b = normal(ks[11], (DEPTH, D_MODEL), 0.02)
    mlp_w1 = normal(ks[12], (DEPTH, D_MODEL, D_FF), D_MODEL ** -0.5)
    mlp_w2 = normal(ks[13], (DEPTH, D_FF, D_MODEL), BETA * D_FF ** -0.5)
    ln2_g = 1.0 + normal(ks[14], (DEPTH, D_MODEL), 0.02)
    ln2_b = normal(ks[15], (DEPTH, D_MODEL), 0.02)
    return {'x': x, 'even_w_in': even_w_in, 'even_sinks': even_sinks, 'even_w_out': even_w_out,
            'odd_w_in': odd_w_in, 'odd_q_norm_g': odd_q_norm_g, 'odd_kv_norm_g': odd_kv_norm_g,
            'odd_w_uq': odd_w_uq, 'odd_w_ukv': odd_w_ukv, 'odd_w_out': odd_w_out,
            'ln1_g': ln1_g, 'ln1_b': ln1_b, 'mlp_w1': mlp_w1, 'mlp_w2': mlp_w2,
            'ln2_g': ln2_g, 'ln2_b': ln2_b}


def reference(x, even_w_in, even_sinks, even_w_out, odd_w_in, odd_q_norm_g, odd_kv_norm_g,
              odd_w_uq, odd_w_ukv, odd_w_out, ln1_g, ln1_b, mlp_w1, mlp_w2, ln2_g, ln2_b):
    for layer in range(DEPTH):
        j = layer // 2
        if layer % 2 == 0:
            mixed = even_mixer(x, even_w_in[j], even_sinks[j], even_w_out[j])
        else:
            mixed = odd_mixer(x, odd_w_in[j], odd_q_norm_g[j], odd_kv_norm_g[j],
                              odd_w_uq[j], odd_w_ukv[j], odd_w_out[j])
        x = layer_norm(ALPHA * x + mixed, ln1_g[layer], ln1_b[layer])
        x = layer_norm(ALPHA * x + sqrelu_mlp(x, mlp_w1[layer], mlp_w2[layer]), ln2_g[layer], ln2_b[layer])
    return x
```

```python
import contextlib
import math
import numpy as np
import concourse.bass as bass
import concourse.mybir as mybir
from concourse.bass_utils import run_bass_kernel_spmd

F32 = mybir.dt.float32
BF16 = mybir.dt.bfloat16
I32 = mybir.dt.int32
AF = mybir.ActivationFunctionType
ALU = mybir.AluOpType
AX = mybir.AxisListType

S = 2048
D = 2048
NT = 16
DFF = 8192
ALPHA = 4.0 ** 0.25
LN_EPS = 1e-5
RMS_EPS = 1e-6
BIG = 1.0e9


class Res:
    __slots__ = ("name", "w", "r")

    def __init__(self, name=""):
        self.name = name
        self.w = None
        self.r = {}


class Buf:
    __slots__ = ("ap", "res", "sem")

    def __init__(self, ap, res, sem=None):
        self.ap = ap
        self.res = res
        self.sem = sem


class Ring:
    def __init__(self, items):
        self.items = items
        self.i = 0

    def next(self):
        it = self.items[self.i % len(self.items)]
        self.i += 1
        return it


class Sched:
    ENG = ("pe", "act", "dve", "pool", "sp")

    def __init__(self, nc, stack):
        self.nc = nc
        self.stack = stack
        self.prog = {e: [] for e in self.ENG}
        self.sem = {}
        self.cnt = {}
        self.known = {e: {} for e in self.ENG}
        self.free_dsems = []
        self.used_dsems = []
        self.ndsem = 0
        for e in self.ENG:
            self.newsem("E_" + e)

    def newsem(self, name):
        self.sem[name] = self.stack.enter_context(self.nc.semaphore(name))
        self.cnt[name] = 0
        return name

    def dsem(self):
        if self.free_dsems:
            n = self.free_dsems.pop()
        else:
            n = self.newsem("D%d" % self.ndsem)
            self.ndsem += 1
        self.used_dsems.append(n)
        return n

    def _deps(self, eng, reads, writes):
        need = {}

        def add(ev):
            if ev is None:
                return
            sm, v = ev
            if need.get(sm, 0) < v:
                need[sm] = v
        for r in reads:
            add(r.w)
        for w in writes:
            add(w.w)
            for sm, v in w.r.items():
                add((sm, v))
        kn = self.known[eng]
        for sm, v in need.items():
            if eng == "pe" and sm == "E_pe":
                continue
            if kn.get(sm, 0) < v:
                kn[sm] = v
                self.prog[eng].append(("wait", sm, v))

    def _commit(self, ev, reads, writes):
        sm, v = ev
        for r in reads:
            if r.r.get(sm, 0) < v:
                r.r[sm] = v
        for w in writes:
            w.w = ev
            w.r = {}

    def op(self, eng, fn, reads=(), writes=()):
        self._deps(eng, reads, writes)
        sm = "E_" + eng
        self.cnt[sm] += 1
        ev = (sm, self.cnt[sm])
        self.prog[eng].append(("op", fn, sm, 1))
        self._commit(ev, reads, writes)

    def dma(self, eng, out, in_, sem, reads=(), writes=()):
        self._deps(eng, reads, writes)
        self.cnt[sem] += 16
        ev = (sem, self.cnt[sem])
        self.prog[eng].append(("op", lambda e: e.dma_start(out=out, in_=in_), sem, 16))
        self._commit(ev, reads, writes)

    def barrier(self):
        for e in self.ENG:
            kn = self.known[e]
            for sm, v in self.cnt.items():
                if v > 0 and kn.get(sm, 0) < v:
                    kn[sm] = v
                    self.prog[e].append(("wait", sm, v))
        self.free_dsems.extend(self.used_dsems)
        self.used_dsems = []

    def emit(self):
        nc = self.nc

        def replay(name):
            def f(eng):
                for it in self.prog[name]:
                    if it[0] == "wait":
                        eng.wait_ge(self.sem[it[1]], it[2])
                    else:
                        it[1](eng).then_inc(self.sem[it[2]], it[3])
            return f

        with nc.Block() as block:
            block.tensor(replay("pe"))
            block.scalar(replay("act"))
            block.vector(replay("dve"))
            block.gpsimd(replay("pool"))
            block.sync(replay("sp"))


def alibi(n):
    return [2.0 ** (-8.0 * (i + 1) / n) for i in range(n)]


def build(dbg=()):
    nc = bass.Bass("TRN2", target_bir_lowering=False)

    def din(name, shape):
        return nc.dram_tensor(name, list(shape), F32, kind="ExternalInput").ap()

    x_in = din("x", [S, D])
    e_win = din("even_w_in", [D, 5888])
    e_sinks = din("even_sinks", [1, 16])
    e_wout = din("even_w_out", [1536, D])
    o_win = din("odd_w_in", [D, 3872])
    o_qg = din("odd_q_norm_g", [1, 512])
    o_kvg = din("odd_kv_norm_g", [1, 256])
    o_wuq = din("odd_w_uq", [512, 1536])
    o_wukv = din("odd_w_ukv", [256, 2048])
    o_wout = din("odd_w_out", [D, D])
    ln1_g = din("ln1_g", [2, D])
    ln1_b = din("ln1_b", [2, D])
    ln2_g = din("ln2_g", [2, D])
    ln2_b = din("ln2_b", [2, D])
    w1_in = din("mlp_w1", [2, D, DFF])
    w2_in = din("mlp_w2", [2, DFF, D])
    out_d = nc.dram_tensor("out", [S, D], F32, kind="ExternalOutput").ap()

    def dscr(name, shape, dt):
        kind = "ExternalOutput" if name in dbg else "Internal"
        return nc.dram_tensor(name, list(shape), dt, kind=kind).ap()

    w1b = dscr("w1b", [2, D, DFF], BF16)
    w2b = dscr("w2b", [2, DFF, D], BF16)
    xs1 = dscr("xs1", [S, D], F32)
    xs2 = dscr("xs2", [S, D], F32)
    QA_d = dscr("QA_d", [1024, S], BF16)
    KA_d = dscr("KA_d", [128, S], BF16)
    VA_d = dscr("VA_d", [S, 128], BF16)
    QB_d = [dscr("QB%d_d" % g, [512, S], BF16) for g in range(3)]
    KB_d = [dscr("KB%d_d" % g, [512, S], BF16) for g in range(3)]
    VB_d = [dscr("VB%d_d" % g, [S, 512], BF16) for g in range(3)]
    OB_d = [dscr("OB%d_d" % g, [S, 512], F32) for g in range(3)]
    LSE_d = [dscr("LSE%d_d" % g, [S, 8], F32) for g in range(3)]
    Y0_d = dscr("Y0_d", [S, 1536], BF16)
    QC_d = dscr("QC_d", [1024, S], BF16)
    KC_d = dscr("KC_d", [1024, S], BF16)
    VC_d = dscr("VC_d", [S, 1024], BF16)
    CQ_d = dscr("CQ_d", [S, 768], F32)
    QD_d = dscr("QD_d", [16, 96, S], BF16)
    KD_d = dscr("KD_d", [16, 96, S], BF16)
    VD_d = dscr("VD_d", [S, 1024], BF16)
    Y1_d = dscr("Y1_d", [S, 2048], BF16)

    with contextlib.ExitStack() as stack:
        s = Sched(nc, stack)
        ARENA_ELEMS = 100 * 1024
        arena = nc.alloc_sbuf_tensor("arena", [128, ARENA_ELEMS], BF16)
        aoff = [0]
        persist_end = [0]

        def T(shape, dt, dma=False, name=""):
            esz = 2 if dt == BF16 else 4
            nel = int(np.prod(shape[1:]))
            nb16 = (nel * esz + 63) // 64 * 32
            assert aoff[0] + nb16 <= ARENA_ELEMS, "arena overflow %s %d" % (name, aoff[0] + nb16)
            v = arena[:, aoff[0]:aoff[0] + nel * esz // 2]
            aoff[0] += nb16
            if dt != BF16:
                v = v.bitcast(dt)
            if len(shape) == 3:
                v = v.rearrange("p (a b) -> p a b", a=shape[1])
            elif len(shape) == 4:
                v = v.rearrange("p (a b c) -> p a b c", a=shape[1], b=shape[2])
            if shape[0] != 128:
                v = v[0:shape[0]]
            return Buf(v, Res(name), s.dsem() if dma else None)

        def TR(n, shape, dt, dma=False, name=""):
            return Ring([T(shape, dt, dma, name) for _ in range(n)])

        def phase_end():
            s.barrier()
            aoff[0] = persist_end[0]

        PB = [Buf(nc.alloc_psum_tensor("pb%d" % i, [128, 512], F32)[:], Res("pb%d" % i)) for i in range(8)]

        def rs(bufs):
            return [b.res for b in bufs]

        def mm(out, lhsT, rhs, start, stop, rd, wr):
            s.op("pe", lambda e: e.matmul(out, lhsT, rhs, start=start, stop=stop), rs(rd), rs(wr))

        def act(out, in_, func, rd, wr, bias=None, scale=None, accum=None, eng="act"):
            kw = {}
            if bias is not None:
                kw["bias"] = bias
            if scale is not None:
                kw["scale"] = scale
            if accum is not None:
                kw["accum_out"] = accum
            s.op(eng, lambda e: e.activation(out=out, in_=in_, func=func, **kw), rs(rd), rs(wr))

        def vop(eng, meth, rd, wr, **kw):
            s.op(eng, lambda e: getattr(e, meth)(**kw), rs(rd), rs(wr))

        def cp(eng, out, in_, rd, wr):
            if eng == "act":
                s.op("act", lambda e: e.copy(out=out, in_=in_), rs(rd), rs(wr))
            else:
                s.op(eng, lambda e: e.tensor_copy(out=out, in_=in_), rs(rd), rs(wr))

        def dma(eng, out, in_, buf, rd=(), wr=()):
            s.dma(eng, out, in_, buf.sem, rs(rd), rs(wr))

        ident = T([128, 128], BF16, name="ident")
        Ustr = T([128, 128], F32, name="Ustr")
        ones = T([128, 128], F32, name="ones")
        mC01 = T([128, 128], F32, name="mC01")
        mC01b = T([128, 128], BF16, name="mC01b")
        McD = T([128, 128], F32, name="McD")
        RA = T([128, 256], F32, name="RA")
        RB = T([128, 256], F32, name="RB")
        sinkt = T([128, 16], F32, dma=True, name="sink")
        sink8 = T([128, 16], F32, name="sink8")
        tmpi = T([128, 256], I32, name="tmpi")
        tmpf = T([128, 256], F32, name="tmpf")
        tmpg = T([128, 256], F32, name="tmpg")
        tmph = T([128, 256], F32, name="tmph")
        vop("pool", "iota", [], [tmpi], out=tmpi.ap[:, 0:128], pattern=[[-1, 128]], base=0, channel_multiplier=1)
        cp("dve", tmpf.ap[:, 0:128], tmpi.ap[:, 0:128], [tmpi], [tmpf])
        vop("dve", "tensor_single_scalar", [tmpf], [ident], out=ident.ap, in_=tmpf.ap[:, 0:128], scalar=0.0, op=ALU.is_equal)
        vop("dve", "tensor_single_scalar", [tmpf], [Ustr], out=Ustr.ap, in_=tmpf.ap[:, 0:128], scalar=0.0, op=ALU.is_gt)
        vop("dve", "tensor_single_scalar", [tmpf], [mC01], out=mC01.ap, in_=tmpf.ap[:, 0:128], scalar=0.0, op=ALU.is_lt)
        cp("dve", mC01b.ap, mC01.ap, [mC01], [mC01b])
        vop("dve", "memset", [], [ones], ap=ones.ap, constant=1.0)
        vop("dve", "tensor_scalar", [tmpf], [McD], out=McD.ap, in0=tmpf.ap[:, 0:128], scalar1=0.0, scalar2=1.0,
            op0=ALU.is_ge, op1=ALU.subtract)
        vop("dve", "tensor_single_scalar", [McD], [McD], out=McD.ap, in_=McD.ap, scalar=BIG, op=ALU.mult)
        vop("pool", "iota", [tmpf], [tmpi], out=tmpi.ap, pattern=[[-1, 256]], base=128, channel_multiplier=1)
        cp("dve", tmpf.ap, tmpi.ap, [tmpi], [tmpf])
        for Rt, nb in ((RA, 127.0), (RB, 128.0)):
            vop("dve", "tensor_single_scalar", [tmpf], [tmpg], out=tmpg.ap, in_=tmpf.ap, scalar=0.0, op=ALU.is_ge)
            vop("dve", "tensor_single_scalar", [tmpf], [tmph], out=tmph.ap, in_=tmpf.ap, scalar=nb, op=ALU.is_le)
            vop("dve", "tensor_tensor", [tmpg, tmph], [tmpg], out=tmpg.ap, in0=tmpg.ap, in1=tmph.ap, op=ALU.mult)
            vop("dve", "tensor_tensor", [tmpg, tmpf], [tmph], out=tmph.ap, in0=tmpg.ap, in1=tmpf.ap, op=ALU.mult)
            vop("dve", "tensor_scalar", [tmpg], [tmpg], out=tmpg.ap, in0=tmpg.ap, scalar1=1.0, scalar2=BIG,
                op0=ALU.subtract, op1=ALU.mult)
            vop("dve", "tensor_tensor", [tmpg, tmph], [Rt], out=Rt.ap, in0=tmpg.ap, in1=tmph.ap, op=ALU.subtract)
        dma("sp", sinkt.ap, e_sinks.partition_broadcast(128), sinkt, [], [sinkt])
        vop("dve", "tensor_single_scalar", [sinkt], [sink8], out=sink8.ap, in_=sinkt.ap, scalar=8.0, op=ALU.mult)
        persist_end[0] = aoff[0]

        wcast = [Buf(None, Res("wc%d" % l), s.newsem("WC%d" % l)) for l in range(2)]
        for l in range(2):
            for r0 in range(0, D, 256):
                dma("pool", w1b[l, r0:r0 + 256, :], w1_in[l, r0:r0 + 256, :], wcast[l], [], [wcast[l]])
            for r0 in range(0, DFF, 1024):
                dma("pool", w2b[l, r0:r0 + 1024, :], w2_in[l, r0:r0 + 1024, :], wcast[l], [], [wcast[l]])
        phase_end()

        evq = [0]

        def ev_eng():
            evq[0] += 1
            return "act" if evq[0] % 2 else "dve"

        def load_T(src, ncol, dst, is_f32, keep=None):
            kc = ncol // 128
            ld = TR(2, [128, ncol], F32 if is_f32 else BF16, dma=True, name="ldT")
            cb = TR(2, [128, ncol], BF16, name="cbT") if is_f32 else None
            pbr = Ring([PB[6], PB[7]])
            for t in range(NT):
                lt = ld.next()
                dma("sp", lt.ap, src[t * 128:(t + 1) * 128, :], lt, [], [lt])
                if is_f32:
                    ct = cb.next()
                    cp("pool", ct.ap, lt.ap, [lt], [ct])
                else:
                    ct = lt
                for k0 in range(0, kc, 8):
                    kn = min(8, kc - k0)
                    pb = pbr.next()
                    pv = pb.ap.bitcast(BF16)
                    for j in range(kn):
                        k = k0 + j
                        s.op("pe", (lambda pv=pv, j=j, ct=ct, k=k: (lambda e: e.transpose(pv[:, j * 128:(j + 1) * 128], ct.ap[:, k * 128:(k + 1) * 128], ident.ap)))(),
                             rs([ct, ident]), rs([pb]))
                    cp(ev_eng(), dst.ap[:, k0:k0 + kn, t * 128:(t + 1) * 128],
                       pv[:, 0:kn * 128].rearrange("p (k t) -> p k t", k=kn), [pb], [dst])

        def wload(dst, wsrc, kc, ncol):
            dma("pool", dst.ap[:, 0:kc, 0:ncol], wsrc.rearrange("(k p) e -> p k e", p=128), dst, [], [dst])

        def proj_F(xT, kc, wsrc, ncol, dst, dil, wring, stg, stg_rows=128):
            wt = wring.next()
            wload(wt, wsrc, kc, ncol)
            pbr = Ring(PB[0:6])
            for c in range(ncol // 128):
                st = stg.next()
                for tg in range(4):
                    pb = pbr.next()
                    for k in range(kc):
                        mm(pb.ap, wt.ap[:, k, c * 128:(c + 1) * 128], xT.ap[:, k, tg * 512:(tg + 1) * 512],
                           k == 0, k == kc - 1, [wt, xT], [pb])
                    if dil == 1:
                        cp(ev_eng(), st.ap[:, tg * 512:(tg + 1) * 512], pb.ap, [pb], [st])
                    else:
                        na = 512 // dil
                        cp(ev_eng(), st.ap.rearrange("p (r a) -> p a r", r=dil)[:, tg * na:(tg + 1) * na, :],
                           pb.ap.rearrange("p (a r) -> p a r", r=dil), [pb], [st])
                dma("sp", dst[c * 128:(c + 1) * 128, :], st.ap, st, [st], [])

        def proj_T(xT, kc, wsrc, ncol, dst, dst_dt, wring, stg):
            wt = wring.next()
            wload(wt, wsrc, kc, ncol)
            pbr = Ring(PB[0:6])
            for t in range(NT):
                pb = pbr.next()
                for k in range(kc):
                    mm(pb.ap[:, 0:ncol], xT.ap[:, k, t * 128:(t + 1) * 128], wt.ap[:, k, 0:ncol],
                       k == 0, k == kc - 1, [wt, xT], [pb])
                st = stg.next()
                cp(ev_eng(), st.ap[:, 0:ncol], pb.ap[:, 0:ncol], [pb], [st])
                dma("sp", dst[t * 128:(t + 1) * 128, :], st.ap[:, 0:ncol], st, [st], [])

        def layer_norm_tile(z, gt, bt, stat):
            st6 = stat.ap[:, 0:24].rearrange("p (c s) -> p c s", c=4)
            for c in range(4):
                vop("dve", "bn_stats", [z], [stat], out=st6[:, c, :], in_=z.ap[:, c * 512:(c + 1) * 512])
            mv = stat.ap[:, 24:26]
            vop("dve", "bn_aggr", [stat], [stat], out=mv, in_=st6)
            vop("dve", "tensor_single_scalar", [stat], [stat], out=stat.ap[:, 26:27], in_=stat.ap[:, 25:26], scalar=LN_EPS, op=ALU.add)
            act(stat.ap[:, 27:28], stat.ap[:, 26:27], AF.Sqrt, [stat], [stat])
            vop("dve", "reciprocal", [stat], [stat], out=stat.ap[:, 28:29], in_=stat.ap[:, 27:28])
            vop("dve", "tensor_scalar", [z, stat], [z], out=z.ap, in0=z.ap, scalar1=stat.ap[:, 24:25], scalar2=stat.ap[:, 28:29],
                op0=ALU.subtract, op1=ALU.mult)
            vop("pool", "tensor_tensor", [z, gt], [z], out=z.ap, in0=z.ap, in1=gt.ap, op=ALU.mult)
            vop("dve", "tensor_tensor", [z, bt], [z], out=z.ap, in0=z.ap, in1=bt.ap, op=ALU.add)

        def load_gb(g_src, b_src):
            gt = T([128, D], F32, dma=True, name="gam")
            bt = T([128, D], F32, dma=True, name="bet")
            dma("sp", gt.ap, g_src.partition_broadcast(128), gt, [], [gt])
            dma("sp", bt.ap, b_src.partition_broadcast(128), bt, [], [bt])
            return gt, bt

        def out_proj_ln(Yd, kc, wsrc, xres, g_src, b_src, dst):
            wt = T([128, kc, D], BF16, dma=True, name="wout")
            for k0 in range(0, kc, 4):
                dma("pool", wt.ap[:, k0:k0 + 4, :], wsrc[k0 * 128:(k0 + 4) * 128, :].rearrange("(k p) e -> p k e", p=128), wt, [], [wt])
            gt, bt = load_gb(g_src, b_src)
            yl = TR(2, [128, kc * 128], BF16, dma=True, name="yl")
            yT = TR(2, [128, kc, 128], BF16, name="yT")
            zr = TR(2, [128, D], F32, dma=True, name="z")
            stat = TR(2, [128, 32], F32, name="stat")
            pbt = Ring([PB[4], PB[5]])
            for t in range(NT):
                y = yl.next()
                dma("sp", y.ap, Yd[t * 128:(t + 1) * 128, :], y, [], [y])
                z = zr.next()
                dma("sp", z.ap, xres[t * 128:(t + 1) * 128, :], z, [], [z])
                yt = yT.next()
                for k0 in range(0, kc, 8):
                    kn = min(8, kc - k0)
                    pb = pbt.next()
                    pv = pb.ap.bitcast(BF16)
                    for j in range(kn):
                        k = k0 + j
                        s.op("pe", (lambda pv=pv, j=j, y=y, k=k: (lambda e: e.transpose(pv[:, j * 128:(j + 1) * 128], y.ap[:, k * 128:(k + 1) * 128], ident.ap)))(),
                             rs([y, ident]), rs([pb]))
                    cp("act", yt.ap[:, k0:k0 + kn, :], pv[:, 0:kn * 128].rearrange("p (k t) -> p k t", k=kn), [pb], [yt])
                for dt in range(4):
                    pb = PB[dt]
                    for k in range(kc):
                        mm(pb.ap, yt.ap[:, k, :], wt.ap[:, k, dt * 512:(dt + 1) * 512], k == 0, k == kc - 1, [yt, wt], [pb])
                    vop("dve", "scalar_tensor_tensor", [z, pb], [z], out=z.ap[:, dt * 512:(dt + 1) * 512],
                        in0=z.ap[:, dt * 512:(dt + 1) * 512], scalar=ALPHA, in1=pb.ap, op0=ALU.mult, op1=ALU.add)
                layer_norm_tile(z, gt, bt, stat.next())
                dma("sp", dst[t * 128:(t + 1) * 128, :], z.ap, z, [z], [])

        def mlp(l, xsrc, g_src, b_src, dst):
            gt, bt = load_gb(g_src, b_src)
            zb = T([128, 4, D], F32, dma=True, name="zb")
            zres = [Res("z%d" % i) for i in range(4)]
            xbr = TR(2, [128, D], BF16, name="xb")
            xT = T([128, 16, 512], BF16, name="xTm")
            hT = T([128, 64, 512], BF16, name="hT")
            w1r = TR(2, [128, 16, 256], BF16, dma=True, name="w1")
            w2r = TR(3, [128, 8, 512], BF16, dma=True, name="w2")
            hr = TR(2, [128, 512], F32, name="hrelu")
            stat = TR(2, [128, 32], F32, name="stat")
            zsem = [s.dsem() for _ in range(4)]
            pbt = Ring([PB[6], PB[7]])
            pbh = Ring(PB[0:6])
            for G in range(4):
                for tt in range(4):
                    t = G * 4 + tt
                    zt = Buf(zb.ap[:, tt, :], zres[tt], zsem[tt])
                    dma("sp", zt.ap, xsrc[t * 128:(t + 1) * 128, :], zt, [], [zt])
                    xb = xbr.next()
                    cp("pool", xb.ap, zt.ap, [zt], [xb])
                    for k0 in (0, 8):
                        pb = pbt.next()
                        pv = pb.ap.bitcast(BF16)
                        for j in range(8):
                            k = k0 + j
                            s.op("pe", (lambda pv=pv, j=j, xb=xb, k=k: (lambda e: e.transpose(pv[:, j * 128:(j + 1) * 128], xb.ap[:, k * 128:(k + 1) * 128], ident.ap)))(),
                                 rs([xb, ident]), rs([pb]))
                        cp("act", xT.ap[:, k0:k0 + 8, tt * 128:(tt + 1) * 128],
                           pv.rearrange("p (k t) -> p k t", k=8), [pb], [xT])
                for f2 in range(32):
                    w1 = w1r.next()
                    dma("sp", w1.ap, w1b[l, :, f2 * 256:(f2 + 1) * 256].rearrange("(k p) f -> p k f", p=128), w1, [wcast[l]], [w1])
                    for fi in range(2):
                        f = f2 * 2 + fi
                        pb = pbh.next()
                        for k in range(16):
                            mm(pb.ap, w1.ap[:, k, fi * 128:(fi + 1) * 128], xT.ap[:, k, :], k == 0, k == 15, [w1, xT], [pb])
                        h = hr.next()
                        act(h.ap, pb.ap, AF.Relu, [pb], [h])
                        vop("pool" if f % 2 else "dve", "tensor_tensor", [h], [hT], out=hT.ap[:, f, :], in0=h.ap, in1=h.ap, op=ALU.mult)
                for dt in range(4):
                    pbs = PB[0:4] if dt % 2 == 0 else PB[4:8]
                    for f8 in range(8):
                        w2 = w2r.next()
                        dma("sp", w2.ap, w2b[l, f8 * 1024:(f8 + 1) * 1024, dt * 512:(dt + 1) * 512].rearrange("(c p) d -> p c d", p=128),
                            w2, [wcast[l]], [w2])
                        for fi in range(8):
                            f = f8 * 8 + fi
                            for tt in range(4):
                                mm(pbs[tt].ap, hT.ap[:, f, tt * 128:(tt + 1) * 128], w2.ap[:, fi, :], f == 0, f == 63, [hT, w2], [pbs[tt]])
                    for tt in range(4):
                        zt = Buf(zb.ap[:, tt, :], zres[tt], zsem[tt])
                        vop("dve", "scalar_tensor_tensor", [zt, pbs[tt]], [zt], out=zt.ap[:, dt * 512:(dt + 1) * 512],
                            in0=zt.ap[:, dt * 512:(dt + 1) * 512], scalar=ALPHA, in1=pbs[tt].ap, op0=ALU.mult, op1=ALU.add)
                for tt in range(4):
                    t = G * 4 + tt
                    zt = Buf(zb.ap[:, tt, :], zres[tt], zsem[tt])
                    layer_norm_tile(zt, gt, bt, stat.next())
                    dma("sp", dst[t * 128:(t + 1) * 128, :], zt.ap, zt, [zt], [])

        def banded(Qd, Kd, kvmap, nh, Vd, nkv, dil, Rm, cvals, use_sink, out_mode, Yd=None, OBd=None, LSEd=None):
            L = S // dil
            nbpl = L // 128
            Vp = T([128, NT, nkv * 64], BF16, dma=True, name="Vp")
            for n in range(NT):
                r = (128 * n) // L
                a0 = (128 * n) % L
                st_ = r + dil * a0
                dma("sp", Vp.ap[:, n, :], Vd[st_:st_ + dil * 127 + 1:dil, :], Vp, [], [Vp])
            if out_mode == "A":
                Oall = T([128, NT, nh * 64], BF16, dma=True, name="Oall")
            else:
                Oall = T([128, NT, nh * 64], F32, dma=True, name="Oall")
                lse = T([128, NT, nh], F32, dma=True, name="lse")
            Qr = TR(2, [64, S], BF16, dma=True, name="Qh")
            Kr = TR(2, [64, S], BF16, dma=True, name="Kh")
            Tr = TR(2, [128, 256], F32, name="T")
            Pr = TR(2, [128, 256], BF16, name="P")
            PTr = TR(2, [128, 256], BF16, name="PT")
            str_ = TR(4, [128, 8], F32, name="st")
            pS = Ring([PB[0], PB[1]])
            pT = Ring([PB[2], PB[3]])
            pO = Ring([PB[4], PB[5]])
            lastkv = -1
            Kh = None
            for h in range(nh):
                Qh = Qr.next()
                dma("sp", Qh.ap, Qd[h * 64:(h + 1) * 64, :], Qh, [], [Qh])
                kv = kvmap(h)
                if kv != lastkv:
                    Kh = Kr.next()
                    dma("sp", Kh.ap, Kd[kv * 64:(kv + 1) * 64, :], Kh, [], [Kh])
                    lastkv = kv
                c_h = cvals[h]
                for n in range(NT):
                    hasprev = (n % nbpl) != 0
                    nk = 256 if hasprev else 128
                    ks = (n - 1) * 128 if hasprev else n * 128
                    Rv = Rm.ap[:, 0:256] if hasprev else Rm.ap[:, 128:256]
                    ps = pS.next()
                    mm(ps.ap[:, 0:nk], Qh.ap[:, n * 128:(n + 1) * 128], Kh.ap[:, ks:ks + nk], True, True, [Qh, Kh], [ps])
                    Tt = Tr.next()
                    vop("dve", "scalar_tensor_tensor", [ps, Rm], [Tt], out=Tt.ap[:, 0:nk], in0=Rv, scalar=c_h, in1=ps.ap[:, 0:nk],
                        op0=ALU.mult, op1=ALU.add)
                    st = str_.next()
                    vop("dve", "reduce_max", [Tt], [st], out=st.ap[:, 0:1], in_=Tt.ap[:, 0:nk], axis=AX.X)
                    if use_sink:
                        vop("dve", "tensor_tensor", [st, sink8], [st], out=st.ap[:, 0:1], in0=st.ap[:, 0:1], in1=sink8.ap[:, h:h + 1], op=ALU.max)
                    vop("dve", "tensor_single_scalar", [st], [st], out=st.ap[:, 1:2], in_=st.ap[:, 0:1], scalar=-0.125, op=ALU.mult)
                    Pt = Pr.next()
                    act(Pt.ap[:, 0:nk], Tt.ap[:, 0:nk], AF.Exp, [Tt, st], [Pt, st], bias=st.ap[:, 1:2], scale=0.125, accum=st.ap[:, 2:3])
                    if use_sink:
                        act(st.ap[:, 3:4], st.ap[:, 1:2], AF.Exp, [st, sinkt], [st], bias=sinkt.ap[:, h:h + 1], scale=1.0)
                        vop("dve", "tensor_tensor", [st], [st], out=st.ap[:, 2:3], in0=st.ap[:, 2:3], in1=st.ap[:, 3:4], op=ALU.add)
                    pt = pT.next()
                    ptv = pt.ap.bitcast(BF16)
                    nkb = nk // 128
                    for kb in range(nkb):
                        s.op("pe", (lambda ptv=ptv, kb=kb, Pt=Pt: (lambda e: e.transpose(ptv[:, kb * 128:(kb + 1) * 128], Pt.ap[:, kb * 128:(kb + 1) * 128], ident.ap)))(),
                             rs([Pt, ident]), rs([pt]))
                    PT = PTr.next()
                    cp("act", PT.ap[:, 0:nk], ptv[:, 0:nk], [pt], [PT])
                    po = pO.next()
                    for kb in range(nkb):
                        blk = ks // 128 + kb
                        mm(po.ap[:, 0:64], PT.ap[:, kb * 128:(kb + 1) * 128], Vp.ap[:, blk, kv * 64:(kv + 1) * 64],
                           kb == 0, kb == nkb - 1, [PT, Vp], [po])
                    vop("dve", "reciprocal", [st], [st], out=st.ap[:, 4:5], in_=st.ap[:, 2:3])
                    vop("dve", "tensor_scalar", [po, st], [Oall], out=Oall.ap[:, n, h * 64:(h + 1) * 64], in0=po.ap[:, 0:64],
                        scalar1=st.ap[:, 4:5], scalar2=None, op0=ALU.mult)
                    if out_mode == "B":
                        act(st.ap[:, 5:6], st.ap[:, 2:3], AF.Ln, [st], [st])
                        vop("dve", "tensor_tensor", [st], [lse], out=lse.ap[:, n, h:h + 1], in0=st.ap[:, 5:6], in1=st.ap[:, 1:2], op=ALU.subtract)
            if out_mode == "A":
                for n in range(NT):
                    dma("sp", Yd[n * 128:(n + 1) * 128, 0:nh * 64], Oall.ap[:, n, :], Oall, [Oall], [])
            else:
                for n in range(NT):
                    r = (128 * n) // L
                    a0 = (128 * n) % L
                    st_ = r + dil * a0
                    dma("sp", OBd[st_:st_ + dil * 127 + 1:dil, :], Oall.ap[:, n, :], Oall, [Oall], [])
                    dma("sp", LSEd[st_:st_ + dil * 127 + 1:dil, :], lse.ap[:, n, :], lse, [lse], [])

        def combine_B():
            ol = [TR(2, [128, 512], F32, dma=True, name="o%d" % g) for g in range(3)]
            ll = TR(2, [128, 3, 8], F32, dma=True, name="l")
            wk = TR(2, [128, 3, 8], F32, name="wk")
            sm = TR(2, [128, 16], F32, name="sm")
            yo = TR(2, [128, 512], BF16, dma=True, name="yo")
            for t in range(NT):
                og = [ol[g].next() for g in range(3)]
                lt = ll.next()
                for g in range(3):
                    dma("sp", og[g].ap, OB_d[g][t * 128:(t + 1) * 128, :], og[g], [], [og[g]])
                    dma("sp", lt.ap[:, g, :], LSE_d[g][t * 128:(t + 1) * 128, :], lt, [], [lt])
                m = sm.next()
                vop("dve", "tensor_tensor", [lt], [m], out=m.ap[:, 0:8], in0=lt.ap[:, 0, :], in1=lt.ap[:, 1, :], op=ALU.max)
                vop("dve", "tensor_tensor", [lt, m], [m], out=m.ap[:, 0:8], in0=m.ap[:, 0:8], in1=lt.ap[:, 2, :], op=ALU.max)
                w = wk.next()
                vop("dve", "tensor_tensor", [lt, m], [w], out=w.ap, in0=lt.ap, in1=m.ap[:, 0:8].unsqueeze(1).broadcast_to([128, 3, 8]), op=ALU.subtract)
                act(w.ap, w.ap, AF.Exp, [w], [w])
                vop("dve", "tensor_tensor", [w], [m], out=m.ap[:, 8:16], in0=w.ap[:, 0, :], in1=w.ap[:, 1, :], op=ALU.add)
                vop("dve", "tensor_tensor", [w, m], [m], out=m.ap[:, 8:16], in0=m.ap[:, 8:16], in1=w.ap[:, 2, :], op=ALU.add)
                vop("dve", "reciprocal", [m], [m], out=m.ap[:, 8:16], in_=m.ap[:, 8:16])
                vop("dve", "tensor_tensor", [w, m], [w], out=w.ap, in0=w.ap, in1=m.ap[:, 8:16].unsqueeze(1).broadcast_to([128, 3, 8]), op=ALU.mult)
                for g in range(3):
                    eng = "pool" if g == 1 else "dve"
                    vop(eng, "tensor_tensor", [og[g], w], [og[g]], out=og[g].ap.rearrange("p (h d) -> p h d", h=8),
                        in0=og[g].ap.rearrange("p (h d) -> p h d", h=8), in1=w.ap[:, g, :].unsqueeze(2).broadcast_to([128, 8, 64]), op=ALU.mult)
                vop("dve", "tensor_tensor", [og[0], og[1]], [og[0]], out=og[0].ap, in0=og[0].ap, in1=og[1].ap, op=ALU.add)
                y = yo.next()
                vop("dve", "tensor_tensor", [og[0], og[2]], [y], out=y.ap, in0=og[0].ap, in1=og[2].ap, op=ALU.add)
                dma("sp", Y0_d[t * 128:(t + 1) * 128, 1024:1536], y.ap, y, [y], [])

        xT = T([128, 16, S], BF16, name="xT")
        load_T(x_in, D, xT, True)
        wring = TR(2, [128, 16, 512], BF16, dma=True, name="wring")
        stgF = TR(2, [128, S], BF16, dma=True, name="stgF")
        stgT = TR(2, [128, 512], BF16, dma=True, name="stgT")
        proj_F(xT, 16, e_win[:, 0:512], 512, QA_d[0:512, :], 1, wring, stgF)
        proj_F(xT, 16, e_win[:, 512:1024], 512, QA_d[512:1024, :], 1, wring, stgF)
        proj_F(xT, 16, e_win[:, 1024:1152], 128, KA_d, 1, wring, stgF)
        proj_T(xT, 16, e_win[:, 1152:1280], 128, VA_d, BF16, wring, stgT)
        for g, dil in enumerate((1, 4, 16)):
            base = 1280 + g * 1536
            proj_F(xT, 16, e_win[:, base:base + 512], 512, QB_d[g], dil, wring, stgF)
            proj_F(xT, 16, e_win[:, base + 512:base + 1024], 512, KB_d[g], dil, wring, stgF)
            proj_T(xT, 16, e_win[:, base + 1024:base + 1536], 512, VB_d[g], BF16, wring, stgT)
        phase_end()
        slA = alibi(16)
        banded(QA_d, KA_d, lambda h: h // 8, 16, VA_d, 2, 1, RA, [8.0 * sl for sl in slA], True, "A", Yd=Y0_d)
        phase_end()
        slB = alibi(8)
        for g, dil in enumerate((1, 4, 16)):
            banded(QB_d[g], KB_d[g], lambda h: h, 8, VB_d[g], 8, dil, RB, [8.0 * sl * dil for sl in slB], False, "B",
                   OBd=OB_d[g], LSEd=LSE_d[g])
            phase_end()
        combine_B()
        phase_end()
        out_proj_ln(Y0_d, 12, e_wout, x_in, ln1_g[0:1, :], ln1_b[0:1, :], xs1)
        phase_end()
        mlp(0, xs1, ln2_g[0:1, :], ln2_b[0:1, :], xs2 if "stop0" not in dbg else out_d)
        phase_end()

        if "stop0" not in dbg:

            base_persist = persist_end[0]
            cosT = T([128, S], F32, name="cosT")
            sinT = T([128, S], F32, name="sinT")
            persist_end[0] = aoff[0]
            pidx = T([128, 2], I32, name="pidx")
            pf = T([128, 2], F32, name="pf")
            vop("pool", "iota", [], [pidx], out=pidx.ap[:, 0:1], pattern=[[0, 1]], base=0, channel_multiplier=1)
            vop("dve", "tensor_single_scalar", [pidx], [pidx], out=pidx.ap[:, 1:2], in_=pidx.ap[:, 0:1], scalar=15, op=ALU.bitwise_and)
            cp("dve", pf.ap[:, 0:1], pidx.ap[:, 1:2], [pidx], [pf])
            act(pf.ap[:, 1:2], pf.ap[:, 0:1], AF.Exp, [pf], [pf], scale=-math.log(10000.0) / 16.0)
            vop("dve", "tensor_single_scalar", [pf], [pf], out=pf.ap[:, 1:2], in_=pf.ap[:, 1:2], scalar=1.0 / (2 * math.pi), op=ALU.mult)
            tpi = T([128, S], I32, name="tpi")
            tpf = T([128, S], F32, name="tpf")
            tq = T([128, S], F32, name="tq")
            vop("pool", "iota", [], [tpi], out=tpi.ap, pattern=[[1, S]], base=0, channel_multiplier=0)
            cp("dve", tpf.ap, tpi.ap, [tpi], [tpf])
            for tab, offs in ((sinT, 0.0), (cosT, 0.25)):
                vop("dve", "tensor_scalar", [tpf, pf], [tq], out=tq.ap, in0=tpf.ap, scalar1=pf.ap[:, 1:2], scalar2=offs,
                    op0=ALU.mult, op1=ALU.add)
                cp("dve", tpi.ap, tq.ap, [tq], [tpi])
                cp("dve", tab.ap, tpi.ap, [tpi], [tab])
                vop("dve", "tensor_tensor", [tq, tab], [tq], out=tq.ap, in0=tq.ap, in1=tab.ap, op=ALU.subtract)
                vop("dve", "scalar_tensor_tensor", [tq], [tq], out=tq.ap, in0=tq.ap, scalar=0.5, in1=tq.ap,
                    op0=ALU.is_gt, op1=ALU.subtract)
                act(tab.ap, tq.ap, AF.Sin, [tq], [tab], scale=-2.0 * math.pi)
            phase_end()

            def rope_evac(ps_m, ps_r, st, tg, tmpr):
                ta = tmpr.next()
                tb = tmpr.next()
                cs = slice(tg * 512, (tg + 1) * 512)
                vop("dve", "tensor_tensor", [ps_r, sinT], [ta], out=ta.ap[64:96, :], in0=ps_r.ap[64:96, :], in1=sinT.ap[64:96, cs], op=ALU.mult)
                vop("dve", "tensor_tensor", [ps_m, cosT], [tb], out=tb.ap[64:96, :], in0=ps_m.ap[64:96, :], in1=cosT.ap[64:96, cs], op=ALU.mult)
                vop("pool", "tensor_tensor", [ta, tb], [st], out=st.ap[64:96, cs], in0=ta.ap[64:96, :], in1=tb.ap[64:96, :], op=ALU.add)

            xT = T([128, 16, S], BF16, name="xT1")
            load_T(xs2, D, xT, True)
            wring = TR(2, [128, 16, 512], BF16, dma=True, name="wring")
            stgF = TR(2, [128, S], BF16, dma=True, name="stgF")
            stgT = TR(2, [128, 512], BF16, dma=True, name="stgT")
            stgT32 = TR(2, [128, 512], F32, dma=True, name="stgT32")
            for i in range(2):
                proj_F(xT, 16, o_win[:, i * 512:(i + 1) * 512], 512, QC_d[i * 512:(i + 1) * 512, :], 1, wring, stgF)
                proj_F(xT, 16, o_win[:, 1024 + i * 512:1024 + (i + 1) * 512], 512, KC_d[i * 512:(i + 1) * 512, :], 1, wring, stgF)
                proj_T(xT, 16, o_win[:, 2048 + i * 512:2048 + (i + 1) * 512], 512, VC_d[:, i * 512:(i + 1) * 512], BF16, wring, stgT)
            proj_T(xT, 16, o_win[:, 3072:3584], 512, CQ_d[:, 0:512], F32, wring, stgT32)
            proj_T(xT, 16, o_win[:, 3584:3840], 256, CQ_d[:, 512:768], F32, wring, stgT32)
            wkr = T([128, 16, 96], BF16, dma=True, name="wkr")
            wkrot = T([128, 16, 96], BF16, name="wkrot")
            vop("dve", "memset", [], [wkr], ap=wkr.ap, constant=0.0)
            vop("pool", "memset", [], [wkrot], ap=wkrot.ap, constant=0.0)
            dma("pool", wkr.ap[:, :, 64:96], o_win[:, 3840:3872].rearrange("(k p) e -> p k e", p=128), wkr, [], [wkr])
            vop("dve", "tensor_single_scalar", [wkr], [wkrot], out=wkrot.ap[:, :, 64:80], in_=wkr.ap[:, :, 80:96], scalar=-1.0, op=ALU.mult)
            cp("dve", wkrot.ap[:, :, 80:96], wkr.ap[:, :, 64:80], [wkr], [wkrot])
            tmpr = TR(4, [128, 512], F32, name="ropetmp")
            stK = T([128, S], BF16, dma=True, name="stKr")
            for tg in range(4):
                pm, pr = PB[0], PB[1]
                for k in range(16):
                    mm(pm.ap[0:96, :], wkr.ap[:, k, :], xT.ap[:, k, tg * 512:(tg + 1) * 512], k == 0, k == 15, [wkr, xT], [pm])
                for k in range(16):
                    mm(pr.ap[0:96, :], wkrot.ap[:, k, :], xT.ap[:, k, tg * 512:(tg + 1) * 512], k == 0, k == 15, [wkrot, xT], [pr])
                rope_evac(pm, pr, stK, tg, tmpr)
            for h in range(16):
                dma("sp", KD_d[h, 64:96, :], stK.ap[64:96, :], stK, [stK], [])
            phase_end()

            cT = T([128, 6, S], BF16, name="cT")
            gq = T([128, 768], F32, dma=True, name="gq")
            dma("sp", gq.ap[:, 0:512], o_qg.partition_broadcast(128), gq, [], [gq])
            dma("sp", gq.ap[:, 512:768], o_kvg.partition_broadcast(128), gq, [], [gq])
            cl = TR(2, [128, 768], F32, dma=True, name="cl")
            junk = TR(2, [128, 768], F32, name="junk")
            cbf = TR(2, [128, 768], BF16, name="cbf")
            str_ = TR(2, [128, 8], F32, name="st")
            pbt = Ring([PB[6], PB[7]])
            for t in range(NT):
                c = cl.next()
                dma("sp", c.ap, CQ_d[t * 128:(t + 1) * 128, :], c, [], [c])
                jk = junk.next()
                st = str_.next()
                act(jk.ap[:, 0:512], c.ap[:, 0:512], AF.Square, [c], [jk, st], accum=st.ap[:, 0:1])
                act(jk.ap[:, 512:768], c.ap[:, 512:768], AF.Square, [c], [jk, st], accum=st.ap[:, 1:2])
                vop("dve", "tensor_scalar", [st], [st], out=st.ap[:, 2:3], in0=st.ap[:, 0:1], scalar1=1.0 / 512.0, scalar2=RMS_EPS, op0=ALU.mult, op1=ALU.add)
                vop("dve", "tensor_scalar", [st], [st], out=st.ap[:, 3:4], in0=st.ap[:, 1:2], scalar1=1.0 / 256.0, scalar2=RMS_EPS, op0=ALU.mult, op1=ALU.add)
                act(st.ap[:, 4:6], st.ap[:, 2:4], AF.Sqrt, [st], [st])
                vop("dve", "reciprocal", [st], [st], out=st.ap[:, 6:8], in_=st.ap[:, 4:6])
                vop("pool", "tensor_tensor", [c, gq], [c], out=c.ap, in0=c.ap, in1=gq.ap, op=ALU.mult)
                cb = cbf.next()
                vop("dve", "tensor_scalar", [c, st], [cb], out=cb.ap[:, 0:512], in0=c.ap[:, 0:512], scalar1=st.ap[:, 6:7], scalar2=None, op0=ALU.mult)
                vop("dve", "tensor_scalar", [c, st], [cb], out=cb.ap[:, 512:768], in0=c.ap[:, 512:768], scalar1=st.ap[:, 7:8], scalar2=None, op0=ALU.mult)
                pb = pbt.next()
                pv = pb.ap.bitcast(BF16)
                for k in range(6):
                    s.op("pe", (lambda pv=pv, k=k, cb=cb: (lambda e: e.transpose(pv[:, k * 128:(k + 1) * 128], cb.ap[:, k * 128:(k + 1) * 128], ident.ap)))(),
                         rs([cb, ident]), rs([pb]))
                cp("act", cT.ap[:, :, t * 128:(t + 1) * 128], pv[:, 0:768].rearrange("p (k t) -> p k t", k=6), [pb], [cT])
            wq = T([128, 4, 1536], BF16, dma=True, name="wq")
            wqr = T([128, 4, 1536], BF16, name="wqr")
            dma("pool", wq.ap, o_wuq.rearrange("(k p) e -> p k e", p=128), wq, [], [wq])
            wq4 = wq.ap.rearrange("p k (h e) -> p k h e", h=16)
            wqr4 = wqr.ap.rearrange("p k (h e) -> p k h e", h=16)
            cp("dve", wqr.ap, wq.ap, [wq], [wqr])
            for k in range(4):
                vop("dve", "tensor_single_scalar", [wq, wqr], [wqr], out=wqr4[:, k, :, 64:80], in_=wq4[:, k, :, 80:96], scalar=-1.0, op=ALU.mult)
                cp("dve", wqr4[:, k, :, 80:96], wq4[:, k, :, 64:80], [wq, wqr], [wqr])
            wkv = T([128, 2, 2048], BF16, dma=True, name="wkv")
            dma("pool", wkv.ap, o_wukv.rearrange("(k p) e -> p k e", p=128), wkv, [], [wkv])
            stQ = TR(2, [128, S], BF16, dma=True, name="stQ")
            stKn = TR(2, [128, S], BF16, dma=True, name="stKn")
            stV = TR(2, [128, 1024], BF16, dma=True, name="stV")
            tmpr = TR(4, [128, 512], F32, name="ropetmp")
            pbq = Ring(PB[0:6])
            for h in range(16):
                sq = stQ.next()
                for tg in range(4):
                    pm = pbq.next()
                    pr = pbq.next()
                    for k in range(4):
                        mm(pm.ap[0:96, :], wq4[:, k, h, :], cT.ap[:, k, tg * 512:(tg + 1) * 512], k == 0, k == 3, [wq, cT], [pm])
                    for k in range(4):
                        mm(pr.ap[0:96, :], wqr4[:, k, h, :], cT.ap[:, k, tg * 512:(tg + 1) * 512], k == 0, k == 3, [wqr, cT], [pr])
                    cp("act", sq.ap[0:64, tg * 512:(tg + 1) * 512], pm.ap[0:64, :], [pm], [sq])
                    rope_evac(pm, pr, sq, tg, tmpr)
                dma("sp", QD_d[h], sq.ap[0:96, :], sq, [sq], [])
                sk = stKn.next()
                for tg in range(4):
                    pk = pbq.next()
                    for k in range(2):
                        mm(pk.ap[0:64, :], wkv.ap[:, k, h * 128:h * 128 + 64], cT.ap[:, 4 + k, tg * 512:(tg + 1) * 512], k == 0, k == 1, [wkv, cT], [pk])
                    cp(ev_eng(), sk.ap[0:64, tg * 512:(tg + 1) * 512], pk.ap[0:64, :], [pk], [sk])
                dma("sp", KD_d[h, 0:64, :], sk.ap[0:64, :], sk, [sk], [])
            wkv5 = wkv.ap.rearrange("p k (h two d) -> p k h two d", two=2, d=64)
            for t in range(NT):
                sv = stV.next()
                for hf in range(2):
                    pb = pbq.next()
                    for k in range(2):
                        mm(pb.ap.rearrange("p (h d) -> p h d", h=8), cT.ap[:, 4 + k, t * 128:(t + 1) * 128], wkv5[:, k, hf * 8:(hf + 1) * 8, 1, :],
                           k == 0, k == 1, [wkv, cT], [pb])
                    cp(ev_eng(), sv.ap[:, hf * 512:(hf + 1) * 512], pb.ap, [pb], [sv])
                dma("sp", VD_d[t * 128:(t + 1) * 128, :], sv.ap, sv, [sv], [])
            persist_end[0] = base_persist
            phase_end()

            def attn_C():
                VCp = T([128, NT, 1024], BF16, dma=True, name="VCp")
                dma("sp", VCp.ap, VC_d.rearrange("(n p) c -> p n c", p=128), VCp, [], [VCp])
                OC = T([128, NT, 1024], BF16, dma=True, name="OC")
                Qr = TR(2, [64, S], BF16, dma=True, name="Qh")
                Kr = TR(2, [64, S], BF16, dma=True, name="Kh")
                Er = TR(2, [128, 512], F32, name="E")
                SPr = TR(2, [128, 512], F32, name="SP")
                LKr = TR(2, [128, 512], F32, name="LK")
                Wr = TR(2, [128, 512], BF16, name="W")
                Srun = TR(2, [128, 512], F32, name="Srun")
                pZ = Ring([PB[0], PB[1]])
                pA = Ring([PB[2], PB[3]])
                pO = PB[4:8]
                for h in range(16):
                    Qh = Qr.next()
                    Kh = Kr.next()
                    dma("sp", Qh.ap, QC_d[h * 64:(h + 1) * 64, :], Qh, [], [Qh])
                    dma("sp", Kh.ap, KC_d[h * 64:(h + 1) * 64, :], Kh, [], [Kh])
                    for G in range(4):
                        Sr = Srun.next()
                        vop("pool", "memset", [], [Sr], ap=Sr.ap, constant=0.0)
                        jtop = 4 * G + 3
                        for j in range(jtop, -1, -1):
                            q0 = max(j - 4 * G, 0)
                            c0 = q0 * 128
                            diag = j >= 4 * G
                            pz = pZ.next()
                            mm(pz.ap[:, c0:512], Kh.ap[:, j * 128:(j + 1) * 128], Qh.ap[:, G * 512 + c0:(G + 1) * 512], True, True, [Kh, Qh], [pz])
                            E = Er.next(); SP = SPr.next(); LK = LKr.next(); W = Wr.next()
                            act(E.ap[:, c0:512], pz.ap[:, c0:512], AF.Exp, [pz], [E], scale=-0.125)
                            act(SP.ap[:, c0:512], E.ap[:, c0:512], AF.Ln, [E], [SP], bias=1.0, scale=1.0)
                            vop("dve", "scalar_tensor_tensor", [pz, SP], [LK], out=LK.ap[:, c0:512], in0=pz.ap[:, c0:512], scalar=-0.125,
                                in1=SP.ap[:, c0:512], op0=ALU.mult, op1=ALU.subtract)
                            if diag:
                                vop("pool", "tensor_tensor", [LK, mC01], [LK], out=LK.ap[:, c0:c0 + 128], in0=LK.ap[:, c0:c0 + 128], in1=mC01.ap, op=ALU.mult)
                            pa = pA.next()
                            first = j == jtop
                            mm(pa.ap[:, c0:512], Ustr.ap, LK.ap[:, c0:512], True, first, [Ustr, LK], [pa])
                            if not first:
                                mm(pa.ap[:, c0:512], ones.ap, Sr.ap[:, c0:512], False, True, [ones, Sr], [pa])
                            vop("dve", "tensor_tensor", [pa, SP], [E], out=E.ap[:, c0:512], in0=pa.ap[:, c0:512], in1=SP.ap[:, c0:512], op=ALU.subtract)
                            act(W.ap[:, c0:512], E.ap[:, c0:512], AF.Exp, [E], [W])
                            if diag:
                                vop("pool", "tensor_tensor", [W, mC01b], [W], out=W.ap[:, c0:c0 + 128], in0=W.ap[:, c0:c0 + 128], in1=mC01b.ap, op=ALU.mult)
                            if j > 0:
                                vop("pool", "tensor_tensor", [Sr, LK], [Sr], out=Sr.ap[:, c0:512], in0=Sr.ap[:, c0:512], in1=LK.ap[:, c0:512], op=ALU.add)
                            for qt in range(q0, 4):
                                mm(pO[qt].ap[:, 0:64], W.ap[:, qt * 128:(qt + 1) * 128], VCp.ap[:, j, h * 64:(h + 1) * 64],
                                   j == 4 * G + qt, j == 0, [W, VCp], [pO[qt]])
                        for qt in range(4):
                            cp("act" if qt % 2 else "dve", OC.ap[:, 4 * G + qt, h * 64:(h + 1) * 64], pO[qt].ap[:, 0:64], [pO[qt]], [OC])
                for n in range(NT):
                    dma("sp", Y1_d[n * 128:(n + 1) * 128, 0:1024], OC.ap[:, n, :], OC, [OC], [])

            def attn_D():
                sc = 1.0 / math.sqrt(96.0)
                VDp = T([128, NT, 1024], BF16, dma=True, name="VDp")
                dma("sp", VDp.ap, VD_d.rearrange("(n p) c -> p n c", p=128), VDp, [], [VDp])
                OD = T([128, NT, 1024], BF16, dma=True, name="OD")
                Qr = TR(2, [96, S], BF16, dma=True, name="Qh")
                Kr = TR(2, [96, S], BF16, dma=True, name="Kh")
                Pr = TR(2, [128, S], BF16, name="P")
                PTr = TR(2, [128, S], BF16, name="PT")
                Sdr = TR(2, [128, 128], F32, name="Sd")
                str_ = TR(4, [128, 16], F32, name="st")
                pS = PB[0:4]
                pT = Ring([PB[4], PB[5]])
                pO = Ring([PB[6], PB[7]])
                for h in range(16):
                    Qh = Qr.next()
                    Kh = Kr.next()
                    dma("sp", Qh.ap, QD_d[h], Qh, [], [Qh])
                    dma("sp", Kh.ap, KD_d[h], Kh, [], [Kh])
                    for i in range(NT):
                        nkb = i + 1
                        nbank = (nkb + 3) // 4
                        widths = [min(512, nkb * 128 - bk * 512) for bk in range(nbank)]
                        for bk in range(nbank):
                            mm(pS[bk].ap[:, 0:widths[bk]], Qh.ap[:, i * 128:(i + 1) * 128], Kh.ap[:, bk * 512:bk * 512 + widths[bk]],
                               True, True, [Qh, Kh], [pS[bk]])
                        bd = nbank - 1
                        dc = widths[bd] - 128
                        Sd = Sdr.next()
                        st = str_.next()
                        vop("dve", "tensor_tensor", [pS[bd], McD], [Sd], out=Sd.ap, in0=pS[bd].ap[:, dc:dc + 128], in1=McD.ap, op=ALU.add)
                        vop("dve", "reduce_max", [Sd], [st], out=st.ap[:, 0:1], in_=Sd.ap, axis=AX.X)
                        ncol = 1
                        for bk in range(nbank):
                            wv = widths[bk] - (128 if bk == bd else 0)
                            if wv > 0:
                                vop("dve", "reduce_max", [pS[bk]], [st], out=st.ap[:, ncol:ncol + 1], in_=pS[bk].ap[:, 0:wv], axis=AX.X)
                                ncol += 1
                        if ncol > 1:
                            vop("dve", "reduce_max", [st], [st], out=st.ap[:, 5:6], in_=st.ap[:, 0:ncol], axis=AX.X)
                            mxc = st.ap[:, 5:6]
                        else:
                            mxc = st.ap[:, 0:1]
                        vop("dve", "tensor_single_scalar", [st], [st], out=st.ap[:, 6:7], in_=mxc, scalar=-sc, op=ALU.mult)
                        Pt = Pr.next()
                        act(Pt.ap[:, i * 128:(i + 1) * 128], Sd.ap, AF.Exp, [Sd, st], [Pt, st], bias=st.ap[:, 6:7], scale=sc, accum=st.ap[:, 8:9])
                        ncol = 1
                        for bk in range(nbank):
                            wv = widths[bk] - (128 if bk == bd else 0)
                            if wv > 0:
                                act(Pt.ap[:, bk * 512:bk * 512 + wv], pS[bk].ap[:, 0:wv], AF.Exp, [pS[bk], st], [Pt, st],
                                    bias=st.ap[:, 6:7], scale=sc, accum=st.ap[:, 8 + ncol:9 + ncol])
                                ncol += 1
                        if ncol > 1:
                            vop("dve", "reduce_sum", [st], [st], out=st.ap[:, 7:8], in_=st.ap[:, 8:8 + ncol], axis=AX.X)
                            den = st.ap[:, 7:8]
                        else:
                            den = st.ap[:, 8:9]
                        vop("dve", "reciprocal", [st], [st], out=st.ap[:, 14:15], in_=den)
                        PT = PTr.next()
                        for k0 in range(0, nkb, 8):
                            kn = min(8, nkb - k0)
                            pt = pT.next()
                            ptv = pt.ap.bitcast(BF16)
                            for jj in range(kn):
                                kb = k0 + jj
                                s.op("pe", (lambda ptv=ptv, jj=jj, Pt=Pt, kb=kb: (lambda e: e.transpose(ptv[:, jj * 128:(jj + 1) * 128], Pt.ap[:, kb * 128:(kb + 1) * 128], ident.ap)))(),
                                     rs([Pt, ident]), rs([pt]))
                            cp("act" if (k0 // 8) % 2 == 0 else "dve", PT.ap[:, k0 * 128:(k0 + kn) * 128], ptv[:, 0:kn * 128], [pt], [PT])
                        po = pO.next()
                        for kb in range(nkb):
                            mm(po.ap[:, 0:64], PT.ap[:, kb * 128:(kb + 1) * 128], VDp.ap[:, kb, h * 64:(h + 1) * 64], kb == 0, kb == nkb - 1, [PT, VDp], [po])
                        vop("dve", "tensor_scalar", [po, st], [OD], out=OD.ap[:, i, h * 64:(h + 1) * 64], in0=po.ap[:, 0:64],
                            scalar1=st.ap[:, 14:15], scalar2=None, op0=ALU.mult)
                for n in range(NT):
                    dma("sp", Y1_d[n * 128:(n + 1) * 128, 1024:2048], OD.ap[:, n, :], OD, [OD], [])

            if "skipC" not in dbg:
                attn_C()
                phase_end()
            if "skipD" not in dbg:
                attn_D()
                phase_end()
            out_proj_ln(Y1_d, 16, o_wout, xs2, ln1_g[1:2, :], ln1_b[1:2, :], xs1)
            phase_end()
            mlp(1, xs1, ln2_g[1:2, :], ln2_b[1:2, :], out_d)

        s.barrier()
        s.emit()
    return nc


_NC_CACHE = {}


def kernel(**inputs):
    B = inputs["x"].shape[0]
    if "nc" not in _NC_CACHE:
        _NC_CACHE["nc"] = build()
    nc = _NC_CACHE["nc"]
    f = lambda a: np.ascontiguousarray(np.asarray(a, dtype=np.float32))
    shared = {
        "even_w_in": f(inputs["even_w_in"][0]),
        "even_sinks": f(inputs["even_sinks"][0]).reshape(1, 16),
        "even_w_out": f(inputs["even_w_out"][0]),
        "odd_w_in": f(inputs["odd_w_in"][0]),
        "odd_q_norm_g": f(inputs["odd_q_norm_g"][0]).reshape(1, 512),
        "odd_kv_norm_g": f(inputs["odd_kv_norm_g"][0]).reshape(1, 256),
        "odd_w_uq": f(inputs["odd_w_uq"][0]),
        "odd_w_ukv": f(inputs["odd_w_ukv"][0]),
        "odd_w_out": f(inputs["odd_w_out"][0]),
        "ln1_g": f(inputs["ln1_g"]), "ln1_b": f(inputs["ln1_b"]),
        "ln2_g": f(inputs["ln2_g"]), "ln2_b": f(inputs["ln2_b"]),
        "mlp_w1": f(inputs["mlp_w1"]), "mlp_w2": f(inputs["mlp_w2"]),
    }
    x = f(inputs["x"])
    in_maps = [dict(shared, x=x[b]) for b in range(B)]
    res = run_bass_kernel_spmd(nc, in_maps, core_ids=list(range(B)))
    return np.stack([r["out"] for r in res.results], axis=0)
```

```python
import contextlib
import math
import numpy as np
import concourse.bass as bass
import concourse.mybir as mybir
from concourse.bass_utils import run_bass_kernel_spmd

F32 = mybir.dt.float32
BF16 = mybir.dt.bfloat16
I32 = mybir.dt.int32
AF = mybir.ActivationFunctionType
ALU = mybir.AluOpType
AX = mybir.AxisListType

S = 2048
D = 2048
NT = 16
DFF = 8192
ALPHA = 4.0 ** 0.25
LN_EPS = 1e-5
RMS_EPS = 1e-6
BIG = 1.0e9


class Res:
    __slots__ = ("name", "w", "r")

    def __init__(self, name=""):
        self.name = name
        self.w = None
        self.r = {}


class Buf:
    __slots__ = ("ap", "res", "sem")

    def __init__(self, ap, res, sem=None):
        self.ap = ap
        self.res = res
        self.sem = sem


class Ring:
    def __init__(self, items):
        self.items = items
        self.i = 0

    def next(self):
        it = self.items[self.i % len(self.items)]
        self.i += 1
        return it


class Sched:
    ENG = ("pe", "act", "dve", "pool", "sp")

    def __init__(self, nc, stack):
        self.nc = nc
        self.stack = stack
        self.prog = {e: [] for e in self.ENG}
        self.sem = {}
        self.cnt = {}
        self.known = {e: {} for e in self.ENG}
        self.free_dsems = []
        self.used_dsems = []
        self.ndsem = 0
        for e in self.ENG:
            self.newsem("E_" + e)

    def newsem(self, name):
        self.sem[name] = self.stack.enter_context(self.nc.semaphore(name))
        self.cnt[name] = 0
        return name

    def dsem(self, kind="H"):
        fl = [x for x in self.free_dsems if x[0] == kind]
        if fl:
            n = fl[-1]
            self.free_dsems.remove(n)
        else:
            n = self.newsem("%s%d" % (kind, self.ndsem))
            self.ndsem += 1
        self.used_dsems.append(n)
        return n

    def _deps(self, eng, reads, writes):
        need = {}

        def add(ev):
            if ev is None:
                return
            sm, v = ev
            if need.get(sm, 0) < v:
                need[sm] = v
        for r in reads:
            add(r.w)
        for w in writes:
            add(w.w)
            for sm, v in w.r.items():
                add((sm, v))
        kn = self.known[eng]
        for sm, v in need.items():
            if eng == "pe" and sm == "E_pe":
                continue
            if kn.get(sm, 0) < v:
                kn[sm] = v
                self.prog[eng].append(("wait", sm, v))

    def _commit(self, ev, reads, writes):
        sm, v = ev
        for r in reads:
            if r.r.get(sm, 0) < v:
                r.r[sm] = v
        for w in writes:
            w.w = ev
            w.r = {}

    def op(self, eng, fn, reads=(), writes=()):
        self._deps(eng, reads, writes)
        sm = "E_" + eng
        self.cnt[sm] += 1
        ev = (sm, self.cnt[sm])
        self.prog[eng].append(("op", fn, sm, 1))
        self._commit(ev, reads, writes)

    def dma(self, eng, out, in_, sem, reads=(), writes=()):
        self._deps(eng, reads, writes)
        self.cnt[sem] += 16
        ev = (sem, self.cnt[sem])
        self.prog[eng].append(("op", lambda e: e.dma_start(out=out, in_=in_), sem, 16))
        self._commit(ev, reads, writes)

    def barrier(self):
        for e in self.ENG:
            kn = self.known[e]
            for sm, v in self.cnt.items():
                if v > 0 and kn.get(sm, 0) < v:
                    kn[sm] = v
                    self.prog[e].append(("wait", sm, v))
        self.free_dsems.extend(self.used_dsems)
        self.used_dsems = []

    def emit(self):
        nc = self.nc

        def replay(name):
            def f(eng):
                for it in self.prog[name]:
                    if it[0] == "wait":
                        eng.wait_ge(self.sem[it[1]], it[2])
                    else:
                        it[1](eng).then_inc(self.sem[it[2]], it[3])
            return f

        with nc.Block() as block:
            block.tensor(replay("pe"))
            block.scalar(replay("act"))
            block.vector(replay("dve"))
            block.gpsimd(replay("pool"))
            block.sync(replay("sp"))


def alibi(n):
    return [2.0 ** (-8.0 * (i + 1) / n) for i in range(n)]


def build(dbg=()):
    nc = bass.Bass("TRN2", target_bir_lowering=False)

    def din(name, shape):
        return nc.dram_tensor(name, list(shape), F32, kind="ExternalInput").ap()

    x_in = din("x", [S, D])
    e_win = din("even_w_in", [D, 5888])
    e_sinks = din("even_sinks", [1, 16])
    e_wout = din("even_w_out", [1536, D])
    o_win = din("odd_w_in", [D, 3872])
    o_qg = din("odd_q_norm_g", [1, 512])
    o_kvg = din("odd_kv_norm_g", [1, 256])
    o_wuq = din("odd_w_uq", [512, 1536])
    o_wukv = din("odd_w_ukv", [256, 2048])
    o_wout = din("odd_w_out", [D, D])
    ln1_g = din("ln1_g", [2, D])
    ln1_b = din("ln1_b", [2, D])
    ln2_g = din("ln2_g", [2, D])
    ln2_b = din("ln2_b", [2, D])
    w1_in = din("mlp_w1", [2, D, DFF])
    w2_in = din("mlp_w2", [2, DFF, D])
    out_d = nc.dram_tensor("out", [S, D], F32, kind="ExternalOutput").ap()

    def dscr(name, shape, dt):
        kind = "ExternalOutput" if name in dbg else "Internal"
        return nc.dram_tensor(name, list(shape), dt, kind=kind).ap()

    w1b = dscr("w1b", [2, D, DFF], BF16)
    w2b = dscr("w2b", [2, DFF, D], BF16)
    xs1 = dscr("xs1", [S, D], F32)
    xs2 = dscr("xs2", [S, D], F32)
    QA_d = dscr("QA_d", [1024, S], BF16)
    KA_d = dscr("KA_d", [128, S], BF16)
    VA_d = dscr("VA_d", [S, 128], BF16)
    QB_d = [dscr("QB%d_d" % g, [512, S], BF16) for g in range(3)]
    KB_d = [dscr("KB%d_d" % g, [512, S], BF16) for g in range(3)]
    VB_d = [dscr("VB%d_d" % g, [S, 512], BF16) for g in range(3)]
    OB_d = [dscr("OB%d_d" % g, [S, 512], F32) for g in range(3)]
    LSE_d = [dscr("LSE%d_d" % g, [S, 8], F32) for g in range(3)]
    Y0_d = dscr("Y0_d", [S, 1536], BF16)
    QC_d = dscr("QC_d", [1024, S], BF16)
    KC_d = dscr("KC_d", [1024, S], BF16)
    VC_d = dscr("VC_d", [S, 1024], BF16)
    CQ_d = dscr("CQ_d", [S, 768], F32)
    QD_d = dscr("QD_d", [16, 96, S], BF16)
    KD_d = dscr("KD_d", [16, 96, S], BF16)
    VD_d = dscr("VD_d", [S, 1024], BF16)
    Y1_d = dscr("Y1_d", [S, 2048], BF16)

    with contextlib.ExitStack() as stack:
        s = Sched(nc, stack)
        ARENA_ELEMS = 100 * 1024
        arena = nc.alloc_sbuf_tensor("arena", [128, ARENA_ELEMS], BF16)
        aoff = [0]
        persist_end = [0]

        def T(shape, dt, dma=False, name=""):
            esz = 2 if dt == BF16 else 4
            nel = int(np.prod(shape[1:]))
            nb16 = (nel * esz + 63) // 64 * 32
            assert aoff[0] + nb16 <= ARENA_ELEMS, "arena overflow %s %d" % (name, aoff[0] + nb16)
            v = arena[:, aoff[0]:aoff[0] + nel * esz // 2]
            aoff[0] += nb16
            if dt != BF16:
                v = v.bitcast(dt)
            if len(shape) == 3:
                v = v.rearrange("p (a b) -> p a b", a=shape[1])
            elif len(shape) == 4:
                v = v.rearrange("p (a b c) -> p a b c", a=shape[1], b=shape[2])
            if shape[0] != 128:
                v = v[0:shape[0]]
            return Buf(v, Res(name), s.dsem("W" if dma == "sw" else "H") if dma else None)

        def TR(n, shape, dt, dma=False, name=""):
            return Ring([T(shape, dt, dma, name) for _ in range(n)])

        def phase_end():
            s.barrier()
            aoff[0] = persist_end[0]

        PB = [Buf(nc.alloc_psum_tensor("pb%d" % i, [128, 512], F32)[:], Res("pb%d" % i)) for i in range(8)]

        def rs(bufs):
            return [b.res for b in bufs]

        def mm(out, lhsT, rhs, start, stop, rd, wr):
            s.op("pe", lambda e: e.matmul(out, lhsT, rhs, start=start, stop=stop), rs(rd), rs(wr))

        def act(out, in_, func, rd, wr, bias=None, scale=None, accum=None, eng="act"):
            kw = {}
            if bias is not None:
                kw["bias"] = bias
            if scale is not None:
                kw["scale"] = scale
            if accum is not None:
                kw["accum_out"] = accum
            s.op(eng, lambda e: e.activation(out=out, in_=in_, func=func, **kw), rs(rd), rs(wr))

        def vop(eng, meth, rd, wr, **kw):
            s.op(eng, lambda e: getattr(e, meth)(**kw), rs(rd), rs(wr))

        def cp(eng, out, in_, rd, wr):
            if eng == "act":
                s.op("act", lambda e: e.copy(out=out, in_=in_), rs(rd), rs(wr))
            else:
                s.op(eng, lambda e: e.tensor_copy(out=out, in_=in_), rs(rd), rs(wr))

        def dma(eng, out, in_, buf, rd=(), wr=()):
            assert (eng == "pool") == (buf.sem[0] == "W"), (eng, buf.sem)
            s.dma(eng, out, in_, buf.sem, rs(rd), rs(wr))

        def run_pipeline(N, stages):
            ns = len(stages)
            for t in range(N + ns - 1):
                for si in range(ns):
                    it = t - si
                    if 0 <= it < N:
                        stages[si](it)

        ident = T([128, 128], BF16, name="ident")
        Ustr = T([128, 128], F32, name="Ustr")
        ones = T([128, 128], F32, name="ones")
        mC01 = T([128, 128], F32, name="mC01")
        mC01b = T([128, 128], BF16, name="mC01b")
        McD = T([128, 128], F32, name="McD")
        RA = T([128, 256], F32, name="RA")
        RB = T([128, 256], F32, name="RB")
        sinkt = T([128, 16], F32, dma=True, name="sink")
        sink8 = T([128, 16], F32, name="sink8")
        tmpi = T([128, 256], I32, name="tmpi")
        tmpf = T([128, 256], F32, name="tmpf")
        tmpg = T([128, 256], F32, name="tmpg")
        tmph = T([128, 256], F32, name="tmph")
        vop("pool", "iota", [], [tmpi], out=tmpi.ap[:, 0:128], pattern=[[-1, 128]], base=0, channel_multiplier=1)
        cp("dve", tmpf.ap[:, 0:128], tmpi.ap[:, 0:128], [tmpi], [tmpf])
        vop("dve", "tensor_single_scalar", [tmpf], [ident], out=ident.ap, in_=tmpf.ap[:, 0:128], scalar=0.0, op=ALU.is_equal)
        vop("dve", "tensor_single_scalar", [tmpf], [Ustr], out=Ustr.ap, in_=tmpf.ap[:, 0:128], scalar=0.0, op=ALU.is_gt)
        vop("dve", "tensor_single_scalar", [tmpf], [mC01], out=mC01.ap, in_=tmpf.ap[:, 0:128], scalar=0.0, op=ALU.is_lt)
        cp("dve", mC01b.ap, mC01.ap, [mC01], [mC01b])
        vop("dve", "memset", [], [ones], ap=ones.ap, constant=1.0)
        vop("dve", "tensor_scalar", [tmpf], [McD], out=McD.ap, in0=tmpf.ap[:, 0:128], scalar1=0.0, scalar2=1.0,
            op0=ALU.is_ge, op1=ALU.subtract)
        vop("dve", "tensor_single_scalar", [McD], [McD], out=McD.ap, in_=McD.ap, scalar=BIG, op=ALU.mult)
        vop("pool", "iota", [tmpf], [tmpi], out=tmpi.ap, pattern=[[-1, 256]], base=128, channel_multiplier=1)
        cp("dve", tmpf.ap, tmpi.ap, [tmpi], [tmpf])
        for Rt, nb in ((RA, 127.0), (RB, 128.0)):
            vop("dve", "tensor_single_scalar", [tmpf], [tmpg], out=tmpg.ap, in_=tmpf.ap, scalar=0.0, op=ALU.is_ge)
            vop("dve", "tensor_single_scalar", [tmpf], [tmph], out=tmph.ap, in_=tmpf.ap, scalar=nb, op=ALU.is_le)
            vop("dve", "tensor_tensor", [tmpg, tmph], [tmpg], out=tmpg.ap, in0=tmpg.ap, in1=tmph.ap, op=ALU.mult)
            vop("dve", "tensor_tensor", [tmpg, tmpf], [tmph], out=tmph.ap, in0=tmpg.ap, in1=tmpf.ap, op=ALU.mult)
            vop("dve", "tensor_scalar", [tmpg], [tmpg], out=tmpg.ap, in0=tmpg.ap, scalar1=1.0, scalar2=BIG,
                op0=ALU.subtract, op1=ALU.mult)
            vop("dve", "tensor_tensor", [tmpg, tmph], [Rt], out=Rt.ap, in0=tmpg.ap, in1=tmph.ap, op=ALU.subtract)
        dma("sp", sinkt.ap, e_sinks.partition_broadcast(128), sinkt, [], [sinkt])
        vop("dve", "tensor_single_scalar", [sinkt], [sink8], out=sink8.ap, in_=sinkt.ap, scalar=8.0, op=ALU.mult)
        persist_end[0] = aoff[0]

        wcast = [Buf(None, Res("wc%d" % l), s.newsem("WC%d" % l)) for l in range(2)]
        for l in range(2):
            for r0 in range(0, D, 256):
                dma("pool", w1b[l, r0:r0 + 256, :], w1_in[l, r0:r0 + 256, :], wcast[l], [], [wcast[l]])
            for r0 in range(0, DFF, 1024):
                dma("pool", w2b[l, r0:r0 + 1024, :], w2_in[l, r0:r0 + 1024, :], wcast[l], [], [wcast[l]])
        phase_end()

        evq = [0]

        def ev_eng():
            evq[0] += 1
            return "act" if evq[0] % 2 else "dve"

        def load_T(src, ncol, dst, is_f32, keep=None):
            kc = ncol // 128
            ld = TR(2, [128, ncol], F32 if is_f32 else BF16, dma=True, name="ldT")
            cb = TR(2, [128, ncol], BF16, name="cbT") if is_f32 else None
            pbr = Ring([PB[6], PB[7]])
            for t in range(NT):
                lt = ld.next()
                dma("sp", lt.ap, src[t * 128:(t + 1) * 128, :], lt, [], [lt])
                if is_f32:
                    ct = cb.next()
                    cp("pool", ct.ap, lt.ap, [lt], [ct])
                else:
                    ct = lt
                for k0 in range(0, kc, 8):
                    kn = min(8, kc - k0)
                    pb = pbr.next()
                    pv = pb.ap.bitcast(BF16)
                    for j in range(kn):
                        k = k0 + j
                        s.op("pe", (lambda pv=pv, j=j, ct=ct, k=k: (lambda e: e.transpose(pv[:, j * 128:(j + 1) * 128], ct.ap[:, k * 128:(k + 1) * 128], ident.ap)))(),
                             rs([ct, ident]), rs([pb]))
                    cp(ev_eng(), dst.ap[:, k0:k0 + kn, t * 128:(t + 1) * 128],
                       pv[:, 0:kn * 128].rearrange("p (k t) -> p k t", k=kn), [pb], [dst])

        def wload(dst, wsrc, kc, ncol):
            dma("pool", dst.ap[:, 0:kc, 0:ncol], wsrc.rearrange("(k p) e -> p k e", p=128), dst, [], [dst])

        def proj_F(xT, kc, wsrc, ncol, dst, dil, wring, stg, stg_rows=128):
            wt = wring.next()
            wload(wt, wsrc, kc, ncol)
            pbr = Ring(PB[0:6])
            for c in range(ncol // 128):
                st = stg.next()
                for tg in range(4):
                    pb = pbr.next()
                    for k in range(kc):
                        mm(pb.ap, wt.ap[:, k, c * 128:(c + 1) * 128], xT.ap[:, k, tg * 512:(tg + 1) * 512],
                           k == 0, k == kc - 1, [wt, xT], [pb])
                    if dil == 1:
                        cp(ev_eng(), st.ap[:, tg * 512:(tg + 1) * 512], pb.ap, [pb], [st])
                    else:
                        na = 512 // dil
                        cp(ev_eng(), st.ap.rearrange("p (r a) -> p a r", r=dil)[:, tg * na:(tg + 1) * na, :],
                           pb.ap.rearrange("p (a r) -> p a r", r=dil), [pb], [st])
                dma("sp", dst[c * 128:(c + 1) * 128, :], st.ap, st, [st], [])

        def proj_T(xT, kc, wsrc, ncol, dst, dst_dt, wring, stg):
            wt = wring.next()
            wload(wt, wsrc, kc, ncol)
            pbr = Ring(PB[0:6])
            for t in range(NT):
                pb = pbr.next()
                for k in range(kc):
                    mm(pb.ap[:, 0:ncol], xT.ap[:, k, t * 128:(t + 1) * 128], wt.ap[:, k, 0:ncol],
                       k == 0, k == kc - 1, [wt, xT], [pb])
                st = stg.next()
                cp(ev_eng(), st.ap[:, 0:ncol], pb.ap[:, 0:ncol], [pb], [st])
                dma("sp", dst[t * 128:(t + 1) * 128, :], st.ap[:, 0:ncol], st, [st], [])

        def layer_norm_tile(z, gt, bt, stat):
            st6 = stat.ap[:, 0:24].rearrange("p (c s) -> p c s", c=4)
            for c in range(4):
                vop("dve", "bn_stats", [z], [stat], out=st6[:, c, :], in_=z.ap[:, c * 512:(c + 1) * 512])
            mv = stat.ap[:, 24:26]
            vop("dve", "bn_aggr", [stat], [stat], out=mv, in_=st6)
            vop("dve", "tensor_single_scalar", [stat], [stat], out=stat.ap[:, 26:27], in_=stat.ap[:, 25:26], scalar=LN_EPS, op=ALU.add)
            act(stat.ap[:, 27:28], stat.ap[:, 26:27], AF.Sqrt, [stat], [stat])
            vop("dve", "reciprocal", [stat], [stat], out=stat.ap[:, 28:29], in_=stat.ap[:, 27:28])
            vop("dve", "tensor_scalar", [z, stat], [z], out=z.ap, in0=z.ap, scalar1=stat.ap[:, 24:25], scalar2=stat.ap[:, 28:29],
                op0=ALU.subtract, op1=ALU.mult)
            vop("pool", "tensor_tensor", [z, gt], [z], out=z.ap, in0=z.ap, in1=gt.ap, op=ALU.mult)
            vop("dve", "tensor_tensor", [z, bt], [z], out=z.ap, in0=z.ap, in1=bt.ap, op=ALU.add)

        def load_gb(g_src, b_src):
            gt = T([128, D], F32, dma=True, name="gam")
            bt = T([128, D], F32, dma=True, name="bet")
            dma("sp", gt.ap, g_src.partition_broadcast(128), gt, [], [gt])
            dma("sp", bt.ap, b_src.partition_broadcast(128), bt, [], [bt])
            return gt, bt

        def out_proj_ln(Yd, kc, wsrc, xres, g_src, b_src, dst):
            wt = T([128, kc, D], BF16, dma="sw", name="wout")
            for k0 in range(0, kc, 4):
                dma("pool", wt.ap[:, k0:k0 + 4, :], wsrc[k0 * 128:(k0 + 4) * 128, :].rearrange("(k p) e -> p k e", p=128), wt, [], [wt])
            gt, bt = load_gb(g_src, b_src)
            yl = TR(2, [128, kc * 128], BF16, dma=True, name="yl")
            yT = TR(2, [128, kc, 128], BF16, name="yT")
            zr = TR(2, [128, D], F32, dma=True, name="z")
            stat = TR(2, [128, 32], F32, name="stat")
            pbt = Ring([PB[4], PB[5]])
            for t in range(NT):
                y = yl.next()
                dma("sp", y.ap, Yd[t * 128:(t + 1) * 128, :], y, [], [y])
                z = zr.next()
                dma("sp", z.ap, xres[t * 128:(t + 1) * 128, :], z, [], [z])
                yt = yT.next()
                for k0 in range(0, kc, 8):
                    kn = min(8, kc - k0)
                    pb = pbt.next()
                    pv = pb.ap.bitcast(BF16)
                    for j in range(kn):
                        k = k0 + j
                        s.op("pe", (lambda pv=pv, j=j, y=y, k=k: (lambda e: e.transpose(pv[:, j * 128:(j + 1) * 128], y.ap[:, k * 128:(k + 1) * 128], ident.ap)))(),
                             rs([y, ident]), rs([pb]))
                    cp("act", yt.ap[:, k0:k0 + kn, :], pv[:, 0:kn * 128].rearrange("p (k t) -> p k t", k=kn), [pb], [yt])
                for dt in range(4):
                    pb = PB[dt]
                    for k in range(kc):
                        mm(pb.ap, yt.ap[:, k, :], wt.ap[:, k, dt * 512:(dt + 1) * 512], k == 0, k == kc - 1, [yt, wt], [pb])
                    vop("dve", "scalar_tensor_tensor", [z, pb], [z], out=z.ap[:, dt * 512:(dt + 1) * 512],
                        in0=z.ap[:, dt * 512:(dt + 1) * 512], scalar=ALPHA, in1=pb.ap, op0=ALU.mult, op1=ALU.add)
                layer_norm_tile(z, gt, bt, stat.next())
                dma("sp", dst[t * 128:(t + 1) * 128, :], z.ap, z, [z], [])

        def mlp(l, xsrc, g_src, b_src, dst):
            gt, bt = load_gb(g_src, b_src)
            zb = T([128, 4, D], F32, dma=True, name="zb")
            zres = [Res("z%d" % i) for i in range(4)]
            xbr = TR(2, [128, D], BF16, name="xb")
            xT = T([128, 16, 512], BF16, name="xTm")
            hT = T([128, 64, 512], BF16, name="hT")
            w1r = TR(2, [128, 16, 256], BF16, dma=True, name="w1")
            w2r = TR(3, [128, 8, 512], BF16, dma=True, name="w2")
            hr = TR(2, [128, 512], F32, name="hrelu")
            stat = TR(2, [128, 32], F32, name="stat")
            zsem = [s.dsem() for _ in range(4)]
            pbt = Ring([PB[6], PB[7]])
            pbh = Ring(PB[0:6])
            for G in range(4):
                for tt in range(4):
                    t = G * 4 + tt
                    zt = Buf(zb.ap[:, tt, :], zres[tt], zsem[tt])
                    dma("sp", zt.ap, xsrc[t * 128:(t + 1) * 128, :], zt, [], [zt])
                    xb = xbr.next()
                    cp("pool", xb.ap, zt.ap, [zt], [xb])
                    for k0 in (0, 8):
                        pb = pbt.next()
                        pv = pb.ap.bitcast(BF16)
                        for j in range(8):
                            k = k0 + j
                            s.op("pe", (lambda pv=pv, j=j, xb=xb, k=k: (lambda e: e.transpose(pv[:, j * 128:(j + 1) * 128], xb.ap[:, k * 128:(k + 1) * 128], ident.ap)))(),
                                 rs([xb, ident]), rs([pb]))
                        cp("act", xT.ap[:, k0:k0 + 8, tt * 128:(tt + 1) * 128],
                           pv.rearrange("p (k t) -> p k t", k=8), [pb], [xT])
                for f2 in range(32):
                    w1 = w1r.next()
                    dma("sp", w1.ap, w1b[l, :, f2 * 256:(f2 + 1) * 256].rearrange("(k p) f -> p k f", p=128), w1, [wcast[l]], [w1])
                    for fi in range(2):
                        f = f2 * 2 + fi
                        pb = pbh.next()
                        for k in range(16):
                            mm(pb.ap, w1.ap[:, k, fi * 128:(fi + 1) * 128], xT.ap[:, k, :], k == 0, k == 15, [w1, xT], [pb])
                        h = hr.next()
                        act(h.ap, pb.ap, AF.Relu, [pb], [h])
                        vop("pool" if f % 2 else "dve", "tensor_tensor", [h], [hT], out=hT.ap[:, f, :], in0=h.ap, in1=h.ap, op=ALU.mult)
                for dt in range(4):
                    pbs = PB[0:4] if dt % 2 == 0 else PB[4:8]
                    for f8 in range(8):
                        w2 = w2r.next()
                        dma("sp", w2.ap, w2b[l, f8 * 1024:(f8 + 1) * 1024, dt * 512:(dt + 1) * 512].rearrange("(c p) d -> p c d", p=128),
                            w2, [wcast[l]], [w2])
                        for fi in range(8):
                            f = f8 * 8 + fi
                            for tt in range(4):
                                mm(pbs[tt].ap, hT.ap[:, f, tt * 128:(tt + 1) * 128], w2.ap[:, fi, :], f == 0, f == 63, [hT, w2], [pbs[tt]])
                    for tt in range(4):
                        zt = Buf(zb.ap[:, tt, :], zres[tt], zsem[tt])
                        vop("dve", "scalar_tensor_tensor", [zt, pbs[tt]], [zt], out=zt.ap[:, dt * 512:(dt + 1) * 512],
                            in0=zt.ap[:, dt * 512:(dt + 1) * 512], scalar=ALPHA, in1=pbs[tt].ap, op0=ALU.mult, op1=ALU.add)
                for tt in range(4):
                    t = G * 4 + tt
                    zt = Buf(zb.ap[:, tt, :], zres[tt], zsem[tt])
                    layer_norm_tile(zt, gt, bt, stat.next())
                    dma("sp", dst[t * 128:(t + 1) * 128, :], zt.ap, zt, [zt], [])

        def banded(Qd, Kd, kvmap, nh, Vd, nkv, dil, Rm, cvals, use_sink, out_mode, Yd=None, OBd=None, LSEd=None):
            L = S // dil
            nbpl = L // 128
            Vp = T([128, NT, nkv * 64], BF16, dma=True, name="Vp")
            for n in range(NT):
                r = (128 * n) // L
                a0 = (128 * n) % L
                st_ = r + dil * a0
                dma("sp", Vp.ap[:, n, :], Vd[st_:st_ + dil * 127 + 1:dil, :], Vp, [], [Vp])
            if out_mode == "A":
                Oall = T([128, NT, nh * 64], BF16, dma=True, name="Oall")
            else:
                Oall = T([128, NT, nh * 64], F32, dma=True, name="Oall")
                lse = T([128, NT, nh], F32, dma=True, name="lse")
            Qr = TR(3, [64, S], BF16, dma=True, name="Qh")
            Kr = TR(3, [64, S], BF16, dma=True, name="Kh")
            Tr = TR(3, [128, 256], F32, name="T")
            Pr = TR(4, [128, 256], BF16, name="P")
            PTr = TR(4, [128, 256], BF16, name="PT")
            str_ = TR(6, [128, 8], F32, name="st")
            pS = Ring([PB[0], PB[1], PB[6]])
            pT = Ring([PB[2], PB[3]])
            pO = Ring([PB[4], PB[5], PB[7]])
            heads = []
            lastkv = -1
            Kh = None
            for h in range(nh):
                Qh = Qr.next()
                kv = kvmap(h)
                newk = kv != lastkv
                if newk:
                    Kh = Kr.next()
                    lastkv = kv
                heads.append((Qh, Kh, kv, newk))

            def load_head(h):
                Qh, Kh, kv, newk = heads[h]
                dma("sp", Qh.ap, Qd[h * 64:(h + 1) * 64, :], Qh, [], [Qh])
                if newk:
                    dma("sp", Kh.ap, Kd[kv * 64:(kv + 1) * 64, :], Kh, [], [Kh])
            items = [(h, n) for h in range(nh) for n in range(NT)]
            ctx = [dict(ps=pS.next(), Tt=Tr.next(), st=str_.next(), Pt=Pr.next(), pt=pT.next(), PT=PTr.next(), po=pO.next())
                   for _ in items]
            load_head(0)

            def geom(n):
                hasprev = (n % nbpl) != 0
                nk = 256 if hasprev else 128
                ks = (n - 1) * 128 if hasprev else n * 128
                return hasprev, nk, ks

            def stA(it):
                h, n = items[it]
                c = ctx[it]
                if n == 0 and h + 1 < nh:
                    load_head(h + 1)
                Qh, Kh, kv, _ = heads[h]
                hasprev, nk, ks = geom(n)
                Rv = Rm.ap[:, 0:256] if hasprev else Rm.ap[:, 128:256]
                ps, Tt, st, Pt = c["ps"], c["Tt"], c["st"], c["Pt"]
                mm(ps.ap[:, 0:nk], Qh.ap[:, n * 128:(n + 1) * 128], Kh.ap[:, ks:ks + nk], True, True, [Qh, Kh], [ps])
                vop("dve", "scalar_tensor_tensor", [ps, Rm], [Tt], out=Tt.ap[:, 0:nk], in0=Rv, scalar=cvals[h], in1=ps.ap[:, 0:nk],
                    op0=ALU.mult, op1=ALU.add)
                vop("dve", "reduce_max", [Tt], [st], out=st.ap[:, 0:1], in_=Tt.ap[:, 0:nk], axis=AX.X)
                if use_sink:
                    vop("dve", "tensor_scalar", [st, sink8], [st], out=st.ap[:, 1:2], in0=st.ap[:, 0:1], scalar1=sink8.ap[:, h:h + 1], scalar2=-0.125,
                        op0=ALU.max, op1=ALU.mult)
                else:
                    vop("dve", "tensor_single_scalar", [st], [st], out=st.ap[:, 1:2], in_=st.ap[:, 0:1], scalar=-0.125, op=ALU.mult)
                act(Pt.ap[:, 0:nk], Tt.ap[:, 0:nk], AF.Exp, [Tt, st], [Pt, st], bias=st.ap[:, 1:2], scale=0.125, accum=st.ap[:, 2:3])
                if use_sink:
                    act(st.ap[:, 3:4], st.ap[:, 1:2], AF.Exp, [st, sinkt], [st], bias=sinkt.ap[:, h:h + 1], scale=1.0)

            def stB(it):
                h, n = items[it]
                c = ctx[it]
                hasprev, nk, ks = geom(n)
                Pt, pt, PT = c["Pt"], c["pt"], c["PT"]
                ptv = pt.ap.bitcast(BF16)
                for kb in range(nk // 128):
                    s.op("pe", (lambda ptv=ptv, kb=kb, Pt=Pt: (lambda e: e.transpose(ptv[:, kb * 128:(kb + 1) * 128], Pt.ap[:, kb * 128:(kb + 1) * 128], ident.ap)))(),
                         rs([Pt, ident]), rs([pt]))
                cp("act", PT.ap[:, 0:nk], ptv[:, 0:nk], [pt], [PT])

            def stC(it):
                h, n = items[it]
                c = ctx[it]
                Qh, Kh, kv, _ = heads[h]
                hasprev, nk, ks = geom(n)
                PT, po, st = c["PT"], c["po"], c["st"]
                nkb = nk // 128
                for kb in range(nkb):
                    blk = ks // 128 + kb
                    mm(po.ap[:, 0:64], PT.ap[:, kb * 128:(kb + 1) * 128], Vp.ap[:, blk, kv * 64:(kv + 1) * 64],
                       kb == 0, kb == nkb - 1, [PT, Vp], [po])
                if use_sink:
                    vop("dve", "tensor_tensor", [st], [st], out=st.ap[:, 2:3], in0=st.ap[:, 2:3], in1=st.ap[:, 3:4], op=ALU.add)
                vop("dve", "reciprocal", [st], [st], out=st.ap[:, 4:5], in_=st.ap[:, 2:3])
                vop("dve", "tensor_scalar", [po, st], [Oall], out=Oall.ap[:, n, h * 64:(h + 1) * 64], in0=po.ap[:, 0:64],
                    scalar1=st.ap[:, 4:5], scalar2=None, op0=ALU.mult)
                if out_mode == "B":
                    act(st.ap[:, 5:6], st.ap[:, 2:3], AF.Ln, [st], [st])
                    vop("dve", "tensor_tensor", [st], [lse], out=lse.ap[:, n, h:h + 1], in0=st.ap[:, 5:6], in1=st.ap[:, 1:2], op=ALU.subtract)

            run_pipeline(len(items), [stA, stB, stC])
            if out_mode == "A":
                for n in range(NT):
                    dma("sp", Yd[n * 128:(n + 1) * 128, 0:nh * 64], Oall.ap[:, n, :], Oall, [Oall], [])
            else:
                for n in range(NT):
                    r = (128 * n) // L
                    a0 = (128 * n) % L
                    st_ = r + dil * a0
                    dma("sp", OBd[st_:st_ + dil * 127 + 1:dil, :], Oall.ap[:, n, :], Oall, [Oall], [])
                    dma("sp", LSEd[st_:st_ + dil * 127 + 1:dil, :], lse.ap[:, n, :], lse, [lse], [])

        def combine_B():
            ol = [TR(2, [128, 512], F32, dma=True, name="o%d" % g) for g in range(3)]
            ll = TR(2, [128, 3, 8], F32, dma=True, name="l")
            wk = TR(2, [128, 3, 8], F32, name="wk")
            sm = TR(2, [128, 16], F32, name="sm")
            yo = TR(2, [128, 512], BF16, dma=True, name="yo")
            for t in range(NT):
                og = [ol[g].next() for g in range(3)]
                lt = ll.next()
                for g in range(3):
                    dma("sp", og[g].ap, OB_d[g][t * 128:(t + 1) * 128, :], og[g], [], [og[g]])
                    dma("sp", lt.ap[:, g, :], LSE_d[g][t * 128:(t + 1) * 128, :], lt, [], [lt])
                m = sm.next()
                vop("dve", "tensor_tensor", [lt], [m], out=m.ap[:, 0:8], in0=lt.ap[:, 0, :], in1=lt.ap[:, 1, :], op=ALU.max)
                vop("dve", "tensor_tensor", [lt, m], [m], out=m.ap[:, 0:8], in0=m.ap[:, 0:8], in1=lt.ap[:, 2, :], op=ALU.max)
                w = wk.next()
                vop("dve", "tensor_tensor", [lt, m], [w], out=w.ap, in0=lt.ap, in1=m.ap[:, 0:8].unsqueeze(1).broadcast_to([128, 3, 8]), op=ALU.subtract)
                act(w.ap, w.ap, AF.Exp, [w], [w])
                vop("dve", "tensor_tensor", [w], [m], out=m.ap[:, 8:16], in0=w.ap[:, 0, :], in1=w.ap[:, 1, :], op=ALU.add)
                vop("dve", "tensor_tensor", [w, m], [m], out=m.ap[:, 8:16], in0=m.ap[:, 8:16], in1=w.ap[:, 2, :], op=ALU.add)
                vop("dve", "reciprocal", [m], [m], out=m.ap[:, 8:16], in_=m.ap[:, 8:16])
                vop("dve", "tensor_tensor", [w, m], [w], out=w.ap, in0=w.ap, in1=m.ap[:, 8:16].unsqueeze(1).broadcast_to([128, 3, 8]), op=ALU.mult)
                for g in range(3):
                    eng = "pool" if g == 1 else "dve"
                    vop(eng, "tensor_tensor", [og[g], w], [og[g]], out=og[g].ap.rearrange("p (h d) -> p h d", h=8),
                        in0=og[g].ap.rearrange("p (h d) -> p h d", h=8), in1=w.ap[:, g, :].unsqueeze(2).broadcast_to([128, 8, 64]), op=ALU.mult)
                vop("dve", "tensor_tensor", [og[0], og[1]], [og[0]], out=og[0].ap, in0=og[0].ap, in1=og[1].ap, op=ALU.add)
                y = yo.next()
                vop("dve", "tensor_tensor", [og[0], og[2]], [y], out=y.ap, in0=og[0].ap, in1=og[2].ap, op=ALU.add)
                dma("sp", Y0_d[t * 128:(t + 1) * 128, 1024:1536], y.ap, y, [y], [])

        xT = T([128, 16, S], BF16, name="xT")
        load_T(x_in, D, xT, True)
        wring = TR(2, [128, 16, 512], BF16, dma="sw", name="wring")
        stgF = TR(2, [128, S], BF16, dma=True, name="stgF")
        stgT = TR(2, [128, 512], BF16, dma=True, name="stgT")
        proj_F(xT, 16, e_win[:, 0:512], 512, QA_d[0:512, :], 1, wring, stgF)
        proj_F(xT, 16, e_win[:, 512:1024], 512, QA_d[512:1024, :], 1, wring, stgF)
        proj_F(xT, 16, e_win[:, 1024:1152], 128, KA_d, 1, wring, stgF)
        proj_T(xT, 16, e_win[:, 1152:1280], 128, VA_d, BF16, wring, stgT)
        for g, dil in enumerate((1, 4, 16)):
            base = 1280 + g * 1536
            proj_F(xT, 16, e_win[:, base:base + 512], 512, QB_d[g], dil, wring, stgF)
            proj_F(xT, 16, e_win[:, base + 512:base + 1024], 512, KB_d[g], dil, wring, stgF)
            proj_T(xT, 16, e_win[:, base + 1024:base + 1536], 512, VB_d[g], BF16, wring, stgT)
        phase_end()
        slA = alibi(16)
        banded(QA_d, KA_d, lambda h: h // 8, 16, VA_d, 2, 1, RA, [8.0 * sl for sl in slA], True, "A", Yd=Y0_d)
        phase_end()
        slB = alibi(8)
        for g, dil in enumerate((1, 4, 16)):
            banded(QB_d[g], KB_d[g], lambda h: h, 8, VB_d[g], 8, dil, RB, [8.0 * sl * dil for sl in slB], False, "B",
                   OBd=OB_d[g], LSEd=LSE_d[g])
            phase_end()
        combine_B()
        phase_end()
        out_proj_ln(Y0_d, 12, e_wout, x_in, ln1_g[0:1, :], ln1_b[0:1, :], xs1)
        phase_end()
        mlp(0, xs1, ln2_g[0:1, :], ln2_b[0:1, :], xs2 if "stop0" not in dbg else out_d)
        phase_end()

        if "stop0" not in dbg:

            base_persist = persist_end[0]
            cosT = T([128, S], F32, name="cosT")
            sinT = T([128, S], F32, name="sinT")
            persist_end[0] = aoff[0]
            pidx = T([128, 2], I32, name="pidx")
            pf = T([128, 2], F32, name="pf")
            vop("pool", "iota", [], [pidx], out=pidx.ap[:, 0:1], pattern=[[0, 1]], base=0, channel_multiplier=1)
            vop("dve", "tensor_single_scalar", [pidx], [pidx], out=pidx.ap[:, 1:2], in_=pidx.ap[:, 0:1], scalar=15, op=ALU.bitwise_and)
            cp("dve", pf.ap[:, 0:1], pidx.ap[:, 1:2], [pidx], [pf])
            act(pf.ap[:, 1:2], pf.ap[:, 0:1], AF.Exp, [pf], [pf], scale=-math.log(10000.0) / 16.0)
            vop("dve", "tensor_single_scalar", [pf], [pf], out=pf.ap[:, 1:2], in_=pf.ap[:, 1:2], scalar=1.0 / (2 * math.pi), op=ALU.mult)
            tpi = T([128, S], I32, name="tpi")
            tpf = T([128, S], F32, name="tpf")
            tq = T([128, S], F32, name="tq")
            vop("pool", "iota", [], [tpi], out=tpi.ap, pattern=[[1, S]], base=0, channel_multiplier=0)
            cp("dve", tpf.ap, tpi.ap, [tpi], [tpf])
            for tab, offs in ((sinT, 0.0), (cosT, 0.25)):
                vop("dve", "tensor_scalar", [tpf, pf], [tq], out=tq.ap, in0=tpf.ap, scalar1=pf.ap[:, 1:2], scalar2=offs,
                    op0=ALU.mult, op1=ALU.add)
                cp("dve", tpi.ap, tq.ap, [tq], [tpi])
                cp("dve", tab.ap, tpi.ap, [tpi], [tab])
                vop("dve", "tensor_tensor", [tq, tab], [tq], out=tq.ap, in0=tq.ap, in1=tab.ap, op=ALU.subtract)
                vop("dve", "scalar_tensor_tensor", [tq], [tq], out=tq.ap, in0=tq.ap, scalar=0.5, in1=tq.ap,
                    op0=ALU.is_gt, op1=ALU.subtract)
                act(tab.ap, tq.ap, AF.Sin, [tq], [tab], scale=-2.0 * math.pi)
            phase_end()

            def rope_evac(ps_m, ps_r, st, tg, tmpr):
                ta = tmpr.next()
                tb = tmpr.next()
                cs = slice(tg * 512, (tg + 1) * 512)
                vop("dve", "tensor_tensor", [ps_r, sinT], [ta], out=ta.ap[64:96, :], in0=ps_r.ap[64:96, :], in1=sinT.ap[64:96, cs], op=ALU.mult)
                vop("dve", "tensor_tensor", [ps_m, cosT], [tb], out=tb.ap[64:96, :], in0=ps_m.ap[64:96, :], in1=cosT.ap[64:96, cs], op=ALU.mult)
                vop("pool", "tensor_tensor", [ta, tb], [st], out=st.ap[64:96, cs], in0=ta.ap[64:96, :], in1=tb.ap[64:96, :], op=ALU.add)

            xT = T([128, 16, S], BF16, name="xT1")
            load_T(xs2, D, xT, True)
            wring = TR(2, [128, 16, 512], BF16, dma="sw", name="wring")
            stgF = TR(2, [128, S], BF16, dma=True, name="stgF")
            stgT = TR(2, [128, 512], BF16, dma=True, name="stgT")
            stgT32 = TR(2, [128, 512], F32, dma=True, name="stgT32")
            for i in range(2):
                proj_F(xT, 16, o_win[:, i * 512:(i + 1) * 512], 512, QC_d[i * 512:(i + 1) * 512, :], 1, wring, stgF)
                proj_F(xT, 16, o_win[:, 1024 + i * 512:1024 + (i + 1) * 512], 512, KC_d[i * 512:(i + 1) * 512, :], 1, wring, stgF)
                proj_T(xT, 16, o_win[:, 2048 + i * 512:2048 + (i + 1) * 512], 512, VC_d[:, i * 512:(i + 1) * 512], BF16, wring, stgT)
            proj_T(xT, 16, o_win[:, 3072:3584], 512, CQ_d[:, 0:512], F32, wring, stgT32)
            proj_T(xT, 16, o_win[:, 3584:3840], 256, CQ_d[:, 512:768], F32, wring, stgT32)
            wkr = T([128, 16, 96], BF16, dma="sw", name="wkr")
            wkrot = T([128, 16, 96], BF16, name="wkrot")
            vop("dve", "memset", [], [wkr], ap=wkr.ap, constant=0.0)
            vop("pool", "memset", [], [wkrot], ap=wkrot.ap, constant=0.0)
            dma("pool", wkr.ap[:, :, 64:96], o_win[:, 3840:3872].rearrange("(k p) e -> p k e", p=128), wkr, [], [wkr])
            vop("dve", "tensor_single_scalar", [wkr], [wkrot], out=wkrot.ap[:, :, 64:80], in_=wkr.ap[:, :, 80:96], scalar=-1.0, op=ALU.mult)
            cp("dve", wkrot.ap[:, :, 80:96], wkr.ap[:, :, 64:80], [wkr], [wkrot])
            tmpr = TR(4, [128, 512], F32, name="ropetmp")
            stK = T([128, S], BF16, dma=True, name="stKr")
            for tg in range(4):
                pm, pr = PB[0], PB[1]
                for k in range(16):
                    mm(pm.ap[0:96, :], wkr.ap[:, k, :], xT.ap[:, k, tg * 512:(tg + 1) * 512], k == 0, k == 15, [wkr, xT], [pm])
                for k in range(16):
                    mm(pr.ap[0:96, :], wkrot.ap[:, k, :], xT.ap[:, k, tg * 512:(tg + 1) * 512], k == 0, k == 15, [wkrot, xT], [pr])
                rope_evac(pm, pr, stK, tg, tmpr)
            for h in range(16):
                dma("sp", KD_d[h, 64:96, :], stK.ap[64:96, :], stK, [stK], [])
            phase_end()

            cT = T([128, 6, S], BF16, name="cT")
            gq = T([128, 768], F32, dma=True, name="gq")
            dma("sp", gq.ap[:, 0:512], o_qg.partition_broadcast(128), gq, [], [gq])
            dma("sp", gq.ap[:, 512:768], o_kvg.partition_broadcast(128), gq, [], [gq])
            cl = TR(2, [128, 768], F32, dma=True, name="cl")
            junk = TR(2, [128, 768], F32, name="junk")
            cbf = TR(2, [128, 768], BF16, name="cbf")
            str_ = TR(2, [128, 8], F32, name="st")
            pbt = Ring([PB[6], PB[7]])
            for t in range(NT):
                c = cl.next()
                dma("sp", c.ap, CQ_d[t * 128:(t + 1) * 128, :], c, [], [c])
                jk = junk.next()
                st = str_.next()
                act(jk.ap[:, 0:512], c.ap[:, 0:512], AF.Square, [c], [jk, st], accum=st.ap[:, 0:1])
                act(jk.ap[:, 512:768], c.ap[:, 512:768], AF.Square, [c], [jk, st], accum=st.ap[:, 1:2])
                vop("dve", "tensor_scalar", [st], [st], out=st.ap[:, 2:3], in0=st.ap[:, 0:1], scalar1=1.0 / 512.0, scalar2=RMS_EPS, op0=ALU.mult, op1=ALU.add)
                vop("dve", "tensor_scalar", [st], [st], out=st.ap[:, 3:4], in0=st.ap[:, 1:2], scalar1=1.0 / 256.0, scalar2=RMS_EPS, op0=ALU.mult, op1=ALU.add)
                act(st.ap[:, 4:6], st.ap[:, 2:4], AF.Sqrt, [st], [st])
                vop("dve", "reciprocal", [st], [st], out=st.ap[:, 6:8], in_=st.ap[:, 4:6])
                vop("pool", "tensor_tensor", [c, gq], [c], out=c.ap, in0=c.ap, in1=gq.ap, op=ALU.mult)
                cb = cbf.next()
                vop("dve", "tensor_scalar", [c, st], [cb], out=cb.ap[:, 0:512], in0=c.ap[:, 0:512], scalar1=st.ap[:, 6:7], scalar2=None, op0=ALU.mult)
                vop("dve", "tensor_scalar", [c, st], [cb], out=cb.ap[:, 512:768], in0=c.ap[:, 512:768], scalar1=st.ap[:, 7:8], scalar2=None, op0=ALU.mult)
                pb = pbt.next()
                pv = pb.ap.bitcast(BF16)
                for k in range(6):
                    s.op("pe", (lambda pv=pv, k=k, cb=cb: (lambda e: e.transpose(pv[:, k * 128:(k + 1) * 128], cb.ap[:, k * 128:(k + 1) * 128], ident.ap)))(),
                         rs([cb, ident]), rs([pb]))
                cp("act", cT.ap[:, :, t * 128:(t + 1) * 128], pv[:, 0:768].rearrange("p (k t) -> p k t", k=6), [pb], [cT])
            wq = T([128, 4, 1536], BF16, dma="sw", name="wq")
            wqr = T([128, 4, 1536], BF16, name="wqr")
            dma("pool", wq.ap, o_wuq.rearrange("(k p) e -> p k e", p=128), wq, [], [wq])
            wq4 = wq.ap.rearrange("p k (h e) -> p k h e", h=16)
            wqr4 = wqr.ap.rearrange("p k (h e) -> p k h e", h=16)
            cp("dve", wqr.ap, wq.ap, [wq], [wqr])
            for k in range(4):
                vop("dve", "tensor_single_scalar", [wq, wqr], [wqr], out=wqr4[:, k, :, 64:80], in_=wq4[:, k, :, 80:96], scalar=-1.0, op=ALU.mult)
                cp("dve", wqr4[:, k, :, 80:96], wq4[:, k, :, 64:80], [wq, wqr], [wqr])
            wkv = T([128, 2, 2048], BF16, dma="sw", name="wkv")
            dma("pool", wkv.ap, o_wukv.rearrange("(k p) e -> p k e", p=128), wkv, [], [wkv])
            stQ = TR(2, [128, S], BF16, dma=True, name="stQ")
            stKn = TR(2, [128, S], BF16, dma=True, name="stKn")
            stV = TR(2, [128, 1024], BF16, dma=True, name="stV")
            tmpr = TR(4, [128, 512], F32, name="ropetmp")
            pbq = Ring(PB[0:6])
            for h in range(16):
                sq = stQ.next()
                for tg in range(4):
                    pm = pbq.next()
                    pr = pbq.next()
                    for k in range(4):
                        mm(pm.ap[0:96, :], wq4[:, k, h, :], cT.ap[:, k, tg * 512:(tg + 1) * 512], k == 0, k == 3, [wq, cT], [pm])
                    for k in range(4):
                        mm(pr.ap[0:96, :], wqr4[:, k, h, :], cT.ap[:, k, tg * 512:(tg + 1) * 512], k == 0, k == 3, [wqr, cT], [pr])
                    cp("act", sq.ap[0:64, tg * 512:(tg + 1) * 512], pm.ap[0:64, :], [pm], [sq])
                    rope_evac(pm, pr, sq, tg, tmpr)
                dma("sp", QD_d[h], sq.ap[0:96, :], sq, [sq], [])
                sk = stKn.next()
                for tg in range(4):
                    pk = pbq.next()
                    for k in range(2):
                        mm(pk.ap[0:64, :], wkv.ap[:, k, h * 128:h * 128 + 64], cT.ap[:, 4 + k, tg * 512:(tg + 1) * 512], k == 0, k == 1, [wkv, cT], [pk])
                    cp(ev_eng(), sk.ap[0:64, tg * 512:(tg + 1) * 512], pk.ap[0:64, :], [pk], [sk])
                dma("sp", KD_d[h, 0:64, :], sk.ap[0:64, :], sk, [sk], [])
            wkv5 = wkv.ap.rearrange("p k (h two d) -> p k h two d", two=2, d=64)
            for t in range(NT):
                sv = stV.next()
                for hf in range(2):
                    pb = pbq.next()
                    for k in range(2):
                        mm(pb.ap.rearrange("p (h d) -> p h d", h=8), cT.ap[:, 4 + k, t * 128:(t + 1) * 128], wkv5[:, k, hf * 8:(hf + 1) * 8, 1, :],
                           k == 0, k == 1, [wkv, cT], [pb])
                    cp(ev_eng(), sv.ap[:, hf * 512:(hf + 1) * 512], pb.ap, [pb], [sv])
                dma("sp", VD_d[t * 128:(t + 1) * 128, :], sv.ap, sv, [sv], [])
            persist_end[0] = base_persist
            phase_end()

            def attn_C():
                VCp = T([128, NT, 1024], BF16, dma=True, name="VCp")
                dma("sp", VCp.ap, VC_d.rearrange("(n p) c -> p n c", p=128), VCp, [], [VCp])
                OC = T([128, NT, 1024], BF16, dma=True, name="OC")
                Qr = TR(3, [64, S], BF16, dma=True, name="Qh")
                Kr = TR(3, [64, S], BF16, dma=True, name="Kh")
                Er = TR(4, [128, 512], F32, name="E")
                SPr = TR(4, [128, 512], F32, name="SP")
                LKr = TR(4, [128, 512], F32, name="LK")
                Wr = TR(4, [128, 512], BF16, name="W")
                Srun = TR(2, [128, 512], F32, name="Srun")
                pZ = Ring([PB[0], PB[1]])
                pA = Ring([PB[2], PB[3]])
                pO = PB[4:8]
                heads = [(Qr.next(), Kr.next()) for _ in range(16)]

                def load_head(h):
                    Qh, Kh = heads[h]
                    dma("sp", Qh.ap, QC_d[h * 64:(h + 1) * 64, :], Qh, [], [Qh])
                    dma("sp", Kh.ap, KC_d[h * 64:(h + 1) * 64, :], Kh, [], [Kh])
                items = []
                for h in range(16):
                    for G in range(4):
                        Sr = Srun.next()
                        for j in range(4 * G + 3, -1, -1):
                            items.append((h, G, j, Sr))
                ctx = [dict(pz=pZ.next(), pa=pA.next(), E=Er.next(), SP=SPr.next(), LK=LKr.next(), W=Wr.next()) for _ in items]
                load_head(0)

                def s1(it):
                    h, G, j, Sr = items[it]
                    c = ctx[it]
                    if G == 0 and j == 3 and h + 1 < 16:
                        load_head(h + 1)
                    Qh, Kh = heads[h]
                    c0 = max(j - 4 * G, 0) * 128
                    pz, E, SP, LK = c["pz"], c["E"], c["SP"], c["LK"]
                    mm(pz.ap[:, c0:512], Kh.ap[:, j * 128:(j + 1) * 128], Qh.ap[:, G * 512 + c0:(G + 1) * 512], True, True, [Kh, Qh], [pz])
                    act(E.ap[:, c0:512], pz.ap[:, c0:512], AF.Exp, [pz], [E], scale=-0.125)
                    act(SP.ap[:, c0:512], E.ap[:, c0:512], AF.Ln, [E], [SP], bias=1.0, scale=1.0)
                    vop("dve", "scalar_tensor_tensor", [pz, SP], [LK], out=LK.ap[:, c0:512], in0=pz.ap[:, c0:512], scalar=-0.125,
                        in1=SP.ap[:, c0:512], op0=ALU.mult, op1=ALU.subtract)
                    if j >= 4 * G:
                        vop("pool", "tensor_tensor", [LK, mC01], [LK], out=LK.ap[:, c0:c0 + 128], in0=LK.ap[:, c0:c0 + 128], in1=mC01.ap, op=ALU.mult)

                def s2(it):
                    h, G, j, Sr = items[it]
                    c = ctx[it]
                    c0 = max(j - 4 * G, 0) * 128
                    pa, E, SP, LK, W = c["pa"], c["E"], c["SP"], c["LK"], c["W"]
                    first = j == 4 * G + 3
                    if first:
                        vop("pool", "memset", [], [Sr], ap=Sr.ap, constant=0.0)
                    mm(pa.ap[:, c0:512], Ustr.ap, LK.ap[:, c0:512], True, first, [Ustr, LK], [pa])
                    if not first:
                        mm(pa.ap[:, c0:512], ones.ap, Sr.ap[:, c0:512], False, True, [ones, Sr], [pa])
                    vop("dve", "tensor_tensor", [pa, SP], [E], out=E.ap[:, c0:512], in0=pa.ap[:, c0:512], in1=SP.ap[:, c0:512], op=ALU.subtract)
                    act(W.ap[:, c0:512], E.ap[:, c0:512], AF.Exp, [E], [W])
                    if j >= 4 * G:
                        vop("pool", "tensor_tensor", [W, mC01b], [W], out=W.ap[:, c0:c0 + 128], in0=W.ap[:, c0:c0 + 128], in1=mC01b.ap, op=ALU.mult)
                    if j > 0:
                        vop("pool", "tensor_tensor", [Sr, LK], [Sr], out=Sr.ap[:, c0:512], in0=Sr.ap[:, c0:512], in1=LK.ap[:, c0:512], op=ALU.add)

                def s3(it):
                    h, G, j, Sr = items[it]
                    c = ctx[it]
                    q0 = max(j - 4 * G, 0)
                    W = c["W"]
                    for qt in range(q0, 4):
                        mm(pO[qt].ap[:, 0:64], W.ap[:, qt * 128:(qt + 1) * 128], VCp.ap[:, j, h * 64:(h + 1) * 64],
                           j == 4 * G + qt, j == 0, [W, VCp], [pO[qt]])
                    if j == 0:
                        for qt in range(4):
                            cp("act" if qt % 2 else "dve", OC.ap[:, 4 * G + qt, h * 64:(h + 1) * 64], pO[qt].ap[:, 0:64], [pO[qt]], [OC])

                run_pipeline(len(items), [s1, s2, s3])
                for n in range(NT):
                    dma("sp", Y1_d[n * 128:(n + 1) * 128, 0:1024], OC.ap[:, n, :], OC, [OC], [])

            def attn_D():
                sc = 1.0 / math.sqrt(96.0)
                VDp = T([128, NT, 1024], BF16, dma=True, name="VDp")
                dma("sp", VDp.ap, VD_d.rearrange("(n p) c -> p n c", p=128), VDp, [], [VDp])
                OD = T([128, NT, 1024], BF16, dma=True, name="OD")
                Qr = TR(3, [96, S], BF16, dma=True, name="Qh")
                Kr = TR(3, [96, S], BF16, dma=True, name="Kh")
                Pr = TR(4, [128, S], BF16, name="P")
                PTr = TR(4, [128, S], BF16, name="PT")
                Sdr = TR(3, [128, 128], F32, name="Sd")
                str_ = TR(6, [128, 16], F32, name="st")
                pSa = Ring([PB[0:2], PB[2:4]])
                pT = Ring([PB[4], PB[5]])
                pO = Ring([PB[6], PB[7]])
                heads = [(Qr.next(), Kr.next()) for _ in range(16)]

                def load_head(h):
                    Qh, Kh = heads[h]
                    dma("sp", Qh.ap, QD_d[h], Qh, [], [Qh])
                    dma("sp", Kh.ap, KD_d[h], Kh, [], [Kh])
                items = [(h, i) for h in range(16) for i in range(NT)]
                ctx = []
                for (h, i) in items:
                    nbank = (i + 4) // 4
                    pS = pSa.next() if nbank <= 2 else PB[0:4]
                    ctx.append(dict(pS=pS, Sd=Sdr.next(), st=str_.next(), Pt=Pr.next(), PT=PTr.next(), po=pO.next()))
                load_head(0)

                def geom(i):
                    nkb = i + 1
                    nbank = (nkb + 3) // 4
                    widths = [min(512, nkb * 128 - bk * 512) for bk in range(nbank)]
                    return nkb, nbank, widths

                def sA(it):
                    h, i = items[it]
                    c = ctx[it]
                    if i == 0 and h + 1 < 16:
                        load_head(h + 1)
                    Qh, Kh = heads[h]
                    nkb, nbank, widths = geom(i)
                    pS, Sd, st, Pt = c["pS"], c["Sd"], c["st"], c["Pt"]
                    for bk in range(nbank):
                        mm(pS[bk].ap[:, 0:widths[bk]], Qh.ap[:, i * 128:(i + 1) * 128], Kh.ap[:, bk * 512:bk * 512 + widths[bk]],
                           True, True, [Qh, Kh], [pS[bk]])
                    bd = nbank - 1
                    dc = widths[bd] - 128
                    vop("dve", "tensor_tensor", [pS[bd], McD], [Sd], out=Sd.ap, in0=pS[bd].ap[:, dc:dc + 128], in1=McD.ap, op=ALU.add)
                    vop("dve", "reduce_max", [Sd], [st], out=st.ap[:, 0:1], in_=Sd.ap, axis=AX.X)
                    ncol = 1
                    for bk in range(nbank):
                        wv = widths[bk] - (128 if bk == bd else 0)
                        if wv > 0:
                            vop("dve", "reduce_max", [pS[bk]], [st], out=st.ap[:, ncol:ncol + 1], in_=pS[bk].ap[:, 0:wv], axis=AX.X)
                            ncol += 1
                    if ncol > 1:
                        vop("dve", "reduce_max", [st], [st], out=st.ap[:, 5:6], in_=st.ap[:, 0:ncol], axis=AX.X)
                        mxc = st.ap[:, 5:6]
                    else:
                        mxc = st.ap[:, 0:1]
                    vop("dve", "tensor_single_scalar", [st], [st], out=st.ap[:, 6:7], in_=mxc, scalar=-sc, op=ALU.mult)
                    act(Pt.ap[:, i * 128:(i + 1) * 128], Sd.ap, AF.Exp, [Sd, st], [Pt, st], bias=st.ap[:, 6:7], scale=sc, accum=st.ap[:, 8:9])
                    ncol = 1
                    for bk in range(nbank):
                        wv = widths[bk] - (128 if bk == bd else 0)
                        if wv > 0:
                            act(Pt.ap[:, bk * 512:bk * 512 + wv], pS[bk].ap[:, 0:wv], AF.Exp, [pS[bk], st], [Pt, st],
                                bias=st.ap[:, 6:7], scale=sc, accum=st.ap[:, 8 + ncol:9 + ncol])
                            ncol += 1
                    c["ncol"] = ncol

                def sB(it):
                    h, i = items[it]
                    c = ctx[it]
                    nkb, nbank, widths = geom(i)
                    Pt, PT = c["Pt"], c["PT"]
                    for k0 in range(0, nkb, 8):
                        kn = min(8, nkb - k0)
                        pt = pT.next()
                        ptv = pt.ap.bitcast(BF16)
                        for jj in range(kn):
                            kb = k0 + jj
                            s.op("pe", (lambda ptv=ptv, jj=jj, Pt=Pt, kb=kb: (lambda e: e.transpose(ptv[:, jj * 128:(jj + 1) * 128], Pt.ap[:, kb * 128:(kb + 1) * 128], ident.ap)))(),
                                 rs([Pt, ident]), rs([pt]))
                        cp("act" if (k0 // 8) % 2 == 0 else "dve", PT.ap[:, k0 * 128:(k0 + kn) * 128], ptv[:, 0:kn * 128], [pt], [PT])

                def sC(it):
                    h, i = items[it]
                    c = ctx[it]
                    nkb, nbank, widths = geom(i)
                    PT, po, st = c["PT"], c["po"], c["st"]
                    ncol = c["ncol"]
                    for kb in range(nkb):
                        mm(po.ap[:, 0:64], PT.ap[:, kb * 128:(kb + 1) * 128], VDp.ap[:, kb, h * 64:(h + 1) * 64], kb == 0, kb == nkb - 1, [PT, VDp], [po])
                    if ncol > 1:
                        vop("dve", "reduce_sum", [st], [st], out=st.ap[:, 7:8], in_=st.ap[:, 8:8 + ncol], axis=AX.X)
                        den = st.ap[:, 7:8]
                    else:
                        den = st.ap[:, 8:9]
                    vop("dve", "reciprocal", [st], [st], out=st.ap[:, 14:15], in_=den)
                    vop("dve", "tensor_scalar", [po, st], [OD], out=OD.ap[:, i, h * 64:(h + 1) * 64], in0=po.ap[:, 0:64],
                        scalar1=st.ap[:, 14:15], scalar2=None, op0=ALU.mult)

                run_pipeline(len(items), [sA, sB, sC])
                for n in range(NT):
                    dma("sp", Y1_d[n * 128:(n + 1) * 128, 1024:2048], OD.ap[:, n, :], OD, [OD], [])

            if "skipC" not in dbg:
                attn_C()
                phase_end()
            if "skipD" not in dbg:
                attn_D()
                phase_end()
            out_proj_ln(Y1_d, 16, o_wout, xs2, ln1_g[1:2, :], ln1_b[1:2, :], xs1)
            phase_end()
            mlp(1, xs1, ln2_g[1:2, :], ln2_b[1:2, :], out_d)

        s.barrier()
        s.emit()
    return nc


_NC_CACHE = {}


def kernel(**inputs):
    B = inputs["x"].shape[0]
    if "nc" not in _NC_CACHE:
        _NC_CACHE["nc"] = build()
    nc = _NC_CACHE["nc"]
    f = lambda a: np.ascontiguousarray(np.asarray(a, dtype=np.float32))
    shared = {
        "even_w_in": f(inputs["even_w_in"][0]),
        "even_sinks": f(inputs["even_sinks"][0]).reshape(1, 16),
        "even_w_out": f(inputs["even_w_out"][0]),
        "odd_w_in": f(inputs["odd_w_in"][0]),
        "odd_q_norm_g": f(inputs["odd_q_norm_g"][0]).reshape(1, 512),
        "odd_kv_norm_g": f(inputs["odd_kv_norm_g"][0]).reshape(1, 256),
        "odd_w_uq": f(inputs["odd_w_uq"][0]),
        "odd_w_ukv": f(inputs["odd_w_ukv"][0]),
        "odd_w_out": f(inputs["odd_w_out"][0]),
        "ln1_g": f(inputs["ln1_g"]), "ln1_b": f(inputs["ln1_b"]),
        "ln2_g": f(inputs["ln2_g"]), "ln2_b": f(inputs["ln2_b"]),
        "mlp_w1": f(inputs["mlp_w1"]), "mlp_w2": f(inputs["mlp_w2"]),
    }
    x = f(inputs["x"])
    in_maps = [dict(shared, x=x[b]) for b in range(B)]
    res = run_bass_kernel_spmd(nc, in_maps, core_ids=list(range(B)))
    return np.stack([r["out"] for r in res.results], axis=0)
```

```python
import contextlib
import math
import numpy as np
import concourse.bass as bass
import concourse.mybir as mybir
from concourse.bass_utils import run_bass_kernel_spmd

F32 = mybir.dt.float32
BF16 = mybir.dt.bfloat16
I32 = mybir.dt.int32
AF = mybir.ActivationFunctionType
ALU = mybir.AluOpType
AX = mybir.AxisListType

S = 2048
D = 2048
NT = 16
DFF = 8192
ALPHA = 4.0 ** 0.25
LN_EPS = 1e-5
RMS_EPS = 1e-6
BIG = 1.0e9


class Res:
    __slots__ = ("name", "w", "r")

    def __init__(self, name=""):
        self.name = name
        self.w = None
        self.r = {}


class Buf:
    __slots__ = ("ap", "res", "sem")

    def __init__(self, ap, res, sem=None):
        self.ap = ap
        self.res = res
        self.sem = sem


class Ring:
    def __init__(self, items):
        self.items = items
        self.i = 0

    def next(self):
        it = self.items[self.i % len(self.items)]
        self.i += 1
        return it


class Sched:
    ENG = ("pe", "act", "dve", "pool", "sp")

    def __init__(self, nc, stack):
        self.nc = nc
        self.stack = stack
        self.prog = {e: [] for e in self.ENG}
        self.sem = {}
        self.cnt = {}
        self.known = {e: {} for e in self.ENG}
        self.free_dsems = []
        self.used_dsems = []
        self.ndsem = 0
        for e in self.ENG:
            self.newsem("E_" + e)

    def newsem(self, name):
        self.sem[name] = self.stack.enter_context(self.nc.semaphore(name))
        self.cnt[name] = 0
        return name

    def dsem(self, kind="H"):
        fl = [x for x in self.free_dsems if x[0] == kind]
        if fl:
            n = fl[-1]
            self.free_dsems.remove(n)
        else:
            n = self.newsem("%s%d" % (kind, self.ndsem))
            self.ndsem += 1
        self.used_dsems.append(n)
        return n

    def _deps(self, eng, reads, writes):
        need = {}

        def add(ev):
            if ev is None:
                return
            sm, v = ev
            if need.get(sm, 0) < v:
                need[sm] = v
        for r in reads:
            add(r.w)
        for w in writes:
            add(w.w)
            for sm, v in w.r.items():
                add((sm, v))
        kn = self.known[eng]
        for sm, v in need.items():
            if eng == "pe" and sm == "E_pe":
                continue
            if kn.get(sm, 0) < v:
                kn[sm] = v
                self.prog[eng].append(("wait", sm, v))

    def _commit(self, ev, reads, writes):
        sm, v = ev
        for r in reads:
            if r.r.get(sm, 0) < v:
                r.r[sm] = v
        for w in writes:
            w.w = ev
            w.r = {}

    def op(self, eng, fn, reads=(), writes=()):
        self._deps(eng, reads, writes)
        sm = "E_" + eng
        self.cnt[sm] += 1
        ev = (sm, self.cnt[sm])
        self.prog[eng].append(("op", fn, sm, 1))
        self._commit(ev, reads, writes)

    def dma(self, eng, out, in_, sem, reads=(), writes=()):
        self._deps(eng, reads, writes)
        self.cnt[sem] += 16
        ev = (sem, self.cnt[sem])
        self.prog[eng].append(("op", lambda e: e.dma_start(out=out, in_=in_), sem, 16))
        self._commit(ev, reads, writes)

    def barrier(self, final=False):
        for e in self.ENG:
            kn = self.known[e]
            for sm, v in self.cnt.items():
                if sm.startswith("WC") and not final:
                    continue
                if v > 0 and kn.get(sm, 0) < v:
                    kn[sm] = v
                    self.prog[e].append(("wait", sm, v))
        self.free_dsems.extend(self.used_dsems)
        self.used_dsems = []

    def emit(self):
        nc = self.nc

        def replay(name):
            def f(eng):
                for it in self.prog[name]:
                    if it[0] == "wait":
                        eng.wait_ge(self.sem[it[1]], it[2])
                    else:
                        it[1](eng).then_inc(self.sem[it[2]], it[3])
            return f

        with nc.Block() as block:
            block.tensor(replay("pe"))
            block.scalar(replay("act"))
            block.vector(replay("dve"))
            block.gpsimd(replay("pool"))
            block.sync(replay("sp"))


def alibi(n):
    return [2.0 ** (-8.0 * (i + 1) / n) for i in range(n)]


def build(dbg=()):
    nc = bass.Bass("TRN2", target_bir_lowering=False)

    def din(name, shape):
        return nc.dram_tensor(name, list(shape), F32, kind="ExternalInput").ap()

    x_in = din("x", [S, D])
    e_win = din("even_w_in", [D, 5888])
    e_sinks = din("even_sinks", [1, 16])
    e_wout = din("even_w_out", [1536, D])
    o_win = din("odd_w_in", [D, 3872])
    o_qg = din("odd_q_norm_g", [1, 512])
    o_kvg = din("odd_kv_norm_g", [1, 256])
    o_wuq = din("odd_w_uq", [512, 1536])
    o_wukv = din("odd_w_ukv", [256, 2048])
    o_wout = din("odd_w_out", [D, D])
    ln1_g = din("ln1_g", [2, D])
    ln1_b = din("ln1_b", [2, D])
    ln2_g = din("ln2_g", [2, D])
    ln2_b = din("ln2_b", [2, D])
    w1_in = din("mlp_w1", [2, D, DFF])
    w2_in = din("mlp_w2", [2, DFF, D])
    out_d = nc.dram_tensor("out", [S, D], F32, kind="ExternalOutput").ap()

    def dscr(name, shape, dt):
        kind = "ExternalOutput" if name in dbg else "Internal"
        return nc.dram_tensor(name, list(shape), dt, kind=kind).ap()

    w1b = dscr("w1b", [2, D, DFF], BF16)
    w2b = dscr("w2b", [2, DFF, D], BF16)
    xs1 = dscr("xs1", [S, D], F32)
    xs2 = dscr("xs2", [S, D], F32)
    QA_d = dscr("QA_d", [1024, S], BF16)
    KA_d = dscr("KA_d", [128, S], BF16)
    VA_d = dscr("VA_d", [S, 128], BF16)
    QB_d = [dscr("QB%d_d" % g, [512, S], BF16) for g in range(3)]
    KB_d = [dscr("KB%d_d" % g, [512, S], BF16) for g in range(3)]
    VB_d = [dscr("VB%d_d" % g, [S, 512], BF16) for g in range(3)]
    OB_d = [dscr("OB%d_d" % g, [S, 512], F32) for g in range(3)]
    LSE_d = [dscr("LSE%d_d" % g, [S, 8], F32) for g in range(3)]
    Y0_d = dscr("Y0_d", [S, 1536], BF16)
    QC_d = dscr("QC_d", [1024, S], BF16)
    KC_d = dscr("KC_d", [1024, S], BF16)
    VC_d = dscr("VC_d", [S, 1024], BF16)
    CQ_d = dscr("CQ_d", [S, 768], F32)
    QD_d = dscr("QD_d", [16, 96, S], BF16)
    KD_d = dscr("KD_d", [16, 96, S], BF16)
    VD_d = dscr("VD_d", [S, 1024], BF16)
    Y1_d = dscr("Y1_d", [S, 2048], BF16)

    with contextlib.ExitStack() as stack:
        s = Sched(nc, stack)
        ARENA_ELEMS = 103 * 1024
        arena = nc.alloc_sbuf_tensor("arena", [128, ARENA_ELEMS], BF16)
        aoff = [0]
        persist_end = [0]

        def T(shape, dt, dma=False, name=""):
            esz = 2 if dt == BF16 else 4
            nel = int(np.prod(shape[1:]))
            nb16 = (nel * esz + 63) // 64 * 32
            assert aoff[0] + nb16 <= ARENA_ELEMS, "arena overflow %s %d" % (name, aoff[0] + nb16)
            v = arena[:, aoff[0]:aoff[0] + nel * esz // 2]
            aoff[0] += nb16
            if dt != BF16:
                v = v.bitcast(dt)
            if len(shape) == 3:
                v = v.rearrange("p (a b) -> p a b", a=shape[1])
            elif len(shape) == 4:
                v = v.rearrange("p (a b c) -> p a b c", a=shape[1], b=shape[2])
            if shape[0] != 128:
                v = v[0:shape[0]]
            return Buf(v, Res(name), s.dsem("W" if dma == "sw" else "H") if dma else None)

        def TR(n, shape, dt, dma=False, name=""):
            return Ring([T(shape, dt, dma, name) for _ in range(n)])

        def phase_end():
            s.barrier()
            aoff[0] = persist_end[0]

        PB = [Buf(nc.alloc_psum_tensor("pb%d" % i, [128, 512], F32)[:], Res("pb%d" % i)) for i in range(8)]

        def rs(bufs):
            return [b.res for b in bufs]

        def mm(out, lhsT, rhs, start, stop, rd, wr):
            s.op("pe", lambda e: e.matmul(out, lhsT, rhs, start=start, stop=stop), rs(rd), rs(wr))

        def act(out, in_, func, rd, wr, bias=None, scale=None, accum=None, eng="act"):
            kw = {}
            if bias is not None:
                kw["bias"] = bias
            if scale is not None:
                kw["scale"] = scale
            if accum is not None:
                kw["accum_out"] = accum
            s.op(eng, lambda e: e.activation(out=out, in_=in_, func=func, **kw), rs(rd), rs(wr))

        def vop(eng, meth, rd, wr, **kw):
            s.op(eng, lambda e: getattr(e, meth)(**kw), rs(rd), rs(wr))

        def cp(eng, out, in_, rd, wr):
            if eng == "act":
                s.op("act", lambda e: e.copy(out=out, in_=in_), rs(rd), rs(wr))
            else:
                s.op(eng, lambda e: e.tensor_copy(out=out, in_=in_), rs(rd), rs(wr))

        def dma(eng, out, in_, buf, rd=(), wr=()):
            assert (eng == "pool") == (buf.sem[0] == "W"), (eng, buf.sem)
            s.dma(eng, out, in_, buf.sem, rs(rd), rs(wr))

        def run_pipeline(N, stages):
            ns = len(stages)
            for t in range(N + ns - 1):
                for si in range(ns):
                    it = t - si
                    if 0 <= it < N:
                        stages[si](it)

        ident = T([128, 128], BF16, name="ident")
        Ustr = T([128, 128], F32, name="Ustr")
        ones = T([128, 128], F32, name="ones")
        mC01 = T([128, 128], F32, name="mC01")
        mC01b = T([128, 128], BF16, name="mC01b")
        UstrB = T([128, 128], BF16, name="UstrB")
        PenC = T([128, 128], F32, name="PenC")
        McD = T([128, 128], F32, name="McD")
        RA = T([128, 256], F32, name="RA")
        RB = T([128, 256], F32, name="RB")
        sinkt = T([128, 16], F32, dma=True, name="sink")
        sink8 = T([128, 16], F32, name="sink8")
        persist_end[0] = aoff[0]
        tmpi = T([128, 256], I32, name="tmpi")
        tmpf = T([128, 256], F32, name="tmpf")
        tmpg = T([128, 256], F32, name="tmpg")
        tmph = T([128, 256], F32, name="tmph")
        vop("pool", "iota", [], [tmpi], out=tmpi.ap[:, 0:128], pattern=[[-1, 128]], base=0, channel_multiplier=1)
        cp("dve", tmpf.ap[:, 0:128], tmpi.ap[:, 0:128], [tmpi], [tmpf])
        vop("dve", "tensor_single_scalar", [tmpf], [ident], out=ident.ap, in_=tmpf.ap[:, 0:128], scalar=0.0, op=ALU.is_equal)
        vop("dve", "tensor_single_scalar", [tmpf], [Ustr], out=Ustr.ap, in_=tmpf.ap[:, 0:128], scalar=0.0, op=ALU.is_gt)
        vop("dve", "tensor_single_scalar", [tmpf], [mC01], out=mC01.ap, in_=tmpf.ap[:, 0:128], scalar=0.0, op=ALU.is_lt)
        cp("dve", mC01b.ap, mC01.ap, [mC01], [mC01b])
        cp("dve", UstrB.ap, Ustr.ap, [Ustr], [UstrB])
        vop("dve", "tensor_scalar", [Ustr], [PenC], out=PenC.ap, in0=Ustr.ap, scalar1=1.0, scalar2=-BIG, op0=ALU.subtract, op1=ALU.mult)
        vop("dve", "memset", [], [ones], ap=ones.ap, constant=1.0)
        vop("dve", "tensor_scalar", [tmpf], [McD], out=McD.ap, in0=tmpf.ap[:, 0:128], scalar1=0.0, scalar2=1.0,
            op0=ALU.is_ge, op1=ALU.subtract)
        vop("dve", "tensor_single_scalar", [McD], [McD], out=McD.ap, in_=McD.ap, scalar=BIG, op=ALU.mult)
        vop("pool", "iota", [tmpf], [tmpi], out=tmpi.ap, pattern=[[-1, 256]], base=128, channel_multiplier=1)
        cp("dve", tmpf.ap, tmpi.ap, [tmpi], [tmpf])
        for Rt, nb in ((RA, 127.0), (RB, 128.0)):
            vop("dve", "tensor_single_scalar", [tmpf], [tmpg], out=tmpg.ap, in_=tmpf.ap, scalar=0.0, op=ALU.is_ge)
            vop("dve", "tensor_single_scalar", [tmpf], [tmph], out=tmph.ap, in_=tmpf.ap, scalar=nb, op=ALU.is_le)
            vop("dve", "tensor_tensor", [tmpg, tmph], [tmpg], out=tmpg.ap, in0=tmpg.ap, in1=tmph.ap, op=ALU.mult)
            vop("dve", "tensor_tensor", [tmpg, tmpf], [tmph], out=tmph.ap, in0=tmpg.ap, in1=tmpf.ap, op=ALU.mult)
            vop("dve", "tensor_scalar", [tmpg], [tmpg], out=tmpg.ap, in0=tmpg.ap, scalar1=1.0, scalar2=BIG,
                op0=ALU.subtract, op1=ALU.mult)
            vop("dve", "tensor_tensor", [tmpg, tmph], [Rt], out=Rt.ap, in0=tmpg.ap, in1=tmph.ap, op=ALU.subtract)
        dma("sp", sinkt.ap, e_sinks.partition_broadcast(128), sinkt, [], [sinkt])
        vop("dve", "tensor_single_scalar", [sinkt], [sink8], out=sink8.ap, in_=sinkt.ap, scalar=8.0, op=ALU.mult)

        wcast = [Buf(None, Res("wc%d" % l), s.newsem("WC%d" % l)) for l in range(2)]
        def issue_wcast(l):
            jobs = [(w1b[l, r0:r0 + 256, :], w1_in[l, r0:r0 + 256, :]) for r0 in range(0, D, 256)]
            jobs += [(w2b[l, r0:r0 + 1024, :], w2_in[l, r0:r0 + 1024, :]) for r0 in range(0, DFF, 1024)]
            for ji, (o_, i_) in enumerate(jobs):
                dma("pool", o_, i_, wcast[l], [], [wcast[l]] if ji == len(jobs) - 1 else [])
        issue_wcast(0)
        phase_end()

        evq = [0]

        def ev_eng():
            evq[0] += 1
            return "act" if evq[0] % 2 else "dve"

        def load_T(src, ncol, dst, is_f32, keep=None):
            kc = ncol // 128
            ld = TR(2, [128, ncol], F32 if is_f32 else BF16, dma=True, name="ldT")
            cb = TR(2, [128, ncol], BF16, name="cbT") if is_f32 else None
            pbr = Ring([PB[6], PB[7]])
            for t in range(NT):
                lt = ld.next()
                dma("sp", lt.ap, src[t * 128:(t + 1) * 128, :], lt, [], [lt])
                if is_f32:
                    ct = cb.next()
                    cp("pool", ct.ap, lt.ap, [lt], [ct])
                else:
                    ct = lt
                for k0 in range(0, kc, 8):
                    kn = min(8, kc - k0)
                    pb = pbr.next()
                    pv = pb.ap.bitcast(BF16)
                    for j in range(kn):
                        k = k0 + j
                        s.op("pe", (lambda pv=pv, j=j, ct=ct, k=k: (lambda e: e.transpose(pv[:, j * 128:(j + 1) * 128], ct.ap[:, k * 128:(k + 1) * 128], ident.ap)))(),
                             rs([ct, ident]), rs([pb]))
                    cp(ev_eng(), dst.ap[:, k0:k0 + kn, t * 128:(t + 1) * 128],
                       pv[:, 0:kn * 128].rearrange("p (k t) -> p k t", k=kn), [pb], [dst])

        def wload(dst, wsrc, kc, ncol):
            dma("pool", dst.ap[:, 0:kc, 0:ncol], wsrc.rearrange("(k p) e -> p k e", p=128), dst, [], [dst])

        def proj_F(xT, kc, wsrc, ncol, dst, dil, wring, stg, oscale=None):
            wt = wring.next()
            wload(wt, wsrc, kc, ncol)
            pbr = Ring(PB[0:6])
            for c in range(ncol // 128):
                st = stg.next()
                for tg in range(4):
                    pb = pbr.next()
                    for k in range(kc):
                        mm(pb.ap, wt.ap[:, k, c * 128:(c + 1) * 128], xT.ap[:, k, tg * 512:(tg + 1) * 512],
                           k == 0, k == kc - 1, [wt, xT], [pb])
                    if oscale is not None:
                        if ev_eng() == "act":
                            s.op("act", (lambda o=st.ap[:, tg * 512:(tg + 1) * 512], i=pb.ap: (lambda e: e.mul(out=o, in_=i, mul=oscale)))(), rs([pb]), rs([st]))
                        else:
                            vop("dve", "tensor_single_scalar", [pb], [st], out=st.ap[:, tg * 512:(tg + 1) * 512], in_=pb.ap, scalar=oscale, op=ALU.mult)
                    elif dil == 1:
                        cp(ev_eng(), st.ap[:, tg * 512:(tg + 1) * 512], pb.ap, [pb], [st])
                    else:
                        na = 512 // dil
                        cp(ev_eng(), st.ap.rearrange("p (r a) -> p a r", r=dil)[:, tg * na:(tg + 1) * na, :],
                           pb.ap.rearrange("p (a r) -> p a r", r=dil), [pb], [st])
                dma("sp", dst[c * 128:(c + 1) * 128, :], st.ap, st, [st], [])

        def proj_T(xT, kc, wsrc, ncol, dst, dst_dt, wring, stg):
            wt = wring.next()
            wload(wt, wsrc, kc, ncol)
            pbr = Ring(PB[0:6])
            for t in range(NT):
                pb = pbr.next()
                for k in range(kc):
                    mm(pb.ap[:, 0:ncol], xT.ap[:, k, t * 128:(t + 1) * 128], wt.ap[:, k, 0:ncol],
                       k == 0, k == kc - 1, [wt, xT], [pb])
                st = stg.next()
                cp(ev_eng(), st.ap[:, 0:ncol], pb.ap[:, 0:ncol], [pb], [st])
                dma("sp", dst[t * 128:(t + 1) * 128, :], st.ap[:, 0:ncol], st, [st], [])

        def layer_norm_tile(z, gt, bt, stat):
            st6 = stat.ap[:, 0:24].rearrange("p (c s) -> p c s", c=4)
            for c in range(4):
                vop("dve", "bn_stats", [z], [stat], out=st6[:, c, :], in_=z.ap[:, c * 512:(c + 1) * 512])
            mv = stat.ap[:, 24:26]
            vop("dve", "bn_aggr", [stat], [stat], out=mv, in_=st6)
            vop("dve", "tensor_single_scalar", [stat], [stat], out=stat.ap[:, 26:27], in_=stat.ap[:, 25:26], scalar=LN_EPS, op=ALU.add)
            act(stat.ap[:, 27:28], stat.ap[:, 26:27], AF.Sqrt, [stat], [stat])
            vop("dve", "reciprocal", [stat], [stat], out=stat.ap[:, 28:29], in_=stat.ap[:, 27:28])
            vop("dve", "tensor_scalar", [z, stat], [z], out=z.ap, in0=z.ap, scalar1=stat.ap[:, 24:25], scalar2=stat.ap[:, 28:29],
                op0=ALU.subtract, op1=ALU.mult)
            vop("pool", "tensor_tensor", [z, gt], [z], out=z.ap, in0=z.ap, in1=gt.ap, op=ALU.mult)
            vop("dve", "tensor_tensor", [z, bt], [z], out=z.ap, in0=z.ap, in1=bt.ap, op=ALU.add)

        def load_gb(g_src, b_src):
            gt = T([128, D], F32, dma=True, name="gam")
            bt = T([128, D], F32, dma=True, name="bet")
            dma("sp", gt.ap, g_src.partition_broadcast(128), gt, [], [gt])
            dma("sp", bt.ap, b_src.partition_broadcast(128), bt, [], [bt])
            return gt, bt

        def out_proj_ln(Yd, kc, wsrc, xres, g_src, b_src, dst):
            wt = T([128, kc, D], BF16, dma="sw", name="wout")
            for k0 in range(0, kc, 4):
                dma("pool", wt.ap[:, k0:k0 + 4, :], wsrc[k0 * 128:(k0 + 4) * 128, :].rearrange("(k p) e -> p k e", p=128), wt, [], [wt])
            gt, bt = load_gb(g_src, b_src)
            yl = TR(2, [128, kc * 128], BF16, dma=True, name="yl")
            yT = TR(2, [128, kc, 128], BF16, name="yT")
            zr = TR(2, [128, D], F32, dma=True, name="z")
            stat = TR(2, [128, 32], F32, name="stat")
            pbt = Ring([PB[4], PB[5]])
            for t in range(NT):
                y = yl.next()
                dma("sp", y.ap, Yd[t * 128:(t + 1) * 128, :], y, [], [y])
                z = zr.next()
                dma("sp", z.ap, xres[t * 128:(t + 1) * 128, :], z, [], [z])
                yt = yT.next()
                for k0 in range(0, kc, 8):
                    kn = min(8, kc - k0)
                    pb = pbt.next()
                    pv = pb.ap.bitcast(BF16)
                    for j in range(kn):
                        k = k0 + j
                        s.op("pe", (lambda pv=pv, j=j, y=y, k=k: (lambda e: e.transpose(pv[:, j * 128:(j + 1) * 128], y.ap[:, k * 128:(k + 1) * 128], ident.ap)))(),
                             rs([y, ident]), rs([pb]))
                    cp("act", yt.ap[:, k0:k0 + kn, :], pv[:, 0:kn * 128].rearrange("p (k t) -> p k t", k=kn), [pb], [yt])
                for dt in range(4):
                    pb = PB[dt]
                    for k in range(kc):
                        mm(pb.ap, yt.ap[:, k, :], wt.ap[:, k, dt * 512:(dt + 1) * 512], k == 0, k == kc - 1, [yt, wt], [pb])
                    vop("dve", "scalar_tensor_tensor", [z, pb], [z], out=z.ap[:, dt * 512:(dt + 1) * 512],
                        in0=z.ap[:, dt * 512:(dt + 1) * 512], scalar=ALPHA, in1=pb.ap, op0=ALU.mult, op1=ALU.add)
                layer_norm_tile(z, gt, bt, stat.next())
                dma("sp", dst[t * 128:(t + 1) * 128, :], z.ap, z, [z], [])

        def mlp(l, xsrc, g_src, b_src, dst):
            gt, bt = load_gb(g_src, b_src)
            zb = T([128, 4, D], F32, name="zb")
            zts = [Buf(zb.ap[:, tt, :], Res("z%d" % tt), s.dsem("H")) for tt in range(4)]
            xbs = [T([128, D], BF16, dma="sw", name="xb%d" % i) for i in range(2)]
            xTr = [T([128, 16, 512], BF16, name="xTm%d" % i) for i in range(2)]
            hT = T([128, 64, 512], BF16, name="hT")
            w1r = TR(2, [128, 16, 256], BF16, dma=True, name="w1")
            w2r = TR(3, [128, 8, 512], BF16, dma=True, name="w2")
            hr = TR(2, [128, 512], F32, name="hrelu")
            stat = TR(2, [128, 32], F32, name="stat")
            pbt = Ring([PB[6], PB[7]])
            pbh = Ring(PB[0:6])

            def issue_xload(G, tts):
                for tt in tts:
                    t = G * 4 + tt
                    xb = xbs[tt % 2]
                    dma("pool", xb.ap, xsrc[t * 128:(t + 1) * 128, :], xb, [], [xb])

            def prefetch_T(G, tts):
                xT = xTr[G % 2]
                for tt in tts:
                    xb = xbs[tt % 2]
                    for k0 in (0, 8):
                        pb = pbt.next()
                        pv = pb.ap.bitcast(BF16)
                        for j in range(8):
                            k = k0 + j
                            s.op("pe", (lambda pv=pv, j=j, xb=xb, k=k: (lambda e: e.transpose(pv[:, j * 128:(j + 1) * 128], xb.ap[:, k * 128:(k + 1) * 128], ident.ap)))(),
                                 rs([xb, ident]), rs([pb]))
                        cp("act", xT.ap[:, k0:k0 + 8, tt * 128:(tt + 1) * 128],
                           pv.rearrange("p (k t) -> p k t", k=8), [pb], [xT])

            issue_xload(0, [0, 1])
            prefetch_T(0, [0, 1])
            issue_xload(0, [2, 3])
            prefetch_T(0, [2, 3])
            for G in range(4):
                xT = xTr[G % 2]
                for f2 in range(32):
                    w1 = w1r.next()
                    dma("sp", w1.ap, w1b[l, :, f2 * 256:(f2 + 1) * 256].rearrange("(k p) f -> p k f", p=128), w1, [wcast[l]], [w1])
                    for fi in range(2):
                        f = f2 * 2 + fi
                        pb = pbh.next()
                        for k in range(16):
                            mm(pb.ap, w1.ap[:, k, fi * 128:(fi + 1) * 128], xT.ap[:, k, :], k == 0, k == 15, [w1, xT], [pb])
                        h = hr.next()
                        act(h.ap, pb.ap, AF.Relu, [pb], [h])
                        vop("pool" if f % 2 else "dve", "tensor_tensor", [h], [hT], out=hT.ap[:, f, :], in0=h.ap, in1=h.ap, op=ALU.mult)
                    if G > 0:
                        for tt in range(4):
                            if f2 == 1 + 4 * tt:
                                layer_norm_tile(zts[tt], gt, bt, stat.next())
                            if f2 == 4 + 4 * tt:
                                t = (G - 1) * 4 + tt
                                dma("sp", dst[t * 128:(t + 1) * 128, :], zts[tt].ap, zts[tt], [zts[tt]], [])
                    if f2 == 22:
                        for tt in range(4):
                            t = G * 4 + tt
                            dma("sp", zts[tt].ap, xsrc[t * 128:(t + 1) * 128, :], zts[tt], [], [zts[tt]])
                    if f2 == 26 and G < 3:
                        issue_xload(G + 1, [0, 1])
                for dt in range(4):
                    if G < 3 and dt == 0:
                        prefetch_T(G + 1, [0, 1])
                        issue_xload(G + 1, [2, 3])
                    if G < 3 and dt == 1:
                        prefetch_T(G + 1, [2, 3])
                    pbs = PB[0:4] if dt % 2 == 0 else PB[4:8]
                    for f8 in range(8):
                        w2 = w2r.next()
                        dma("sp", w2.ap, w2b[l, f8 * 1024:(f8 + 1) * 1024, dt * 512:(dt + 1) * 512].rearrange("(c p) d -> p c d", p=128),
                            w2, [wcast[l]], [w2])
                        for fi in range(8):
                            f = f8 * 8 + fi
                            for tt in range(4):
                                mm(pbs[tt].ap, hT.ap[:, f, tt * 128:(tt + 1) * 128], w2.ap[:, fi, :], f == 0, f == 63, [hT, w2], [pbs[tt]])
                    for tt in range(4):
                        zt = zts[tt]
                        vop("dve", "scalar_tensor_tensor", [zt, pbs[tt]], [zt], out=zt.ap[:, dt * 512:(dt + 1) * 512],
                            in0=zt.ap[:, dt * 512:(dt + 1) * 512], scalar=ALPHA, in1=pbs[tt].ap, op0=ALU.mult, op1=ALU.add)
            for tt in range(4):
                t = 12 + tt
                layer_norm_tile(zts[tt], gt, bt, stat.next())
                dma("sp", dst[t * 128:(t + 1) * 128, :], zts[tt].ap, zts[tt], [zts[tt]], [])

        def banded(Qd, Kd, kvmap, nh, Vd, nkv, dil, Rm, cvals, use_sink, out_mode, Yd=None, OBd=None, LSEd=None):
            L = S // dil
            nbpl = L // 128
            Vp = T([128, NT, nkv * 64], BF16, dma=True, name="Vp")
            for n in range(NT):
                r = (128 * n) // L
                a0 = (128 * n) % L
                st_ = r + dil * a0
                dma("sp", Vp.ap[:, n, :], Vd[st_:st_ + dil * 127 + 1:dil, :], Vp, [], [Vp])
            if out_mode == "A":
                Oall = T([128, NT, nh * 64], BF16, dma=True, name="Oall")
            else:
                Oall = T([128, NT, nh * 64], F32, dma=True, name="Oall")
                lse = T([128, NT, nh], F32, dma=True, name="lse")
            Qr = TR(3, [64, S], BF16, dma=True, name="Qh")
            Kr = TR(3, [64, S], BF16, dma=True, name="Kh")
            Tr = TR(3, [128, 256], F32, name="T")
            Pr = TR(4, [128, 256], BF16, name="P")
            PTr = TR(4, [128, 256], BF16, name="PT")
            str_ = TR(6, [128, 8], F32, name="st")
            pS = Ring([PB[0], PB[1], PB[6]])
            pT = Ring([PB[2], PB[3]])
            pO = Ring([PB[4], PB[5], PB[7]])
            heads = []
            lastkv = -1
            Kh = None
            for h in range(nh):
                Qh = Qr.next()
                kv = kvmap(h)
                newk = kv != lastkv
                if newk:
                    Kh = Kr.next()
                    lastkv = kv
                heads.append((Qh, Kh, kv, newk))

            def load_head(h):
                Qh, Kh, kv, newk = heads[h]
                dma("sp", Qh.ap, Qd[h * 64:(h + 1) * 64, :], Qh, [], [Qh])
                if newk:
                    dma("sp", Kh.ap, Kd[kv * 64:(kv + 1) * 64, :], Kh, [], [Kh])
            items = [(h, n) for h in range(nh) for n in range(NT)]
            ctx = [dict(ps=pS.next(), Tt=Tr.next(), st=str_.next(), Pt=Pr.next(), pt=pT.next(), PT=PTr.next(), po=pO.next())
                   for _ in items]
            load_head(0)

            def geom(n):
                hasprev = (n % nbpl) != 0
                nk = 256 if hasprev else 128
                ks = (n - 1) * 128 if hasprev else n * 128
                return hasprev, nk, ks

            def stA(it):
                h, n = items[it]
                c = ctx[it]
                if n == 0 and h + 1 < nh:
                    load_head(h + 1)
                Qh, Kh, kv, _ = heads[h]
                hasprev, nk, ks = geom(n)
                Rv = Rm.ap[:, 0:256] if hasprev else Rm.ap[:, 128:256]
                ps, Tt, st, Pt = c["ps"], c["Tt"], c["st"], c["Pt"]
                mm(ps.ap[:, 0:nk], Qh.ap[:, n * 128:(n + 1) * 128], Kh.ap[:, ks:ks + nk], True, True, [Qh, Kh], [ps])
                vop("dve", "scalar_tensor_tensor", [ps, Rm], [Tt], out=Tt.ap[:, 0:nk], in0=Rv, scalar=cvals[h], in1=ps.ap[:, 0:nk],
                    op0=ALU.mult, op1=ALU.add)
                vop("dve", "reduce_max", [Tt], [st], out=st.ap[:, 0:1], in_=Tt.ap[:, 0:nk], axis=AX.X)
                if use_sink:
                    vop("dve", "tensor_scalar", [st, sink8], [st], out=st.ap[:, 1:2], in0=st.ap[:, 0:1], scalar1=sink8.ap[:, h:h + 1], scalar2=-0.125,
                        op0=ALU.max, op1=ALU.mult)
                else:
                    vop("dve", "tensor_single_scalar", [st], [st], out=st.ap[:, 1:2], in_=st.ap[:, 0:1], scalar=-0.125, op=ALU.mult)
                act(Pt.ap[:, 0:nk], Tt.ap[:, 0:nk], AF.Exp, [Tt, st], [Pt, st], bias=st.ap[:, 1:2], scale=0.125, accum=st.ap[:, 2:3])
                if use_sink:
                    act(st.ap[:, 3:4], st.ap[:, 1:2], AF.Exp, [st, sinkt], [st], bias=sinkt.ap[:, h:h + 1], scale=1.0)

            def stB(it):
                h, n = items[it]
                c = ctx[it]
                hasprev, nk, ks = geom(n)
                Pt, pt, PT = c["Pt"], c["pt"], c["PT"]
                ptv = pt.ap.bitcast(BF16)
                for kb in range(nk // 128):
                    s.op("pe", (lambda ptv=ptv, kb=kb, Pt=Pt: (lambda e: e.transpose(ptv[:, kb * 128:(kb + 1) * 128], Pt.ap[:, kb * 128:(kb + 1) * 128], ident.ap)))(),
                         rs([Pt, ident]), rs([pt]))
                cp("act", PT.ap[:, 0:nk], ptv[:, 0:nk], [pt], [PT])

            def stC(it):
                h, n = items[it]
                c = ctx[it]
                Qh, Kh, kv, _ = heads[h]
                hasprev, nk, ks = geom(n)
                PT, po, st = c["PT"], c["po"], c["st"]
                nkb = nk // 128
                for kb in range(nkb):
                    blk = ks // 128 + kb
                    mm(po.ap[:, 0:64], PT.ap[:, kb * 128:(kb + 1) * 128], Vp.ap[:, blk, kv * 64:(kv + 1) * 64],
                       kb == 0, kb == nkb - 1, [PT, Vp], [po])
                if use_sink:
                    vop("dve", "tensor_tensor", [st], [st], out=st.ap[:, 2:3], in0=st.ap[:, 2:3], in1=st.ap[:, 3:4], op=ALU.add)
                vop("dve", "reciprocal", [st], [st], out=st.ap[:, 4:5], in_=st.ap[:, 2:3])
                vop("dve", "tensor_scalar", [po, st], [Oall], out=Oall.ap[:, n, h * 64:(h + 1) * 64], in0=po.ap[:, 0:64],
                    scalar1=st.ap[:, 4:5], scalar2=None, op0=ALU.mult)
                if out_mode == "B":
                    act(st.ap[:, 5:6], st.ap[:, 2:3], AF.Ln, [st], [st])
                    vop("dve", "tensor_tensor", [st], [lse], out=lse.ap[:, n, h:h + 1], in0=st.ap[:, 5:6], in1=st.ap[:, 1:2], op=ALU.subtract)

            run_pipeline(len(items), [stA, stB, stC])
            if out_mode == "A":
                for n in range(NT):
                    dma("sp", Yd[n * 128:(n + 1) * 128, 0:nh * 64], Oall.ap[:, n, :], Oall, [Oall], [])
            else:
                for n in range(NT):
                    r = (128 * n) // L
                    a0 = (128 * n) % L
                    st_ = r + dil * a0
                    dma("sp", OBd[st_:st_ + dil * 127 + 1:dil, :], Oall.ap[:, n, :], Oall, [Oall], [])
                    dma("sp", LSEd[st_:st_ + dil * 127 + 1:dil, :], lse.ap[:, n, :], lse, [lse], [])

        def combine_B():
            ol = [TR(2, [128, 512], F32, dma=True, name="o%d" % g) for g in range(3)]
            ll = TR(2, [128, 3, 8], F32, dma=True, name="l")
            wk = TR(2, [128, 3, 8], F32, name="wk")
            sm = TR(2, [128, 16], F32, name="sm")
            yo = TR(2, [128, 512], BF16, dma=True, name="yo")
            for t in range(NT):
                og = [ol[g].next() for g in range(3)]
                lt = ll.next()
                for g in range(3):
                    dma("sp", og[g].ap, OB_d[g][t * 128:(t + 1) * 128, :], og[g], [], [og[g]])
                    dma("sp", lt.ap[:, g, :], LSE_d[g][t * 128:(t + 1) * 128, :], lt, [], [lt])
                m = sm.next()
                vop("dve", "tensor_tensor", [lt], [m], out=m.ap[:, 0:8], in0=lt.ap[:, 0, :], in1=lt.ap[:, 1, :], op=ALU.max)
                vop("dve", "tensor_tensor", [lt, m], [m], out=m.ap[:, 0:8], in0=m.ap[:, 0:8], in1=lt.ap[:, 2, :], op=ALU.max)
                w = wk.next()
                vop("dve", "tensor_tensor", [lt, m], [w], out=w.ap, in0=lt.ap, in1=m.ap[:, 0:8].unsqueeze(1).broadcast_to([128, 3, 8]), op=ALU.subtract)
                act(w.ap, w.ap, AF.Exp, [w], [w])
                vop("dve", "tensor_tensor", [w], [m], out=m.ap[:, 8:16], in0=w.ap[:, 0, :], in1=w.ap[:, 1, :], op=ALU.add)
                vop("dve", "tensor_tensor", [w, m], [m], out=m.ap[:, 8:16], in0=m.ap[:, 8:16], in1=w.ap[:, 2, :], op=ALU.add)
                vop("dve", "reciprocal", [m], [m], out=m.ap[:, 8:16], in_=m.ap[:, 8:16])
                vop("dve", "tensor_tensor", [w, m], [w], out=w.ap, in0=w.ap, in1=m.ap[:, 8:16].unsqueeze(1).broadcast_to([128, 3, 8]), op=ALU.mult)
                for g in range(3):
                    eng = "pool" if g == 1 else "dve"
                    vop(eng, "tensor_tensor", [og[g], w], [og[g]], out=og[g].ap.rearrange("p (h d) -> p h d", h=8),
                        in0=og[g].ap.rearrange("p (h d) -> p h d", h=8), in1=w.ap[:, g, :].unsqueeze(2).broadcast_to([128, 8, 64]), op=ALU.mult)
                vop("dve", "tensor_tensor", [og[0], og[1]], [og[0]], out=og[0].ap, in0=og[0].ap, in1=og[1].ap, op=ALU.add)
                y = yo.next()
                vop("dve", "tensor_tensor", [og[0], og[2]], [y], out=y.ap, in0=og[0].ap, in1=og[2].ap, op=ALU.add)
                dma("sp", Y0_d[t * 128:(t + 1) * 128, 1024:1536], y.ap, y, [y], [])

        xT = T([128, 16, S], BF16, name="xT")
        load_T(x_in, D, xT, True)
        wring = TR(2, [128, 16, 512], BF16, dma="sw", name="wring")
        stgF = TR(2, [128, S], BF16, dma=True, name="stgF")
        stgT = TR(2, [128, 512], BF16, dma=True, name="stgT")
        proj_F(xT, 16, e_win[:, 0:512], 512, QA_d[0:512, :], 1, wring, stgF)
        proj_F(xT, 16, e_win[:, 512:1024], 512, QA_d[512:1024, :], 1, wring, stgF)
        proj_F(xT, 16, e_win[:, 1024:1152], 128, KA_d, 1, wring, stgF)
        proj_T(xT, 16, e_win[:, 1152:1280], 128, VA_d, BF16, wring, stgT)
        for g, dil in enumerate((1, 4, 16)):
            base = 1280 + g * 1536
            proj_F(xT, 16, e_win[:, base:base + 512], 512, QB_d[g], dil, wring, stgF)
            proj_F(xT, 16, e_win[:, base + 512:base + 1024], 512, KB_d[g], dil, wring, stgF)
            proj_T(xT, 16, e_win[:, base + 1024:base + 1536], 512, VB_d[g], BF16, wring, stgT)
        phase_end()
        slA = alibi(16)
        banded(QA_d, KA_d, lambda h: h // 8, 16, VA_d, 2, 1, RA, [8.0 * sl for sl in slA], True, "A", Yd=Y0_d)
        phase_end()
        slB = alibi(8)
        for g, dil in enumerate((1, 4, 16)):
            banded(QB_d[g], KB_d[g], lambda h: h, 8, VB_d[g], 8, dil, RB, [8.0 * sl * dil for sl in slB], False, "B",
                   OBd=OB_d[g], LSEd=LSE_d[g])
            phase_end()
        combine_B()
        phase_end()
        out_proj_ln(Y0_d, 12, e_wout, x_in, ln1_g[0:1, :], ln1_b[0:1, :], xs1)
        phase_end()
        issue_wcast(1)
        mlp(0, xs1, ln2_g[0:1, :], ln2_b[0:1, :], xs2 if "stop0" not in dbg else out_d)
        phase_end()

        if "stop0" not in dbg:

            base_persist = persist_end[0]
            cosT = T([128, S], F32, name="cosT")
            sinT = T([128, S], F32, name="sinT")
            persist_end[0] = aoff[0]
            pidx = T([128, 2], I32, name="pidx")
            pf = T([128, 2], F32, name="pf")
            vop("pool", "iota", [], [pidx], out=pidx.ap[:, 0:1], pattern=[[0, 1]], base=0, channel_multiplier=1)
            vop("dve", "tensor_single_scalar", [pidx], [pidx], out=pidx.ap[:, 1:2], in_=pidx.ap[:, 0:1], scalar=15, op=ALU.bitwise_and)
            cp("dve", pf.ap[:, 0:1], pidx.ap[:, 1:2], [pidx], [pf])
            act(pf.ap[:, 1:2], pf.ap[:, 0:1], AF.Exp, [pf], [pf], scale=-math.log(10000.0) / 16.0)
            vop("dve", "tensor_single_scalar", [pf], [pf], out=pf.ap[:, 1:2], in_=pf.ap[:, 1:2], scalar=1.0 / (2 * math.pi), op=ALU.mult)
            tpi = T([128, S], I32, name="tpi")
            tpf = T([128, S], F32, name="tpf")
            tq = T([128, S], F32, name="tq")
            vop("pool", "iota", [], [tpi], out=tpi.ap, pattern=[[1, S]], base=0, channel_multiplier=0)
            cp("dve", tpf.ap, tpi.ap, [tpi], [tpf])
            for tab, offs in ((sinT, 0.0), (cosT, 0.25)):
                vop("dve", "tensor_scalar", [tpf, pf], [tq], out=tq.ap, in0=tpf.ap, scalar1=pf.ap[:, 1:2], scalar2=offs,
                    op0=ALU.mult, op1=ALU.add)
                cp("dve", tpi.ap, tq.ap, [tq], [tpi])
                cp("dve", tab.ap, tpi.ap, [tpi], [tab])
                vop("dve", "tensor_tensor", [tq, tab], [tq], out=tq.ap, in0=tq.ap, in1=tab.ap, op=ALU.subtract)
                vop("dve", "scalar_tensor_tensor", [tq], [tq], out=tq.ap, in0=tq.ap, scalar=0.5, in1=tq.ap,
                    op0=ALU.is_gt, op1=ALU.subtract)
                act(tab.ap, tq.ap, AF.Sin, [tq], [tab], scale=-2.0 * math.pi)
            phase_end()

            def rope_evac(ps_m, ps_r, st, tg, tmpr):
                ta = tmpr.next()
                tb = tmpr.next()
                cs = slice(tg * 512, (tg + 1) * 512)
                vop("dve", "tensor_tensor", [ps_r, sinT], [ta], out=ta.ap[64:96, :], in0=ps_r.ap[64:96, :], in1=sinT.ap[64:96, cs], op=ALU.mult)
                vop("dve", "tensor_tensor", [ps_m, cosT], [tb], out=tb.ap[64:96, :], in0=ps_m.ap[64:96, :], in1=cosT.ap[64:96, cs], op=ALU.mult)
                vop("pool", "tensor_tensor", [ta, tb], [st], out=st.ap[64:96, cs], in0=ta.ap[64:96, :], in1=tb.ap[64:96, :], op=ALU.add)

            xT = T([128, 16, S], BF16, name="xT1")
            load_T(xs2, D, xT, True)
            wring = TR(2, [128, 16, 512], BF16, dma="sw", name="wring")
            stgF = TR(2, [128, S], BF16, dma=True, name="stgF")
            stgT = TR(2, [128, 512], BF16, dma=True, name="stgT")
            stgT32 = TR(2, [128, 512], F32, dma=True, name="stgT32")
            for i in range(2):
                proj_F(xT, 16, o_win[:, i * 512:(i + 1) * 512], 512, QC_d[i * 512:(i + 1) * 512, :], 1, wring, stgF)
                proj_F(xT, 16, o_win[:, 1024 + i * 512:1024 + (i + 1) * 512], 512, KC_d[i * 512:(i + 1) * 512, :], 1, wring, stgF)
                proj_T(xT, 16, o_win[:, 2048 + i * 512:2048 + (i + 1) * 512], 512, VC_d[:, i * 512:(i + 1) * 512], BF16, wring, stgT)
            proj_T(xT, 16, o_win[:, 3072:3584], 512, CQ_d[:, 0:512], F32, wring, stgT32)
            proj_T(xT, 16, o_win[:, 3584:3840], 256, CQ_d[:, 512:768], F32, wring, stgT32)
            wkr = T([128, 16, 96], BF16, dma="sw", name="wkr")
            wkrot = T([128, 16, 96], BF16, name="wkrot")
            vop("dve", "memset", [], [wkr], ap=wkr.ap, constant=0.0)
            vop("pool", "memset", [], [wkrot], ap=wkrot.ap, constant=0.0)
            dma("pool", wkr.ap[:, :, 64:96], o_win[:, 3840:3872].rearrange("(k p) e -> p k e", p=128), wkr, [], [wkr])
            vop("dve", "tensor_single_scalar", [wkr], [wkrot], out=wkrot.ap[:, :, 64:80], in_=wkr.ap[:, :, 80:96], scalar=-1.0, op=ALU.mult)
            cp("dve", wkrot.ap[:, :, 80:96], wkr.ap[:, :, 64:80], [wkr], [wkrot])
            tmpr = TR(4, [128, 512], F32, name="ropetmp")
            stK = T([128, S], BF16, dma=True, name="stKr")
            for tg in range(4):
                pm, pr = PB[0], PB[1]
                for k in range(16):
                    mm(pm.ap[0:96, :], wkr.ap[:, k, :], xT.ap[:, k, tg * 512:(tg + 1) * 512], k == 0, k == 15, [wkr, xT], [pm])
                for k in range(16):
                    mm(pr.ap[0:96, :], wkrot.ap[:, k, :], xT.ap[:, k, tg * 512:(tg + 1) * 512], k == 0, k == 15, [wkrot, xT], [pr])
                rope_evac(pm, pr, stK, tg, tmpr)
            for h in range(16):
                dma("sp", KD_d[h, 64:96, :], stK.ap[64:96, :], stK, [stK], [])
            phase_end()

            cT = T([128, 6, S], BF16, name="cT")
            gq = T([128, 768], F32, dma=True, name="gq")
            dma("sp", gq.ap[:, 0:512], o_qg.partition_broadcast(128), gq, [], [gq])
            dma("sp", gq.ap[:, 512:768], o_kvg.partition_broadcast(128), gq, [], [gq])
            cl = TR(2, [128, 768], F32, dma=True, name="cl")
            junk = TR(2, [128, 768], F32, name="junk")
            cbf = TR(2, [128, 768], BF16, name="cbf")
            str_ = TR(2, [128, 8], F32, name="st")
            pbt = Ring([PB[6], PB[7]])
            for t in range(NT):
                c = cl.next()
                dma("sp", c.ap, CQ_d[t * 128:(t + 1) * 128, :], c, [], [c])
                jk = junk.next()
                st = str_.next()
                act(jk.ap[:, 0:512], c.ap[:, 0:512], AF.Square, [c], [jk, st], accum=st.ap[:, 0:1])
                act(jk.ap[:, 512:768], c.ap[:, 512:768], AF.Square, [c], [jk, st], accum=st.ap[:, 1:2])
                vop("dve", "tensor_scalar", [st], [st], out=st.ap[:, 2:3], in0=st.ap[:, 0:1], scalar1=1.0 / 512.0, scalar2=RMS_EPS, op0=ALU.mult, op1=ALU.add)
                vop("dve", "tensor_scalar", [st], [st], out=st.ap[:, 3:4], in0=st.ap[:, 1:2], scalar1=1.0 / 256.0, scalar2=RMS_EPS, op0=ALU.mult, op1=ALU.add)
                act(st.ap[:, 4:6], st.ap[:, 2:4], AF.Sqrt, [st], [st])
                vop("dve", "reciprocal", [st], [st], out=st.ap[:, 6:8], in_=st.ap[:, 4:6])
                vop("pool", "tensor_tensor", [c, gq], [c], out=c.ap, in0=c.ap, in1=gq.ap, op=ALU.mult)
                cb = cbf.next()
                vop("dve", "tensor_scalar", [c, st], [cb], out=cb.ap[:, 0:512], in0=c.ap[:, 0:512], scalar1=st.ap[:, 6:7], scalar2=None, op0=ALU.mult)
                vop("dve", "tensor_scalar", [c, st], [cb], out=cb.ap[:, 512:768], in0=c.ap[:, 512:768], scalar1=st.ap[:, 7:8], scalar2=None, op0=ALU.mult)
                pb = pbt.next()
                pv = pb.ap.bitcast(BF16)
                for k in range(6):
                    s.op("pe", (lambda pv=pv, k=k, cb=cb: (lambda e: e.transpose(pv[:, k * 128:(k + 1) * 128], cb.ap[:, k * 128:(k + 1) * 128], ident.ap)))(),
                         rs([cb, ident]), rs([pb]))
                cp("act", cT.ap[:, :, t * 128:(t + 1) * 128], pv[:, 0:768].rearrange("p (k t) -> p k t", k=6), [pb], [cT])
            wq = T([128, 4, 1536], BF16, dma="sw", name="wq")
            wqr = T([128, 4, 1536], BF16, name="wqr")
            dma("pool", wq.ap, o_wuq.rearrange("(k p) e -> p k e", p=128), wq, [], [wq])
            wq4 = wq.ap.rearrange("p k (h e) -> p k h e", h=16)
            wqr4 = wqr.ap.rearrange("p k (h e) -> p k h e", h=16)
            cp("dve", wqr.ap, wq.ap, [wq], [wqr])
            for k in range(4):
                vop("dve", "tensor_single_scalar", [wq, wqr], [wqr], out=wqr4[:, k, :, 64:80], in_=wq4[:, k, :, 80:96], scalar=-1.0, op=ALU.mult)
                cp("dve", wqr4[:, k, :, 80:96], wq4[:, k, :, 64:80], [wq, wqr], [wqr])
            wkv = T([128, 2, 2048], BF16, dma="sw", name="wkv")
            dma("pool", wkv.ap, o_wukv.rearrange("(k p) e -> p k e", p=128), wkv, [], [wkv])
            stQ = TR(2, [128, S], BF16, dma=True, name="stQ")
            stKn = TR(2, [128, S], BF16, dma=True, name="stKn")
            stV = TR(2, [128, 1024], BF16, dma=True, name="stV")
            tmpr = TR(4, [128, 512], F32, name="ropetmp")
            pbq = Ring(PB[0:6])
            for h in range(16):
                sq = stQ.next()
                for tg in range(4):
                    pm = pbq.next()
                    pr = pbq.next()
                    for k in range(4):
                        mm(pm.ap[0:96, :], wq4[:, k, h, :], cT.ap[:, k, tg * 512:(tg + 1) * 512], k == 0, k == 3, [wq, cT], [pm])
                    for k in range(4):
                        mm(pr.ap[0:96, :], wqr4[:, k, h, :], cT.ap[:, k, tg * 512:(tg + 1) * 512], k == 0, k == 3, [wqr, cT], [pr])
                    cp("act", sq.ap[0:64, tg * 512:(tg + 1) * 512], pm.ap[0:64, :], [pm], [sq])
                    rope_evac(pm, pr, sq, tg, tmpr)
                dma("sp", QD_d[h], sq.ap[0:96, :], sq, [sq], [])
                sk = stKn.next()
                for tg in range(4):
                    pk = pbq.next()
                    for k in range(2):
                        mm(pk.ap[0:64, :], wkv.ap[:, k, h * 128:h * 128 + 64], cT.ap[:, 4 + k, tg * 512:(tg + 1) * 512], k == 0, k == 1, [wkv, cT], [pk])
                    cp(ev_eng(), sk.ap[0:64, tg * 512:(tg + 1) * 512], pk.ap[0:64, :], [pk], [sk])
                dma("sp", KD_d[h, 0:64, :], sk.ap[0:64, :], sk, [sk], [])
            wkv5 = wkv.ap.rearrange("p k (h two d) -> p k h two d", two=2, d=64)
            for t in range(NT):
                sv = stV.next()
                for hf in range(2):
                    pb = pbq.next()
                    for k in range(2):
                        mm(pb.ap.rearrange("p (h d) -> p h d", h=8), cT.ap[:, 4 + k, t * 128:(t + 1) * 128], wkv5[:, k, hf * 8:(hf + 1) * 8, 1, :],
                           k == 0, k == 1, [wkv, cT], [pb])
                    cp(ev_eng(), sv.ap[:, hf * 512:(hf + 1) * 512], pb.ap, [pb], [sv])
                dma("sp", VD_d[t * 128:(t + 1) * 128, :], sv.ap, sv, [sv], [])
            persist_end[0] = base_persist
            phase_end()

            def attn_C():
                VCp = T([128, NT, 1024], BF16, dma=True, name="VCp")
                dma("sp", VCp.ap, VC_d.rearrange("(n p) c -> p n c", p=128), VCp, [], [VCp])
                OC = T([128, NT, 1024], BF16, dma=True, name="OC")
                Qr = TR(3, [64, S], BF16, dma=True, name="Qh")
                Kr = TR(3, [64, S], BF16, dma=True, name="Kh")
                Er = TR(4, [128, 512], F32, name="E")
                SPr = TR(4, [128, 512], F32, name="SP")
                LKr = TR(4, [128, 512], F32, name="LK")
                Wr = TR(4, [128, 512], BF16, name="W")
                Srun = TR(2, [128, 512], F32, name="Srun")
                pZ = Ring([PB[0], PB[1]])
                pA = Ring([PB[2], PB[3]])
                pO = PB[4:8]
                heads = [(Qr.next(), Kr.next()) for _ in range(16)]

                def load_head(h):
                    Qh, Kh = heads[h]
                    dma("sp", Qh.ap, QC_d[h * 64:(h + 1) * 64, :], Qh, [], [Qh])
                    dma("sp", Kh.ap, KC_d[h * 64:(h + 1) * 64, :], Kh, [], [Kh])
                items = []
                for h in range(16):
                    for G in range(4):
                        Sr = Srun.next()
                        for j in range(4 * G + 3, -1, -1):
                            items.append((h, G, j, Sr))
                ctx = [dict(pz=pZ.next(), pa=pA.next(), E=Er.next(), SP=SPr.next(), LK=LKr.next(), W=Wr.next()) for _ in items]
                load_head(0)

                def s1(it):
                    h, G, j, Sr = items[it]
                    c = ctx[it]
                    if G == 0 and j == 3 and h + 1 < 16:
                        load_head(h + 1)
                    Qh, Kh = heads[h]
                    c0 = max(j - 4 * G, 0) * 128
                    pz, E, SP, LK = c["pz"], c["E"], c["SP"], c["LK"]
                    mm(pz.ap[:, c0:512], Kh.ap[:, j * 128:(j + 1) * 128], Qh.ap[:, G * 512 + c0:(G + 1) * 512], True, True, [Kh, Qh], [pz])
                    act(E.ap[:, c0:512], pz.ap[:, c0:512], AF.Exp, [pz], [E], scale=-0.125)
                    act(SP.ap[:, c0:512], E.ap[:, c0:512], AF.Ln, [E], [SP], bias=1.0, scale=1.0)
                    vop("dve", "scalar_tensor_tensor", [pz, SP], [LK], out=LK.ap[:, c0:512], in0=pz.ap[:, c0:512], scalar=-0.125,
                        in1=SP.ap[:, c0:512], op0=ALU.mult, op1=ALU.subtract)
                    if j >= 4 * G:
                        vop("pool", "tensor_tensor", [LK, mC01], [LK], out=LK.ap[:, c0:c0 + 128], in0=LK.ap[:, c0:c0 + 128], in1=mC01.ap, op=ALU.mult)

                def s2(it):
                    h, G, j, Sr = items[it]
                    c = ctx[it]
                    c0 = max(j - 4 * G, 0) * 128
                    pa, E, SP, LK, W = c["pa"], c["E"], c["SP"], c["LK"], c["W"]
                    first = j == 4 * G + 3
                    if first:
                        vop("pool", "memset", [], [Sr], ap=Sr.ap, constant=0.0)
                    mm(pa.ap[:, c0:512], Ustr.ap, LK.ap[:, c0:512], True, first, [Ustr, LK], [pa])
                    if not first:
                        mm(pa.ap[:, c0:512], ones.ap, Sr.ap[:, c0:512], False, True, [ones, Sr], [pa])
                    vop("dve", "tensor_tensor", [pa, SP], [E], out=E.ap[:, c0:512], in0=pa.ap[:, c0:512], in1=SP.ap[:, c0:512], op=ALU.subtract)
                    act(W.ap[:, c0:512], E.ap[:, c0:512], AF.Exp, [E], [W])
                    if j >= 4 * G:
                        vop("pool", "tensor_tensor", [W, mC01b], [W], out=W.ap[:, c0:c0 + 128], in0=W.ap[:, c0:c0 + 128], in1=mC01b.ap, op=ALU.mult)
                    if j > 0:
                        vop("pool", "tensor_tensor", [Sr, LK], [Sr], out=Sr.ap[:, c0:512], in0=Sr.ap[:, c0:512], in1=LK.ap[:, c0:512], op=ALU.add)

                def s3(it):
                    h, G, j, Sr = items[it]
                    c = ctx[it]
                    q0 = max(j - 4 * G, 0)
                    W = c["W"]
                    for qt in range(q0, 4):
                        mm(pO[qt].ap[:, 0:64], W.ap[:, qt * 128:(qt + 1) * 128], VCp.ap[:, j, h * 64:(h + 1) * 64],
                           j == 4 * G + qt, j == 0, [W, VCp], [pO[qt]])
                    if j == 0:
                        for qt in range(4):
                            cp("act" if qt % 2 else "dve", OC.ap[:, 4 * G + qt, h * 64:(h + 1) * 64], pO[qt].ap[:, 0:64], [pO[qt]], [OC])

                run_pipeline(len(items), [s1, s2, s3])
                for n in range(NT):
                    dma("sp", Y1_d[n * 128:(n + 1) * 128, 0:1024], OC.ap[:, n, :], OC, [OC], [])

            def attn_D():
                sc = 1.0 / math.sqrt(96.0)
                VDp = T([128, NT, 1024], BF16, dma=True, name="VDp")
                dma("sp", VDp.ap, VD_d.rearrange("(n p) c -> p n c", p=128), VDp, [], [VDp])
                OD = T([128, NT, 1024], BF16, dma=True, name="OD")
                Qr = TR(3, [96, S], BF16, dma=True, name="Qh")
                Kr = TR(3, [96, S], BF16, dma=True, name="Kh")
                Pr = TR(4, [128, S], BF16, name="P")
                PTr = TR(4, [128, S], BF16, name="PT")
                Sdr = TR(3, [128, 128], F32, name="Sd")
                str_ = TR(6, [128, 16], F32, name="st")
                pSa = Ring([PB[0:2], PB[2:4]])
                pT = Ring([PB[4], PB[5]])
                pO = Ring([PB[6], PB[7]])
                heads = [(Qr.next(), Kr.next()) for _ in range(16)]

                def load_head(h):
                    Qh, Kh = heads[h]
                    dma("sp", Qh.ap, QD_d[h], Qh, [], [Qh])
                    dma("sp", Kh.ap, KD_d[h], Kh, [], [Kh])
                items = [(h, i) for h in range(16) for i in range(NT)]
                ctx = []
                for (h, i) in items:
                    nbank = (i + 4) // 4
                    pS = pSa.next() if nbank <= 2 else PB[0:4]
                    ctx.append(dict(pS=pS, Sd=Sdr.next(), st=str_.next(), Pt=Pr.next(), PT=PTr.next(), po=pO.next()))
                load_head(0)

                def geom(i):
                    nkb = i + 1
                    nbank = (nkb + 3) // 4
                    widths = [min(512, nkb * 128 - bk * 512) for bk in range(nbank)]
                    return nkb, nbank, widths

                def sA(it):
                    h, i = items[it]
                    c = ctx[it]
                    if i == 0 and h + 1 < 16:
                        load_head(h + 1)
                    Qh, Kh = heads[h]
                    nkb, nbank, widths = geom(i)
                    pS, Sd, st, Pt = c["pS"], c["Sd"], c["st"], c["Pt"]
                    for bk in range(nbank):
                        mm(pS[bk].ap[:, 0:widths[bk]], Qh.ap[:, i * 128:(i + 1) * 128], Kh.ap[:, bk * 512:bk * 512 + widths[bk]],
                           True, True, [Qh, Kh], [pS[bk]])
                    bd = nbank - 1
                    dc = widths[bd] - 128
                    vop("dve", "tensor_tensor", [pS[bd], McD], [Sd], out=Sd.ap, in0=pS[bd].ap[:, dc:dc + 128], in1=McD.ap, op=ALU.add)
                    vop("dve", "reduce_max", [Sd], [st], out=st.ap[:, 0:1], in_=Sd.ap, axis=AX.X)
                    ncol = 1
                    for bk in range(nbank):
                        wv = widths[bk] - (128 if bk == bd else 0)
                        if wv > 0:
                            vop("dve", "reduce_max", [pS[bk]], [st], out=st.ap[:, ncol:ncol + 1], in_=pS[bk].ap[:, 0:wv], axis=AX.X)
                            ncol += 1
                    if ncol > 1:
                        vop("dve", "reduce_max", [st], [st], out=st.ap[:, 5:6], in_=st.ap[:, 0:ncol], axis=AX.X)
                        mxc = st.ap[:, 5:6]
                    else:
                        mxc = st.ap[:, 0:1]
                    vop("dve", "tensor_single_scalar", [st], [st], out=st.ap[:, 6:7], in_=mxc, scalar=-sc, op=ALU.mult)
                    act(Pt.ap[:, i * 128:(i + 1) * 128], Sd.ap, AF.Exp, [Sd, st], [Pt, st], bias=st.ap[:, 6:7], scale=sc, accum=st.ap[:, 8:9])
                    ncol = 1
                    for bk in range(nbank):
                        wv = widths[bk] - (128 if bk == bd else 0)
                        if wv > 0:
                            act(Pt.ap[:, bk * 512:bk * 512 + wv], pS[bk].ap[:, 0:wv], AF.Exp, [pS[bk], st], [Pt, st],
                                bias=st.ap[:, 6:7], scale=sc, accum=st.ap[:, 8 + ncol:9 + ncol])
                            ncol += 1
                    c["ncol"] = ncol

                def sB(it):
                    h, i = items[it]
                    c = ctx[it]
                    nkb, nbank, widths = geom(i)
                    Pt, PT = c["Pt"], c["PT"]
                    for k0 in range(0, nkb, 8):
                        kn = min(8, nkb - k0)
                        pt = pT.next()
                        ptv = pt.ap.bitcast(BF16)
                        for jj in range(kn):
                            kb = k0 + jj
                            s.op("pe", (lambda ptv=ptv, jj=jj, Pt=Pt, kb=kb: (lambda e: e.transpose(ptv[:, jj * 128:(jj + 1) * 128], Pt.ap[:, kb * 128:(kb + 1) * 128], ident.ap)))(),
                                 rs([Pt, ident]), rs([pt]))
                        cp("act" if (k0 // 8) % 2 == 0 else "dve", PT.ap[:, k0 * 128:(k0 + kn) * 128], ptv[:, 0:kn * 128], [pt], [PT])

                def sC(it):
                    h, i = items[it]
                    c = ctx[it]
                    nkb, nbank, widths = geom(i)
                    PT, po, st = c["PT"], c["po"], c["st"]
                    ncol = c["ncol"]
                    for kb in range(nkb):
                        mm(po.ap[:, 0:64], PT.ap[:, kb * 128:(kb + 1) * 128], VDp.ap[:, kb, h * 64:(h + 1) * 64], kb == 0, kb == nkb - 1, [PT, VDp], [po])
                    if ncol > 1:
                        vop("dve", "reduce_sum", [st], [st], out=st.ap[:, 7:8], in_=st.ap[:, 8:8 + ncol], axis=AX.X)
                        den = st.ap[:, 7:8]
                    else:
                        den = st.ap[:, 8:9]
                    vop("dve", "reciprocal", [st], [st], out=st.ap[:, 14:15], in_=den)
                    vop("dve", "tensor_scalar", [po, st], [OD], out=OD.ap[:, i, h * 64:(h + 1) * 64], in0=po.ap[:, 0:64],
                        scalar1=st.ap[:, 14:15], scalar2=None, op0=ALU.mult)

                run_pipeline(len(items), [sA, sB, sC])
                for n in range(NT):
                    dma("sp", Y1_d[n * 128:(n + 1) * 128, 1024:2048], OD.ap[:, n, :], OD, [OD], [])

            if "skipC" not in dbg:
                attn_C()
                phase_end()
            if "skipD" not in dbg:
                attn_D()
                phase_end()
            out_proj_ln(Y1_d, 16, o_wout, xs2, ln1_g[1:2, :], ln1_b[1:2, :], xs1)
            phase_end()
            mlp(1, xs1, ln2_g[1:2, :], ln2_b[1:2, :], out_d)

        s.barrier(final=True)
        s.emit()
    return nc


_NC_CACHE = {}


def kernel(**inputs):
    B = inputs["x"].shape[0]
    if "nc" not in _NC_CACHE:
        _NC_CACHE["nc"] = build()
    nc = _NC_CACHE["nc"]
    f = lambda a: np.ascontiguousarray(np.asarray(a, dtype=np.float32))
    shared = {
        "even_w_in": f(inputs["even_w_in"][0]),
        "even_sinks": f(inputs["even_sinks"][0]).reshape(1, 16),
        "even_w_out": f(inputs["even_w_out"][0]),
        "odd_w_in": f(inputs["odd_w_in"][0]),
        "odd_q_norm_g": f(inputs["odd_q_norm_g"][0]).reshape(1, 512),
        "odd_kv_norm_g": f(inputs["odd_kv_norm_g"][0]).reshape(1, 256),
        "odd_w_uq": f(inputs["odd_w_uq"][0]),
        "odd_w_ukv": f(inputs["odd_w_ukv"][0]),
        "odd_w_out": f(inputs["odd_w_out"][0]),
        "ln1_g": f(inputs["ln1_g"]), "ln1_b": f(inputs["ln1_b"]),
        "ln2_g": f(inputs["ln2_g"]), "ln2_b": f(inputs["ln2_b"]),
        "mlp_w1": f(inputs["mlp_w1"]), "mlp_w2": f(inputs["mlp_w2"]),
    }
    x = f(inputs["x"])
    in_maps = [dict(shared, x=x[b]) for b in range(B)]
    res = run_bass_kernel_spmd(nc, in_maps, core_ids=list(range(B)))
    return np.stack([r["out"] for r in res.results], axis=0)
```

```python
import contextlib
import math
import numpy as np
import concourse.bass as bass
import concourse.mybir as mybir
from concourse.bass_utils import run_bass_kernel_spmd

F32 = mybir.dt.float32
BF16 = mybir.dt.bfloat16
I32 = mybir.dt.int32
AF = mybir.ActivationFunctionType
ALU = mybir.AluOpType
AX = mybir.AxisListType

S = 2048
D = 2048
NT = 16
DFF = 8192
ALPHA = 4.0 ** 0.25
LN_EPS = 1e-5
RMS_EPS = 1e-6
BIG = 1.0e9


class Res:
    __slots__ = ("name", "w", "r")

    def __init__(self, name=""):
        self.name = name
        self.w = None
        self.r = {}


class Buf:
    __slots__ = ("ap", "res", "sem")

    def __init__(self, ap, res, sem=None):
        self.ap = ap
        self.res = res
        self.sem = sem


class Ring:
    def __init__(self, items):
        self.items = items
        self.i = 0

    def next(self):
        it = self.items[self.i % len(self.items)]
        self.i += 1
        return it


class Sched:
    ENG = ("pe", "act", "dve", "pool", "sp")

    def __init__(self, nc, stack):
        self.nc = nc
        self.stack = stack
        self.prog = {e: [] for e in self.ENG}
        self.sem = {}
        self.cnt = {}
        self.known = {e: {} for e in self.ENG}
        self.free_dsems = []
        self.used_dsems = []
        self.ndsem = 0
        for e in self.ENG:
            self.newsem("E_" + e)

    def newsem(self, name):
        self.sem[name] = self.stack.enter_context(self.nc.semaphore(name))
        self.cnt[name] = 0
        return name

    def dsem(self, kind="H"):
        fl = [x for x in self.free_dsems if x[0] == kind]
        if fl:
            n = fl[-1]
            self.free_dsems.remove(n)
        else:
            n = self.newsem("%s%d" % (kind, self.ndsem))
            self.ndsem += 1
        self.used_dsems.append(n)
        return n

    def _deps(self, eng, reads, writes):
        need = {}

        def add(ev):
            if ev is None:
                return
            sm, v = ev
            if need.get(sm, 0) < v:
                need[sm] = v
        for r in reads:
            add(r.w)
        for w in writes:
            add(w.w)
            for sm, v in w.r.items():
                add((sm, v))
        kn = self.known[eng]
        for sm, v in need.items():
            if eng == "pe" and sm == "E_pe":
                continue
            if kn.get(sm, 0) < v:
                kn[sm] = v
                self.prog[eng].append(("wait", sm, v))

    def _commit(self, ev, reads, writes):
        sm, v = ev
        for r in reads:
            if r.r.get(sm, 0) < v:
                r.r[sm] = v
        for w in writes:
            w.w = ev
            w.r = {}

    def op(self, eng, fn, reads=(), writes=()):
        self._deps(eng, reads, writes)
        sm = "E_" + eng
        self.cnt[sm] += 1
        ev = (sm, self.cnt[sm])
        self.prog[eng].append(("op", fn, sm, 1))
        self._commit(ev, reads, writes)

    def dma(self, eng, out, in_, sem, reads=(), writes=()):
        self._deps(eng, reads, writes)
        self.cnt[sem] += 16
        ev = (sem, self.cnt[sem])
        self.prog[eng].append(("op", lambda e: e.dma_start(out=out, in_=in_), sem, 16))
        self._commit(ev, reads, writes)

    def barrier(self, final=False):
        for e in self.ENG:
            kn = self.known[e]
            for sm, v in self.cnt.items():
                if sm.startswith("WC") and not final:
                    continue
                if v > 0 and kn.get(sm, 0) < v:
                    kn[sm] = v
                    self.prog[e].append(("wait", sm, v))
        self.free_dsems.extend(self.used_dsems)
        self.used_dsems = []

    def emit(self):
        nc = self.nc

        def replay(name):
            def f(eng):
                for it in self.prog[name]:
                    if it[0] == "wait":
                        eng.wait_ge(self.sem[it[1]], it[2])
                    else:
                        it[1](eng).then_inc(self.sem[it[2]], it[3])
            return f

        with nc.Block() as block:
            block.tensor(replay("pe"))
            block.scalar(replay("act"))
            block.vector(replay("dve"))
            block.gpsimd(replay("pool"))
            block.sync(replay("sp"))


def alibi(n):
    return [2.0 ** (-8.0 * (i + 1) / n) for i in range(n)]


def build(dbg=()):
    nc = bass.Bass("TRN2", target_bir_lowering=False)

    def din(name, shape):
        return nc.dram_tensor(name, list(shape), F32, kind="ExternalInput").ap()

    x_in = din("x", [S, D])
    e_win = din("even_w_in", [D, 5888])
    e_sinks = din("even_sinks", [1, 16])
    e_wout = din("even_w_out", [1536, D])
    o_win = din("odd_w_in", [D, 3872])
    o_qg = din("odd_q_norm_g", [1, 512])
    o_kvg = din("odd_kv_norm_g", [1, 256])
    o_wuq = din("odd_w_uq", [512, 1536])
    o_wukv = din("odd_w_ukv", [256, 2048])
    o_wout = din("odd_w_out", [D, D])
    ln1_g = din("ln1_g", [2, D])
    ln1_b = din("ln1_b", [2, D])
    ln2_g = din("ln2_g", [2, D])
    ln2_b = din("ln2_b", [2, D])
    w1_in = din("mlp_w1", [2, D, DFF])
    w2_in = din("mlp_w2", [2, DFF, D])
    out_d = nc.dram_tensor("out", [S, D], F32, kind="ExternalOutput").ap()

    def dscr(name, shape, dt):
        kind = "ExternalOutput" if name in dbg else "Internal"
        return nc.dram_tensor(name, list(shape), dt, kind=kind).ap()

    w1b = dscr("w1b", [2, D, DFF], BF16)
    w2b = dscr("w2b", [2, DFF, D], BF16)
    xs1 = dscr("xs1", [S, D], F32)
    xs2 = dscr("xs2", [S, D], F32)
    QA_d = dscr("QA_d", [1024, S], BF16)
    KA_d = dscr("KA_d", [128, S], BF16)
    VA_d = dscr("VA_d", [S, 128], BF16)
    QB_d = [dscr("QB%d_d" % g, [512, S], BF16) for g in range(3)]
    KB_d = [dscr("KB%d_d" % g, [512, S], BF16) for g in range(3)]
    VB_d = [dscr("VB%d_d" % g, [S, 512], BF16) for g in range(3)]
    OB_d = [dscr("OB%d_d" % g, [S, 512], F32) for g in range(3)]
    LSE_d = [dscr("LSE%d_d" % g, [S, 8], F32) for g in range(3)]
    Y0_d = dscr("Y0_d", [S, 1536], BF16)
    QC_d = dscr("QC_d", [1024, S], BF16)
    KC_d = dscr("KC_d", [1024, S], BF16)
    VC_d = dscr("VC_d", [S, 1024], BF16)
    CQ_d = dscr("CQ_d", [S, 768], F32)
    QD_d = dscr("QD_d", [16, 96, S], BF16)
    KD_d = dscr("KD_d", [16, 96, S], BF16)
    VD_d = dscr("VD_d", [S, 1024], BF16)
    Y1_d = dscr("Y1_d", [S, 2048], BF16)

    with contextlib.ExitStack() as stack:
        s = Sched(nc, stack)
        ARENA_ELEMS = 103 * 1024
        arena = nc.alloc_sbuf_tensor("arena", [128, ARENA_ELEMS], BF16)
        aoff = [0]
        persist_end = [0]

        def T(shape, dt, dma=False, name=""):
            esz = 2 if dt == BF16 else 4
            nel = int(np.prod(shape[1:]))
            nb16 = (nel * esz + 63) // 64 * 32
            assert aoff[0] + nb16 <= ARENA_ELEMS, "arena overflow %s %d" % (name, aoff[0] + nb16)
            v = arena[:, aoff[0]:aoff[0] + nel * esz // 2]
            aoff[0] += nb16
            if dt != BF16:
                v = v.bitcast(dt)
            if len(shape) == 3:
                v = v.rearrange("p (a b) -> p a b", a=shape[1])
            elif len(shape) == 4:
                v = v.rearrange("p (a b c) -> p a b c", a=shape[1], b=shape[2])
            if shape[0] != 128:
                v = v[0:shape[0]]
            return Buf(v, Res(name), s.dsem("W" if dma == "sw" else "H") if dma else None)

        def TR(n, shape, dt, dma=False, name=""):
            return Ring([T(shape, dt, dma, name) for _ in range(n)])

        def phase_end():
            s.barrier()
            aoff[0] = persist_end[0]

        PB = [Buf(nc.alloc_psum_tensor("pb%d" % i, [128, 512], F32)[:], Res("pb%d" % i)) for i in range(8)]

        def rs(bufs):
            return [b.res for b in bufs]

        def mm(out, lhsT, rhs, start, stop, rd, wr):
            s.op("pe", lambda e: e.matmul(out, lhsT, rhs, start=start, stop=stop), rs(rd), rs(wr))

        def act(out, in_, func, rd, wr, bias=None, scale=None, accum=None, eng="act"):
            kw = {}
            if bias is not None:
                kw["bias"] = bias
            if scale is not None:
                kw["scale"] = scale
            if accum is not None:
                kw["accum_out"] = accum
            s.op(eng, lambda e: e.activation(out=out, in_=in_, func=func, **kw), rs(rd), rs(wr))

        def vop(eng, meth, rd, wr, **kw):
            s.op(eng, lambda e: getattr(e, meth)(**kw), rs(rd), rs(wr))

        def cp(eng, out, in_, rd, wr):
            if eng == "act":
                s.op("act", lambda e: e.copy(out=out, in_=in_), rs(rd), rs(wr))
            else:
                s.op(eng, lambda e: e.tensor_copy(out=out, in_=in_), rs(rd), rs(wr))

        def dma(eng, out, in_, buf, rd=(), wr=()):
            assert (eng == "pool") == (buf.sem[0] == "W"), (eng, buf.sem)
            s.dma(eng, out, in_, buf.sem, rs(rd), rs(wr))

        def run_pipeline(N, stages):
            ns = len(stages)
            for t in range(N + ns - 1):
                for si in range(ns):
                    it = t - si
                    if 0 <= it < N:
                        stages[si](it)

        ident = T([128, 128], BF16, name="ident")
        Ustr = T([128, 128], F32, name="Ustr")
        ones = T([128, 128], F32, name="ones")
        mC01 = T([128, 128], F32, name="mC01")
        mC01b = T([128, 128], BF16, name="mC01b")
        UstrB = T([128, 128], BF16, name="UstrB")
        PenC = T([128, 128], F32, name="PenC")
        McD = T([128, 128], F32, name="McD")
        RA = T([128, 256], F32, name="RA")
        RB = T([128, 256], F32, name="RB")
        sinkt = T([128, 16], F32, dma=True, name="sink")
        sink8 = T([128, 16], F32, name="sink8")
        persist_end[0] = aoff[0]
        tmpi = T([128, 256], I32, name="tmpi")
        tmpf = T([128, 256], F32, name="tmpf")
        tmpg = T([128, 256], F32, name="tmpg")
        tmph = T([128, 256], F32, name="tmph")
        vop("pool", "iota", [], [tmpi], out=tmpi.ap[:, 0:128], pattern=[[-1, 128]], base=0, channel_multiplier=1)
        cp("dve", tmpf.ap[:, 0:128], tmpi.ap[:, 0:128], [tmpi], [tmpf])
        vop("dve", "tensor_single_scalar", [tmpf], [ident], out=ident.ap, in_=tmpf.ap[:, 0:128], scalar=0.0, op=ALU.is_equal)
        vop("dve", "tensor_single_scalar", [tmpf], [Ustr], out=Ustr.ap, in_=tmpf.ap[:, 0:128], scalar=0.0, op=ALU.is_gt)
        vop("dve", "tensor_single_scalar", [tmpf], [mC01], out=mC01.ap, in_=tmpf.ap[:, 0:128], scalar=0.0, op=ALU.is_lt)
        cp("dve", mC01b.ap, mC01.ap, [mC01], [mC01b])
        cp("dve", UstrB.ap, Ustr.ap, [Ustr], [UstrB])
        vop("dve", "tensor_scalar", [Ustr], [PenC], out=PenC.ap, in0=Ustr.ap, scalar1=1.0, scalar2=-BIG, op0=ALU.subtract, op1=ALU.mult)
        vop("dve", "memset", [], [ones], ap=ones.ap, constant=1.0)
        vop("dve", "tensor_scalar", [tmpf], [McD], out=McD.ap, in0=tmpf.ap[:, 0:128], scalar1=0.0, scalar2=1.0,
            op0=ALU.is_ge, op1=ALU.subtract)
        vop("dve", "tensor_single_scalar", [McD], [McD], out=McD.ap, in_=McD.ap, scalar=BIG, op=ALU.mult)
        vop("pool", "iota", [tmpf], [tmpi], out=tmpi.ap, pattern=[[-1, 256]], base=128, channel_multiplier=1)
        cp("dve", tmpf.ap, tmpi.ap, [tmpi], [tmpf])
        for Rt, nb in ((RA, 127.0), (RB, 128.0)):
            vop("dve", "tensor_single_scalar", [tmpf], [tmpg], out=tmpg.ap, in_=tmpf.ap, scalar=0.0, op=ALU.is_ge)
            vop("dve", "tensor_single_scalar", [tmpf], [tmph], out=tmph.ap, in_=tmpf.ap, scalar=nb, op=ALU.is_le)
            vop("dve", "tensor_tensor", [tmpg, tmph], [tmpg], out=tmpg.ap, in0=tmpg.ap, in1=tmph.ap, op=ALU.mult)
            vop("dve", "tensor_tensor", [tmpg, tmpf], [tmph], out=tmph.ap, in0=tmpg.ap, in1=tmpf.ap, op=ALU.mult)
            vop("dve", "tensor_scalar", [tmpg], [tmpg], out=tmpg.ap, in0=tmpg.ap, scalar1=1.0, scalar2=BIG,
                op0=ALU.subtract, op1=ALU.mult)
            vop("dve", "tensor_tensor", [tmpg, tmph], [Rt], out=Rt.ap, in0=tmpg.ap, in1=tmph.ap, op=ALU.subtract)
        dma("sp", sinkt.ap, e_sinks.partition_broadcast(128), sinkt, [], [sinkt])
        vop("dve", "tensor_single_scalar", [sinkt], [sink8], out=sink8.ap, in_=sinkt.ap, scalar=8.0, op=ALU.mult)

        wcast = [Buf(None, Res("wc%d" % l), s.newsem("WC%d" % l)) for l in range(2)]
        bgq = []

        def bg_fill(l):
            jobs = [(w1b[l, r0:r0 + 128, :], w1_in[l, r0:r0 + 128, :]) for r0 in range(0, D, 128)]
            jobs += [(w2b[l, r0:r0 + 512, :], w2_in[l, r0:r0 + 512, :]) for r0 in range(0, DFF, 512)]
            for ji, (o_, i_) in enumerate(jobs):
                bgq.append((o_, i_, l, ji == len(jobs) - 1))

        def bg_tick(k=1):
            for _ in range(k):
                if not bgq:
                    return
                o_, i_, l, last = bgq.pop(0)
                dma("pool", o_, i_, wcast[l], [], [wcast[l]] if last else [])

        def bg_flush():
            bg_tick(len(bgq))
        phase_end()

        evq = [0]

        def ev_eng():
            evq[0] += 1
            return "act" if evq[0] % 2 else "dve"

        def load_T(src, ncol, dst, is_f32, keep=None):
            kc = ncol // 128
            ld = TR(2, [128, ncol], F32 if is_f32 else BF16, dma=True, name="ldT")
            cb = TR(2, [128, ncol], BF16, name="cbT") if is_f32 else None
            pbr = Ring([PB[6], PB[7]])
            for t in range(NT):
                lt = ld.next()
                dma("sp", lt.ap, src[t * 128:(t + 1) * 128, :], lt, [], [lt])
                if is_f32:
                    ct = cb.next()
                    cp("pool", ct.ap, lt.ap, [lt], [ct])
                else:
                    ct = lt
                for k0 in range(0, kc, 8):
                    kn = min(8, kc - k0)
                    pb = pbr.next()
                    pv = pb.ap.bitcast(BF16)
                    for j in range(kn):
                        k = k0 + j
                        s.op("pe", (lambda pv=pv, j=j, ct=ct, k=k: (lambda e: e.transpose(pv[:, j * 128:(j + 1) * 128], ct.ap[:, k * 128:(k + 1) * 128], ident.ap)))(),
                             rs([ct, ident]), rs([pb]))
                    cp(ev_eng(), dst.ap[:, k0:k0 + kn, t * 128:(t + 1) * 128],
                       pv[:, 0:kn * 128].rearrange("p (k t) -> p k t", k=kn), [pb], [dst])

        def wload(dst, wsrc, kc, ncol):
            dma("pool", dst.ap[:, 0:kc, 0:ncol], wsrc.rearrange("(k p) e -> p k e", p=128), dst, [], [dst])

        def proj_F(xT, kc, wsrc, ncol, dst, dil, wring, stg, oscale=None):
            wt = wring.next()
            wload(wt, wsrc, kc, ncol)
            pbr = Ring(PB[0:6])
            for c in range(ncol // 128):
                st = stg.next()
                for tg in range(4):
                    pb = pbr.next()
                    for k in range(kc):
                        mm(pb.ap, wt.ap[:, k, c * 128:(c + 1) * 128], xT.ap[:, k, tg * 512:(tg + 1) * 512],
                           k == 0, k == kc - 1, [wt, xT], [pb])
                    if oscale is not None:
                        if ev_eng() == "act":
                            s.op("act", (lambda o=st.ap[:, tg * 512:(tg + 1) * 512], i=pb.ap: (lambda e: e.mul(out=o, in_=i, mul=oscale)))(), rs([pb]), rs([st]))
                        else:
                            vop("dve", "tensor_single_scalar", [pb], [st], out=st.ap[:, tg * 512:(tg + 1) * 512], in_=pb.ap, scalar=oscale, op=ALU.mult)
                    elif dil == 1:
                        cp(ev_eng(), st.ap[:, tg * 512:(tg + 1) * 512], pb.ap, [pb], [st])
                    else:
                        na = 512 // dil
                        cp(ev_eng(), st.ap.rearrange("p (r a) -> p a r", r=dil)[:, tg * na:(tg + 1) * na, :],
                           pb.ap.rearrange("p (a r) -> p a r", r=dil), [pb], [st])
                dma("sp", dst[c * 128:(c + 1) * 128, :], st.ap, st, [st], [])

        def proj_T(xT, kc, wsrc, ncol, dst, dst_dt, wring, stg):
            wt = wring.next()
            wload(wt, wsrc, kc, ncol)
            pbr = Ring(PB[0:6])
            for t in range(NT):
                pb = pbr.next()
                for k in range(kc):
                    mm(pb.ap[:, 0:ncol], xT.ap[:, k, t * 128:(t + 1) * 128], wt.ap[:, k, 0:ncol],
                       k == 0, k == kc - 1, [wt, xT], [pb])
                st = stg.next()
                cp(ev_eng(), st.ap[:, 0:ncol], pb.ap[:, 0:ncol], [pb], [st])
                dma("sp", dst[t * 128:(t + 1) * 128, :], st.ap[:, 0:ncol], st, [st], [])

        def layer_norm_tile(z, gt, bt, stat):
            st6 = stat.ap[:, 0:24].rearrange("p (c s) -> p c s", c=4)
            for c in range(4):
                vop("dve", "bn_stats", [z], [stat], out=st6[:, c, :], in_=z.ap[:, c * 512:(c + 1) * 512])
            mv = stat.ap[:, 24:26]
            vop("dve", "bn_aggr", [stat], [stat], out=mv, in_=st6)
            vop("dve", "tensor_single_scalar", [stat], [stat], out=stat.ap[:, 26:27], in_=stat.ap[:, 25:26], scalar=LN_EPS, op=ALU.add)
            act(stat.ap[:, 27:28], stat.ap[:, 26:27], AF.Sqrt, [stat], [stat])
            vop("dve", "reciprocal", [stat], [stat], out=stat.ap[:, 28:29], in_=stat.ap[:, 27:28])
            vop("dve", "tensor_scalar", [z, stat], [z], out=z.ap, in0=z.ap, scalar1=stat.ap[:, 24:25], scalar2=stat.ap[:, 28:29],
                op0=ALU.subtract, op1=ALU.mult)
            vop("pool", "tensor_tensor", [z, gt], [z], out=z.ap, in0=z.ap, in1=gt.ap, op=ALU.mult)
            vop("dve", "tensor_tensor", [z, bt], [z], out=z.ap, in0=z.ap, in1=bt.ap, op=ALU.add)

        def load_gb(g_src, b_src):
            gt = T([128, D], F32, dma=True, name="gam")
            bt = T([128, D], F32, dma=True, name="bet")
            dma("sp", gt.ap, g_src.partition_broadcast(128), gt, [], [gt])
            dma("sp", bt.ap, b_src.partition_broadcast(128), bt, [], [bt])
            return gt, bt

        def out_proj_ln(Yd, kc, wsrc, xres, g_src, b_src, dst, ncl=None, extra=False):
            ncl = ncl or kc * 128
            wt = T([128, kc, D], BF16, dma="sw", name="wout")
            for k0 in range(0, kc, 4):
                dma("pool", wt.ap[:, k0:k0 + 4, :], wsrc[k0 * 128:(k0 + 4) * 128, :].rearrange("(k p) e -> p k e", p=128), wt, [], [wt])
            gt, bt = load_gb(g_src, b_src)
            yl = TR(2, [128, kc * 128], BF16, dma=True, name="yl")
            yT = TR(2, [128, kc, 128], BF16, name="yT")
            zr = TR(2, [128, D], F32, dma=True, name="z")
            stat = TR(2, [128, 32], F32, name="stat")
            pbt = Ring([PB[4], PB[5]])
            if extra:
                ol = [TR(2, [128, 512], F32, dma=True, name="o%d" % g) for g in range(3)]
                ll = TR(2, [128, 3, 8], F32, dma=True, name="l")
                wk = TR(2, [128, 3, 8], F32, name="wk")
                sm = TR(2, [128, 16], F32, name="sm")

            def mix_tile(t, y):
                og = [ol[g].next() for g in range(3)]
                lt = ll.next()
                for g in range(3):
                    dma("sp", og[g].ap, OB_d[g][t * 128:(t + 1) * 128, :], og[g], [], [og[g]])
                    dma("sp", lt.ap[:, g, :], LSE_d[g][t * 128:(t + 1) * 128, :], lt, [], [lt])
                m = sm.next()
                vop("dve", "tensor_tensor", [lt], [m], out=m.ap[:, 0:8], in0=lt.ap[:, 0, :], in1=lt.ap[:, 1, :], op=ALU.max)
                vop("dve", "tensor_tensor", [lt, m], [m], out=m.ap[:, 0:8], in0=m.ap[:, 0:8], in1=lt.ap[:, 2, :], op=ALU.max)
                w = wk.next()
                vop("dve", "tensor_tensor", [lt, m], [w], out=w.ap, in0=lt.ap, in1=m.ap[:, 0:8].unsqueeze(1).broadcast_to([128, 3, 8]), op=ALU.subtract)
                act(w.ap, w.ap, AF.Exp, [w], [w])
                vop("dve", "tensor_tensor", [w], [m], out=m.ap[:, 8:16], in0=w.ap[:, 0, :], in1=w.ap[:, 1, :], op=ALU.add)
                vop("dve", "tensor_tensor", [w, m], [m], out=m.ap[:, 8:16], in0=m.ap[:, 8:16], in1=w.ap[:, 2, :], op=ALU.add)
                vop("dve", "reciprocal", [m], [m], out=m.ap[:, 8:16], in_=m.ap[:, 8:16])
                vop("dve", "tensor_tensor", [w, m], [w], out=w.ap, in0=w.ap, in1=m.ap[:, 8:16].unsqueeze(1).broadcast_to([128, 3, 8]), op=ALU.mult)
                for g in range(3):
                    eng = "dve" if g == 1 else "pool"
                    vop(eng, "tensor_tensor", [og[g], w], [og[g]], out=og[g].ap.rearrange("p (h d) -> p h d", h=8),
                        in0=og[g].ap.rearrange("p (h d) -> p h d", h=8), in1=w.ap[:, g, :].unsqueeze(2).broadcast_to([128, 8, 64]), op=ALU.mult)
                vop("pool", "tensor_tensor", [og[0], og[2]], [og[0]], out=og[0].ap, in0=og[0].ap, in1=og[2].ap, op=ALU.add)
                vop("dve", "tensor_tensor", [og[0], og[1]], [y], out=y.ap[:, 1024:1536], in0=og[0].ap, in1=og[1].ap, op=ALU.add)

            def prep(t):
                y = yl.next()
                dma("sp", y.ap[:, 0:ncl], Yd[t * 128:(t + 1) * 128, 0:ncl], y, [], [y])
                if extra:
                    mix_tile(t, y)
                z = zr.next()
                dma("sp", z.ap, xres[t * 128:(t + 1) * 128, :], z, [], [z])
                yt = yT.next()
                for k0 in range(0, kc, 8):
                    kn = min(8, kc - k0)
                    pb = pbt.next()
                    pv = pb.ap.bitcast(BF16)
                    for j in range(kn):
                        k = k0 + j
                        s.op("pe", (lambda pv=pv, j=j, y=y, k=k: (lambda e: e.transpose(pv[:, j * 128:(j + 1) * 128], y.ap[:, k * 128:(k + 1) * 128], ident.ap)))(),
                             rs([y, ident]), rs([pb]))
                    cp("act", yt.ap[:, k0:k0 + kn, :], pv[:, 0:kn * 128].rearrange("p (k t) -> p k t", k=kn), [pb], [yt])
                return yt, z

            cur = prep(0)
            for t in range(NT):
                nxt = prep(t + 1) if t + 1 < NT else None
                yt, z = cur
                for dt in range(4):
                    pb = PB[dt]
                    for k in range(kc):
                        mm(pb.ap, yt.ap[:, k, :], wt.ap[:, k, dt * 512:(dt + 1) * 512], k == 0, k == kc - 1, [yt, wt], [pb])
                    vop("dve", "scalar_tensor_tensor", [z, pb], [z], out=z.ap[:, dt * 512:(dt + 1) * 512],
                        in0=z.ap[:, dt * 512:(dt + 1) * 512], scalar=ALPHA, in1=pb.ap, op0=ALU.mult, op1=ALU.add)
                layer_norm_tile(z, gt, bt, stat.next())
                dma("sp", dst[t * 128:(t + 1) * 128, :], z.ap, z, [z], [])
                cur = nxt

        def mlp(l, xsrc, g_src, b_src, dst):
            gt, bt = load_gb(g_src, b_src)
            zb = T([128, 4, D], F32, name="zb")
            zts = [Buf(zb.ap[:, tt, :], Res("z%d" % tt), s.dsem("H")) for tt in range(4)]
            xbs = [T([128, D], BF16, dma="sw", name="xb%d" % i) for i in range(2)]
            xTr = [T([128, 16, 512], BF16, name="xTm%d" % i) for i in range(2)]
            hT = T([128, 64, 512], BF16, name="hT")
            w1r = TR(2, [128, 16, 256], BF16, dma=True, name="w1")
            w2r = TR(3, [128, 8, 512], BF16, dma=True, name="w2")
            hr = TR(2, [128, 512], F32, name="hrelu")
            stat = TR(2, [128, 32], F32, name="stat")
            pbt = Ring([PB[6], PB[7]])
            pbh = Ring(PB[0:6])

            def issue_xload(G, tts):
                for tt in tts:
                    t = G * 4 + tt
                    xb = xbs[tt % 2]
                    dma("pool", xb.ap, xsrc[t * 128:(t + 1) * 128, :], xb, [], [xb])

            def prefetch_T(G, tts):
                xT = xTr[G % 2]
                for tt in tts:
                    xb = xbs[tt % 2]
                    for k0 in (0, 8):
                        pb = pbt.next()
                        pv = pb.ap.bitcast(BF16)
                        for j in range(8):
                            k = k0 + j
                            s.op("pe", (lambda pv=pv, j=j, xb=xb, k=k: (lambda e: e.transpose(pv[:, j * 128:(j + 1) * 128], xb.ap[:, k * 128:(k + 1) * 128], ident.ap)))(),
                                 rs([xb, ident]), rs([pb]))
                        cp("act", xT.ap[:, k0:k0 + 8, tt * 128:(tt + 1) * 128],
                           pv.rearrange("p (k t) -> p k t", k=8), [pb], [xT])

            issue_xload(0, [0, 1])
            prefetch_T(0, [0, 1])
            issue_xload(0, [2, 3])
            prefetch_T(0, [2, 3])
            for G in range(4):
                xT = xTr[G % 2]
                for f2 in range(32):
                    w1 = w1r.next()
                    dma("sp", w1.ap, w1b[l, :, f2 * 256:(f2 + 1) * 256].rearrange("(k p) f -> p k f", p=128), w1, [wcast[l]], [w1])
                    for fi in range(2):
                        f = f2 * 2 + fi
                        pb = pbh.next()
                        for k in range(16):
                            mm(pb.ap, w1.ap[:, k, fi * 128:(fi + 1) * 128], xT.ap[:, k, :], k == 0, k == 15, [w1, xT], [pb])
                        h = hr.next()
                        act(h.ap, pb.ap, AF.Relu, [pb], [h])
                        vop("pool" if f % 2 else "dve", "tensor_tensor", [h], [hT], out=hT.ap[:, f, :], in0=h.ap, in1=h.ap, op=ALU.mult)
                    if G > 0:
                        for tt in range(4):
                            if f2 == 1 + 4 * tt:
                                layer_norm_tile(zts[tt], gt, bt, stat.next())
                            if f2 == 4 + 4 * tt:
                                t = (G - 1) * 4 + tt
                                dma("sp", dst[t * 128:(t + 1) * 128, :], zts[tt].ap, zts[tt], [zts[tt]], [])
                    if f2 == 22:
                        for tt in range(4):
                            t = G * 4 + tt
                            dma("sp", zts[tt].ap, xsrc[t * 128:(t + 1) * 128, :], zts[tt], [], [zts[tt]])
                    if f2 == 26 and G < 3:
                        issue_xload(G + 1, [0, 1])
                for dt in range(4):
                    if G < 3 and dt == 0:
                        prefetch_T(G + 1, [0, 1])
                        issue_xload(G + 1, [2, 3])
                    if G < 3 and dt == 1:
                        prefetch_T(G + 1, [2, 3])
                    pbs = PB[0:4] if dt % 2 == 0 else PB[4:8]
                    for f8 in range(8):
                        w2 = w2r.next()
                        dma("sp", w2.ap, w2b[l, f8 * 1024:(f8 + 1) * 1024, dt * 512:(dt + 1) * 512].rearrange("(c p) d -> p c d", p=128),
                            w2, [wcast[l]], [w2])
                        for fi in range(8):
                            f = f8 * 8 + fi
                            for tt in range(4):
                                mm(pbs[tt].ap, hT.ap[:, f, tt * 128:(tt + 1) * 128], w2.ap[:, fi, :], f == 0, f == 63, [hT, w2], [pbs[tt]])
                    for tt in range(4):
                        zt = zts[tt]
                        vop("dve", "scalar_tensor_tensor", [zt, pbs[tt]], [zt], out=zt.ap[:, dt * 512:(dt + 1) * 512],
                            in0=zt.ap[:, dt * 512:(dt + 1) * 512], scalar=ALPHA, in1=pbs[tt].ap, op0=ALU.mult, op1=ALU.add)
            for tt in range(4):
                t = 12 + tt
                layer_norm_tile(zts[tt], gt, bt, stat.next())
                dma("sp", dst[t * 128:(t + 1) * 128, :], zts[tt].ap, zts[tt], [zts[tt]], [])

        def banded(Qd, Kd, kvmap, nh, Vd, nkv, dil, Rm, cvals, use_sink, out_mode, Yd=None, OBd=None, LSEd=None):
            L = S // dil
            nbpl = L // 128
            Vp = T([128, NT, nkv * 64], BF16, dma=True, name="Vp")
            for n in range(NT):
                r = (128 * n) // L
                a0 = (128 * n) % L
                st_ = r + dil * a0
                dma("sp", Vp.ap[:, n, :], Vd[st_:st_ + dil * 127 + 1:dil, :], Vp, [], [Vp])
            if out_mode == "A":
                Oall = T([128, NT, nh * 64], BF16, dma=True, name="Oall")
            else:
                Oall = T([128, NT, nh * 64], F32, dma=True, name="Oall")
                lse = T([128, NT, nh], F32, dma=True, name="lse")
            Qr = TR(3, [64, S], BF16, dma=True, name="Qh")
            Kr = TR(3, [64, S], BF16, dma=True, name="Kh")
            Tr = TR(3, [128, 256], F32, name="T")
            Pr = TR(4, [128, 256], BF16, name="P")
            PTr = TR(4, [128, 256], BF16, name="PT")
            str_ = TR(6, [128, 8], F32, name="st")
            pS = Ring([PB[0], PB[1], PB[6]])
            pT = Ring([PB[2], PB[3]])
            pO = Ring([PB[4], PB[5], PB[7]])
            heads = []
            lastkv = -1
            Kh = None
            for h in range(nh):
                Qh = Qr.next()
                kv = kvmap(h)
                newk = kv != lastkv
                if newk:
                    Kh = Kr.next()
                    lastkv = kv
                heads.append((Qh, Kh, kv, newk))

            def load_head(h):
                Qh, Kh, kv, newk = heads[h]
                dma("sp", Qh.ap, Qd[h * 64:(h + 1) * 64, :], Qh, [], [Qh])
                if newk:
                    dma("sp", Kh.ap, Kd[kv * 64:(kv + 1) * 64, :], Kh, [], [Kh])
            items = [(h, n) for h in range(nh) for n in range(NT)]
            ctx = [dict(ps=pS.next(), Tt=Tr.next(), st=str_.next(), Pt=Pr.next(), pt=pT.next(), PT=PTr.next(), po=pO.next())
                   for _ in items]
            load_head(0)

            def geom(n):
                hasprev = (n % nbpl) != 0
                nk = 256 if hasprev else 128
                ks = (n - 1) * 128 if hasprev else n * 128
                return hasprev, nk, ks

            def stA(it):
                h, n = items[it]
                c = ctx[it]
                if n == 0 and h + 1 < nh:
                    load_head(h + 1)
                if it % 12 == 5:
                    bg_tick()
                Qh, Kh, kv, _ = heads[h]
                hasprev, nk, ks = geom(n)
                Rv = Rm.ap[:, 0:256] if hasprev else Rm.ap[:, 128:256]
                ps, Tt, st, Pt = c["ps"], c["Tt"], c["st"], c["Pt"]
                mm(ps.ap[:, 0:nk], Qh.ap[:, n * 128:(n + 1) * 128], Kh.ap[:, ks:ks + nk], True, True, [Qh, Kh], [ps])
                vop("dve", "scalar_tensor_tensor", [ps, Rm], [Tt], out=Tt.ap[:, 0:nk], in0=Rv, scalar=cvals[h], in1=ps.ap[:, 0:nk],
                    op0=ALU.mult, op1=ALU.add)
                vop("dve", "reduce_max", [Tt], [st], out=st.ap[:, 0:1], in_=Tt.ap[:, 0:nk], axis=AX.X)
                if use_sink:
                    vop("dve", "tensor_scalar", [st, sink8], [st], out=st.ap[:, 1:2], in0=st.ap[:, 0:1], scalar1=sink8.ap[:, h:h + 1], scalar2=-0.125,
                        op0=ALU.max, op1=ALU.mult)
                else:
                    vop("dve", "tensor_single_scalar", [st], [st], out=st.ap[:, 1:2], in_=st.ap[:, 0:1], scalar=-0.125, op=ALU.mult)
                act(Pt.ap[:, 0:nk], Tt.ap[:, 0:nk], AF.Exp, [Tt, st], [Pt, st], bias=st.ap[:, 1:2], scale=0.125, accum=st.ap[:, 2:3])
                if use_sink:
                    act(st.ap[:, 3:4], st.ap[:, 1:2], AF.Exp, [st, sinkt], [st], bias=sinkt.ap[:, h:h + 1], scale=1.0)

            def stB(it):
                h, n = items[it]
                c = ctx[it]
                hasprev, nk, ks = geom(n)
                Pt, pt, PT = c["Pt"], c["pt"], c["PT"]
                ptv = pt.ap.bitcast(BF16)
                for kb in range(nk // 128):
                    s.op("pe", (lambda ptv=ptv, kb=kb, Pt=Pt: (lambda e: e.transpose(ptv[:, kb * 128:(kb + 1) * 128], Pt.ap[:, kb * 128:(kb + 1) * 128], ident.ap)))(),
                         rs([Pt, ident]), rs([pt]))
                cp("act", PT.ap[:, 0:nk], ptv[:, 0:nk], [pt], [PT])

            def stC(it):
                h, n = items[it]
                c = ctx[it]
                Qh, Kh, kv, _ = heads[h]
                hasprev, nk, ks = geom(n)
                PT, po, st = c["PT"], c["po"], c["st"]
                nkb = nk // 128
                for kb in range(nkb):
                    blk = ks // 128 + kb
                    mm(po.ap[:, 0:64], PT.ap[:, kb * 128:(kb + 1) * 128], Vp.ap[:, blk, kv * 64:(kv + 1) * 64],
                       kb == 0, kb == nkb - 1, [PT, Vp], [po])
                if use_sink:
                    vop("dve", "tensor_tensor", [st], [st], out=st.ap[:, 2:3], in0=st.ap[:, 2:3], in1=st.ap[:, 3:4], op=ALU.add)
                vop("dve", "reciprocal", [st], [st], out=st.ap[:, 4:5], in_=st.ap[:, 2:3])
                vop("dve", "tensor_scalar", [po, st], [Oall], out=Oall.ap[:, n, h * 64:(h + 1) * 64], in0=po.ap[:, 0:64],
                    scalar1=st.ap[:, 4:5], scalar2=None, op0=ALU.mult)
                if out_mode == "B":
                    act(st.ap[:, 5:6], st.ap[:, 2:3], AF.Ln, [st], [st])
                    vop("dve", "tensor_tensor", [st], [lse], out=lse.ap[:, n, h:h + 1], in0=st.ap[:, 5:6], in1=st.ap[:, 1:2], op=ALU.subtract)

            run_pipeline(len(items), [stA, stB, stC])
            if out_mode == "A":
                for n in range(NT):
                    dma("sp", Yd[n * 128:(n + 1) * 128, 0:nh * 64], Oall.ap[:, n, :], Oall, [Oall], [])
            else:
                for n in range(NT):
                    r = (128 * n) // L
                    a0 = (128 * n) % L
                    st_ = r + dil * a0
                    dma("sp", OBd[st_:st_ + dil * 127 + 1:dil, :], Oall.ap[:, n, :], Oall, [Oall], [])
                    dma("sp", LSEd[st_:st_ + dil * 127 + 1:dil, :], lse.ap[:, n, :], lse, [lse], [])

        def combine_B():
            ol = [TR(2, [128, 512], F32, dma=True, name="o%d" % g) for g in range(3)]
            ll = TR(2, [128, 3, 8], F32, dma=True, name="l")
            wk = TR(2, [128, 3, 8], F32, name="wk")
            sm = TR(2, [128, 16], F32, name="sm")
            yo = TR(2, [128, 512], BF16, dma=True, name="yo")
            for t in range(NT):
                og = [ol[g].next() for g in range(3)]
                lt = ll.next()
                for g in range(3):
                    dma("sp", og[g].ap, OB_d[g][t * 128:(t + 1) * 128, :], og[g], [], [og[g]])
                    dma("sp", lt.ap[:, g, :], LSE_d[g][t * 128:(t + 1) * 128, :], lt, [], [lt])
                m = sm.next()
                vop("dve", "tensor_tensor", [lt], [m], out=m.ap[:, 0:8], in0=lt.ap[:, 0, :], in1=lt.ap[:, 1, :], op=ALU.max)
                vop("dve", "tensor_tensor", [lt, m], [m], out=m.ap[:, 0:8], in0=m.ap[:, 0:8], in1=lt.ap[:, 2, :], op=ALU.max)
                w = wk.next()
                vop("dve", "tensor_tensor", [lt, m], [w], out=w.ap, in0=lt.ap, in1=m.ap[:, 0:8].unsqueeze(1).broadcast_to([128, 3, 8]), op=ALU.subtract)
                act(w.ap, w.ap, AF.Exp, [w], [w])
                vop("dve", "tensor_tensor", [w], [m], out=m.ap[:, 8:16], in0=w.ap[:, 0, :], in1=w.ap[:, 1, :], op=ALU.add)
                vop("dve", "tensor_tensor", [w, m], [m], out=m.ap[:, 8:16], in0=m.ap[:, 8:16], in1=w.ap[:, 2, :], op=ALU.add)
                vop("dve", "reciprocal", [m], [m], out=m.ap[:, 8:16], in_=m.ap[:, 8:16])
                vop("dve", "tensor_tensor", [w, m], [w], out=w.ap, in0=w.ap, in1=m.ap[:, 8:16].unsqueeze(1).broadcast_to([128, 3, 8]), op=ALU.mult)
                for g in range(3):
                    eng = "pool" if g == 1 else "dve"
                    vop(eng, "tensor_tensor", [og[g], w], [og[g]], out=og[g].ap.rearrange("p (h d) -> p h d", h=8),
                        in0=og[g].ap.rearrange("p (h d) -> p h d", h=8), in1=w.ap[:, g, :].unsqueeze(2).broadcast_to([128, 8, 64]), op=ALU.mult)
                vop("dve", "tensor_tensor", [og[0], og[1]], [og[0]], out=og[0].ap, in0=og[0].ap, in1=og[1].ap, op=ALU.add)
                y = yo.next()
                vop("dve", "tensor_tensor", [og[0], og[2]], [y], out=y.ap, in0=og[0].ap, in1=og[2].ap, op=ALU.add)
                dma("sp", Y0_d[t * 128:(t + 1) * 128, 1024:1536], y.ap, y, [y], [])

        xT = T([128, 16, S], BF16, name="xT")
        load_T(x_in, D, xT, True)
        wring = TR(2, [128, 16, 512], BF16, dma="sw", name="wring")
        stgF = TR(2, [128, S], BF16, dma=True, name="stgF")
        stgT = TR(2, [128, 512], BF16, dma=True, name="stgT")
        proj_F(xT, 16, e_win[:, 0:512], 512, QA_d[0:512, :], 1, wring, stgF)
        proj_F(xT, 16, e_win[:, 512:1024], 512, QA_d[512:1024, :], 1, wring, stgF)
        proj_F(xT, 16, e_win[:, 1024:1152], 128, KA_d, 1, wring, stgF)
        proj_T(xT, 16, e_win[:, 1152:1280], 128, VA_d, BF16, wring, stgT)
        for g, dil in enumerate((1, 4, 16)):
            base = 1280 + g * 1536
            proj_F(xT, 16, e_win[:, base:base + 512], 512, QB_d[g], dil, wring, stgF)
            proj_F(xT, 16, e_win[:, base + 512:base + 1024], 512, KB_d[g], dil, wring, stgF)
            proj_T(xT, 16, e_win[:, base + 1024:base + 1536], 512, VB_d[g], BF16, wring, stgT)
        phase_end()
        bg_fill(0)
        slA = alibi(16)
        banded(QA_d, KA_d, lambda h: h // 8, 16, VA_d, 2, 1, RA, [8.0 * sl for sl in slA], True, "A", Yd=Y0_d)
        phase_end()
        slB = alibi(8)
        for g, dil in enumerate((1, 4, 16)):
            banded(QB_d[g], KB_d[g], lambda h: h, 8, VB_d[g], 8, dil, RB, [8.0 * sl * dil for sl in slB], False, "B",
                   OBd=OB_d[g], LSEd=LSE_d[g])
            phase_end()
        out_proj_ln(Y0_d, 12, e_wout, x_in, ln1_g[0:1, :], ln1_b[0:1, :], xs1, ncl=1024, extra=True)
        phase_end()
        bg_flush()
        mlp(0, xs1, ln2_g[0:1, :], ln2_b[0:1, :], xs2 if "stop0" not in dbg else out_d)
        phase_end()

        if "stop0" not in dbg:

            base_persist = persist_end[0]
            bg_fill(1)
            cosT = T([128, S], F32, name="cosT")
            sinT = T([128, S], F32, name="sinT")
            persist_end[0] = aoff[0]
            pidx = T([128, 2], I32, name="pidx")
            pf = T([128, 2], F32, name="pf")
            vop("pool", "iota", [], [pidx], out=pidx.ap[:, 0:1], pattern=[[0, 1]], base=0, channel_multiplier=1)
            vop("dve", "tensor_single_scalar", [pidx], [pidx], out=pidx.ap[:, 1:2], in_=pidx.ap[:, 0:1], scalar=15, op=ALU.bitwise_and)
            cp("dve", pf.ap[:, 0:1], pidx.ap[:, 1:2], [pidx], [pf])
            act(pf.ap[:, 1:2], pf.ap[:, 0:1], AF.Exp, [pf], [pf], scale=-math.log(10000.0) / 16.0)
            vop("dve", "tensor_single_scalar", [pf], [pf], out=pf.ap[:, 1:2], in_=pf.ap[:, 1:2], scalar=1.0 / (2 * math.pi), op=ALU.mult)
            tpi = T([128, S], I32, name="tpi")
            tpf = T([128, S], F32, name="tpf")
            tq = T([128, S], F32, name="tq")
            vop("pool", "iota", [], [tpi], out=tpi.ap, pattern=[[1, S]], base=0, channel_multiplier=0)
            cp("dve", tpf.ap, tpi.ap, [tpi], [tpf])
            for tab, offs in ((sinT, 0.0), (cosT, 0.25)):
                vop("dve", "tensor_scalar", [tpf, pf], [tq], out=tq.ap, in0=tpf.ap, scalar1=pf.ap[:, 1:2], scalar2=offs,
                    op0=ALU.mult, op1=ALU.add)
                cp("dve", tpi.ap, tq.ap, [tq], [tpi])
                cp("dve", tab.ap, tpi.ap, [tpi], [tab])
                vop("dve", "tensor_tensor", [tq, tab], [tq], out=tq.ap, in0=tq.ap, in1=tab.ap, op=ALU.subtract)
                vop("dve", "scalar_tensor_tensor", [tq], [tq], out=tq.ap, in0=tq.ap, scalar=0.5, in1=tq.ap,
                    op0=ALU.is_gt, op1=ALU.subtract)
                act(tab.ap, tq.ap, AF.Sin, [tq], [tab], scale=-2.0 * math.pi)
            phase_end()

            def rope_evac(ps_m, ps_r, st, tg, tmpr):
                ta = tmpr.next()
                tb = tmpr.next()
                cs = slice(tg * 512, (tg + 1) * 512)
                vop("dve", "tensor_tensor", [ps_r, sinT], [ta], out=ta.ap[64:96, :], in0=ps_r.ap[64:96, :], in1=sinT.ap[64:96, cs], op=ALU.mult)
                vop("dve", "tensor_tensor", [ps_m, cosT], [tb], out=tb.ap[64:96, :], in0=ps_m.ap[64:96, :], in1=cosT.ap[64:96, cs], op=ALU.mult)
                vop("pool", "tensor_tensor", [ta, tb], [st], out=st.ap[64:96, cs], in0=ta.ap[64:96, :], in1=tb.ap[64:96, :], op=ALU.add)

            xT = T([128, 16, S], BF16, name="xT1")
            load_T(xs2, D, xT, True)
            wring = TR(2, [128, 16, 512], BF16, dma="sw", name="wring")
            stgF = TR(2, [128, S], BF16, dma=True, name="stgF")
            stgT = TR(2, [128, 512], BF16, dma=True, name="stgT")
            stgT32 = TR(2, [128, 512], F32, dma=True, name="stgT32")
            for i in range(2):
                proj_F(xT, 16, o_win[:, i * 512:(i + 1) * 512], 512, QC_d[i * 512:(i + 1) * 512, :], 1, wring, stgF)
                proj_F(xT, 16, o_win[:, 1024 + i * 512:1024 + (i + 1) * 512], 512, KC_d[i * 512:(i + 1) * 512, :], 1, wring, stgF)
                proj_T(xT, 16, o_win[:, 2048 + i * 512:2048 + (i + 1) * 512], 512, VC_d[:, i * 512:(i + 1) * 512], BF16, wring, stgT)
            proj_T(xT, 16, o_win[:, 3072:3584], 512, CQ_d[:, 0:512], F32, wring, stgT32)
            proj_T(xT, 16, o_win[:, 3584:3840], 256, CQ_d[:, 512:768], F32, wring, stgT32)
            wkr = T([128, 16, 96], BF16, dma="sw", name="wkr")
            wkrot = T([128, 16, 96], BF16, name="wkrot")
            vop("dve", "memset", [], [wkr], ap=wkr.ap, constant=0.0)
            vop("pool", "memset", [], [wkrot], ap=wkrot.ap, constant=0.0)
            dma("pool", wkr.ap[:, :, 64:96], o_win[:, 3840:3872].rearrange("(k p) e -> p k e", p=128), wkr, [], [wkr])
            vop("dve", "tensor_single_scalar", [wkr], [wkrot], out=wkrot.ap[:, :, 64:80], in_=wkr.ap[:, :, 80:96], scalar=-1.0, op=ALU.mult)
            cp("dve", wkrot.ap[:, :, 80:96], wkr.ap[:, :, 64:80], [wkr], [wkrot])
            tmpr = TR(4, [128, 512], F32, name="ropetmp")
            stK = T([128, S], BF16, dma=True, name="stKr")
            for tg in range(4):
                pm, pr = PB[0], PB[1]
                for k in range(16):
                    mm(pm.ap[0:96, :], wkr.ap[:, k, :], xT.ap[:, k, tg * 512:(tg + 1) * 512], k == 0, k == 15, [wkr, xT], [pm])
                for k in range(16):
                    mm(pr.ap[0:96, :], wkrot.ap[:, k, :], xT.ap[:, k, tg * 512:(tg + 1) * 512], k == 0, k == 15, [wkrot, xT], [pr])
                rope_evac(pm, pr, stK, tg, tmpr)
            for h in range(16):
                dma("sp", KD_d[h, 64:96, :], stK.ap[64:96, :], stK, [stK], [])
            phase_end()

            cT = T([128, 6, S], BF16, name="cT")
            gq = T([128, 768], F32, dma=True, name="gq")
            dma("sp", gq.ap[:, 0:512], o_qg.partition_broadcast(128), gq, [], [gq])
            dma("sp", gq.ap[:, 512:768], o_kvg.partition_broadcast(128), gq, [], [gq])
            cl = TR(2, [128, 768], F32, dma=True, name="cl")
            junk = TR(2, [128, 768], F32, name="junk")
            cbf = TR(2, [128, 768], BF16, name="cbf")
            str_ = TR(2, [128, 8], F32, name="st")
            pbt = Ring([PB[6], PB[7]])
            for t in range(NT):
                c = cl.next()
                dma("sp", c.ap, CQ_d[t * 128:(t + 1) * 128, :], c, [], [c])
                jk = junk.next()
                st = str_.next()
                act(jk.ap[:, 0:512], c.ap[:, 0:512], AF.Square, [c], [jk, st], accum=st.ap[:, 0:1])
                act(jk.ap[:, 512:768], c.ap[:, 512:768], AF.Square, [c], [jk, st], accum=st.ap[:, 1:2])
                vop("dve", "tensor_scalar", [st], [st], out=st.ap[:, 2:3], in0=st.ap[:, 0:1], scalar1=1.0 / 512.0, scalar2=RMS_EPS, op0=ALU.mult, op1=ALU.add)
                vop("dve", "tensor_scalar", [st], [st], out=st.ap[:, 3:4], in0=st.ap[:, 1:2], scalar1=1.0 / 256.0, scalar2=RMS_EPS, op0=ALU.mult, op1=ALU.add)
                act(st.ap[:, 4:6], st.ap[:, 2:4], AF.Sqrt, [st], [st])
                vop("dve", "reciprocal", [st], [st], out=st.ap[:, 6:8], in_=st.ap[:, 4:6])
                vop("pool", "tensor_tensor", [c, gq], [c], out=c.ap, in0=c.ap, in1=gq.ap, op=ALU.mult)
                cb = cbf.next()
                vop("dve", "tensor_scalar", [c, st], [cb], out=cb.ap[:, 0:512], in0=c.ap[:, 0:512], scalar1=st.ap[:, 6:7], scalar2=None, op0=ALU.mult)
                vop("dve", "tensor_scalar", [c, st], [cb], out=cb.ap[:, 512:768], in0=c.ap[:, 512:768], scalar1=st.ap[:, 7:8], scalar2=None, op0=ALU.mult)
                pb = pbt.next()
                pv = pb.ap.bitcast(BF16)
                for k in range(6):
                    s.op("pe", (lambda pv=pv, k=k, cb=cb: (lambda e: e.transpose(pv[:, k * 128:(k + 1) * 128], cb.ap[:, k * 128:(k + 1) * 128], ident.ap)))(),
                         rs([cb, ident]), rs([pb]))
                cp("act", cT.ap[:, :, t * 128:(t + 1) * 128], pv[:, 0:768].rearrange("p (k t) -> p k t", k=6), [pb], [cT])
            wq = T([128, 4, 1536], BF16, dma="sw", name="wq")
            wqr = T([128, 4, 1536], BF16, name="wqr")
            dma("pool", wq.ap, o_wuq.rearrange("(k p) e -> p k e", p=128), wq, [], [wq])
            wq4 = wq.ap.rearrange("p k (h e) -> p k h e", h=16)
            wqr4 = wqr.ap.rearrange("p k (h e) -> p k h e", h=16)
            cp("dve", wqr.ap, wq.ap, [wq], [wqr])
            for k in range(4):
                vop("dve", "tensor_single_scalar", [wq, wqr], [wqr], out=wqr4[:, k, :, 64:80], in_=wq4[:, k, :, 80:96], scalar=-1.0, op=ALU.mult)
                cp("dve", wqr4[:, k, :, 80:96], wq4[:, k, :, 64:80], [wq, wqr], [wqr])
            wkv = T([128, 2, 2048], BF16, dma="sw", name="wkv")
            dma("pool", wkv.ap, o_wukv.rearrange("(k p) e -> p k e", p=128), wkv, [], [wkv])
            stQ = TR(2, [128, S], BF16, dma=True, name="stQ")
            stKn = TR(2, [128, S], BF16, dma=True, name="stKn")
            stV = TR(2, [128, 1024], BF16, dma=True, name="stV")
            tmpr = TR(4, [128, 512], F32, name="ropetmp")
            pbq = Ring(PB[0:6])
            for h in range(16):
                sq = stQ.next()
                for tg in range(4):
                    pm = pbq.next()
                    pr = pbq.next()
                    for k in range(4):
                        mm(pm.ap[0:96, :], wq4[:, k, h, :], cT.ap[:, k, tg * 512:(tg + 1) * 512], k == 0, k == 3, [wq, cT], [pm])
                    for k in range(4):
                        mm(pr.ap[0:96, :], wqr4[:, k, h, :], cT.ap[:, k, tg * 512:(tg + 1) * 512], k == 0, k == 3, [wqr, cT], [pr])
                    cp("act", sq.ap[0:64, tg * 512:(tg + 1) * 512], pm.ap[0:64, :], [pm], [sq])
                    rope_evac(pm, pr, sq, tg, tmpr)
                dma("sp", QD_d[h], sq.ap[0:96, :], sq, [sq], [])
                sk = stKn.next()
                for tg in range(4):
                    pk = pbq.next()
                    for k in range(2):
                        mm(pk.ap[0:64, :], wkv.ap[:, k, h * 128:h * 128 + 64], cT.ap[:, 4 + k, tg * 512:(tg + 1) * 512], k == 0, k == 1, [wkv, cT], [pk])
                    cp(ev_eng(), sk.ap[0:64, tg * 512:(tg + 1) * 512], pk.ap[0:64, :], [pk], [sk])
                dma("sp", KD_d[h, 0:64, :], sk.ap[0:64, :], sk, [sk], [])
            wkv5 = wkv.ap.rearrange("p k (h two d) -> p k h two d", two=2, d=64)
            for t in range(NT):
                sv = stV.next()
                for hf in range(2):
                    pb = pbq.next()
                    for k in range(2):
                        mm(pb.ap.rearrange("p (h d) -> p h d", h=8), cT.ap[:, 4 + k, t * 128:(t + 1) * 128], wkv5[:, k, hf * 8:(hf + 1) * 8, 1, :],
                           k == 0, k == 1, [wkv, cT], [pb])
                    cp(ev_eng(), sv.ap[:, hf * 512:(hf + 1) * 512], pb.ap, [pb], [sv])
                dma("sp", VD_d[t * 128:(t + 1) * 128, :], sv.ap, sv, [sv], [])
            persist_end[0] = base_persist
            phase_end()

            def attn_C():
                VCp = T([128, NT, 1024], BF16, dma=True, name="VCp")
                dma("sp", VCp.ap, VC_d.rearrange("(n p) c -> p n c", p=128), VCp, [], [VCp])
                OC = T([128, NT, 1024], BF16, dma=True, name="OC")
                Qr = TR(3, [64, S], BF16, dma=True, name="Qh")
                Kr = TR(3, [64, S], BF16, dma=True, name="Kh")
                Er = TR(4, [128, 512], F32, name="E")
                SPr = TR(4, [128, 512], F32, name="SP")
                LKr = TR(4, [128, 512], F32, name="LK")
                Wr = TR(4, [128, 512], BF16, name="W")
                Srun = TR(2, [128, 512], F32, name="Srun")
                pZ = Ring([PB[0], PB[1]])
                pA = Ring([PB[2], PB[3]])
                pO = PB[4:8]
                heads = [(Qr.next(), Kr.next()) for _ in range(16)]

                def load_head(h):
                    Qh, Kh = heads[h]
                    dma("sp", Qh.ap, QC_d[h * 64:(h + 1) * 64, :], Qh, [], [Qh])
                    dma("sp", Kh.ap, KC_d[h * 64:(h + 1) * 64, :], Kh, [], [Kh])
                items = []
                for h in range(16):
                    for G in range(4):
                        Sr = Srun.next()
                        for j in range(4 * G + 3, -1, -1):
                            items.append((h, G, j, Sr))
                ctx = [dict(pz=pZ.next(), pa=pA.next(), E=Er.next(), SP=SPr.next(), LK=LKr.next(), W=Wr.next()) for _ in items]
                load_head(0)

                def s1(it):
                    h, G, j, Sr = items[it]
                    c = ctx[it]
                    if G == 0 and j == 3 and h + 1 < 16:
                        load_head(h + 1)
                    if it % 16 == 7:
                        bg_tick()
                    Qh, Kh = heads[h]
                    c0 = max(j - 4 * G, 0) * 128
                    pz, E, SP, LK = c["pz"], c["E"], c["SP"], c["LK"]
                    mm(pz.ap[:, c0:512], Kh.ap[:, j * 128:(j + 1) * 128], Qh.ap[:, G * 512 + c0:(G + 1) * 512], True, True, [Kh, Qh], [pz])
                    act(E.ap[:, c0:512], pz.ap[:, c0:512], AF.Exp, [pz], [E], scale=-0.125)
                    act(SP.ap[:, c0:512], E.ap[:, c0:512], AF.Ln, [E], [SP], bias=1.0, scale=1.0)
                    vop("dve", "scalar_tensor_tensor", [pz, SP], [LK], out=LK.ap[:, c0:512], in0=pz.ap[:, c0:512], scalar=-0.125,
                        in1=SP.ap[:, c0:512], op0=ALU.mult, op1=ALU.subtract)
                    if j >= 4 * G:
                        vop("pool", "tensor_tensor", [LK, mC01], [LK], out=LK.ap[:, c0:c0 + 128], in0=LK.ap[:, c0:c0 + 128], in1=mC01.ap, op=ALU.mult)

                def s2(it):
                    h, G, j, Sr = items[it]
                    c = ctx[it]
                    c0 = max(j - 4 * G, 0) * 128
                    pa, E, SP, LK, W = c["pa"], c["E"], c["SP"], c["LK"], c["W"]
                    first = j == 4 * G + 3
                    if first:
                        vop("pool", "memset", [], [Sr], ap=Sr.ap, constant=0.0)
                    mm(pa.ap[:, c0:512], Ustr.ap, LK.ap[:, c0:512], True, first, [Ustr, LK], [pa])
                    if not first:
                        mm(pa.ap[:, c0:512], ones.ap, Sr.ap[:, c0:512], False, True, [ones, Sr], [pa])
                    vop("dve", "tensor_tensor", [pa, SP], [E], out=E.ap[:, c0:512], in0=pa.ap[:, c0:512], in1=SP.ap[:, c0:512], op=ALU.subtract)
                    act(W.ap[:, c0:512], E.ap[:, c0:512], AF.Exp, [E], [W])
                    if j >= 4 * G:
                        vop("pool", "tensor_tensor", [W, mC01b], [W], out=W.ap[:, c0:c0 + 128], in0=W.ap[:, c0:c0 + 128], in1=mC01b.ap, op=ALU.mult)
                    if j > 0:
                        vop("pool", "tensor_tensor", [Sr, LK], [Sr], out=Sr.ap[:, c0:512], in0=Sr.ap[:, c0:512], in1=LK.ap[:, c0:512], op=ALU.add)

                def s3(it):
                    h, G, j, Sr = items[it]
                    c = ctx[it]
                    q0 = max(j - 4 * G, 0)
                    W = c["W"]
                    for qt in range(q0, 4):
                        mm(pO[qt].ap[:, 0:64], W.ap[:, qt * 128:(qt + 1) * 128], VCp.ap[:, j, h * 64:(h + 1) * 64],
                           j == 4 * G + qt, j == 0, [W, VCp], [pO[qt]])
                    if j == 0:
                        for qt in range(4):
                            cp("act" if qt % 2 else "dve", OC.ap[:, 4 * G + qt, h * 64:(h + 1) * 64], pO[qt].ap[:, 0:64], [pO[qt]], [OC])

                run_pipeline(len(items), [s1, s2, s3])
                for n in range(NT):
                    dma("sp", Y1_d[n * 128:(n + 1) * 128, 0:1024], OC.ap[:, n, :], OC, [OC], [])

            def attn_D():
                sc = 1.0 / math.sqrt(96.0)
                VDp = T([128, NT, 1024], BF16, dma=True, name="VDp")
                dma("sp", VDp.ap, VD_d.rearrange("(n p) c -> p n c", p=128), VDp, [], [VDp])
                OD = T([128, NT, 1024], BF16, dma=True, name="OD")
                Qr = TR(3, [96, S], BF16, dma=True, name="Qh")
                Kr = TR(3, [96, S], BF16, dma=True, name="Kh")
                Pr = TR(4, [128, S], BF16, name="P")
                PTr = TR(4, [128, S], BF16, name="PT")
                Sdr = TR(3, [128, 128], F32, name="Sd")
                str_ = TR(6, [128, 16], F32, name="st")
                pSa = Ring([PB[0:2], PB[2:4]])
                pT = Ring([PB[4], PB[5]])
                pO = Ring([PB[6], PB[7]])
                heads = [(Qr.next(), Kr.next()) for _ in range(16)]

                def load_head(h):
                    Qh, Kh = heads[h]
                    dma("sp", Qh.ap, QD_d[h], Qh, [], [Qh])
                    dma("sp", Kh.ap, KD_d[h], Kh, [], [Kh])
                items = [(h, i) for h in range(16) for i in range(NT)]
                ctx = []
                for (h, i) in items:
                    nbank = (i + 4) // 4
                    pS = pSa.next() if nbank <= 2 else PB[0:4]
                    ctx.append(dict(pS=pS, Sd=Sdr.next(), st=str_.next(), Pt=Pr.next(), PT=PTr.next(), po=pO.next()))
                load_head(0)

                def geom(i):
                    nkb = i + 1
                    nbank = (nkb + 3) // 4
                    widths = [min(512, nkb * 128 - bk * 512) for bk in range(nbank)]
                    return nkb, nbank, widths

                def sA(it):
                    h, i = items[it]
                    c = ctx[it]
                    if i == 0 and h + 1 < 16:
                        load_head(h + 1)
                    Qh, Kh = heads[h]
                    nkb, nbank, widths = geom(i)
                    pS, Sd, st, Pt = c["pS"], c["Sd"], c["st"], c["Pt"]
                    for bk in range(nbank):
                        mm(pS[bk].ap[:, 0:widths[bk]], Qh.ap[:, i * 128:(i + 1) * 128], Kh.ap[:, bk * 512:bk * 512 + widths[bk]],
                           True, True, [Qh, Kh], [pS[bk]])
                    bd = nbank - 1
                    dc = widths[bd] - 128
                    vop("dve", "tensor_tensor", [pS[bd], McD], [Sd], out=Sd.ap, in0=pS[bd].ap[:, dc:dc + 128], in1=McD.ap, op=ALU.add)
                    vop("dve", "reduce_max", [Sd], [st], out=st.ap[:, 0:1], in_=Sd.ap, axis=AX.X)
                    ncol = 1
                    for bk in range(nbank):
                        wv = widths[bk] - (128 if bk == bd else 0)
                        if wv > 0:
                            vop("dve", "reduce_max", [pS[bk]], [st], out=st.ap[:, ncol:ncol + 1], in_=pS[bk].ap[:, 0:wv], axis=AX.X)
                            ncol += 1
                    if ncol > 1:
                        vop("dve", "reduce_max", [st], [st], out=st.ap[:, 5:6], in_=st.ap[:, 0:ncol], axis=AX.X)
                        mxc = st.ap[:, 5:6]
                    else:
                        mxc = st.ap[:, 0:1]
                    vop("dve", "tensor_single_scalar", [st], [st], out=st.ap[:, 6:7], in_=mxc, scalar=-sc, op=ALU.mult)
                    act(Pt.ap[:, i * 128:(i + 1) * 128], Sd.ap, AF.Exp, [Sd, st], [Pt, st], bias=st.ap[:, 6:7], scale=sc, accum=st.ap[:, 8:9])
                    ncol = 1
                    for bk in range(nbank):
                        wv = widths[bk] - (128 if bk == bd else 0)
                        if wv > 0:
                            act(Pt.ap[:, bk * 512:bk * 512 + wv], pS[bk].ap[:, 0:wv], AF.Exp, [pS[bk], st], [Pt, st],
                                bias=st.ap[:, 6:7], scale=sc, accum=st.ap[:, 8 + ncol:9 + ncol])
                            ncol += 1
                    c["ncol"] = ncol

                def sB(it):
                    h, i = items[it]
                    c = ctx[it]
                    nkb, nbank, widths = geom(i)
                    Pt, PT = c["Pt"], c["PT"]
                    for k0 in range(0, nkb, 8):
                        kn = min(8, nkb - k0)
                        pt = pT.next()
                        ptv = pt.ap.bitcast(BF16)
                        for jj in range(kn):
                            kb = k0 + jj
                            s.op("pe", (lambda ptv=ptv, jj=jj, Pt=Pt, kb=kb: (lambda e: e.transpose(ptv[:, jj * 128:(jj + 1) * 128], Pt.ap[:, kb * 128:(kb + 1) * 128], ident.ap)))(),
                                 rs([Pt, ident]), rs([pt]))
                        cp("act" if (k0 // 8) % 2 == 0 else "dve", PT.ap[:, k0 * 128:(k0 + kn) * 128], ptv[:, 0:kn * 128], [pt], [PT])

                def sC(it):
                    h, i = items[it]
                    c = ctx[it]
                    nkb, nbank, widths = geom(i)
                    PT, po, st = c["PT"], c["po"], c["st"]
                    ncol = c["ncol"]
                    for kb in range(nkb):
                        mm(po.ap[:, 0:64], PT.ap[:, kb * 128:(kb + 1) * 128], VDp.ap[:, kb, h * 64:(h + 1) * 64], kb == 0, kb == nkb - 1, [PT, VDp], [po])
                    if ncol > 1:
                        vop("dve", "reduce_sum", [st], [st], out=st.ap[:, 7:8], in_=st.ap[:, 8:8 + ncol], axis=AX.X)
                        den = st.ap[:, 7:8]
                    else:
                        den = st.ap[:, 8:9]
                    vop("dve", "reciprocal", [st], [st], out=st.ap[:, 14:15], in_=den)
                    vop("dve", "tensor_scalar", [po, st], [OD], out=OD.ap[:, i, h * 64:(h + 1) * 64], in0=po.ap[:, 0:64],
                        scalar1=st.ap[:, 14:15], scalar2=None, op0=ALU.mult)

                run_pipeline(len(items), [sA, sB, sC])
                for n in range(NT):
                    dma("sp", Y1_d[n * 128:(n + 1) * 128, 1024:2048], OD.ap[:, n, :], OD, [OD], [])

            if "skipC" not in dbg:
                attn_C()
                phase_end()
            if "skipD" not in dbg:
                attn_D()
                phase_end()
            out_proj_ln(Y1_d, 16, o_wout, xs2, ln1_g[1:2, :], ln1_b[1:2, :], xs1)
            phase_end()
            bg_flush()
            mlp(1, xs1, ln2_g[1:2, :], ln2_b[1:2, :], out_d)

        s.barrier(final=True)
        s.emit()
    return nc


_NC_CACHE = {}


def kernel(**inputs):
    B = inputs["x"].shape[0]
    if "nc" not in _NC_CACHE:
        _NC_CACHE["nc"] = build()
    nc = _NC_CACHE["nc"]
    f = lambda a: np.ascontiguousarray(np.asarray(a, dtype=np.float32))
    shared = {
        "even_w_in": f(inputs["even_w_in"][0]),
        "even_sinks": f(inputs["even_sinks"][0]).reshape(1, 16),
        "even_w_out": f(inputs["even_w_out"][0]),
        "odd_w_in": f(inputs["odd_w_in"][0]),
        "odd_q_norm_g": f(inputs["odd_q_norm_g"][0]).reshape(1, 512),
        "odd_kv_norm_g": f(inputs["odd_kv_norm_g"][0]).reshape(1, 256),
        "odd_w_uq": f(inputs["odd_w_uq"][0]),
        "odd_w_ukv": f(inputs["odd_w_ukv"][0]),
        "odd_w_out": f(inputs["odd_w_out"][0]),
        "ln1_g": f(inputs["ln1_g"]), "ln1_b": f(inputs["ln1_b"]),
        "ln2_g": f(inputs["ln2_g"]), "ln2_b": f(inputs["ln2_b"]),
        "mlp_w1": f(inputs["mlp_w1"]), "mlp_w2": f(inputs["mlp_w2"]),
    }
    x = f(inputs["x"])
    in_maps = [dict(shared, x=x[b]) for b in range(B)]
    res = run_bass_kernel_spmd(nc, in_maps, core_ids=list(range(B)))
    return np.stack([r["out"] for r in res.results], axis=0)
```

```python
import contextlib
import math
import numpy as np
import concourse.bass as bass
import concourse.mybir as mybir
from concourse.bass_utils import run_bass_kernel_spmd

F32 = mybir.dt.float32
BF16 = mybir.dt.bfloat16
I32 = mybir.dt.int32
AF = mybir.ActivationFunctionType
ALU = mybir.AluOpType
AX = mybir.AxisListType

S = 2048
D = 2048
NT = 16
DFF = 8192
ALPHA = 4.0 ** 0.25
LN_EPS = 1e-5
RMS_EPS = 1e-6
BIG = 1.0e9


class Res:
    __slots__ = ("name", "w", "r")

    def __init__(self, name=""):
        self.name = name
        self.w = None
        self.r = {}


class Buf:
    __slots__ = ("ap", "res", "sem")

    def __init__(self, ap, res, sem=None):
        self.ap = ap
        self.res = res
        self.sem = sem


class Ring:
    def __init__(self, items):
        self.items = items
        self.i = 0

    def next(self):
        it = self.items[self.i % len(self.items)]
        self.i += 1
        return it


class Sched:
    ENG = ("pe", "act", "dve", "pool", "sp")

    def __init__(self, nc, stack):
        self.nc = nc
        self.stack = stack
        self.prog = {e: [] for e in self.ENG}
        self.sem = {}
        self.cnt = {}
        self.known = {e: {} for e in self.ENG}
        self.free_dsems = []
        self.used_dsems = []
        self.ndsem = 0
        for e in self.ENG:
            self.newsem("E_" + e)

    def newsem(self, name):
        self.sem[name] = self.stack.enter_context(self.nc.semaphore(name))
        self.cnt[name] = 0
        return name

    def dsem(self, kind="H"):
        fl = [x for x in self.free_dsems if x[0] == kind]
        if fl:
            n = fl[-1]
            self.free_dsems.remove(n)
        else:
            n = self.newsem("%s%d" % (kind, self.ndsem))
            self.ndsem += 1
        self.used_dsems.append(n)
        return n

    def _deps(self, eng, reads, writes):
        need = {}

        def add(ev):
            if ev is None:
                return
            sm, v = ev
            if need.get(sm, 0) < v:
                need[sm] = v
        for r in reads:
            add(r.w)
        for w in writes:
            add(w.w)
            for sm, v in w.r.items():
                add((sm, v))
        kn = self.known[eng]
        for sm, v in need.items():
            if eng == "pe" and sm == "E_pe":
                continue
            if kn.get(sm, 0) < v:
                kn[sm] = v
                self.prog[eng].append(("wait", sm, v))

    def _commit(self, ev, reads, writes):
        sm, v = ev
        for r in reads:
            if r.r.get(sm, 0) < v:
                r.r[sm] = v
        for w in writes:
            w.w = ev
            w.r = {}

    def op(self, eng, fn, reads=(), writes=()):
        self._deps(eng, reads, writes)
        sm = "E_" + eng
        self.cnt[sm] += 1
        ev = (sm, self.cnt[sm])
        self.prog[eng].append(("op", fn, sm, 1))
        self._commit(ev, reads, writes)

    def dma(self, eng, out, in_, sem, reads=(), writes=()):
        self._deps(eng, reads, writes)
        self.cnt[sem] += 16
        ev = (sem, self.cnt[sem])
        self.prog[eng].append(("op", lambda e: e.dma_start(out=out, in_=in_), sem, 16))
        self._commit(ev, reads, writes)

    def barrier(self, final=False):
        for e in self.ENG:
            kn = self.known[e]
            for sm, v in self.cnt.items():
                if sm.startswith("WC") and not final:
                    continue
                if v > 0 and kn.get(sm, 0) < v:
                    kn[sm] = v
                    self.prog[e].append(("wait", sm, v))
        self.free_dsems.extend(self.used_dsems)
        self.used_dsems = []

    def emit(self):
        nc = self.nc

        def replay(name):
            def f(eng):
                for it in self.prog[name]:
                    if it[0] == "wait":
                        eng.wait_ge(self.sem[it[1]], it[2])
                    else:
                        it[1](eng).then_inc(self.sem[it[2]], it[3])
            return f

        with nc.Block() as block:
            block.tensor(replay("pe"))
            block.scalar(replay("act"))
            block.vector(replay("dve"))
            block.gpsimd(replay("pool"))
            block.sync(replay("sp"))


def alibi(n):
    return [2.0 ** (-8.0 * (i + 1) / n) for i in range(n)]


def build(dbg=()):
    nc = bass.Bass("TRN2", target_bir_lowering=False)

    def din(name, shape):
        return nc.dram_tensor(name, list(shape), F32, kind="ExternalInput").ap()

    x_in = din("x", [S, D])
    e_win = din("even_w_in", [D, 5888])
    e_sinks = din("even_sinks", [1, 16])
    e_wout = din("even_w_out", [1536, D])
    o_win = din("odd_w_in", [D, 3872])
    o_qg = din("odd_q_norm_g", [1, 512])
    o_kvg = din("odd_kv_norm_g", [1, 256])
    o_wuq = din("odd_w_uq", [512, 1536])
    o_wukv = din("odd_w_ukv", [256, 2048])
    o_wout = din("odd_w_out", [D, D])
    ln1_g = din("ln1_g", [2, D])
    ln1_b = din("ln1_b", [2, D])
    ln2_g = din("ln2_g", [2, D])
    ln2_b = din("ln2_b", [2, D])
    w1_in = din("mlp_w1", [2, D, DFF])
    w2_in = din("mlp_w2", [2, DFF, D])
    out_d = nc.dram_tensor("out", [S, D], F32, kind="ExternalOutput").ap()

    def dscr(name, shape, dt):
        kind = "ExternalOutput" if name in dbg else "Internal"
        return nc.dram_tensor(name, list(shape), dt, kind=kind).ap()

    w1b = dscr("w1b", [2, D, DFF], BF16)
    w2b = dscr("w2b", [2, DFF, D], BF16)
    xs1 = dscr("xs1", [S, D], F32)
    xs2 = dscr("xs2", [S, D], F32)
    QA_d = dscr("QA_d", [1024, S], BF16)
    KA_d = dscr("KA_d", [128, S], BF16)
    VA_d = dscr("VA_d", [S, 128], BF16)
    QB_d = [dscr("QB%d_d" % g, [512, S], BF16) for g in range(3)]
    KB_d = [dscr("KB%d_d" % g, [512, S], BF16) for g in range(3)]
    VB_d = [dscr("VB%d_d" % g, [S, 512], BF16) for g in range(3)]
    OB_d = [dscr("OB%d_d" % g, [S, 512], F32) for g in range(3)]
    LSE_d = [dscr("LSE%d_d" % g, [S, 8], F32) for g in range(3)]
    Y0_d = dscr("Y0_d", [S, 1536], BF16)
    QC_d = dscr("QC_d", [1024, S], BF16)
    KC_d = dscr("KC_d", [1024, S], BF16)
    VC_d = dscr("VC_d", [S, 1024], BF16)
    CQ_d = dscr("CQ_d", [S, 768], F32)
    QD_d = dscr("QD_d", [16, 96, S], BF16)
    KD_d = dscr("KD_d", [16, 96, S], BF16)
    VD_d = dscr("VD_d", [S, 1024], BF16)
    Y1_d = dscr("Y1_d", [S, 2048], BF16)

    with contextlib.ExitStack() as stack:
        s = Sched(nc, stack)
        ARENA_ELEMS = 103 * 1024
        arena = nc.alloc_sbuf_tensor("arena", [128, ARENA_ELEMS], BF16)
        aoff = [0]
        persist_end = [0]

        def T(shape, dt, dma=False, name=""):
            esz = 2 if dt == BF16 else 4
            nel = int(np.prod(shape[1:]))
            nb16 = (nel * esz + 63) // 64 * 32
            assert aoff[0] + nb16 <= ARENA_ELEMS, "arena overflow %s %d" % (name, aoff[0] + nb16)
            v = arena[:, aoff[0]:aoff[0] + nel * esz // 2]
            aoff[0] += nb16
            if dt != BF16:
                v = v.bitcast(dt)
            if len(shape) == 3:
                v = v.rearrange("p (a b) -> p a b", a=shape[1])
            elif len(shape) == 4:
                v = v.rearrange("p (a b c) -> p a b c", a=shape[1], b=shape[2])
            if shape[0] != 128:
                v = v[0:shape[0]]
            return Buf(v, Res(name), s.dsem("W" if dma == "sw" else "H") if dma else None)

        def TR(n, shape, dt, dma=False, name=""):
            return Ring([T(shape, dt, dma, name) for _ in range(n)])

        def phase_end():
            s.barrier()
            aoff[0] = persist_end[0]

        PB = [Buf(nc.alloc_psum_tensor("pb%d" % i, [128, 512], F32)[:], Res("pb%d" % i)) for i in range(8)]

        def rs(bufs):
            return [b.res for b in bufs]

        def mm(out, lhsT, rhs, start, stop, rd, wr):
            s.op("pe", lambda e: e.matmul(out, lhsT, rhs, start=start, stop=stop), rs(rd), rs(wr))

        def act(out, in_, func, rd, wr, bias=None, scale=None, accum=None, eng="act"):
            kw = {}
            if bias is not None:
                kw["bias"] = bias
            if scale is not None:
                kw["scale"] = scale
            if accum is not None:
                kw["accum_out"] = accum
            s.op(eng, lambda e: e.activation(out=out, in_=in_, func=func, **kw), rs(rd), rs(wr))

        def vop(eng, meth, rd, wr, **kw):
            s.op(eng, lambda e: getattr(e, meth)(**kw), rs(rd), rs(wr))

        def cp(eng, out, in_, rd, wr):
            if eng == "act":
                s.op("act", lambda e: e.copy(out=out, in_=in_), rs(rd), rs(wr))
            else:
                s.op(eng, lambda e: e.tensor_copy(out=out, in_=in_), rs(rd), rs(wr))

        def dma(eng, out, in_, buf, rd=(), wr=()):
            assert (eng == "pool") == (buf.sem[0] == "W"), (eng, buf.sem)
            s.dma(eng, out, in_, buf.sem, rs(rd), rs(wr))

        def run_pipeline(N, stages):
            ns = len(stages)
            for t in range(N + ns - 1):
                for si in range(ns):
                    it = t - si
                    if 0 <= it < N:
                        stages[si](it)

        ident = T([128, 128], BF16, name="ident")
        Ustr = T([128, 128], F32, name="Ustr")
        ones = T([128, 128], F32, name="ones")
        mC01 = T([128, 128], F32, name="mC01")
        mC01b = T([128, 128], BF16, name="mC01b")
        UstrB = T([128, 128], BF16, name="UstrB")
        PenC = T([128, 128], F32, name="PenC")
        McD = T([128, 128], F32, name="McD")
        RA = T([128, 256], F32, name="RA")
        RB = T([128, 256], F32, name="RB")
        sinkt = T([128, 16], F32, dma=True, name="sink")
        sink8 = T([128, 16], F32, name="sink8")
        persist_end[0] = aoff[0]
        tmpi = T([128, 256], I32, name="tmpi")
        tmpf = T([128, 256], F32, name="tmpf")
        tmpg = T([128, 256], F32, name="tmpg")
        tmph = T([128, 256], F32, name="tmph")
        vop("pool", "iota", [], [tmpi], out=tmpi.ap[:, 0:128], pattern=[[-1, 128]], base=0, channel_multiplier=1)
        cp("dve", tmpf.ap[:, 0:128], tmpi.ap[:, 0:128], [tmpi], [tmpf])
        vop("dve", "tensor_single_scalar", [tmpf], [ident], out=ident.ap, in_=tmpf.ap[:, 0:128], scalar=0.0, op=ALU.is_equal)
        vop("dve", "tensor_single_scalar", [tmpf], [Ustr], out=Ustr.ap, in_=tmpf.ap[:, 0:128], scalar=0.0, op=ALU.is_gt)
        vop("dve", "tensor_single_scalar", [tmpf], [mC01], out=mC01.ap, in_=tmpf.ap[:, 0:128], scalar=0.0, op=ALU.is_lt)
        cp("dve", mC01b.ap, mC01.ap, [mC01], [mC01b])
        cp("dve", UstrB.ap, Ustr.ap, [Ustr], [UstrB])
        vop("dve", "tensor_scalar", [Ustr], [PenC], out=PenC.ap, in0=Ustr.ap, scalar1=1.0, scalar2=-BIG, op0=ALU.subtract, op1=ALU.mult)
        vop("dve", "memset", [], [ones], ap=ones.ap, constant=1.0)
        vop("dve", "tensor_scalar", [tmpf], [McD], out=McD.ap, in0=tmpf.ap[:, 0:128], scalar1=0.0, scalar2=1.0,
            op0=ALU.is_ge, op1=ALU.subtract)
        vop("dve", "tensor_single_scalar", [McD], [McD], out=McD.ap, in_=McD.ap, scalar=BIG, op=ALU.mult)
        vop("pool", "iota", [tmpf], [tmpi], out=tmpi.ap, pattern=[[-1, 256]], base=128, channel_multiplier=1)
        cp("dve", tmpf.ap, tmpi.ap, [tmpi], [tmpf])
        for Rt, nb in ((RA, 127.0), (RB, 128.0)):
            vop("dve", "tensor_single_scalar", [tmpf], [tmpg], out=tmpg.ap, in_=tmpf.ap, scalar=0.0, op=ALU.is_ge)
            vop("dve", "tensor_single_scalar", [tmpf], [tmph], out=tmph.ap, in_=tmpf.ap, scalar=nb, op=ALU.is_le)
            vop("dve", "tensor_tensor", [tmpg, tmph], [tmpg], out=tmpg.ap, in0=tmpg.ap, in1=tmph.ap, op=ALU.mult)
            vop("dve", "tensor_tensor", [tmpg, tmpf], [tmph], out=tmph.ap, in0=tmpg.ap, in1=tmpf.ap, op=ALU.mult)
            vop("dve", "tensor_scalar", [tmpg], [tmpg], out=tmpg.ap, in0=tmpg.ap, scalar1=1.0, scalar2=BIG,
                op0=ALU.subtract, op1=ALU.mult)
            vop("dve", "tensor_tensor", [tmpg, tmph], [Rt], out=Rt.ap, in0=tmpg.ap, in1=tmph.ap, op=ALU.subtract)
        dma("sp", sinkt.ap, e_sinks.partition_broadcast(128), sinkt, [], [sinkt])
        vop("dve", "tensor_single_scalar", [sinkt], [sink8], out=sink8.ap, in_=sinkt.ap, scalar=8.0, op=ALU.mult)

        wcast = [Buf(None, Res("wc%d" % l), s.newsem("WC%d" % l)) for l in range(2)]
        bgq = []

        def bg_fill(l):
            jobs = [(w1b[l, r0:r0 + 128, :], w1_in[l, r0:r0 + 128, :]) for r0 in range(0, D, 128)]
            jobs += [(w2b[l, r0:r0 + 512, :], w2_in[l, r0:r0 + 512, :]) for r0 in range(0, DFF, 512)]
            for ji, (o_, i_) in enumerate(jobs):
                bgq.append((o_, i_, l, ji == len(jobs) - 1))

        def bg_tick(k=1):
            for _ in range(k):
                if not bgq:
                    return
                o_, i_, l, last = bgq.pop(0)
                dma("pool", o_, i_, wcast[l], [], [wcast[l]] if last else [])

        def bg_flush():
            bg_tick(len(bgq))
        phase_end()

        evq = [0]

        def ev_eng():
            evq[0] += 1
            return "act" if evq[0] % 2 else "dve"

        def load_T(src, ncol, dst, is_f32, keep=None):
            kc = ncol // 128
            ld = TR(2, [128, ncol], F32 if is_f32 else BF16, dma=True, name="ldT")
            cb = TR(2, [128, ncol], BF16, name="cbT") if is_f32 else None
            pbr = Ring([PB[6], PB[7]])
            for t in range(NT):
                lt = ld.next()
                dma("sp", lt.ap, src[t * 128:(t + 1) * 128, :], lt, [], [lt])
                if is_f32:
                    ct = cb.next()
                    cp("pool", ct.ap, lt.ap, [lt], [ct])
                else:
                    ct = lt
                for k0 in range(0, kc, 8):
                    kn = min(8, kc - k0)
                    pb = pbr.next()
                    pv = pb.ap.bitcast(BF16)
                    for j in range(kn):
                        k = k0 + j
                        s.op("pe", (lambda pv=pv, j=j, ct=ct, k=k: (lambda e: e.transpose(pv[:, j * 128:(j + 1) * 128], ct.ap[:, k * 128:(k + 1) * 128], ident.ap)))(),
                             rs([ct, ident]), rs([pb]))
                    cp(ev_eng(), dst.ap[:, k0:k0 + kn, t * 128:(t + 1) * 128],
                       pv[:, 0:kn * 128].rearrange("p (k t) -> p k t", k=kn), [pb], [dst])

        def wload(dst, wsrc, kc, ncol):
            dma("pool", dst.ap[:, 0:kc, 0:ncol], wsrc.rearrange("(k p) e -> p k e", p=128), dst, [], [dst])

        def proj_F(xT, kc, wsrc, ncol, dst, dil, wring, stg, oscale=None):
            wt = wring.next()
            wload(wt, wsrc, kc, ncol)
            pbr = Ring(PB[0:6])
            for c in range(ncol // 128):
                st = stg.next()
                for tg in range(4):
                    pb = pbr.next()
                    for k in range(kc):
                        mm(pb.ap, wt.ap[:, k, c * 128:(c + 1) * 128], xT.ap[:, k, tg * 512:(tg + 1) * 512],
                           k == 0, k == kc - 1, [wt, xT], [pb])
                    if oscale is not None:
                        if ev_eng() == "act":
                            s.op("act", (lambda o=st.ap[:, tg * 512:(tg + 1) * 512], i=pb.ap: (lambda e: e.mul(out=o, in_=i, mul=oscale)))(), rs([pb]), rs([st]))
                        else:
                            vop("dve", "tensor_single_scalar", [pb], [st], out=st.ap[:, tg * 512:(tg + 1) * 512], in_=pb.ap, scalar=oscale, op=ALU.mult)
                    elif dil == 1:
                        cp(ev_eng(), st.ap[:, tg * 512:(tg + 1) * 512], pb.ap, [pb], [st])
                    else:
                        na = 512 // dil
                        cp(ev_eng(), st.ap.rearrange("p (r a) -> p a r", r=dil)[:, tg * na:(tg + 1) * na, :],
                           pb.ap.rearrange("p (a r) -> p a r", r=dil), [pb], [st])
                dma("sp", dst[c * 128:(c + 1) * 128, :], st.ap, st, [st], [])

        def proj_T(xT, kc, wsrc, ncol, dst, dst_dt, wring, stg):
            wt = wring.next()
            wload(wt, wsrc, kc, ncol)
            pbr = Ring(PB[0:6])
            for t in range(NT):
                pb = pbr.next()
                for k in range(kc):
                    mm(pb.ap[:, 0:ncol], xT.ap[:, k, t * 128:(t + 1) * 128], wt.ap[:, k, 0:ncol],
                       k == 0, k == kc - 1, [wt, xT], [pb])
                st = stg.next()
                cp(ev_eng(), st.ap[:, 0:ncol], pb.ap[:, 0:ncol], [pb], [st])
                dma("sp", dst[t * 128:(t + 1) * 128, :], st.ap[:, 0:ncol], st, [st], [])

        def layer_norm_tile(z, gt, bt, stat):
            st6 = stat.ap[:, 0:24].rearrange("p (c s) -> p c s", c=4)
            for c in range(4):
                vop("dve", "bn_stats", [z], [stat], out=st6[:, c, :], in_=z.ap[:, c * 512:(c + 1) * 512])
            mv = stat.ap[:, 24:26]
            vop("dve", "bn_aggr", [stat], [stat], out=mv, in_=st6)
            vop("dve", "tensor_single_scalar", [stat], [stat], out=stat.ap[:, 26:27], in_=stat.ap[:, 25:26], scalar=LN_EPS, op=ALU.add)
            act(stat.ap[:, 27:28], stat.ap[:, 26:27], AF.Ln, [stat], [stat])
            act(stat.ap[:, 28:29], stat.ap[:, 27:28], AF.Exp, [stat], [stat], scale=-0.5)
            vop("dve", "tensor_scalar", [stat], [stat], out=stat.ap[:, 29:30], in0=stat.ap[:, 24:25], scalar1=stat.ap[:, 28:29], scalar2=-1.0,
                op0=ALU.mult, op1=ALU.mult)
            act(z.ap, z.ap, AF.Identity, [z, stat], [z], bias=stat.ap[:, 29:30], scale=stat.ap[:, 28:29])
            vop("dve", "tensor_tensor", [z, gt], [z], out=z.ap, in0=z.ap, in1=gt.ap, op=ALU.mult)
            vop("dve", "tensor_tensor", [z, bt], [z], out=z.ap, in0=z.ap, in1=bt.ap, op=ALU.add)

        def load_gb(g_src, b_src):
            gt = T([128, D], F32, dma=True, name="gam")
            bt = T([128, D], F32, dma=True, name="bet")
            dma("sp", gt.ap, g_src.partition_broadcast(128), gt, [], [gt])
            dma("sp", bt.ap, b_src.partition_broadcast(128), bt, [], [bt])
            return gt, bt

        def out_proj_ln(Yd, kc, wsrc, xres, g_src, b_src, dst, ncl=None, extra=False):
            ncl = ncl or kc * 128
            wt = T([128, kc, D], BF16, dma="sw", name="wout")
            for k0 in range(0, kc, 4):
                dma("pool", wt.ap[:, k0:k0 + 4, :], wsrc[k0 * 128:(k0 + 4) * 128, :].rearrange("(k p) e -> p k e", p=128), wt, [], [wt])
            gt, bt = load_gb(g_src, b_src)
            yl = TR(4, [128, kc * 128], BF16, dma=True, name="yl")
            yT = TR(3, [128, kc, 128], BF16, name="yT")
            zr = TR(4, [128, D], F32, dma=True, name="z")
            stat = TR(2, [128, 32], F32, name="stat")
            pbt = Ring([PB[4], PB[5], PB[6], PB[7]])
            if extra:
                ol = [TR(4, [128, 512], F32, dma=True, name="o%d" % g) for g in range(3)]
                ll = TR(4, [128, 3, 8], F32, dma=True, name="l")
                wk = TR(3, [128, 3, 8], F32, name="wk")
                sm = TR(3, [128, 16], F32, name="sm")

            def loads(t):
                y = yl.next()
                dma("sp", y.ap[:, 0:ncl], Yd[t * 128:(t + 1) * 128, 0:ncl], y, [], [y])
                og = lt = None
                if extra:
                    og = [ol[g].next() for g in range(3)]
                    lt = ll.next()
                    for g in range(3):
                        dma("sp", og[g].ap, OB_d[g][t * 128:(t + 1) * 128, :], og[g], [], [og[g]])
                        dma("sp", lt.ap[:, g, :], LSE_d[g][t * 128:(t + 1) * 128, :], lt, [], [lt])
                z = zr.next()
                dma("sp", z.ap, xres[t * 128:(t + 1) * 128, :], z, [], [z])
                return (t, y, og, lt, z)

            def mix_tile(t, y, og, lt):
                m = sm.next()
                vop("dve", "tensor_tensor", [lt], [m], out=m.ap[:, 0:8], in0=lt.ap[:, 0, :], in1=lt.ap[:, 1, :], op=ALU.max)
                vop("dve", "tensor_tensor", [lt, m], [m], out=m.ap[:, 0:8], in0=m.ap[:, 0:8], in1=lt.ap[:, 2, :], op=ALU.max)
                w = wk.next()
                vop("dve", "tensor_tensor", [lt, m], [w], out=w.ap, in0=lt.ap, in1=m.ap[:, 0:8].unsqueeze(1).broadcast_to([128, 3, 8]), op=ALU.subtract)
                act(w.ap, w.ap, AF.Exp, [w], [w])
                vop("dve", "tensor_tensor", [w], [m], out=m.ap[:, 8:16], in0=w.ap[:, 0, :], in1=w.ap[:, 1, :], op=ALU.add)
                vop("dve", "tensor_tensor", [w, m], [m], out=m.ap[:, 8:16], in0=m.ap[:, 8:16], in1=w.ap[:, 2, :], op=ALU.add)
                vop("dve", "reciprocal", [m], [m], out=m.ap[:, 8:16], in_=m.ap[:, 8:16])
                vop("dve", "tensor_tensor", [w, m], [w], out=w.ap, in0=w.ap, in1=m.ap[:, 8:16].unsqueeze(1).broadcast_to([128, 3, 8]), op=ALU.mult)
                for g in range(3):
                    eng = "dve" if g == 1 else "pool"
                    vop(eng, "tensor_tensor", [og[g], w], [og[g]], out=og[g].ap.rearrange("p (h d) -> p h d", h=8),
                        in0=og[g].ap.rearrange("p (h d) -> p h d", h=8), in1=w.ap[:, g, :].unsqueeze(2).broadcast_to([128, 8, 64]), op=ALU.mult)
                vop("pool", "tensor_tensor", [og[0], og[2]], [og[0]], out=og[0].ap, in0=og[0].ap, in1=og[2].ap, op=ALU.add)
                vop("dve", "tensor_tensor", [og[0], og[1]], [y], out=y.ap[:, 1024:1536], in0=og[0].ap, in1=og[1].ap, op=ALU.add)

            def prep(ld):
                t, y, og, lt, z = ld
                if extra:
                    mix_tile(t, y, og, lt)
                yt = yT.next()
                for k0 in range(0, kc, 8):
                    kn = min(8, kc - k0)
                    pb = pbt.next()
                    pv = pb.ap.bitcast(BF16)
                    for j in range(kn):
                        k = k0 + j
                        s.op("pe", (lambda pv=pv, j=j, y=y, k=k: (lambda e: e.transpose(pv[:, j * 128:(j + 1) * 128], y.ap[:, k * 128:(k + 1) * 128], ident.ap)))(),
                             rs([y, ident]), rs([pb]))
                    cp("act", yt.ap[:, k0:k0 + kn, :], pv[:, 0:kn * 128].rearrange("p (k t) -> p k t", k=kn), [pb], [yt])
                return yt, z

            lds = [loads(0), loads(1), loads(2)]
            pend = [prep(lds.pop(0)), prep(lds.pop(0))]
            for t in range(NT):
                if t + 3 < NT:
                    lds.append(loads(t + 3))
                if t + 2 < NT:
                    pend.append(prep(lds.pop(0)))
                yt, z = pend.pop(0)
                for dt in range(4):
                    pb = PB[dt]
                    for k in range(kc):
                        mm(pb.ap, yt.ap[:, k, :], wt.ap[:, k, dt * 512:(dt + 1) * 512], k == 0, k == kc - 1, [yt, wt], [pb])
                    vop("dve", "scalar_tensor_tensor", [z, pb], [z], out=z.ap[:, dt * 512:(dt + 1) * 512],
                        in0=z.ap[:, dt * 512:(dt + 1) * 512], scalar=ALPHA, in1=pb.ap, op0=ALU.mult, op1=ALU.add)
                layer_norm_tile(z, gt, bt, stat.next())
                dma("sp", dst[t * 128:(t + 1) * 128, :], z.ap, z, [z], [])

        def mlp(l, xsrc, g_src, b_src, dst):
            gt, bt = load_gb(g_src, b_src)
            zb = T([128, 4, D], F32, name="zb")
            zts = [Buf(zb.ap[:, tt, :], Res("z%d" % tt), s.dsem("H")) for tt in range(4)]
            xbs = [T([128, D], BF16, dma="sw", name="xb%d" % i) for i in range(2)]
            xTr = [T([128, 16, 512], BF16, name="xTm%d" % i) for i in range(2)]
            hT = T([128, 64, 512], BF16, name="hT")
            w1r = TR(2, [128, 16, 256], BF16, dma=True, name="w1")
            w2r = TR(3, [128, 8, 512], BF16, dma=True, name="w2")
            hr = TR(2, [128, 512], F32, name="hrelu")
            stat = TR(2, [128, 32], F32, name="stat")
            pbt = Ring([PB[6], PB[7]])
            pbh = Ring(PB[0:6])

            def issue_xload(G, tts):
                for tt in tts:
                    t = G * 4 + tt
                    xb = xbs[tt % 2]
                    dma("pool", xb.ap, xsrc[t * 128:(t + 1) * 128, :], xb, [], [xb])

            def prefetch_T(G, tts):
                xT = xTr[G % 2]
                for tt in tts:
                    xb = xbs[tt % 2]
                    for k0 in (0, 8):
                        pb = pbt.next()
                        pv = pb.ap.bitcast(BF16)
                        for j in range(8):
                            k = k0 + j
                            s.op("pe", (lambda pv=pv, j=j, xb=xb, k=k: (lambda e: e.transpose(pv[:, j * 128:(j + 1) * 128], xb.ap[:, k * 128:(k + 1) * 128], ident.ap)))(),
                                 rs([xb, ident]), rs([pb]))
                        cp("act", xT.ap[:, k0:k0 + 8, tt * 128:(tt + 1) * 128],
                           pv.rearrange("p (k t) -> p k t", k=8), [pb], [xT])

            issue_xload(0, [0, 1])
            prefetch_T(0, [0, 1])
            issue_xload(0, [2, 3])
            prefetch_T(0, [2, 3])
            for G in range(4):
                xT = xTr[G % 2]
                for f2 in range(32):
                    w1 = w1r.next()
                    dma("sp", w1.ap, w1b[l, :, f2 * 256:(f2 + 1) * 256].rearrange("(k p) f -> p k f", p=128), w1, [wcast[l]], [w1])
                    for fi in range(2):
                        f = f2 * 2 + fi
                        pb = pbh.next()
                        for k in range(16):
                            mm(pb.ap, w1.ap[:, k, fi * 128:(fi + 1) * 128], xT.ap[:, k, :], k == 0, k == 15, [w1, xT], [pb])
                        h = hr.next()
                        act(h.ap, pb.ap, AF.Relu, [pb], [h])
                        vop("pool" if f % 2 else "dve", "tensor_tensor", [h], [hT], out=hT.ap[:, f, :], in0=h.ap, in1=h.ap, op=ALU.mult)
                    if G > 0:
                        for tt in range(4):
                            if f2 == 1 + 4 * tt:
                                layer_norm_tile(zts[tt], gt, bt, stat.next())
                            if f2 == 4 + 4 * tt:
                                t = (G - 1) * 4 + tt
                                dma("sp", dst[t * 128:(t + 1) * 128, :], zts[tt].ap, zts[tt], [zts[tt]], [])
                    if f2 == 22:
                        for tt in range(4):
                            t = G * 4 + tt
                            dma("sp", zts[tt].ap, xsrc[t * 128:(t + 1) * 128, :], zts[tt], [], [zts[tt]])
                    if f2 == 26 and G < 3:
                        issue_xload(G + 1, [0, 1])
                for dt in range(4):
                    if G < 3 and dt == 0:
                        prefetch_T(G + 1, [0, 1])
                        issue_xload(G + 1, [2, 3])
                    if G < 3 and dt == 1:
                        prefetch_T(G + 1, [2, 3])
                    pbs = PB[0:4] if dt % 2 == 0 else PB[4:8]
                    for f8 in range(8):
                        w2 = w2r.next()
                        dma("sp", w2.ap, w2b[l, f8 * 1024:(f8 + 1) * 1024, dt * 512:(dt + 1) * 512].rearrange("(c p) d -> p c d", p=128),
                            w2, [wcast[l]], [w2])
                        for fi in range(8):
                            f = f8 * 8 + fi
                            for tt in range(4):
                                mm(pbs[tt].ap, hT.ap[:, f, tt * 128:(tt + 1) * 128], w2.ap[:, fi, :], f == 0, f == 63, [hT, w2], [pbs[tt]])
                    for tt in range(4):
                        zt = zts[tt]
                        vop("dve", "scalar_tensor_tensor", [zt, pbs[tt]], [zt], out=zt.ap[:, dt * 512:(dt + 1) * 512],
                            in0=zt.ap[:, dt * 512:(dt + 1) * 512], scalar=ALPHA, in1=pbs[tt].ap, op0=ALU.mult, op1=ALU.add)
            for tt in range(4):
                t = 12 + tt
                layer_norm_tile(zts[tt], gt, bt, stat.next())
                dma("sp", dst[t * 128:(t + 1) * 128, :], zts[tt].ap, zts[tt], [zts[tt]], [])

        def banded(Qd, Kd, kvmap, nh, Vd, nkv, dil, Rm, cvals, use_sink, out_mode, Yd=None, OBd=None, LSEd=None):
            L = S // dil
            nbpl = L // 128
            Vp = T([128, NT, nkv * 64], BF16, dma=True, name="Vp")
            for n in range(NT):
                r = (128 * n) // L
                a0 = (128 * n) % L
                st_ = r + dil * a0
                dma("sp", Vp.ap[:, n, :], Vd[st_:st_ + dil * 127 + 1:dil, :], Vp, [], [Vp])
            if out_mode == "A":
                Oall = T([128, NT, nh * 64], BF16, dma=True, name="Oall")
            else:
                Oall = T([128, NT, nh * 64], F32, dma=True, name="Oall")
                lse = T([128, NT, nh], F32, dma=True, name="lse")
            Qr = TR(4, [64, S], BF16, dma=True, name="Qh")
            Kr = TR(4, [64, S], BF16, dma=True, name="Kh")
            Tr = TR(3, [128, 256], F32, name="T")
            Pr = TR(4, [128, 256], BF16, name="P")
            PTr = TR(4, [128, 256], BF16, name="PT")
            str_ = TR(6, [128, 8], F32, name="st")
            pS = Ring([PB[0], PB[1], PB[6]])
            pT = Ring([PB[2], PB[3]])
            pO = Ring([PB[4], PB[5], PB[7]])
            heads = []
            lastkv = -1
            Kh = None
            for h in range(nh):
                Qh = Qr.next()
                kv = kvmap(h)
                newk = kv != lastkv
                if newk:
                    Kh = Kr.next()
                    lastkv = kv
                heads.append((Qh, Kh, kv, newk))

            def load_head(h):
                Qh, Kh, kv, newk = heads[h]
                dma("sp", Qh.ap, Qd[h * 64:(h + 1) * 64, :], Qh, [], [Qh])
                if newk:
                    dma("sp", Kh.ap, Kd[kv * 64:(kv + 1) * 64, :], Kh, [], [Kh])
            items = [(h, n) for h in range(nh) for n in range(NT)]
            ctx = [dict(ps=pS.next(), Tt=Tr.next(), st=str_.next(), Pt=Pr.next(), pt=pT.next(), PT=PTr.next(), po=pO.next())
                   for _ in items]
            load_head(0)
            if nh > 1:
                load_head(1)

            def geom(n):
                hasprev = (n % nbpl) != 0
                nk = 256 if hasprev else 128
                ks = (n - 1) * 128 if hasprev else n * 128
                return hasprev, nk, ks

            def stA(it):
                h, n = items[it]
                c = ctx[it]
                if n == 0 and h + 2 < nh:
                    load_head(h + 2)
                if it % 12 == 5:
                    bg_tick()
                Qh, Kh, kv, _ = heads[h]
                hasprev, nk, ks = geom(n)
                Rv = Rm.ap[:, 0:256] if hasprev else Rm.ap[:, 128:256]
                ps, Tt, st, Pt = c["ps"], c["Tt"], c["st"], c["Pt"]
                mm(ps.ap[:, 0:nk], Qh.ap[:, n * 128:(n + 1) * 128], Kh.ap[:, ks:ks + nk], True, True, [Qh, Kh], [ps])
                vop("dve", "scalar_tensor_tensor", [ps, Rm], [Tt], out=Tt.ap[:, 0:nk], in0=Rv, scalar=cvals[h], in1=ps.ap[:, 0:nk],
                    op0=ALU.mult, op1=ALU.add)
                vop("dve", "reduce_max", [Tt], [st], out=st.ap[:, 0:1], in_=Tt.ap[:, 0:nk], axis=AX.X)
                if use_sink:
                    vop("dve", "tensor_scalar", [st, sink8], [st], out=st.ap[:, 1:2], in0=st.ap[:, 0:1], scalar1=sink8.ap[:, h:h + 1], scalar2=-0.125,
                        op0=ALU.max, op1=ALU.mult)
                else:
                    vop("dve", "tensor_single_scalar", [st], [st], out=st.ap[:, 1:2], in_=st.ap[:, 0:1], scalar=-0.125, op=ALU.mult)
                act(Pt.ap[:, 0:nk], Tt.ap[:, 0:nk], AF.Exp, [Tt, st], [Pt, st], bias=st.ap[:, 1:2], scale=0.125, accum=st.ap[:, 2:3])
                if use_sink:
                    act(st.ap[:, 3:4], st.ap[:, 1:2], AF.Exp, [st, sinkt], [st], bias=sinkt.ap[:, h:h + 1], scale=1.0)

            def stB(it):
                h, n = items[it]
                c = ctx[it]
                hasprev, nk, ks = geom(n)
                Pt, pt, PT = c["Pt"], c["pt"], c["PT"]
                ptv = pt.ap.bitcast(BF16)
                for kb in range(nk // 128):
                    s.op("pe", (lambda ptv=ptv, kb=kb, Pt=Pt: (lambda e: e.transpose(ptv[:, kb * 128:(kb + 1) * 128], Pt.ap[:, kb * 128:(kb + 1) * 128], ident.ap)))(),
                         rs([Pt, ident]), rs([pt]))
                cp("act", PT.ap[:, 0:nk], ptv[:, 0:nk], [pt], [PT])

            def stC(it):
                h, n = items[it]
                c = ctx[it]
                Qh, Kh, kv, _ = heads[h]
                hasprev, nk, ks = geom(n)
                PT, po, st = c["PT"], c["po"], c["st"]
                nkb = nk // 128
                for kb in range(nkb):
                    blk = ks // 128 + kb
                    mm(po.ap[:, 0:64], PT.ap[:, kb * 128:(kb + 1) * 128], Vp.ap[:, blk, kv * 64:(kv + 1) * 64],
                       kb == 0, kb == nkb - 1, [PT, Vp], [po])
                if use_sink:
                    vop("dve", "tensor_tensor", [st], [st], out=st.ap[:, 2:3], in0=st.ap[:, 2:3], in1=st.ap[:, 3:4], op=ALU.add)
                vop("dve", "reciprocal", [st], [st], out=st.ap[:, 4:5], in_=st.ap[:, 2:3])
                vop("dve", "tensor_scalar", [po, st], [Oall], out=Oall.ap[:, n, h * 64:(h + 1) * 64], in0=po.ap[:, 0:64],
                    scalar1=st.ap[:, 4:5], scalar2=None, op0=ALU.mult)
                if out_mode == "B":
                    act(st.ap[:, 5:6], st.ap[:, 2:3], AF.Ln, [st], [st])
                    vop("dve", "tensor_tensor", [st], [lse], out=lse.ap[:, n, h:h + 1], in0=st.ap[:, 5:6], in1=st.ap[:, 1:2], op=ALU.subtract)

            run_pipeline(len(items), [stA, stB, stC])
            if out_mode == "A":
                for n in range(NT):
                    dma("sp", Yd[n * 128:(n + 1) * 128, 0:nh * 64], Oall.ap[:, n, :], Oall, [Oall], [])
            else:
                for n in range(NT):
                    r = (128 * n) // L
                    a0 = (128 * n) % L
                    st_ = r + dil * a0
                    dma("sp", OBd[st_:st_ + dil * 127 + 1:dil, :], Oall.ap[:, n, :], Oall, [Oall], [])
                    dma("sp", LSEd[st_:st_ + dil * 127 + 1:dil, :], lse.ap[:, n, :], lse, [lse], [])

        def combine_B():
            ol = [TR(2, [128, 512], F32, dma=True, name="o%d" % g) for g in range(3)]
            ll = TR(2, [128, 3, 8], F32, dma=True, name="l")
            wk = TR(2, [128, 3, 8], F32, name="wk")
            sm = TR(2, [128, 16], F32, name="sm")
            yo = TR(2, [128, 512], BF16, dma=True, name="yo")
            for t in range(NT):
                og = [ol[g].next() for g in range(3)]
                lt = ll.next()
                for g in range(3):
                    dma("sp", og[g].ap, OB_d[g][t * 128:(t + 1) * 128, :], og[g], [], [og[g]])
                    dma("sp", lt.ap[:, g, :], LSE_d[g][t * 128:(t + 1) * 128, :], lt, [], [lt])
                m = sm.next()
                vop("dve", "tensor_tensor", [lt], [m], out=m.ap[:, 0:8], in0=lt.ap[:, 0, :], in1=lt.ap[:, 1, :], op=ALU.max)
                vop("dve", "tensor_tensor", [lt, m], [m], out=m.ap[:, 0:8], in0=m.ap[:, 0:8], in1=lt.ap[:, 2, :], op=ALU.max)
                w = wk.next()
                vop("dve", "tensor_tensor", [lt, m], [w], out=w.ap, in0=lt.ap, in1=m.ap[:, 0:8].unsqueeze(1).broadcast_to([128, 3, 8]), op=ALU.subtract)
                act(w.ap, w.ap, AF.Exp, [w], [w])
                vop("dve", "tensor_tensor", [w], [m], out=m.ap[:, 8:16], in0=w.ap[:, 0, :], in1=w.ap[:, 1, :], op=ALU.add)
                vop("dve", "tensor_tensor", [w, m], [m], out=m.ap[:, 8:16], in0=m.ap[:, 8:16], in1=w.ap[:, 2, :], op=ALU.add)
                vop("dve", "reciprocal", [m], [m], out=m.ap[:, 8:16], in_=m.ap[:, 8:16])
                vop("dve", "tensor_tensor", [w, m], [w], out=w.ap, in0=w.ap, in1=m.ap[:, 8:16].unsqueeze(1).broadcast_to([128, 3, 8]), op=ALU.mult)
                for g in range(3):
                    eng = "pool" if g == 1 else "dve"
                    vop(eng, "tensor_tensor", [og[g], w], [og[g]], out=og[g].ap.rearrange("p (h d) -> p h d", h=8),
                        in0=og[g].ap.rearrange("p (h d) -> p h d", h=8), in1=w.ap[:, g, :].unsqueeze(2).broadcast_to([128, 8, 64]), op=ALU.mult)
                vop("dve", "tensor_tensor", [og[0], og[1]], [og[0]], out=og[0].ap, in0=og[0].ap, in1=og[1].ap, op=ALU.add)
                y = yo.next()
                vop("dve", "tensor_tensor", [og[0], og[2]], [y], out=y.ap, in0=og[0].ap, in1=og[2].ap, op=ALU.add)
                dma("sp", Y0_d[t * 128:(t + 1) * 128, 1024:1536], y.ap, y, [y], [])

        xT = T([128, 16, S], BF16, name="xT")
        load_T(x_in, D, xT, True)
        wring = TR(2, [128, 16, 512], BF16, dma="sw", name="wring")
        stgF = TR(2, [128, S], BF16, dma=True, name="stgF")
        stgT = TR(2, [128, 512], BF16, dma=True, name="stgT")
        proj_F(xT, 16, e_win[:, 0:512], 512, QA_d[0:512, :], 1, wring, stgF)
        proj_F(xT, 16, e_win[:, 512:1024], 512, QA_d[512:1024, :], 1, wring, stgF)
        proj_F(xT, 16, e_win[:, 1024:1152], 128, KA_d, 1, wring, stgF)
        proj_T(xT, 16, e_win[:, 1152:1280], 128, VA_d, BF16, wring, stgT)
        for g, dil in enumerate((1, 4, 16)):
            base = 1280 + g * 1536
            proj_F(xT, 16, e_win[:, base:base + 512], 512, QB_d[g], dil, wring, stgF)
            proj_F(xT, 16, e_win[:, base + 512:base + 1024], 512, KB_d[g], dil, wring, stgF)
            proj_T(xT, 16, e_win[:, base + 1024:base + 1536], 512, VB_d[g], BF16, wring, stgT)
        phase_end()
        bg_fill(0)
        slA = alibi(16)
        banded(QA_d, KA_d, lambda h: h // 8, 16, VA_d, 2, 1, RA, [8.0 * sl for sl in slA], True, "A", Yd=Y0_d)
        phase_end()
        slB = alibi(8)
        for g, dil in enumerate((1, 4, 16)):
            banded(QB_d[g], KB_d[g], lambda h: h, 8, VB_d[g], 8, dil, RB, [8.0 * sl * dil for sl in slB], False, "B",
                   OBd=OB_d[g], LSEd=LSE_d[g])
            phase_end()
        out_proj_ln(Y0_d, 12, e_wout, x_in, ln1_g[0:1, :], ln1_b[0:1, :], xs1, ncl=1024, extra=True)
        phase_end()
        bg_flush()
        mlp(0, xs1, ln2_g[0:1, :], ln2_b[0:1, :], xs2 if "stop0" not in dbg else out_d)
        phase_end()

        if "stop0" not in dbg:

            base_persist = persist_end[0]
            bg_fill(1)
            cosT = T([128, S], F32, name="cosT")
            sinT = T([128, S], F32, name="sinT")
            persist_end[0] = aoff[0]
            pidx = T([128, 2], I32, name="pidx")
            pf = T([128, 2], F32, name="pf")
            vop("pool", "iota", [], [pidx], out=pidx.ap[:, 0:1], pattern=[[0, 1]], base=0, channel_multiplier=1)
            vop("dve", "tensor_single_scalar", [pidx], [pidx], out=pidx.ap[:, 1:2], in_=pidx.ap[:, 0:1], scalar=15, op=ALU.bitwise_and)
            cp("dve", pf.ap[:, 0:1], pidx.ap[:, 1:2], [pidx], [pf])
            act(pf.ap[:, 1:2], pf.ap[:, 0:1], AF.Exp, [pf], [pf], scale=-math.log(10000.0) / 16.0)
            vop("dve", "tensor_single_scalar", [pf], [pf], out=pf.ap[:, 1:2], in_=pf.ap[:, 1:2], scalar=1.0 / (2 * math.pi), op=ALU.mult)
            tpi = T([128, S], I32, name="tpi")
            tpf = T([128, S], F32, name="tpf")
            tq = T([128, S], F32, name="tq")
            vop("pool", "iota", [], [tpi], out=tpi.ap, pattern=[[1, S]], base=0, channel_multiplier=0)
            cp("dve", tpf.ap, tpi.ap, [tpi], [tpf])
            for tab, offs in ((sinT, 0.0), (cosT, 0.25)):
                vop("dve", "tensor_scalar", [tpf, pf], [tq], out=tq.ap, in0=tpf.ap, scalar1=pf.ap[:, 1:2], scalar2=offs,
                    op0=ALU.mult, op1=ALU.add)
                cp("dve", tpi.ap, tq.ap, [tq], [tpi])
                cp("dve", tab.ap, tpi.ap, [tpi], [tab])
                vop("dve", "tensor_tensor", [tq, tab], [tq], out=tq.ap, in0=tq.ap, in1=tab.ap, op=ALU.subtract)
                vop("dve", "scalar_tensor_tensor", [tq], [tq], out=tq.ap, in0=tq.ap, scalar=0.5, in1=tq.ap,
                    op0=ALU.is_gt, op1=ALU.subtract)
                act(tab.ap, tq.ap, AF.Sin, [tq], [tab], scale=-2.0 * math.pi)
            phase_end()

            def rope_evac(ps_m, ps_r, st, tg, tmpr):
                ta = tmpr.next()
                tb = tmpr.next()
                cs = slice(tg * 512, (tg + 1) * 512)
                vop("dve", "tensor_tensor", [ps_r, sinT], [ta], out=ta.ap[64:96, :], in0=ps_r.ap[64:96, :], in1=sinT.ap[64:96, cs], op=ALU.mult)
                vop("dve", "tensor_tensor", [ps_m, cosT], [tb], out=tb.ap[64:96, :], in0=ps_m.ap[64:96, :], in1=cosT.ap[64:96, cs], op=ALU.mult)
                vop("pool", "tensor_tensor", [ta, tb], [st], out=st.ap[64:96, cs], in0=ta.ap[64:96, :], in1=tb.ap[64:96, :], op=ALU.add)

            xT = T([128, 16, S], BF16, name="xT1")
            load_T(xs2, D, xT, True)
            wring = TR(2, [128, 16, 512], BF16, dma="sw", name="wring")
            stgF = TR(2, [128, S], BF16, dma=True, name="stgF")
            stgT = TR(2, [128, 512], BF16, dma=True, name="stgT")
            stgT32 = TR(2, [128, 512], F32, dma=True, name="stgT32")
            for i in range(2):
                proj_F(xT, 16, o_win[:, i * 512:(i + 1) * 512], 512, QC_d[i * 512:(i + 1) * 512, :], 1, wring, stgF)
                proj_F(xT, 16, o_win[:, 1024 + i * 512:1024 + (i + 1) * 512], 512, KC_d[i * 512:(i + 1) * 512, :], 1, wring, stgF)
                proj_T(xT, 16, o_win[:, 2048 + i * 512:2048 + (i + 1) * 512], 512, VC_d[:, i * 512:(i + 1) * 512], BF16, wring, stgT)
            proj_T(xT, 16, o_win[:, 3072:3584], 512, CQ_d[:, 0:512], F32, wring, stgT32)
            proj_T(xT, 16, o_win[:, 3584:3840], 256, CQ_d[:, 512:768], F32, wring, stgT32)
            wkr = T([128, 16, 96], BF16, dma="sw", name="wkr")
            wkrot = T([128, 16, 96], BF16, name="wkrot")
            vop("dve", "memset", [], [wkr], ap=wkr.ap, constant=0.0)
            vop("pool", "memset", [], [wkrot], ap=wkrot.ap, constant=0.0)
            dma("pool", wkr.ap[:, :, 64:96], o_win[:, 3840:3872].rearrange("(k p) e -> p k e", p=128), wkr, [], [wkr])
            vop("dve", "tensor_single_scalar", [wkr], [wkrot], out=wkrot.ap[:, :, 64:80], in_=wkr.ap[:, :, 80:96], scalar=-1.0, op=ALU.mult)
            cp("dve", wkrot.ap[:, :, 80:96], wkr.ap[:, :, 64:80], [wkr], [wkrot])
            tmpr = TR(4, [128, 512], F32, name="ropetmp")
            stK = T([128, S], BF16, dma=True, name="stKr")
            for tg in range(4):
                pm, pr = PB[0], PB[1]
                for k in range(16):
                    mm(pm.ap[0:96, :], wkr.ap[:, k, :], xT.ap[:, k, tg * 512:(tg + 1) * 512], k == 0, k == 15, [wkr, xT], [pm])
                for k in range(16):
                    mm(pr.ap[0:96, :], wkrot.ap[:, k, :], xT.ap[:, k, tg * 512:(tg + 1) * 512], k == 0, k == 15, [wkrot, xT], [pr])
                rope_evac(pm, pr, stK, tg, tmpr)
            for h in range(16):
                dma("sp", KD_d[h, 64:96, :], stK.ap[64:96, :], stK, [stK], [])
            phase_end()

            cT = T([128, 6, S], BF16, name="cT")
            gq = T([128, 768], F32, dma=True, name="gq")
            dma("sp", gq.ap[:, 0:512], o_qg.partition_broadcast(128), gq, [], [gq])
            dma("sp", gq.ap[:, 512:768], o_kvg.partition_broadcast(128), gq, [], [gq])
            cl = TR(2, [128, 768], F32, dma=True, name="cl")
            junk = TR(2, [128, 768], F32, name="junk")
            cbf = TR(2, [128, 768], BF16, name="cbf")
            str_ = TR(2, [128, 8], F32, name="st")
            pbt = Ring([PB[6], PB[7]])
            for t in range(NT):
                c = cl.next()
                dma("sp", c.ap, CQ_d[t * 128:(t + 1) * 128, :], c, [], [c])
                jk = junk.next()
                st = str_.next()
                act(jk.ap[:, 0:512], c.ap[:, 0:512], AF.Square, [c], [jk, st], accum=st.ap[:, 0:1])
                act(jk.ap[:, 512:768], c.ap[:, 512:768], AF.Square, [c], [jk, st], accum=st.ap[:, 1:2])
                vop("dve", "tensor_scalar", [st], [st], out=st.ap[:, 2:3], in0=st.ap[:, 0:1], scalar1=1.0 / 512.0, scalar2=RMS_EPS, op0=ALU.mult, op1=ALU.add)
                vop("dve", "tensor_scalar", [st], [st], out=st.ap[:, 3:4], in0=st.ap[:, 1:2], scalar1=1.0 / 256.0, scalar2=RMS_EPS, op0=ALU.mult, op1=ALU.add)
                act(st.ap[:, 4:6], st.ap[:, 2:4], AF.Sqrt, [st], [st])
                vop("dve", "reciprocal", [st], [st], out=st.ap[:, 6:8], in_=st.ap[:, 4:6])
                vop("pool", "tensor_tensor", [c, gq], [c], out=c.ap, in0=c.ap, in1=gq.ap, op=ALU.mult)
                cb = cbf.next()
                vop("dve", "tensor_scalar", [c, st], [cb], out=cb.ap[:, 0:512], in0=c.ap[:, 0:512], scalar1=st.ap[:, 6:7], scalar2=None, op0=ALU.mult)
                vop("dve", "tensor_scalar", [c, st], [cb], out=cb.ap[:, 512:768], in0=c.ap[:, 512:768], scalar1=st.ap[:, 7:8], scalar2=None, op0=ALU.mult)
                pb = pbt.next()
                pv = pb.ap.bitcast(BF16)
                for k in range(6):
                    s.op("pe", (lambda pv=pv, k=k, cb=cb: (lambda e: e.transpose(pv[:, k * 128:(k + 1) * 128], cb.ap[:, k * 128:(k + 1) * 128], ident.ap)))(),
                         rs([cb, ident]), rs([pb]))
                cp("act", cT.ap[:, :, t * 128:(t + 1) * 128], pv[:, 0:768].rearrange("p (k t) -> p k t", k=6), [pb], [cT])
            wq = T([128, 4, 1536], BF16, dma="sw", name="wq")
            wqr = T([128, 4, 1536], BF16, name="wqr")
            dma("pool", wq.ap, o_wuq.rearrange("(k p) e -> p k e", p=128), wq, [], [wq])
            wq4 = wq.ap.rearrange("p k (h e) -> p k h e", h=16)
            wqr4 = wqr.ap.rearrange("p k (h e) -> p k h e", h=16)
            cp("dve", wqr.ap, wq.ap, [wq], [wqr])
            for k in range(4):
                vop("dve", "tensor_single_scalar", [wq, wqr], [wqr], out=wqr4[:, k, :, 64:80], in_=wq4[:, k, :, 80:96], scalar=-1.0, op=ALU.mult)
                cp("dve", wqr4[:, k, :, 80:96], wq4[:, k, :, 64:80], [wq, wqr], [wqr])
            wkv = T([128, 2, 2048], BF16, dma="sw", name="wkv")
            dma("pool", wkv.ap, o_wukv.rearrange("(k p) e -> p k e", p=128), wkv, [], [wkv])
            stQ = TR(2, [128, S], BF16, dma=True, name="stQ")
            stKn = TR(2, [128, S], BF16, dma=True, name="stKn")
            stV = TR(2, [128, 1024], BF16, dma=True, name="stV")
            tmpr = TR(4, [128, 512], F32, name="ropetmp")
            pbq = Ring(PB[0:6])
            for h in range(16):
                sq = stQ.next()
                for tg in range(4):
                    pm = pbq.next()
                    pr = pbq.next()
                    for k in range(4):
                        mm(pm.ap[0:96, :], wq4[:, k, h, :], cT.ap[:, k, tg * 512:(tg + 1) * 512], k == 0, k == 3, [wq, cT], [pm])
                    for k in range(4):
                        mm(pr.ap[0:96, :], wqr4[:, k, h, :], cT.ap[:, k, tg * 512:(tg + 1) * 512], k == 0, k == 3, [wqr, cT], [pr])
                    cp("act", sq.ap[0:64, tg * 512:(tg + 1) * 512], pm.ap[0:64, :], [pm], [sq])
                    rope_evac(pm, pr, sq, tg, tmpr)
                dma("sp", QD_d[h], sq.ap[0:96, :], sq, [sq], [])
                sk = stKn.next()
                for tg in range(4):
                    pk = pbq.next()
                    for k in range(2):
                        mm(pk.ap[0:64, :], wkv.ap[:, k, h * 128:h * 128 + 64], cT.ap[:, 4 + k, tg * 512:(tg + 1) * 512], k == 0, k == 1, [wkv, cT], [pk])
                    cp(ev_eng(), sk.ap[0:64, tg * 512:(tg + 1) * 512], pk.ap[0:64, :], [pk], [sk])
                dma("sp", KD_d[h, 0:64, :], sk.ap[0:64, :], sk, [sk], [])
            wkv5 = wkv.ap.rearrange("p k (h two d) -> p k h two d", two=2, d=64)
            for t in range(NT):
                sv = stV.next()
                for hf in range(2):
                    pb = pbq.next()
                    for k in range(2):
                        mm(pb.ap.rearrange("p (h d) -> p h d", h=8), cT.ap[:, 4 + k, t * 128:(t + 1) * 128], wkv5[:, k, hf * 8:(hf + 1) * 8, 1, :],
                           k == 0, k == 1, [wkv, cT], [pb])
                    cp(ev_eng(), sv.ap[:, hf * 512:(hf + 1) * 512], pb.ap, [pb], [sv])
                dma("sp", VD_d[t * 128:(t + 1) * 128, :], sv.ap, sv, [sv], [])
            persist_end[0] = base_persist
            phase_end()

            def attn_C():
                VCp = T([128, NT, 1024], BF16, dma=True, name="VCp")
                dma("sp", VCp.ap, VC_d.rearrange("(n p) c -> p n c", p=128), VCp, [], [VCp])
                OC = T([128, NT, 1024], BF16, dma=True, name="OC")
                Qr = TR(3, [64, S], BF16, dma=True, name="Qh")
                Kr = TR(3, [64, S], BF16, dma=True, name="Kh")
                Er = TR(4, [128, 512], F32, name="E")
                SPr = TR(4, [128, 512], F32, name="SP")
                LKr = TR(4, [128, 512], F32, name="LK")
                Wr = TR(4, [128, 512], BF16, name="W")
                Srun = TR(2, [128, 512], F32, name="Srun")
                pZ = Ring([PB[0], PB[1]])
                pA = Ring([PB[2], PB[3]])
                pO = PB[4:8]
                heads = [(Qr.next(), Kr.next()) for _ in range(16)]

                def load_head(h):
                    Qh, Kh = heads[h]
                    dma("sp", Qh.ap, QC_d[h * 64:(h + 1) * 64, :], Qh, [], [Qh])
                    dma("sp", Kh.ap, KC_d[h * 64:(h + 1) * 64, :], Kh, [], [Kh])
                items = []
                for h in range(16):
                    for G in range(4):
                        Sr = Srun.next()
                        for j in range(4 * G + 3, -1, -1):
                            items.append((h, G, j, Sr))
                ctx = [dict(pz=pZ.next(), pa=pA.next(), E=Er.next(), SP=SPr.next(), LK=LKr.next(), W=Wr.next()) for _ in items]
                load_head(0)

                def s1(it):
                    h, G, j, Sr = items[it]
                    c = ctx[it]
                    if G == 0 and j == 3 and h + 1 < 16:
                        load_head(h + 1)
                    if it % 16 == 7:
                        bg_tick()
                    Qh, Kh = heads[h]
                    c0 = max(j - 4 * G, 0) * 128
                    pz, E, SP, LK = c["pz"], c["E"], c["SP"], c["LK"]
                    mm(pz.ap[:, c0:512], Kh.ap[:, j * 128:(j + 1) * 128], Qh.ap[:, G * 512 + c0:(G + 1) * 512], True, True, [Kh, Qh], [pz])
                    act(E.ap[:, c0:512], pz.ap[:, c0:512], AF.Exp, [pz], [E], scale=-0.125)
                    act(SP.ap[:, c0:512], E.ap[:, c0:512], AF.Ln, [E], [SP], bias=1.0, scale=1.0)
                    vop("dve", "scalar_tensor_tensor", [pz, SP], [LK], out=LK.ap[:, c0:512], in0=pz.ap[:, c0:512], scalar=-0.125,
                        in1=SP.ap[:, c0:512], op0=ALU.mult, op1=ALU.subtract)
                    if j >= 4 * G:
                        vop("pool", "tensor_tensor", [LK, mC01], [LK], out=LK.ap[:, c0:c0 + 128], in0=LK.ap[:, c0:c0 + 128], in1=mC01.ap, op=ALU.mult)

                def s2(it):
                    h, G, j, Sr = items[it]
                    c = ctx[it]
                    c0 = max(j - 4 * G, 0) * 128
                    pa, E, SP, LK, W = c["pa"], c["E"], c["SP"], c["LK"], c["W"]
                    first = j == 4 * G + 3
                    if first:
                        vop("pool", "memset", [], [Sr], ap=Sr.ap, constant=0.0)
                    mm(pa.ap[:, c0:512], Ustr.ap, LK.ap[:, c0:512], True, first, [Ustr, LK], [pa])
                    if not first:
                        mm(pa.ap[:, c0:512], ones.ap, Sr.ap[:, c0:512], False, True, [ones, Sr], [pa])
                    vop("dve", "tensor_tensor", [pa, SP], [E], out=E.ap[:, c0:512], in0=pa.ap[:, c0:512], in1=SP.ap[:, c0:512], op=ALU.subtract)
                    act(W.ap[:, c0:512], E.ap[:, c0:512], AF.Exp, [E], [W])
                    if j >= 4 * G:
                        vop("pool", "tensor_tensor", [W, mC01b], [W], out=W.ap[:, c0:c0 + 128], in0=W.ap[:, c0:c0 + 128], in1=mC01b.ap, op=ALU.mult)
                    if j > 0:
                        vop("pool", "tensor_tensor", [Sr, LK], [Sr], out=Sr.ap[:, c0:512], in0=Sr.ap[:, c0:512], in1=LK.ap[:, c0:512], op=ALU.add)

                def s3(it):
                    h, G, j, Sr = items[it]
                    c = ctx[it]
                    q0 = max(j - 4 * G, 0)
                    W = c["W"]
                    for qt in range(q0, 4):
                        mm(pO[qt].ap[:, 0:64], W.ap[:, qt * 128:(qt + 1) * 128], VCp.ap[:, j, h * 64:(h + 1) * 64],
                           j == 4 * G + qt, j == 0, [W, VCp], [pO[qt]])
                    if j == 0:
                        for qt in range(4):
                            cp("act" if qt % 2 else "dve", OC.ap[:, 4 * G + qt, h * 64:(h + 1) * 64], pO[qt].ap[:, 0:64], [pO[qt]], [OC])

                run_pipeline(len(items), [s1, s2, s3])
                for n in range(NT):
                    dma("sp", Y1_d[n * 128:(n + 1) * 128, 0:1024], OC.ap[:, n, :], OC, [OC], [])

            def attn_D():
                sc = 1.0 / math.sqrt(96.0)
                VDp = T([128, NT, 1024], BF16, dma=True, name="VDp")
                dma("sp", VDp.ap, VD_d.rearrange("(n p) c -> p n c", p=128), VDp, [], [VDp])
                OD = T([128, NT, 1024], BF16, dma=True, name="OD")
                Qr = TR(3, [96, S], BF16, dma=True, name="Qh")
                Kr = TR(3, [96, S], BF16, dma=True, name="Kh")
                Pr = TR(4, [128, S], BF16, name="P")
                PTr = TR(4, [128, S], BF16, name="PT")
                Sdr = TR(3, [128, 128], F32, name="Sd")
                str_ = TR(6, [128, 16], F32, name="st")
                pSa = Ring([PB[0:2], PB[2:4]])
                pT = Ring([PB[4], PB[5]])
                pO = Ring([PB[6], PB[7]])
                heads = [(Qr.next(), Kr.next()) for _ in range(16)]

                def load_head(h):
                    Qh, Kh = heads[h]
                    dma("sp", Qh.ap, QD_d[h], Qh, [], [Qh])
                    dma("sp", Kh.ap, KD_d[h], Kh, [], [Kh])
                items = [(h, i) for h in range(16) for i in range(NT)]
                ctx = []
                for (h, i) in items:
                    nbank = (i + 4) // 4
                    pS = pSa.next() if nbank <= 2 else PB[0:4]
                    ctx.append(dict(pS=pS, Sd=Sdr.next(), st=str_.next(), Pt=Pr.next(), PT=PTr.next(), po=pO.next()))
                load_head(0)

                def geom(i):
                    nkb = i + 1
                    nbank = (nkb + 3) // 4
                    widths = [min(512, nkb * 128 - bk * 512) for bk in range(nbank)]
                    return nkb, nbank, widths

                def sA(it):
                    h, i = items[it]
                    c = ctx[it]
                    if i == 0 and h + 1 < 16:
                        load_head(h + 1)
                    Qh, Kh = heads[h]
                    nkb, nbank, widths = geom(i)
                    pS, Sd, st, Pt = c["pS"], c["Sd"], c["st"], c["Pt"]
                    for bk in range(nbank):
                        mm(pS[bk].ap[:, 0:widths[bk]], Qh.ap[:, i * 128:(i + 1) * 128], Kh.ap[:, bk * 512:bk * 512 + widths[bk]],
                           True, True, [Qh, Kh], [pS[bk]])
                    bd = nbank - 1
                    dc = widths[bd] - 128
                    vop("dve", "tensor_tensor", [pS[bd], McD], [Sd], out=Sd.ap, in0=pS[bd].ap[:, dc:dc + 128], in1=McD.ap, op=ALU.add)
                    vop("dve", "reduce_max", [Sd], [st], out=st.ap[:, 0:1], in_=Sd.ap, axis=AX.X)
                    ncol = 1
                    for bk in range(nbank):
                        wv = widths[bk] - (128 if bk == bd else 0)
                        if wv > 0:
                            vop("dve", "reduce_max", [pS[bk]], [st], out=st.ap[:, ncol:ncol + 1], in_=pS[bk].ap[:, 0:wv], axis=AX.X)
                            ncol += 1
                    if ncol > 1:
                        vop("dve", "reduce_max", [st], [st], out=st.ap[:, 5:6], in_=st.ap[:, 0:ncol], axis=AX.X)
                        mxc = st.ap[:, 5:6]
                    else:
                        mxc = st.ap[:, 0:1]
                    vop("dve", "tensor_single_scalar", [st], [st], out=st.ap[:, 6:7], in_=mxc, scalar=-sc, op=ALU.mult)
                    act(Pt.ap[:, i * 128:(i + 1) * 128], Sd.ap, AF.Exp, [Sd, st], [Pt, st], bias=st.ap[:, 6:7], scale=sc, accum=st.ap[:, 8:9])
                    ncol = 1
                    for bk in range(nbank):
                        wv = widths[bk] - (128 if bk == bd else 0)
                        if wv > 0:
                            act(Pt.ap[:, bk * 512:bk * 512 + wv], pS[bk].ap[:, 0:wv], AF.Exp, [pS[bk], st], [Pt, st],
                                bias=st.ap[:, 6:7], scale=sc, accum=st.ap[:, 8 + ncol:9 + ncol])
                            ncol += 1
                    c["ncol"] = ncol

                def sB(it):
                    h, i = items[it]
                    c = ctx[it]
                    nkb, nbank, widths = geom(i)
                    Pt, PT = c["Pt"], c["PT"]
                    for k0 in range(0, nkb, 8):
                        kn = min(8, nkb - k0)
                        pt = pT.next()
                        ptv = pt.ap.bitcast(BF16)
                        for jj in range(kn):
                            kb = k0 + jj
                            s.op("pe", (lambda ptv=ptv, jj=jj, Pt=Pt, kb=kb: (lambda e: e.transpose(ptv[:, jj * 128:(jj + 1) * 128], Pt.ap[:, kb * 128:(kb + 1) * 128], ident.ap)))(),
                                 rs([Pt, ident]), rs([pt]))
                        cp("act" if (k0 // 8) % 2 == 0 else "dve", PT.ap[:, k0 * 128:(k0 + kn) * 128], ptv[:, 0:kn * 128], [pt], [PT])

                def sC(it):
                    h, i = items[it]
                    c = ctx[it]
                    nkb, nbank, widths = geom(i)
                    PT, po, st = c["PT"], c["po"], c["st"]
                    ncol = c["ncol"]
                    for kb in range(nkb):
                        mm(po.ap[:, 0:64], PT.ap[:, kb * 128:(kb + 1) * 128], VDp.ap[:, kb, h * 64:(h + 1) * 64], kb == 0, kb == nkb - 1, [PT, VDp], [po])
                    if ncol > 1:
                        vop("dve", "reduce_sum", [st], [st], out=st.ap[:, 7:8], in_=st.ap[:, 8:8 + ncol], axis=AX.X)
                        den = st.ap[:, 7:8]
                    else:
                        den = st.ap[:, 8:9]
                    vop("dve", "reciprocal", [st], [st], out=st.ap[:, 14:15], in_=den)
                    vop("dve", "tensor_scalar", [po, st], [OD], out=OD.ap[:, i, h * 64:(h + 1) * 64], in0=po.ap[:, 0:64],
                        scalar1=st.ap[:, 14:15], scalar2=None, op0=ALU.mult)

                run_pipeline(len(items), [sA, sB, sC])
                for n in range(NT):
                    dma("sp", Y1_d[n * 128:(n + 1) * 128, 1024:2048], OD.ap[:, n, :], OD, [OD], [])

            if "skipC" not in dbg:
                attn_C()
                phase_end()
            if "skipD" not in dbg:
                attn_D()
                phase_end()
            out_proj_ln(Y1_d, 16, o_wout, xs2, ln1_g[1:2, :], ln1_b[1:2, :], xs1)
            phase_end()
            bg_flush()
            mlp(1, xs1, ln2_g[1:2, :], ln2_b[1:2, :], out_d)

        s.barrier(final=True)
        s.emit()
    return nc


_NC_CACHE = {}


def kernel(**inputs):
    B = inputs["x"].shape[0]
    if "nc" not in _NC_CACHE:
        _NC_CACHE["nc"] = build()
    nc = _NC_CACHE["nc"]
    f = lambda a: np.ascontiguousarray(np.asarray(a, dtype=np.float32))
    shared = {
        "even_w_in": f(inputs["even_w_in"][0]),
        "even_sinks": f(inputs["even_sinks"][0]).reshape(1, 16),
        "even_w_out": f(inputs["even_w_out"][0]),
        "odd_w_in": f(inputs["odd_w_in"][0]),
        "odd_q_norm_g": f(inputs["odd_q_norm_g"][0]).reshape(1, 512),
        "odd_kv_norm_g": f(inputs["odd_kv_norm_g"][0]).reshape(1, 256),
        "odd_w_uq": f(inputs["odd_w_uq"][0]),
        "odd_w_ukv": f(inputs["odd_w_ukv"][0]),
        "odd_w_out": f(inputs["odd_w_out"][0]),
        "ln1_g": f(inputs["ln1_g"]), "ln1_b": f(inputs["ln1_b"]),
        "ln2_g": f(inputs["ln2_g"]), "ln2_b": f(inputs["ln2_b"]),
        "mlp_w1": f(inputs["mlp_w1"]), "mlp_w2": f(inputs["mlp_w2"]),
    }
    x = f(inputs["x"])
    in_maps = [dict(shared, x=x[b]) for b in range(B)]
    res = run_bass_kernel_spmd(nc, in_maps, core_ids=list(range(B)))
    return np.stack([r["out"] for r in res.results], axis=0)
```

```python
import contextlib
import math
import numpy as np
import concourse.bass as bass
import concourse.mybir as mybir
from concourse.bass_utils import run_bass_kernel_spmd

F32 = mybir.dt.float32
BF16 = mybir.dt.bfloat16
I32 = mybir.dt.int32
AF = mybir.ActivationFunctionType
ALU = mybir.AluOpType
AX = mybir.AxisListType

S = 2048
D = 2048
NT = 16
DFF = 8192
ALPHA = 4.0 ** 0.25
LN_EPS = 1e-5
RMS_EPS = 1e-6
BIG = 1.0e9


class Res:
    __slots__ = ("name", "w", "r")

    def __init__(self, name=""):
        self.name = name
        self.w = None
        self.r = {}


class Buf:
    __slots__ = ("ap", "res", "sem")

    def __init__(self, ap, res, sem=None):
        self.ap = ap
        self.res = res
        self.sem = sem


class Ring:
    def __init__(self, items):
        self.items = items
        self.i = 0

    def next(self):
        it = self.items[self.i % len(self.items)]
        self.i += 1
        return it


class Sched:
    ENG = ("pe", "act", "dve", "pool", "sp")

    def __init__(self, nc, stack):
        self.nc = nc
        self.stack = stack
        self.prog = {e: [] for e in self.ENG}
        self.sem = {}
        self.cnt = {}
        self.known = {e: {} for e in self.ENG}
        self.free_dsems = []
        self.used_dsems = []
        self.ndsem = 0
        for e in self.ENG:
            self.newsem("E_" + e)

    def newsem(self, name):
        self.sem[name] = self.stack.enter_context(self.nc.semaphore(name))
        self.cnt[name] = 0
        return name

    def dsem(self, kind="H"):
        fl = [x for x in self.free_dsems if x[0] == kind]
        if fl:
            n = fl[-1]
            self.free_dsems.remove(n)
        else:
            n = self.newsem("%s%d" % (kind, self.ndsem))
            self.ndsem += 1
        self.used_dsems.append(n)
        return n

    def _deps(self, eng, reads, writes):
        need = {}

        def add(ev):
            if ev is None:
                return
            sm, v = ev
            if need.get(sm, 0) < v:
                need[sm] = v
        for r in reads:
            add(r.w)
        for w in writes:
            add(w.w)
            for sm, v in w.r.items():
                add((sm, v))
        kn = self.known[eng]
        for sm, v in need.items():
            if eng == "pe" and sm == "E_pe":
                continue
            if kn.get(sm, 0) < v:
                kn[sm] = v
                self.prog[eng].append(("wait", sm, v))

    def _commit(self, ev, reads, writes):
        sm, v = ev
        for r in reads:
            if r.r.get(sm, 0) < v:
                r.r[sm] = v
        for w in writes:
            w.w = ev
            w.r = {}

    def op(self, eng, fn, reads=(), writes=()):
        self._deps(eng, reads, writes)
        sm = "E_" + eng
        self.cnt[sm] += 1
        ev = (sm, self.cnt[sm])
        self.prog[eng].append(("op", fn, sm, 1))
        self._commit(ev, reads, writes)

    def dma(self, eng, out, in_, sem, reads=(), writes=()):
        self._deps(eng, reads, writes)
        self.cnt[sem] += 16
        ev = (sem, self.cnt[sem])
        self.prog[eng].append(("op", lambda e: e.dma_start(out=out, in_=in_), sem, 16))
        self._commit(ev, reads, writes)

    def barrier(self, final=False):
        for e in self.ENG:
            kn = self.known[e]
            for sm, v in self.cnt.items():
                if sm.startswith("WC") and not final:
                    continue
                if v > 0 and kn.get(sm, 0) < v:
                    kn[sm] = v
                    self.prog[e].append(("wait", sm, v))
        self.free_dsems.extend(self.used_dsems)
        self.used_dsems = []

    def emit(self):
        nc = self.nc

        def replay(name):
            def f(eng):
                for it in self.prog[name]:
                    if it[0] == "wait":
                        eng.wait_ge(self.sem[it[1]], it[2])
                    else:
                        it[1](eng).then_inc(self.sem[it[2]], it[3])
            return f

        with nc.Block() as block:
            block.tensor(replay("pe"))
            block.scalar(replay("act"))
            block.vector(replay("dve"))
            block.gpsimd(replay("pool"))
            block.sync(replay("sp"))


def alibi(n):
    return [2.0 ** (-8.0 * (i + 1) / n) for i in range(n)]


def build(dbg=()):
    nc = bass.Bass("TRN2", target_bir_lowering=False)

    def din(name, shape):
        return nc.dram_tensor(name, list(shape), F32, kind="ExternalInput").ap()

    x_in = din("x", [S, D])
    e_win = din("even_w_in", [D, 5888])
    e_sinks = din("even_sinks", [1, 16])
    e_wout = din("even_w_out", [1536, D])
    o_win = din("odd_w_in", [D, 3872])
    o_qg = din("odd_q_norm_g", [1, 512])
    o_kvg = din("odd_kv_norm_g", [1, 256])
    o_wuq = din("odd_w_uq", [512, 1536])
    o_wukv = din("odd_w_ukv", [256, 2048])
    o_wout = din("odd_w_out", [D, D])
    ln1_g = din("ln1_g", [2, D])
    ln1_b = din("ln1_b", [2, D])
    ln2_g = din("ln2_g", [2, D])
    ln2_b = din("ln2_b", [2, D])
    w1_in = din("mlp_w1", [2, D, DFF])
    w2_in = din("mlp_w2", [2, DFF, D])
    out_d = nc.dram_tensor("out", [S, D], F32, kind="ExternalOutput").ap()

    def dscr(name, shape, dt):
        kind = "ExternalOutput" if name in dbg else "Internal"
        return nc.dram_tensor(name, list(shape), dt, kind=kind).ap()

    w1b = dscr("w1b", [2, D, DFF], BF16)
    w2b = dscr("w2b", [2, DFF, D], BF16)
    xs1 = dscr("xs1", [S, D], F32)
    xs2 = dscr("xs2", [S, D], F32)
    QA_d = dscr("QA_d", [1024, S], BF16)
    KA_d = dscr("KA_d", [128, S], BF16)
    VA_d = dscr("VA_d", [S, 128], BF16)
    QB_d = [dscr("QB%d_d" % g, [512, S], BF16) for g in range(3)]
    KB_d = [dscr("KB%d_d" % g, [512, S], BF16) for g in range(3)]
    VB_d = [dscr("VB%d_d" % g, [S, 512], BF16) for g in range(3)]
    OB_d = [dscr("OB%d_d" % g, [S, 512], F32) for g in range(3)]
    LSE_d = [dscr("LSE%d_d" % g, [S, 8], F32) for g in range(3)]
    Y0_d = dscr("Y0_d", [S, 1536], BF16)
    QC_d = dscr("QC_d", [1024, S], BF16)
    KC_d = dscr("KC_d", [1024, S], BF16)
    VC_d = dscr("VC_d", [S, 1024], BF16)
    CQ_d = dscr("CQ_d", [S, 768], F32)
    QD_d = dscr("QD_d", [16, 96, S], BF16)
    KD_d = dscr("KD_d", [16, 96, S], BF16)
    VD_d = dscr("VD_d", [S, 1024], BF16)
    Y1_d = dscr("Y1_d", [S, 2048], BF16)

    with contextlib.ExitStack() as stack:
        s = Sched(nc, stack)
        ARENA_ELEMS = 103 * 1024
        arena = nc.alloc_sbuf_tensor("arena", [128, ARENA_ELEMS], BF16)
        aoff = [0]
        persist_end = [0]

        def T(shape, dt, dma=False, name=""):
            esz = 2 if dt == BF16 else 4
            nel = int(np.prod(shape[1:]))
            nb16 = (nel * esz + 63) // 64 * 32
            assert aoff[0] + nb16 <= ARENA_ELEMS, "arena overflow %s %d" % (name, aoff[0] + nb16)
            v = arena[:, aoff[0]:aoff[0] + nel * esz // 2]
            aoff[0] += nb16
            if dt != BF16:
                v = v.bitcast(dt)
            if len(shape) == 3:
                v = v.rearrange("p (a b) -> p a b", a=shape[1])
            elif len(shape) == 4:
                v = v.rearrange("p (a b c) -> p a b c", a=shape[1], b=shape[2])
            if shape[0] != 128:
                v = v[0:shape[0]]
            return Buf(v, Res(name), s.dsem("W" if dma == "sw" else "H") if dma else None)

        def TR(n, shape, dt, dma=False, name=""):
            return Ring([T(shape, dt, dma, name) for _ in range(n)])

        def phase_end():
            s.barrier()
            aoff[0] = persist_end[0]

        PB = [Buf(nc.alloc_psum_tensor("pb%d" % i, [128, 512], F32)[:], Res("pb%d" % i)) for i in range(8)]

        def rs(bufs):
            return [b.res for b in bufs]

        def mm(out, lhsT, rhs, start, stop, rd, wr):
            s.op("pe", lambda e: e.matmul(out, lhsT, rhs, start=start, stop=stop), rs(rd), rs(wr))

        def act(out, in_, func, rd, wr, bias=None, scale=None, accum=None, eng="act"):
            kw = {}
            if bias is not None:
                kw["bias"] = bias
            if scale is not None:
                kw["scale"] = scale
            if accum is not None:
                kw["accum_out"] = accum
            s.op(eng, lambda e: e.activation(out=out, in_=in_, func=func, **kw), rs(rd), rs(wr))

        def vop(eng, meth, rd, wr, **kw):
            s.op(eng, lambda e: getattr(e, meth)(**kw), rs(rd), rs(wr))

        def cp(eng, out, in_, rd, wr):
            if eng == "act":
                s.op("act", lambda e: e.copy(out=out, in_=in_), rs(rd), rs(wr))
            else:
                s.op(eng, lambda e: e.tensor_copy(out=out, in_=in_), rs(rd), rs(wr))

        def dma(eng, out, in_, buf, rd=(), wr=()):
            assert (eng == "pool") == (buf.sem[0] == "W"), (eng, buf.sem)
            s.dma(eng, out, in_, buf.sem, rs(rd), rs(wr))

        def run_pipeline(N, stages):
            ns = len(stages)
            for t in range(N + ns - 1):
                for si in range(ns):
                    it = t - si
                    if 0 <= it < N:
                        stages[si](it)

        ident = T([128, 128], BF16, name="ident")
        Ustr = T([128, 128], F32, name="Ustr")
        ones = T([128, 128], F32, name="ones")
        mC01 = T([128, 128], F32, name="mC01")
        mC01b = T([128, 128], BF16, name="mC01b")
        UstrB = T([128, 128], BF16, name="UstrB")
        PenC = T([128, 128], F32, name="PenC")
        McD = T([128, 128], F32, name="McD")
        RA = T([128, 256], F32, name="RA")
        RB = T([128, 256], F32, name="RB")
        sinkt = T([128, 16], F32, dma=True, name="sink")
        sink8 = T([128, 16], F32, name="sink8")
        persist_end[0] = aoff[0]
        tmpi = T([128, 256], I32, name="tmpi")
        tmpf = T([128, 256], F32, name="tmpf")
        tmpg = T([128, 256], F32, name="tmpg")
        tmph = T([128, 256], F32, name="tmph")
        vop("pool", "iota", [], [tmpi], out=tmpi.ap[:, 0:128], pattern=[[-1, 128]], base=0, channel_multiplier=1)
        cp("dve", tmpf.ap[:, 0:128], tmpi.ap[:, 0:128], [tmpi], [tmpf])
        vop("dve", "tensor_single_scalar", [tmpf], [ident], out=ident.ap, in_=tmpf.ap[:, 0:128], scalar=0.0, op=ALU.is_equal)
        vop("dve", "tensor_single_scalar", [tmpf], [Ustr], out=Ustr.ap, in_=tmpf.ap[:, 0:128], scalar=0.0, op=ALU.is_gt)
        vop("dve", "tensor_single_scalar", [tmpf], [mC01], out=mC01.ap, in_=tmpf.ap[:, 0:128], scalar=0.0, op=ALU.is_lt)
        cp("dve", mC01b.ap, mC01.ap, [mC01], [mC01b])
        cp("dve", UstrB.ap, Ustr.ap, [Ustr], [UstrB])
        vop("dve", "tensor_scalar", [Ustr], [PenC], out=PenC.ap, in0=Ustr.ap, scalar1=1.0, scalar2=-BIG, op0=ALU.subtract, op1=ALU.mult)
        vop("dve", "memset", [], [ones], ap=ones.ap, constant=1.0)
        vop("dve", "tensor_scalar", [tmpf], [McD], out=McD.ap, in0=tmpf.ap[:, 0:128], scalar1=0.0, scalar2=1.0,
            op0=ALU.is_ge, op1=ALU.subtract)
        vop("dve", "tensor_single_scalar", [McD], [McD], out=McD.ap, in_=McD.ap, scalar=BIG, op=ALU.mult)
        vop("pool", "iota", [tmpf], [tmpi], out=tmpi.ap, pattern=[[-1, 256]], base=128, channel_multiplier=1)
        cp("dve", tmpf.ap, tmpi.ap, [tmpi], [tmpf])
        for Rt, nb in ((RA, 127.0), (RB, 128.0)):
            vop("dve", "tensor_single_scalar", [tmpf], [tmpg], out=tmpg.ap, in_=tmpf.ap, scalar=0.0, op=ALU.is_ge)
            vop("dve", "tensor_single_scalar", [tmpf], [tmph], out=tmph.ap, in_=tmpf.ap, scalar=nb, op=ALU.is_le)
            vop("dve", "tensor_tensor", [tmpg, tmph], [tmpg], out=tmpg.ap, in0=tmpg.ap, in1=tmph.ap, op=ALU.mult)
            vop("dve", "tensor_tensor", [tmpg, tmpf], [tmph], out=tmph.ap, in0=tmpg.ap, in1=tmpf.ap, op=ALU.mult)
            vop("dve", "tensor_scalar", [tmpg], [tmpg], out=tmpg.ap, in0=tmpg.ap, scalar1=1.0, scalar2=BIG,
                op0=ALU.subtract, op1=ALU.mult)
            vop("dve", "tensor_tensor", [tmpg, tmph], [Rt], out=Rt.ap, in0=tmpg.ap, in1=tmph.ap, op=ALU.subtract)
        dma("sp", sinkt.ap, e_sinks.partition_broadcast(128), sinkt, [], [sinkt])
        vop("dve", "tensor_single_scalar", [sinkt], [sink8], out=sink8.ap, in_=sinkt.ap, scalar=8.0, op=ALU.mult)

        wcast = [Buf(None, Res("wc%d" % l), s.newsem("WC%d" % l)) for l in range(2)]
        bgq = []

        def bg_fill(l):
            jobs = [(w1b[l, r0:r0 + 128, :], w1_in[l, r0:r0 + 128, :]) for r0 in range(0, D, 128)]
            jobs += [(w2b[l, r0:r0 + 512, :], w2_in[l, r0:r0 + 512, :]) for r0 in range(0, DFF, 512)]
            for ji, (o_, i_) in enumerate(jobs):
                bgq.append((o_, i_, l, ji == len(jobs) - 1))

        def bg_tick(k=1):
            for _ in range(k):
                if not bgq:
                    return
                o_, i_, l, last = bgq.pop(0)
                dma("pool", o_, i_, wcast[l], [], [wcast[l]] if last else [])

        def bg_flush():
            bg_tick(len(bgq))
        phase_end()

        evq = [0]

        def ev_eng():
            evq[0] += 1
            return "act" if evq[0] % 2 else "dve"

        def load_T(src, ncol, dst, is_f32, keep=None):
            kc = ncol // 128
            ld = TR(2, [128, ncol], F32 if is_f32 else BF16, dma=True, name="ldT")
            cb = TR(2, [128, ncol], BF16, name="cbT") if is_f32 else None
            pbr = Ring([PB[6], PB[7]])
            for t in range(NT):
                lt = ld.next()
                dma("sp", lt.ap, src[t * 128:(t + 1) * 128, :], lt, [], [lt])
                if is_f32:
                    ct = cb.next()
                    cp("pool", ct.ap, lt.ap, [lt], [ct])
                else:
                    ct = lt
                for k0 in range(0, kc, 8):
                    kn = min(8, kc - k0)
                    pb = pbr.next()
                    pv = pb.ap.bitcast(BF16)
                    for j in range(kn):
                        k = k0 + j
                        s.op("pe", (lambda pv=pv, j=j, ct=ct, k=k: (lambda e: e.transpose(pv[:, j * 128:(j + 1) * 128], ct.ap[:, k * 128:(k + 1) * 128], ident.ap)))(),
                             rs([ct, ident]), rs([pb]))
                    cp(ev_eng(), dst.ap[:, k0:k0 + kn, t * 128:(t + 1) * 128],
                       pv[:, 0:kn * 128].rearrange("p (k t) -> p k t", k=kn), [pb], [dst])

        def wload(dst, wsrc, kc, ncol):
            dma("pool", dst.ap[:, 0:kc, 0:ncol], wsrc.rearrange("(k p) e -> p k e", p=128), dst, [], [dst])

        def proj_F(xT, kc, wsrc, ncol, dst, dil, wring, stg, oscale=None):
            wt = wring.next()
            wload(wt, wsrc, kc, ncol)
            pbr = Ring(PB[0:6])
            for c in range(ncol // 128):
                st = stg.next()
                for tg in range(4):
                    pb = pbr.next()
                    for k in range(kc):
                        mm(pb.ap, wt.ap[:, k, c * 128:(c + 1) * 128], xT.ap[:, k, tg * 512:(tg + 1) * 512],
                           k == 0, k == kc - 1, [wt, xT], [pb])
                    if oscale is not None:
                        if ev_eng() == "act":
                            s.op("act", (lambda o=st.ap[:, tg * 512:(tg + 1) * 512], i=pb.ap: (lambda e: e.mul(out=o, in_=i, mul=oscale)))(), rs([pb]), rs([st]))
                        else:
                            vop("dve", "tensor_single_scalar", [pb], [st], out=st.ap[:, tg * 512:(tg + 1) * 512], in_=pb.ap, scalar=oscale, op=ALU.mult)
                    elif dil == 1:
                        cp(ev_eng(), st.ap[:, tg * 512:(tg + 1) * 512], pb.ap, [pb], [st])
                    else:
                        na = 512 // dil
                        cp(ev_eng(), st.ap.rearrange("p (r a) -> p a r", r=dil)[:, tg * na:(tg + 1) * na, :],
                           pb.ap.rearrange("p (a r) -> p a r", r=dil), [pb], [st])
                dma("sp", dst[c * 128:(c + 1) * 128, :], st.ap, st, [st], [])

        def proj_T(xT, kc, wsrc, ncol, dst, dst_dt, wring, stg):
            wt = wring.next()
            wload(wt, wsrc, kc, ncol)
            pbr = Ring(PB[0:6])
            for t in range(NT):
                pb = pbr.next()
                for k in range(kc):
                    mm(pb.ap[:, 0:ncol], xT.ap[:, k, t * 128:(t + 1) * 128], wt.ap[:, k, 0:ncol],
                       k == 0, k == kc - 1, [wt, xT], [pb])
                st = stg.next()
                cp(ev_eng(), st.ap[:, 0:ncol], pb.ap[:, 0:ncol], [pb], [st])
                dma("sp", dst[t * 128:(t + 1) * 128, :], st.ap[:, 0:ncol], st, [st], [])

        def layer_norm_tile(z, gt, bt, stat):
            st6 = stat.ap[:, 0:24].rearrange("p (c s) -> p c s", c=4)
            for c in range(4):
                vop("dve", "bn_stats", [z], [stat], out=st6[:, c, :], in_=z.ap[:, c * 512:(c + 1) * 512])
            mv = stat.ap[:, 24:26]
            vop("dve", "bn_aggr", [stat], [stat], out=mv, in_=st6)
            vop("dve", "tensor_single_scalar", [stat], [stat], out=stat.ap[:, 26:27], in_=stat.ap[:, 25:26], scalar=LN_EPS, op=ALU.add)
            act(stat.ap[:, 27:28], stat.ap[:, 26:27], AF.Ln, [stat], [stat])
            act(stat.ap[:, 28:29], stat.ap[:, 27:28], AF.Exp, [stat], [stat], scale=-0.5)
            vop("dve", "tensor_scalar", [stat], [stat], out=stat.ap[:, 29:30], in0=stat.ap[:, 24:25], scalar1=stat.ap[:, 28:29], scalar2=-1.0,
                op0=ALU.mult, op1=ALU.mult)
            act(z.ap, z.ap, AF.Identity, [z, stat], [z], bias=stat.ap[:, 29:30], scale=stat.ap[:, 28:29])
            vop("dve", "tensor_tensor", [z, gt], [z], out=z.ap, in0=z.ap, in1=gt.ap, op=ALU.mult)
            vop("dve", "tensor_tensor", [z, bt], [z], out=z.ap, in0=z.ap, in1=bt.ap, op=ALU.add)

        def load_gb(g_src, b_src):
            gt = T([128, D], F32, dma=True, name="gam")
            bt = T([128, D], F32, dma=True, name="bet")
            dma("sp", gt.ap, g_src.partition_broadcast(128), gt, [], [gt])
            dma("sp", bt.ap, b_src.partition_broadcast(128), bt, [], [bt])
            return gt, bt

        def out_proj_ln(Yd, kc, wsrc, xres, g_src, b_src, dst, ncl=None, extra=False):
            ncl = ncl or kc * 128
            wt = T([128, kc, D], BF16, dma="sw", name="wout")
            for k0 in range(0, kc, 4):
                dma("pool", wt.ap[:, k0:k0 + 4, :], wsrc[k0 * 128:(k0 + 4) * 128, :].rearrange("(k p) e -> p k e", p=128), wt, [], [wt])
            gt, bt = load_gb(g_src, b_src)
            yl = TR(4, [128, kc * 128], BF16, dma=True, name="yl")
            yT = TR(3, [128, kc, 128], BF16, name="yT")
            zr = TR(4, [128, D], F32, dma=True, name="z")
            stat = TR(2, [128, 32], F32, name="stat")
            pbt = Ring([PB[4], PB[5], PB[6], PB[7]])
            if extra:
                ol = [TR(4, [128, 512], F32, dma=True, name="o%d" % g) for g in range(3)]
                ll = TR(4, [128, 3, 8], F32, dma=True, name="l")
                wk = TR(3, [128, 3, 8], F32, name="wk")
                sm = TR(3, [128, 16], F32, name="sm")

            def loads(t):
                y = yl.next()
                dma("sp", y.ap[:, 0:ncl], Yd[t * 128:(t + 1) * 128, 0:ncl], y, [], [y])
                og = lt = None
                if extra:
                    og = [ol[g].next() for g in range(3)]
                    lt = ll.next()
                    for g in range(3):
                        dma("sp", og[g].ap, OB_d[g][t * 128:(t + 1) * 128, :], og[g], [], [og[g]])
                        dma("sp", lt.ap[:, g, :], LSE_d[g][t * 128:(t + 1) * 128, :], lt, [], [lt])
                z = zr.next()
                dma("sp", z.ap, xres[t * 128:(t + 1) * 128, :], z, [], [z])
                return (t, y, og, lt, z)

            def mix_tile(t, y, og, lt):
                m = sm.next()
                vop("dve", "tensor_tensor", [lt], [m], out=m.ap[:, 0:8], in0=lt.ap[:, 0, :], in1=lt.ap[:, 1, :], op=ALU.max)
                vop("dve", "tensor_tensor", [lt, m], [m], out=m.ap[:, 0:8], in0=m.ap[:, 0:8], in1=lt.ap[:, 2, :], op=ALU.max)
                w = wk.next()
                vop("dve", "tensor_tensor", [lt, m], [w], out=w.ap, in0=lt.ap, in1=m.ap[:, 0:8].unsqueeze(1).broadcast_to([128, 3, 8]), op=ALU.subtract)
                act(w.ap, w.ap, AF.Exp, [w], [w])
                vop("dve", "tensor_tensor", [w], [m], out=m.ap[:, 8:16], in0=w.ap[:, 0, :], in1=w.ap[:, 1, :], op=ALU.add)
                vop("dve", "tensor_tensor", [w, m], [m], out=m.ap[:, 8:16], in0=m.ap[:, 8:16], in1=w.ap[:, 2, :], op=ALU.add)
                vop("dve", "reciprocal", [m], [m], out=m.ap[:, 8:16], in_=m.ap[:, 8:16])
                vop("dve", "tensor_tensor", [w, m], [w], out=w.ap, in0=w.ap, in1=m.ap[:, 8:16].unsqueeze(1).broadcast_to([128, 3, 8]), op=ALU.mult)
                for g in range(3):
                    eng = "dve" if g == 1 else "pool"
                    vop(eng, "tensor_tensor", [og[g], w], [og[g]], out=og[g].ap.rearrange("p (h d) -> p h d", h=8),
                        in0=og[g].ap.rearrange("p (h d) -> p h d", h=8), in1=w.ap[:, g, :].unsqueeze(2).broadcast_to([128, 8, 64]), op=ALU.mult)
                vop("pool", "tensor_tensor", [og[0], og[2]], [og[0]], out=og[0].ap, in0=og[0].ap, in1=og[2].ap, op=ALU.add)
                vop("dve", "tensor_tensor", [og[0], og[1]], [y], out=y.ap[:, 1024:1536], in0=og[0].ap, in1=og[1].ap, op=ALU.add)

            def prep(ld):
                t, y, og, lt, z = ld
                if extra:
                    mix_tile(t, y, og, lt)
                yt = yT.next()
                for k0 in range(0, kc, 8):
                    kn = min(8, kc - k0)
                    pb = pbt.next()
                    pv = pb.ap.bitcast(BF16)
                    for j in range(kn):
                        k = k0 + j
                        s.op("pe", (lambda pv=pv, j=j, y=y, k=k: (lambda e: e.transpose(pv[:, j * 128:(j + 1) * 128], y.ap[:, k * 128:(k + 1) * 128], ident.ap)))(),
                             rs([y, ident]), rs([pb]))
                    cp("act", yt.ap[:, k0:k0 + kn, :], pv[:, 0:kn * 128].rearrange("p (k t) -> p k t", k=kn), [pb], [yt])
                return yt, z

            lds = [loads(0), loads(1), loads(2)]
            pend = [prep(lds.pop(0)), prep(lds.pop(0))]
            for t in range(NT):
                if t + 3 < NT:
                    lds.append(loads(t + 3))
                if t + 2 < NT:
                    pend.append(prep(lds.pop(0)))
                yt, z = pend.pop(0)
                for dt in range(4):
                    pb = PB[dt]
                    for k in range(kc):
                        mm(pb.ap, yt.ap[:, k, :], wt.ap[:, k, dt * 512:(dt + 1) * 512], k == 0, k == kc - 1, [yt, wt], [pb])
                    vop("dve", "scalar_tensor_tensor", [z, pb], [z], out=z.ap[:, dt * 512:(dt + 1) * 512],
                        in0=z.ap[:, dt * 512:(dt + 1) * 512], scalar=ALPHA, in1=pb.ap, op0=ALU.mult, op1=ALU.add)
                layer_norm_tile(z, gt, bt, stat.next())
                dma("sp", dst[t * 128:(t + 1) * 128, :], z.ap, z, [z], [])

        def mlp(l, xsrc, g_src, b_src, dst):
            gt, bt = load_gb(g_src, b_src)
            zb = T([128, 4, D], F32, name="zb")
            zts = [Buf(zb.ap[:, tt, :], Res("z%d" % tt), s.dsem("H")) for tt in range(4)]
            xbs = [T([128, D], BF16, dma="sw", name="xb%d" % i) for i in range(2)]
            xTr = [T([128, 16, 512], BF16, name="xTm%d" % i) for i in range(2)]
            hT = T([128, 64, 512], BF16, name="hT")
            w1r = TR(2, [128, 16, 256], BF16, dma=True, name="w1")
            w2r = TR(3, [128, 8, 512], BF16, dma=True, name="w2")
            hr = TR(2, [128, 512], F32, name="hrelu")
            stat = TR(2, [128, 32], F32, name="stat")
            pbt = Ring([PB[6], PB[7]])
            pbh = Ring(PB[0:6])

            def issue_xload(G, tts):
                for tt in tts:
                    t = G * 4 + tt
                    xb = xbs[tt % 2]
                    dma("pool", xb.ap, xsrc[t * 128:(t + 1) * 128, :], xb, [], [xb])

            def prefetch_T(G, tts):
                xT = xTr[G % 2]
                for tt in tts:
                    xb = xbs[tt % 2]
                    for k0 in (0, 8):
                        pb = pbt.next()
                        pv = pb.ap.bitcast(BF16)
                        for j in range(8):
                            k = k0 + j
                            s.op("pe", (lambda pv=pv, j=j, xb=xb, k=k: (lambda e: e.transpose(pv[:, j * 128:(j + 1) * 128], xb.ap[:, k * 128:(k + 1) * 128], ident.ap)))(),
                                 rs([xb, ident]), rs([pb]))
                        cp("act", xT.ap[:, k0:k0 + 8, tt * 128:(tt + 1) * 128],
                           pv.rearrange("p (k t) -> p k t", k=8), [pb], [xT])

            issue_xload(0, [0, 1])
            prefetch_T(0, [0, 1])
            issue_xload(0, [2, 3])
            prefetch_T(0, [2, 3])
            for G in range(4):
                xT = xTr[G % 2]
                for f2 in range(32):
                    w1 = w1r.next()
                    dma("sp", w1.ap, w1b[l, :, f2 * 256:(f2 + 1) * 256].rearrange("(k p) f -> p k f", p=128), w1, [wcast[l]], [w1])
                    for fi in range(2):
                        f = f2 * 2 + fi
                        pb = pbh.next()
                        for k in range(16):
                            mm(pb.ap, w1.ap[:, k, fi * 128:(fi + 1) * 128], xT.ap[:, k, :], k == 0, k == 15, [w1, xT], [pb])
                        h = hr.next()
                        act(h.ap, pb.ap, AF.Relu, [pb], [h])
                        vop("pool" if f % 2 else "dve", "tensor_tensor", [h], [hT], out=hT.ap[:, f, :], in0=h.ap, in1=h.ap, op=ALU.mult)
                    if G > 0:
                        for tt in range(4):
                            if f2 == 1 + 4 * tt:
                                layer_norm_tile(zts[tt], gt, bt, stat.next())
                            if f2 == 4 + 4 * tt:
                                t = (G - 1) * 4 + tt
                                dma("sp", dst[t * 128:(t + 1) * 128, :], zts[tt].ap, zts[tt], [zts[tt]], [])
                    if f2 == 22:
                        for tt in range(4):
                            t = G * 4 + tt
                            dma("sp", zts[tt].ap, xsrc[t * 128:(t + 1) * 128, :], zts[tt], [], [zts[tt]])
                    if f2 == 26 and G < 3:
                        issue_xload(G + 1, [0, 1])
                for dt in range(4):
                    if G < 3 and dt == 0:
                        prefetch_T(G + 1, [0, 1])
                        issue_xload(G + 1, [2, 3])
                    if G < 3 and dt == 1:
                        prefetch_T(G + 1, [2, 3])
                    pbs = PB[0:4] if dt % 2 == 0 else PB[4:8]
                    for f8 in range(8):
                        w2 = w2r.next()
                        dma("sp", w2.ap, w2b[l, f8 * 1024:(f8 + 1) * 1024, dt * 512:(dt + 1) * 512].rearrange("(c p) d -> p c d", p=128),
                            w2, [wcast[l]], [w2])
                        for fi in range(8):
                            f = f8 * 8 + fi
                            for tt in range(4):
                                mm(pbs[tt].ap, hT.ap[:, f, tt * 128:(tt + 1) * 128], w2.ap[:, fi, :], f == 0, f == 63, [hT, w2], [pbs[tt]])
                    for tt in range(4):
                        zt = zts[tt]
                        vop("dve", "scalar_tensor_tensor", [zt, pbs[tt]], [zt], out=zt.ap[:, dt * 512:(dt + 1) * 512],
                            in0=zt.ap[:, dt * 512:(dt + 1) * 512], scalar=ALPHA, in1=pbs[tt].ap, op0=ALU.mult, op1=ALU.add)
            for tt in range(4):
                t = 12 + tt
                layer_norm_tile(zts[tt], gt, bt, stat.next())
                dma("sp", dst[t * 128:(t + 1) * 128, :], zts[tt].ap, zts[tt], [zts[tt]], [])

        def banded(Qd, Kd, kvmap, nh, Vd, nkv, dil, Rm, cvals, use_sink, out_mode, Yd=None, OBd=None, LSEd=None):
            L = S // dil
            nbpl = L // 128
            Vp = T([128, NT, nkv * 64], BF16, dma=True, name="Vp")
            for n in range(NT):
                r = (128 * n) // L
                a0 = (128 * n) % L
                st_ = r + dil * a0
                dma("sp", Vp.ap[:, n, :], Vd[st_:st_ + dil * 127 + 1:dil, :], Vp, [], [Vp])
            OF = T([128, NT, nh * 64], F32, dma=True, name="OF")
            OFr = [Res("OF%d" % n) for n in range(NT)]
            denAll = T([128, NT, nh], F32, name="denAll")
            nmAll = T([128, NT, nh], F32, name="nmAll")
            esAll = T([128, NT, nh], F32, name="esAll")
            denR = {}
            nmR = {}
            esR = {}
            for n_ in range(NT):
                for h_ in range(nh):
                    denR[(n_, h_)] = Buf(denAll.ap[:, n_, h_:h_ + 1], Res("den"))
                    nmR[(n_, h_)] = Buf(nmAll.ap[:, n_, h_:h_ + 1], Res("nm"))
                    esR[(n_, h_)] = Buf(esAll.ap[:, n_, h_:h_ + 1], Res("es"))
            if out_mode == "A":
                Oall = T([128, NT, nh * 64], BF16, dma=True, name="Oall")
            else:
                Oall = OF
                lse = T([128, NT, nh], F32, dma=True, name="lse")
            Qr = TR(4, [64, S], BF16, dma=True, name="Qh")
            Kr = TR(4, [64, S], BF16, dma=True, name="Kh")
            Tr = TR(3, [128, 256], F32, name="T")
            Pr = TR(4, [128, 256], BF16, name="P")
            PTr = TR(4, [128, 256], BF16, name="PT")
            str_ = TR(6, [128, 8], F32, name="st")
            pS = Ring([PB[0], PB[1], PB[6]])
            pT = Ring([PB[2], PB[3]])
            pO = Ring([PB[4], PB[5], PB[7]])
            heads = []
            lastkv = -1
            Kh = None
            for h in range(nh):
                Qh = Qr.next()
                kv = kvmap(h)
                newk = kv != lastkv
                if newk:
                    Kh = Kr.next()
                    lastkv = kv
                heads.append((Qh, Kh, kv, newk))

            def load_head(h):
                Qh, Kh, kv, newk = heads[h]
                dma("sp", Qh.ap, Qd[h * 64:(h + 1) * 64, :], Qh, [], [Qh])
                if newk:
                    dma("sp", Kh.ap, Kd[kv * 64:(kv + 1) * 64, :], Kh, [], [Kh])
            items = [(h, n) for h in range(nh) for n in range(NT)]
            ctx = [dict(ps=pS.next(), Tt=Tr.next(), st=str_.next(), Pt=Pr.next(), pt=pT.next(), PT=PTr.next(), po=pO.next())
                   for _ in items]
            load_head(0)
            if nh > 1:
                load_head(1)

            def geom(n):
                hasprev = (n % nbpl) != 0
                nk = 256 if hasprev else 128
                ks = (n - 1) * 128 if hasprev else n * 128
                return hasprev, nk, ks

            def stA(it):
                h, n = items[it]
                c = ctx[it]
                if n == 0 and h + 2 < nh:
                    load_head(h + 2)
                if it % 12 == 5:
                    bg_tick()
                Qh, Kh, kv, _ = heads[h]
                hasprev, nk, ks = geom(n)
                Rv = Rm.ap[:, 0:256] if hasprev else Rm.ap[:, 128:256]
                ps, Tt, st, Pt = c["ps"], c["Tt"], c["st"], c["Pt"]
                mm(ps.ap[:, 0:nk], Qh.ap[:, n * 128:(n + 1) * 128], Kh.ap[:, ks:ks + nk], True, True, [Qh, Kh], [ps])
                vop("dve", "scalar_tensor_tensor", [ps, Rm], [Tt], out=Tt.ap[:, 0:nk], in0=Rv, scalar=cvals[h], in1=ps.ap[:, 0:nk],
                    op0=ALU.mult, op1=ALU.add)
                vop("dve", "reduce_max", [Tt], [st], out=st.ap[:, 0:1], in_=Tt.ap[:, 0:nk], axis=AX.X)
                nmb, denb, esb = nmR[(n, h)], denR[(n, h)], esR[(n, h)]
                if use_sink:
                    vop("dve", "tensor_scalar", [st, sink8], [nmb], out=nmb.ap, in0=st.ap[:, 0:1], scalar1=sink8.ap[:, h:h + 1], scalar2=-0.125,
                        op0=ALU.max, op1=ALU.mult)
                else:
                    vop("dve", "tensor_single_scalar", [st], [nmb], out=nmb.ap, in_=st.ap[:, 0:1], scalar=-0.125, op=ALU.mult)
                act(Pt.ap[:, 0:nk], Tt.ap[:, 0:nk], AF.Exp, [Tt, nmb], [Pt, denb], bias=nmb.ap, scale=0.125, accum=denb.ap)
                if use_sink:
                    act(esb.ap, nmb.ap, AF.Exp, [nmb, sinkt], [esb], bias=sinkt.ap[:, h:h + 1], scale=1.0)

            def stB(it):
                h, n = items[it]
                c = ctx[it]
                hasprev, nk, ks = geom(n)
                Pt, pt, PT = c["Pt"], c["pt"], c["PT"]
                ptv = pt.ap.bitcast(BF16)
                for kb in range(nk // 128):
                    s.op("pe", (lambda ptv=ptv, kb=kb, Pt=Pt: (lambda e: e.transpose(ptv[:, kb * 128:(kb + 1) * 128], Pt.ap[:, kb * 128:(kb + 1) * 128], ident.ap)))(),
                         rs([Pt, ident]), rs([pt]))
                cp("act", PT.ap[:, 0:nk], ptv[:, 0:nk], [pt], [PT])

            def stC(it):
                h, n = items[it]
                c = ctx[it]
                Qh, Kh, kv, _ = heads[h]
                hasprev, nk, ks = geom(n)
                PT, po, st = c["PT"], c["po"], c["st"]
                nkb = nk // 128
                for kb in range(nkb):
                    blk = ks // 128 + kb
                    mm(po.ap[:, 0:64], PT.ap[:, kb * 128:(kb + 1) * 128], Vp.ap[:, blk, kv * 64:(kv + 1) * 64],
                       kb == 0, kb == nkb - 1, [PT, Vp], [po])
                ofb = Buf(OF.ap[:, n, h * 64:(h + 1) * 64], OFr[n])
                cp("act" if it % 2 else "dve", ofb.ap, po.ap[:, 0:64], [po], [ofb])

            run_pipeline(len(items), [stA, stB, stC])
            allden = list(denR.values())
            allnm = list(nmR.values())
            alles = list(esR.values())
            ofall = [Buf(None, r) for r in OFr]
            if use_sink:
                vop("dve", "tensor_tensor", allden + alles, [denAll], out=denAll.ap, in0=denAll.ap, in1=esAll.ap, op=ALU.add)
            if out_mode == "B":
                act(esAll.ap, denAll.ap, AF.Ln, allden + [denAll], [esAll])
                vop("dve", "tensor_tensor", [esAll] + allnm, [lse], out=lse.ap, in0=esAll.ap, in1=nmAll.ap, op=ALU.subtract)
            vop("dve", "reciprocal", allden + [denAll, esAll], [denAll], out=denAll.ap, in_=denAll.ap)
            for q4 in range(4):
                eng = "pool" if q4 % 2 else "dve"
                ns = slice(q4 * 4, (q4 + 1) * 4)
                vop(eng, "tensor_tensor", ofall + [denAll], [Oall],
                    out=Oall.ap[:, ns, :].rearrange("p n (h d) -> p n h d", h=nh),
                    in0=OF.ap[:, ns, :].rearrange("p n (h d) -> p n h d", h=nh),
                    in1=denAll.ap[:, ns, :].unsqueeze(3).broadcast_to([128, 4, nh, 64]), op=ALU.mult)
            if out_mode == "A":
                for n in range(NT):
                    dma("sp", Yd[n * 128:(n + 1) * 128, 0:nh * 64], Oall.ap[:, n, :], Oall, [Oall], [])
            else:
                for n in range(NT):
                    r = (128 * n) // L
                    a0 = (128 * n) % L
                    st_ = r + dil * a0
                    dma("sp", OBd[st_:st_ + dil * 127 + 1:dil, :], Oall.ap[:, n, :], Oall, [Oall], [])
                    dma("sp", LSEd[st_:st_ + dil * 127 + 1:dil, :], lse.ap[:, n, :], lse, [lse], [])

        def combine_B():
            ol = [TR(2, [128, 512], F32, dma=True, name="o%d" % g) for g in range(3)]
            ll = TR(2, [128, 3, 8], F32, dma=True, name="l")
            wk = TR(2, [128, 3, 8], F32, name="wk")
            sm = TR(2, [128, 16], F32, name="sm")
            yo = TR(2, [128, 512], BF16, dma=True, name="yo")
            for t in range(NT):
                og = [ol[g].next() for g in range(3)]
                lt = ll.next()
                for g in range(3):
                    dma("sp", og[g].ap, OB_d[g][t * 128:(t + 1) * 128, :], og[g], [], [og[g]])
                    dma("sp", lt.ap[:, g, :], LSE_d[g][t * 128:(t + 1) * 128, :], lt, [], [lt])
                m = sm.next()
                vop("dve", "tensor_tensor", [lt], [m], out=m.ap[:, 0:8], in0=lt.ap[:, 0, :], in1=lt.ap[:, 1, :], op=ALU.max)
                vop("dve", "tensor_tensor", [lt, m], [m], out=m.ap[:, 0:8], in0=m.ap[:, 0:8], in1=lt.ap[:, 2, :], op=ALU.max)
                w = wk.next()
                vop("dve", "tensor_tensor", [lt, m], [w], out=w.ap, in0=lt.ap, in1=m.ap[:, 0:8].unsqueeze(1).broadcast_to([128, 3, 8]), op=ALU.subtract)
                act(w.ap, w.ap, AF.Exp, [w], [w])
                vop("dve", "tensor_tensor", [w], [m], out=m.ap[:, 8:16], in0=w.ap[:, 0, :], in1=w.ap[:, 1, :], op=ALU.add)
                vop("dve", "tensor_tensor", [w, m], [m], out=m.ap[:, 8:16], in0=m.ap[:, 8:16], in1=w.ap[:, 2, :], op=ALU.add)
                vop("dve", "reciprocal", [m], [m], out=m.ap[:, 8:16], in_=m.ap[:, 8:16])
                vop("dve", "tensor_tensor", [w, m], [w], out=w.ap, in0=w.ap, in1=m.ap[:, 8:16].unsqueeze(1).broadcast_to([128, 3, 8]), op=ALU.mult)
                for g in range(3):
                    eng = "pool" if g == 1 else "dve"
                    vop(eng, "tensor_tensor", [og[g], w], [og[g]], out=og[g].ap.rearrange("p (h d) -> p h d", h=8),
                        in0=og[g].ap.rearrange("p (h d) -> p h d", h=8), in1=w.ap[:, g, :].unsqueeze(2).broadcast_to([128, 8, 64]), op=ALU.mult)
                vop("dve", "tensor_tensor", [og[0], og[1]], [og[0]], out=og[0].ap, in0=og[0].ap, in1=og[1].ap, op=ALU.add)
                y = yo.next()
                vop("dve", "tensor_tensor", [og[0], og[2]], [y], out=y.ap, in0=og[0].ap, in1=og[2].ap, op=ALU.add)
                dma("sp", Y0_d[t * 128:(t + 1) * 128, 1024:1536], y.ap, y, [y], [])

        xT = T([128, 16, S], BF16, name="xT")
        load_T(x_in, D, xT, True)
        wring = TR(2, [128, 16, 512], BF16, dma="sw", name="wring")
        stgF = TR(2, [128, S], BF16, dma=True, name="stgF")
        stgT = TR(2, [128, 512], BF16, dma=True, name="stgT")
        proj_F(xT, 16, e_win[:, 0:512], 512, QA_d[0:512, :], 1, wring, stgF)
        proj_F(xT, 16, e_win[:, 512:1024], 512, QA_d[512:1024, :], 1, wring, stgF)
        proj_F(xT, 16, e_win[:, 1024:1152], 128, KA_d, 1, wring, stgF)
        proj_T(xT, 16, e_win[:, 1152:1280], 128, VA_d, BF16, wring, stgT)
        for g, dil in enumerate((1, 4, 16)):
            base = 1280 + g * 1536
            proj_F(xT, 16, e_win[:, base:base + 512], 512, QB_d[g], dil, wring, stgF)
            proj_F(xT, 16, e_win[:, base + 512:base + 1024], 512, KB_d[g], dil, wring, stgF)
            proj_T(xT, 16, e_win[:, base + 1024:base + 1536], 512, VB_d[g], BF16, wring, stgT)
        phase_end()
        bg_fill(0)
        slA = alibi(16)
        banded(QA_d, KA_d, lambda h: h // 8, 16, VA_d, 2, 1, RA, [8.0 * sl for sl in slA], True, "A", Yd=Y0_d)
        phase_end()
        slB = alibi(8)
        for g, dil in enumerate((1, 4, 16)):
            banded(QB_d[g], KB_d[g], lambda h: h, 8, VB_d[g], 8, dil, RB, [8.0 * sl * dil for sl in slB], False, "B",
                   OBd=OB_d[g], LSEd=LSE_d[g])
            phase_end()
        out_proj_ln(Y0_d, 12, e_wout, x_in, ln1_g[0:1, :], ln1_b[0:1, :], xs1, ncl=1024, extra=True)
        phase_end()
        bg_flush()
        mlp(0, xs1, ln2_g[0:1, :], ln2_b[0:1, :], xs2 if "stop0" not in dbg else out_d)
        phase_end()

        if "stop0" not in dbg:

            base_persist = persist_end[0]
            bg_fill(1)
            cosT = T([128, S], F32, name="cosT")
            sinT = T([128, S], F32, name="sinT")
            persist_end[0] = aoff[0]
            pidx = T([128, 2], I32, name="pidx")
            pf = T([128, 2], F32, name="pf")
            vop("pool", "iota", [], [pidx], out=pidx.ap[:, 0:1], pattern=[[0, 1]], base=0, channel_multiplier=1)
            vop("dve", "tensor_single_scalar", [pidx], [pidx], out=pidx.ap[:, 1:2], in_=pidx.ap[:, 0:1], scalar=15, op=ALU.bitwise_and)
            cp("dve", pf.ap[:, 0:1], pidx.ap[:, 1:2], [pidx], [pf])
            act(pf.ap[:, 1:2], pf.ap[:, 0:1], AF.Exp, [pf], [pf], scale=-math.log(10000.0) / 16.0)
            vop("dve", "tensor_single_scalar", [pf], [pf], out=pf.ap[:, 1:2], in_=pf.ap[:, 1:2], scalar=1.0 / (2 * math.pi), op=ALU.mult)
            tpi = T([128, S], I32, name="tpi")
            tpf = T([128, S], F32, name="tpf")
            tq = T([128, S], F32, name="tq")
            vop("pool", "iota", [], [tpi], out=tpi.ap, pattern=[[1, S]], base=0, channel_multiplier=0)
            cp("dve", tpf.ap, tpi.ap, [tpi], [tpf])
            for tab, offs in ((sinT, 0.0), (cosT, 0.25)):
                vop("dve", "tensor_scalar", [tpf, pf], [tq], out=tq.ap, in0=tpf.ap, scalar1=pf.ap[:, 1:2], scalar2=offs,
                    op0=ALU.mult, op1=ALU.add)
                cp("dve", tpi.ap, tq.ap, [tq], [tpi])
                cp("dve", tab.ap, tpi.ap, [tpi], [tab])
                vop("dve", "tensor_tensor", [tq, tab], [tq], out=tq.ap, in0=tq.ap, in1=tab.ap, op=ALU.subtract)
                vop("dve", "scalar_tensor_tensor", [tq], [tq], out=tq.ap, in0=tq.ap, scalar=0.5, in1=tq.ap,
                    op0=ALU.is_gt, op1=ALU.subtract)
                act(tab.ap, tq.ap, AF.Sin, [tq], [tab], scale=-2.0 * math.pi)
            phase_end()

            def rope_evac(ps_m, ps_r, st, tg, tmpr):
                ta = tmpr.next()
                tb = tmpr.next()
                cs = slice(tg * 512, (tg + 1) * 512)
                vop("dve", "tensor_tensor", [ps_r, sinT], [ta], out=ta.ap[64:96, :], in0=ps_r.ap[64:96, :], in1=sinT.ap[64:96, cs], op=ALU.mult)
                vop("dve", "tensor_tensor", [ps_m, cosT], [tb], out=tb.ap[64:96, :], in0=ps_m.ap[64:96, :], in1=cosT.ap[64:96, cs], op=ALU.mult)
                vop("pool", "tensor_tensor", [ta, tb], [st], out=st.ap[64:96, cs], in0=ta.ap[64:96, :], in1=tb.ap[64:96, :], op=ALU.add)

            xT = T([128, 16, S], BF16, name="xT1")
            load_T(xs2, D, xT, True)
            wring = TR(2, [128, 16, 512], BF16, dma="sw", name="wring")
            stgF = TR(2, [128, S], BF16, dma=True, name="stgF")
            stgT = TR(2, [128, 512], BF16, dma=True, name="stgT")
            stgT32 = TR(2, [128, 512], F32, dma=True, name="stgT32")
            for i in range(2):
                proj_F(xT, 16, o_win[:, i * 512:(i + 1) * 512], 512, QC_d[i * 512:(i + 1) * 512, :], 1, wring, stgF)
                proj_F(xT, 16, o_win[:, 1024 + i * 512:1024 + (i + 1) * 512], 512, KC_d[i * 512:(i + 1) * 512, :], 1, wring, stgF)
                proj_T(xT, 16, o_win[:, 2048 + i * 512:2048 + (i + 1) * 512], 512, VC_d[:, i * 512:(i + 1) * 512], BF16, wring, stgT)
            proj_T(xT, 16, o_win[:, 3072:3584], 512, CQ_d[:, 0:512], F32, wring, stgT32)
            proj_T(xT, 16, o_win[:, 3584:3840], 256, CQ_d[:, 512:768], F32, wring, stgT32)
            wkr = T([128, 16, 96], BF16, dma="sw", name="wkr")
            wkrot = T([128, 16, 96], BF16, name="wkrot")
            vop("dve", "memset", [], [wkr], ap=wkr.ap, constant=0.0)
            vop("pool", "memset", [], [wkrot], ap=wkrot.ap, constant=0.0)
            dma("pool", wkr.ap[:, :, 64:96], o_win[:, 3840:3872].rearrange("(k p) e -> p k e", p=128), wkr, [], [wkr])
            vop("dve", "tensor_single_scalar", [wkr], [wkrot], out=wkrot.ap[:, :, 64:80], in_=wkr.ap[:, :, 80:96], scalar=-1.0, op=ALU.mult)
            cp("dve", wkrot.ap[:, :, 80:96], wkr.ap[:, :, 64:80], [wkr], [wkrot])
            tmpr = TR(4, [128, 512], F32, name="ropetmp")
            stK = T([128, S], BF16, dma=True, name="stKr")
            for tg in range(4):
                pm, pr = PB[0], PB[1]
                for k in range(16):
                    mm(pm.ap[0:96, :], wkr.ap[:, k, :], xT.ap[:, k, tg * 512:(tg + 1) * 512], k == 0, k == 15, [wkr, xT], [pm])
                for k in range(16):
                    mm(pr.ap[0:96, :], wkrot.ap[:, k, :], xT.ap[:, k, tg * 512:(tg + 1) * 512], k == 0, k == 15, [wkrot, xT], [pr])
                rope_evac(pm, pr, stK, tg, tmpr)
            for h in range(16):
                dma("sp", KD_d[h, 64:96, :], stK.ap[64:96, :], stK, [stK], [])
            phase_end()

            cT = T([128, 6, S], BF16, name="cT")
            gq = T([128, 768], F32, dma=True, name="gq")
            dma("sp", gq.ap[:, 0:512], o_qg.partition_broadcast(128), gq, [], [gq])
            dma("sp", gq.ap[:, 512:768], o_kvg.partition_broadcast(128), gq, [], [gq])
            cl = TR(2, [128, 768], F32, dma=True, name="cl")
            junk = TR(2, [128, 768], F32, name="junk")
            cbf = TR(2, [128, 768], BF16, name="cbf")
            str_ = TR(2, [128, 8], F32, name="st")
            pbt = Ring([PB[6], PB[7]])
            for t in range(NT):
                c = cl.next()
                dma("sp", c.ap, CQ_d[t * 128:(t + 1) * 128, :], c, [], [c])
                jk = junk.next()
                st = str_.next()
                act(jk.ap[:, 0:512], c.ap[:, 0:512], AF.Square, [c], [jk, st], accum=st.ap[:, 0:1])
                act(jk.ap[:, 512:768], c.ap[:, 512:768], AF.Square, [c], [jk, st], accum=st.ap[:, 1:2])
                vop("dve", "tensor_scalar", [st], [st], out=st.ap[:, 2:3], in0=st.ap[:, 0:1], scalar1=1.0 / 512.0, scalar2=RMS_EPS, op0=ALU.mult, op1=ALU.add)
                vop("dve", "tensor_scalar", [st], [st], out=st.ap[:, 3:4], in0=st.ap[:, 1:2], scalar1=1.0 / 256.0, scalar2=RMS_EPS, op0=ALU.mult, op1=ALU.add)
                act(st.ap[:, 4:6], st.ap[:, 2:4], AF.Sqrt, [st], [st])
                vop("dve", "reciprocal", [st], [st], out=st.ap[:, 6:8], in_=st.ap[:, 4:6])
                vop("pool", "tensor_tensor", [c, gq], [c], out=c.ap, in0=c.ap, in1=gq.ap, op=ALU.mult)
                cb = cbf.next()
                vop("dve", "tensor_scalar", [c, st], [cb], out=cb.ap[:, 0:512], in0=c.ap[:, 0:512], scalar1=st.ap[:, 6:7], scalar2=None, op0=ALU.mult)
                vop("dve", "tensor_scalar", [c, st], [cb], out=cb.ap[:, 512:768], in0=c.ap[:, 512:768], scalar1=st.ap[:, 7:8], scalar2=None, op0=ALU.mult)
                pb = pbt.next()
                pv = pb.ap.bitcast(BF16)
                for k in range(6):
                    s.op("pe", (lambda pv=pv, k=k, cb=cb: (lambda e: e.transpose(pv[:, k * 128:(k + 1) * 128], cb.ap[:, k * 128:(k + 1) * 128], ident.ap)))(),
                         rs([cb, ident]), rs([pb]))
                cp("act", cT.ap[:, :, t * 128:(t + 1) * 128], pv[:, 0:768].rearrange("p (k t) -> p k t", k=6), [pb], [cT])
            wq = T([128, 4, 1536], BF16, dma="sw", name="wq")
            wqr = T([128, 4, 1536], BF16, name="wqr")
            dma("pool", wq.ap, o_wuq.rearrange("(k p) e -> p k e", p=128), wq, [], [wq])
            wq4 = wq.ap.rearrange("p k (h e) -> p k h e", h=16)
            wqr4 = wqr.ap.rearrange("p k (h e) -> p k h e", h=16)
            cp("dve", wqr.ap, wq.ap, [wq], [wqr])
            for k in range(4):
                vop("dve", "tensor_single_scalar", [wq, wqr], [wqr], out=wqr4[:, k, :, 64:80], in_=wq4[:, k, :, 80:96], scalar=-1.0, op=ALU.mult)
                cp("dve", wqr4[:, k, :, 80:96], wq4[:, k, :, 64:80], [wq, wqr], [wqr])
            wkv = T([128, 2, 2048], BF16, dma="sw", name="wkv")
            dma("pool", wkv.ap, o_wukv.rearrange("(k p) e -> p k e", p=128), wkv, [], [wkv])
            stQ = TR(2, [128, S], BF16, dma=True, name="stQ")
            stKn = TR(2, [128, S], BF16, dma=True, name="stKn")
            stV = TR(2, [128, 1024], BF16, dma=True, name="stV")
            tmpr = TR(4, [128, 512], F32, name="ropetmp")
            pbq = Ring(PB[0:6])
            for h in range(16):
                sq = stQ.next()
                for tg in range(4):
                    pm = pbq.next()
                    pr = pbq.next()
                    for k in range(4):
                        mm(pm.ap[0:96, :], wq4[:, k, h, :], cT.ap[:, k, tg * 512:(tg + 1) * 512], k == 0, k == 3, [wq, cT], [pm])
                    for k in range(4):
                        mm(pr.ap[0:96, :], wqr4[:, k, h, :], cT.ap[:, k, tg * 512:(tg + 1) * 512], k == 0, k == 3, [wqr, cT], [pr])
                    cp("act", sq.ap[0:64, tg * 512:(tg + 1) * 512], pm.ap[0:64, :], [pm], [sq])
                    rope_evac(pm, pr, sq, tg, tmpr)
                dma("sp", QD_d[h], sq.ap[0:96, :], sq, [sq], [])
                sk = stKn.next()
                for tg in range(4):
                    pk = pbq.next()
                    for k in range(2):
                        mm(pk.ap[0:64, :], wkv.ap[:, k, h * 128:h * 128 + 64], cT.ap[:, 4 + k, tg * 512:(tg + 1) * 512], k == 0, k == 1, [wkv, cT], [pk])
                    cp(ev_eng(), sk.ap[0:64, tg * 512:(tg + 1) * 512], pk.ap[0:64, :], [pk], [sk])
                dma("sp", KD_d[h, 0:64, :], sk.ap[0:64, :], sk, [sk], [])
            wkv5 = wkv.ap.rearrange("p k (h two d) -> p k h two d", two=2, d=64)
            for t in range(NT):
                sv = stV.next()
                for hf in range(2):
                    pb = pbq.next()
                    for k in range(2):
                        mm(pb.ap.rearrange("p (h d) -> p h d", h=8), cT.ap[:, 4 + k, t * 128:(t + 1) * 128], wkv5[:, k, hf * 8:(hf + 1) * 8, 1, :],
                           k == 0, k == 1, [wkv, cT], [pb])
                    cp(ev_eng(), sv.ap[:, hf * 512:(hf + 1) * 512], pb.ap, [pb], [sv])
                dma("sp", VD_d[t * 128:(t + 1) * 128, :], sv.ap, sv, [sv], [])
            persist_end[0] = base_persist
            phase_end()

            def attn_C():
                VCp = T([128, NT, 1024], BF16, dma=True, name="VCp")
                dma("sp", VCp.ap, VC_d.rearrange("(n p) c -> p n c", p=128), VCp, [], [VCp])
                OC = T([128, NT, 1024], BF16, dma=True, name="OC")
                Qr = TR(3, [64, S], BF16, dma=True, name="Qh")
                Kr = TR(3, [64, S], BF16, dma=True, name="Kh")
                Er = TR(4, [128, 512], F32, name="E")
                SPr = TR(4, [128, 512], F32, name="SP")
                LKr = TR(4, [128, 512], F32, name="LK")
                Wr = TR(4, [128, 512], BF16, name="W")
                Srun = TR(2, [128, 512], F32, name="Srun")
                pZ = Ring([PB[0], PB[1]])
                pA = Ring([PB[2], PB[3]])
                pO = PB[4:8]
                heads = [(Qr.next(), Kr.next()) for _ in range(16)]

                def load_head(h):
                    Qh, Kh = heads[h]
                    dma("sp", Qh.ap, QC_d[h * 64:(h + 1) * 64, :], Qh, [], [Qh])
                    dma("sp", Kh.ap, KC_d[h * 64:(h + 1) * 64, :], Kh, [], [Kh])
                items = []
                for h in range(16):
                    for G in range(4):
                        Sr = Srun.next()
                        for j in range(4 * G + 3, -1, -1):
                            items.append((h, G, j, Sr))
                ctx = [dict(pz=pZ.next(), pa=pA.next(), E=Er.next(), SP=SPr.next(), LK=LKr.next(), W=Wr.next()) for _ in items]
                load_head(0)

                def s1(it):
                    h, G, j, Sr = items[it]
                    c = ctx[it]
                    if G == 0 and j == 3 and h + 1 < 16:
                        load_head(h + 1)
                    if it % 16 == 7:
                        bg_tick()
                    Qh, Kh = heads[h]
                    c0 = max(j - 4 * G, 0) * 128
                    pz, E, SP, LK = c["pz"], c["E"], c["SP"], c["LK"]
                    mm(pz.ap[:, c0:512], Kh.ap[:, j * 128:(j + 1) * 128], Qh.ap[:, G * 512 + c0:(G + 1) * 512], True, True, [Kh, Qh], [pz])
                    act(E.ap[:, c0:512], pz.ap[:, c0:512], AF.Exp, [pz], [E], scale=-0.125)
                    act(SP.ap[:, c0:512], E.ap[:, c0:512], AF.Ln, [E], [SP], bias=1.0, scale=1.0)
                    vop("dve", "scalar_tensor_tensor", [pz, SP], [LK], out=LK.ap[:, c0:512], in0=pz.ap[:, c0:512], scalar=-0.125,
                        in1=SP.ap[:, c0:512], op0=ALU.mult, op1=ALU.subtract)
                    if j >= 4 * G:
                        vop("pool", "tensor_tensor", [LK, mC01], [LK], out=LK.ap[:, c0:c0 + 128], in0=LK.ap[:, c0:c0 + 128], in1=mC01.ap, op=ALU.mult)

                def s2(it):
                    h, G, j, Sr = items[it]
                    c = ctx[it]
                    c0 = max(j - 4 * G, 0) * 128
                    pa, E, SP, LK, W = c["pa"], c["E"], c["SP"], c["LK"], c["W"]
                    first = j == 4 * G + 3
                    if first:
                        vop("pool", "memset", [], [Sr], ap=Sr.ap, constant=0.0)
                    mm(pa.ap[:, c0:512], Ustr.ap, LK.ap[:, c0:512], True, first, [Ustr, LK], [pa])
                    if not first:
                        mm(pa.ap[:, c0:512], ones.ap, Sr.ap[:, c0:512], False, True, [ones, Sr], [pa])
                    vop("dve", "tensor_tensor", [pa, SP], [E], out=E.ap[:, c0:512], in0=pa.ap[:, c0:512], in1=SP.ap[:, c0:512], op=ALU.subtract)
                    act(W.ap[:, c0:512], E.ap[:, c0:512], AF.Exp, [E], [W])
                    if j >= 4 * G:
                        vop("pool", "tensor_tensor", [W, mC01b], [W], out=W.ap[:, c0:c0 + 128], in0=W.ap[:, c0:c0 + 128], in1=mC01b.ap, op=ALU.mult)
                    if j > 0:
                        vop("pool", "tensor_tensor", [Sr, LK], [Sr], out=Sr.ap[:, c0:512], in0=Sr.ap[:, c0:512], in1=LK.ap[:, c0:512], op=ALU.add)

                def s3(it):
                    h, G, j, Sr = items[it]
                    c = ctx[it]
                    q0 = max(j - 4 * G, 0)
                    W = c["W"]
                    for qt in range(q0, 4):
                        mm(pO[qt].ap[:, 0:64], W.ap[:, qt * 128:(qt + 1) * 128], VCp.ap[:, j, h * 64:(h + 1) * 64],
                           j == 4 * G + qt, j == 0, [W, VCp], [pO[qt]])
                    if j == 0:
                        for qt in range(4):
                            cp("act" if qt % 2 else "dve", OC.ap[:, 4 * G + qt, h * 64:(h + 1) * 64], pO[qt].ap[:, 0:64], [pO[qt]], [OC])

                run_pipeline(len(items), [s1, s2, s3])
                for n in range(NT):
                    dma("sp", Y1_d[n * 128:(n + 1) * 128, 0:1024], OC.ap[:, n, :], OC, [OC], [])

            def attn_D():
                sc = 1.0 / math.sqrt(96.0)
                VDp = T([128, NT, 1024], BF16, dma=True, name="VDp")
                dma("sp", VDp.ap, VD_d.rearrange("(n p) c -> p n c", p=128), VDp, [], [VDp])
                OD = T([128, NT, 1024], BF16, dma=True, name="OD")
                Qr = TR(3, [96, S], BF16, dma=True, name="Qh")
                Kr = TR(3, [96, S], BF16, dma=True, name="Kh")
                Pr = TR(4, [128, S], BF16, name="P")
                PTr = TR(4, [128, S], BF16, name="PT")
                Sdr = TR(3, [128, 128], F32, name="Sd")
                str_ = TR(6, [128, 16], F32, name="st")
                pSa = Ring([PB[0:2], PB[2:4]])
                pT = Ring([PB[4], PB[5]])
                pO = Ring([PB[6], PB[7]])
                heads = [(Qr.next(), Kr.next()) for _ in range(16)]

                def load_head(h):
                    Qh, Kh = heads[h]
                    dma("sp", Qh.ap, QD_d[h], Qh, [], [Qh])
                    dma("sp", Kh.ap, KD_d[h], Kh, [], [Kh])
                items = [(h, i) for h in range(16) for i in range(NT)]
                ctx = []
                for (h, i) in items:
                    nbank = (i + 4) // 4
                    pS = pSa.next() if nbank <= 2 else PB[0:4]
                    ctx.append(dict(pS=pS, Sd=Sdr.next(), st=str_.next(), Pt=Pr.next(), PT=PTr.next(), po=pO.next()))
                load_head(0)

                def geom(i):
                    nkb = i + 1
                    nbank = (nkb + 3) // 4
                    widths = [min(512, nkb * 128 - bk * 512) for bk in range(nbank)]
                    return nkb, nbank, widths

                def sA(it):
                    h, i = items[it]
                    c = ctx[it]
                    if i == 0 and h + 1 < 16:
                        load_head(h + 1)
                    Qh, Kh = heads[h]
                    nkb, nbank, widths = geom(i)
                    pS, Sd, st, Pt = c["pS"], c["Sd"], c["st"], c["Pt"]
                    for bk in range(nbank):
                        mm(pS[bk].ap[:, 0:widths[bk]], Qh.ap[:, i * 128:(i + 1) * 128], Kh.ap[:, bk * 512:bk * 512 + widths[bk]],
                           True, True, [Qh, Kh], [pS[bk]])
                    bd = nbank - 1
                    dc = widths[bd] - 128
                    vop("dve", "tensor_tensor", [pS[bd], McD], [Sd], out=Sd.ap, in0=pS[bd].ap[:, dc:dc + 128], in1=McD.ap, op=ALU.add)
                    vop("dve", "reduce_max", [Sd], [st], out=st.ap[:, 0:1], in_=Sd.ap, axis=AX.X)
                    ncol = 1
                    for bk in range(nbank):
                        wv = widths[bk] - (128 if bk == bd else 0)
                        if wv > 0:
                            vop("dve", "reduce_max", [pS[bk]], [st], out=st.ap[:, ncol:ncol + 1], in_=pS[bk].ap[:, 0:wv], axis=AX.X)
                            ncol += 1
                    if ncol > 1:
                        vop("dve", "reduce_max", [st], [st], out=st.ap[:, 5:6], in_=st.ap[:, 0:ncol], axis=AX.X)
                        mxc = st.ap[:, 5:6]
                    else:
                        mxc = st.ap[:, 0:1]
                    vop("dve", "tensor_single_scalar", [st], [st], out=st.ap[:, 6:7], in_=mxc, scalar=-sc, op=ALU.mult)
                    act(Pt.ap[:, i * 128:(i + 1) * 128], Sd.ap, AF.Exp, [Sd, st], [Pt, st], bias=st.ap[:, 6:7], scale=sc, accum=st.ap[:, 8:9])
                    ncol = 1
                    for bk in range(nbank):
                        wv = widths[bk] - (128 if bk == bd else 0)
                        if wv > 0:
                            act(Pt.ap[:, bk * 512:bk * 512 + wv], pS[bk].ap[:, 0:wv], AF.Exp, [pS[bk], st], [Pt, st],
                                bias=st.ap[:, 6:7], scale=sc, accum=st.ap[:, 8 + ncol:9 + ncol])
                            ncol += 1
                    c["ncol"] = ncol

                def sB(it):
                    h, i = items[it]
                    c = ctx[it]
                    nkb, nbank, widths = geom(i)
                    Pt, PT = c["Pt"], c["PT"]
                    for k0 in range(0, nkb, 8):
                        kn = min(8, nkb - k0)
                        pt = pT.next()
                        ptv = pt.ap.bitcast(BF16)
                        for jj in range(kn):
                            kb = k0 + jj
                            s.op("pe", (lambda ptv=ptv, jj=jj, Pt=Pt, kb=kb: (lambda e: e.transpose(ptv[:, jj * 128:(jj + 1) * 128], Pt.ap[:, kb * 128:(kb + 1) * 128], ident.ap)))(),
                                 rs([Pt, ident]), rs([pt]))
                        cp("act" if (k0 // 8) % 2 == 0 else "dve", PT.ap[:, k0 * 128:(k0 + kn) * 128], ptv[:, 0:kn * 128], [pt], [PT])

                def sC(it):
                    h, i = items[it]
                    c = ctx[it]
                    nkb, nbank, widths = geom(i)
                    PT, po, st = c["PT"], c["po"], c["st"]
                    ncol = c["ncol"]
                    for kb in range(nkb):
                        mm(po.ap[:, 0:64], PT.ap[:, kb * 128:(kb + 1) * 128], VDp.ap[:, kb, h * 64:(h + 1) * 64], kb == 0, kb == nkb - 1, [PT, VDp], [po])
                    if ncol > 1:
                        vop("dve", "reduce_sum", [st], [st], out=st.ap[:, 7:8], in_=st.ap[:, 8:8 + ncol], axis=AX.X)
                        den = st.ap[:, 7:8]
                    else:
                        den = st.ap[:, 8:9]
                    vop("dve", "reciprocal", [st], [st], out=st.ap[:, 14:15], in_=den)
                    vop("dve", "tensor_scalar", [po, st], [OD], out=OD.ap[:, i, h * 64:(h + 1) * 64], in0=po.ap[:, 0:64],
                        scalar1=st.ap[:, 14:15], scalar2=None, op0=ALU.mult)

                run_pipeline(len(items), [sA, sB, sC])
                for n in range(NT):
                    dma("sp", Y1_d[n * 128:(n + 1) * 128, 1024:2048], OD.ap[:, n, :], OD, [OD], [])

            if "skipC" not in dbg:
                attn_C()
                phase_end()
            if "skipD" not in dbg:
                attn_D()
                phase_end()
            out_proj_ln(Y1_d, 16, o_wout, xs2, ln1_g[1:2, :], ln1_b[1:2, :], xs1)
            phase_end()
            bg_flush()
            mlp(1, xs1, ln2_g[1:2, :], ln2_b[1:2, :], out_d)

        s.barrier(final=True)
        s.emit()
    return nc


_NC_CACHE = {}


def kernel(**inputs):
    B = inputs["x"].shape[0]
    if "nc" not in _NC_CACHE:
        _NC_CACHE["nc"] = build()
    nc = _NC_CACHE["nc"]
    f = lambda a: np.ascontiguousarray(np.asarray(a, dtype=np.float32))
    shared = {
        "even_w_in": f(inputs["even_w_in"][0]),
        "even_sinks": f(inputs["even_sinks"][0]).reshape(1, 16),
        "even_w_out": f(inputs["even_w_out"][0]),
        "odd_w_in": f(inputs["odd_w_in"][0]),
        "odd_q_norm_g": f(inputs["odd_q_norm_g"][0]).reshape(1, 512),
        "odd_kv_norm_g": f(inputs["odd_kv_norm_g"][0]).reshape(1, 256),
        "odd_w_uq": f(inputs["odd_w_uq"][0]),
        "odd_w_ukv": f(inputs["odd_w_ukv"][0]),
        "odd_w_out": f(inputs["odd_w_out"][0]),
        "ln1_g": f(inputs["ln1_g"]), "ln1_b": f(inputs["ln1_b"]),
        "ln2_g": f(inputs["ln2_g"]), "ln2_b": f(inputs["ln2_b"]),
        "mlp_w1": f(inputs["mlp_w1"]), "mlp_w2": f(inputs["mlp_w2"]),
    }
    x = f(inputs["x"])
    in_maps = [dict(shared, x=x[b]) for b in range(B)]
    res = run_bass_kernel_spmd(nc, in_maps, core_ids=list(range(B)))
    return np.stack([r["out"] for r in res.results], axis=0)
```

```python
import contextlib
import math
import numpy as np
import concourse.bass as bass
import concourse.mybir as mybir
from concourse.bass_utils import run_bass_kernel_spmd

F32 = mybir.dt.float32
BF16 = mybir.dt.bfloat16
I32 = mybir.dt.int32
AF = mybir.ActivationFunctionType
ALU = mybir.AluOpType
AX = mybir.AxisListType

S = 2048
D = 2048
NT = 16
DFF = 8192
ALPHA = 4.0 ** 0.25
LN_EPS = 1e-5
RMS_EPS = 1e-6
BIG = 1.0e9


class Res:
    __slots__ = ("name", "w", "r")

    def __init__(self, name=""):
        self.name = name
        self.w = None
        self.r = {}


class Buf:
    __slots__ = ("ap", "res", "sem")

    def __init__(self, ap, res, sem=None):
        self.ap = ap
        self.res = res
        self.sem = sem


class Ring:
    def __init__(self, items):
        self.items = items
        self.i = 0

    def next(self):
        it = self.items[self.i % len(self.items)]
        self.i += 1
        return it


class Sched:
    ENG = ("pe", "act", "dve", "pool", "sp")

    def __init__(self, nc, stack):
        self.nc = nc
        self.stack = stack
        self.prog = {e: [] for e in self.ENG}
        self.sem = {}
        self.cnt = {}
        self.known = {e: {} for e in self.ENG}
        self.free_dsems = []
        self.used_dsems = []
        self.ndsem = 0
        for e in self.ENG:
            self.newsem("E_" + e)

    def newsem(self, name):
        self.sem[name] = self.stack.enter_context(self.nc.semaphore(name))
        self.cnt[name] = 0
        return name

    def dsem(self, kind="H"):
        fl = [x for x in self.free_dsems if x[0] == kind]
        if fl:
            n = fl[-1]
            self.free_dsems.remove(n)
        else:
            n = self.newsem("%s%d" % (kind, self.ndsem))
            self.ndsem += 1
        self.used_dsems.append(n)
        return n

    def _deps(self, eng, reads, writes):
        need = {}

        def add(ev):
            if ev is None:
                return
            sm, v = ev
            if need.get(sm, 0) < v:
                need[sm] = v
        for r in reads:
            add(r.w)
        for w in writes:
            add(w.w)
            for sm, v in w.r.items():
                add((sm, v))
        kn = self.known[eng]
        for sm, v in need.items():
            if eng == "pe" and sm == "E_pe":
                continue
            if kn.get(sm, 0) < v:
                kn[sm] = v
                self.prog[eng].append(("wait", sm, v))

    def _commit(self, ev, reads, writes):
        sm, v = ev
        for r in reads:
            if r.r.get(sm, 0) < v:
                r.r[sm] = v
        for w in writes:
            w.w = ev
            w.r = {}

    def op(self, eng, fn, reads=(), writes=()):
        self._deps(eng, reads, writes)
        sm = "E_" + eng
        self.cnt[sm] += 1
        ev = (sm, self.cnt[sm])
        self.prog[eng].append(("op", fn, sm, 1))
        self._commit(ev, reads, writes)

    def dma(self, eng, out, in_, sem, reads=(), writes=()):
        self._deps(eng, reads, writes)
        self.cnt[sem] += 16
        ev = (sem, self.cnt[sem])
        self.prog[eng].append(("op", lambda e: e.dma_start(out=out, in_=in_), sem, 16))
        self._commit(ev, reads, writes)

    def barrier(self, final=False):
        for e in self.ENG:
            kn = self.known[e]
            for sm, v in self.cnt.items():
                if sm.startswith("WC") and not final:
                    continue
                if v > 0 and kn.get(sm, 0) < v:
                    kn[sm] = v
                    self.prog[e].append(("wait", sm, v))
        self.free_dsems.extend(self.used_dsems)
        self.used_dsems = []

    def emit(self):
        nc = self.nc

        def replay(name):
            def f(eng):
                for it in self.prog[name]:
                    if it[0] == "wait":
                        eng.wait_ge(self.sem[it[1]], it[2])
                    else:
                        it[1](eng).then_inc(self.sem[it[2]], it[3])
            return f

        with nc.Block() as block:
            block.tensor(replay("pe"))
            block.scalar(replay("act"))
            block.vector(replay("dve"))
            block.gpsimd(replay("pool"))
            block.sync(replay("sp"))


def alibi(n):
    return [2.0 ** (-8.0 * (i + 1) / n) for i in range(n)]


def build(dbg=()):
    nc = bass.Bass("TRN2", target_bir_lowering=False)

    def din(name, shape):
        return nc.dram_tensor(name, list(shape), F32, kind="ExternalInput").ap()

    x_in = din("x", [S, D])
    e_win = din("even_w_in", [D, 5888])
    e_sinks = din("even_sinks", [1, 16])
    e_wout = din("even_w_out", [1536, D])
    o_win = din("odd_w_in", [D, 3872])
    o_qg = din("odd_q_norm_g", [1, 512])
    o_kvg = din("odd_kv_norm_g", [1, 256])
    o_wuq = din("odd_w_uq", [512, 1536])
    o_wukv = din("odd_w_ukv", [256, 2048])
    o_wout = din("odd_w_out", [D, D])
    ln1_g = din("ln1_g", [2, D])
    ln1_b = din("ln1_b", [2, D])
    ln2_g = din("ln2_g", [2, D])
    ln2_b = din("ln2_b", [2, D])
    w1_in = din("mlp_w1", [2, D, DFF])
    w2_in = din("mlp_w2", [2, DFF, D])
    out_d = nc.dram_tensor("out", [S, D], F32, kind="ExternalOutput").ap()

    def dscr(name, shape, dt):
        kind = "ExternalOutput" if name in dbg else "Internal"
        return nc.dram_tensor(name, list(shape), dt, kind=kind).ap()

    w1b = dscr("w1b", [2, D, DFF], BF16)
    w2b = dscr("w2b", [2, DFF, D], BF16)
    xs1 = dscr("xs1", [S, D], F32)
    xs2 = dscr("xs2", [S, D], F32)
    QA_d = dscr("QA_d", [1024, S], BF16)
    KA_d = dscr("KA_d", [128, S], BF16)
    VA_d = dscr("VA_d", [S, 128], BF16)
    QB_d = [dscr("QB%d_d" % g, [512, S], BF16) for g in range(3)]
    KB_d = [dscr("KB%d_d" % g, [512, S], BF16) for g in range(3)]
    VB_d = [dscr("VB%d_d" % g, [S, 512], BF16) for g in range(3)]
    OB_d = [dscr("OB%d_d" % g, [S, 512], F32) for g in range(3)]
    LSE_d = [dscr("LSE%d_d" % g, [S, 8], F32) for g in range(3)]
    Y0_d = dscr("Y0_d", [S, 1536], BF16)
    QC_d = dscr("QC_d", [1024, S], BF16)
    KC_d = dscr("KC_d", [1024, S], BF16)
    VC_d = dscr("VC_d", [S, 1024], BF16)
    CQ_d = dscr("CQ_d", [S, 768], F32)
    QD_d = dscr("QD_d", [16, 96, S], BF16)
    KD_d = dscr("KD_d", [16, 96, S], BF16)
    VD_d = dscr("VD_d", [S, 1024], BF16)
    Y1_d = dscr("Y1_d", [S, 2048], BF16)

    with contextlib.ExitStack() as stack:
        s = Sched(nc, stack)
        ARENA_ELEMS = 103 * 1024
        arena = nc.alloc_sbuf_tensor("arena", [128, ARENA_ELEMS], BF16)
        aoff = [0]
        persist_end = [0]

        def T(shape, dt, dma=False, name=""):
            esz = 2 if dt == BF16 else 4
            nel = int(np.prod(shape[1:]))
            nb16 = (nel * esz + 63) // 64 * 32
            assert aoff[0] + nb16 <= ARENA_ELEMS, "arena overflow %s %d" % (name, aoff[0] + nb16)
            v = arena[:, aoff[0]:aoff[0] + nel * esz // 2]
            aoff[0] += nb16
            if dt != BF16:
                v = v.bitcast(dt)
            if len(shape) == 3:
                v = v.rearrange("p (a b) -> p a b", a=shape[1])
            elif len(shape) == 4:
                v = v.rearrange("p (a b c) -> p a b c", a=shape[1], b=shape[2])
            if shape[0] != 128:
                v = v[0:shape[0]]
            return Buf(v, Res(name), s.dsem("W" if dma == "sw" else "H") if dma else None)

        def TR(n, shape, dt, dma=False, name=""):
            return Ring([T(shape, dt, dma, name) for _ in range(n)])

        def phase_end():
            s.barrier()
            aoff[0] = persist_end[0]

        PB = [Buf(nc.alloc_psum_tensor("pb%d" % i, [128, 512], F32)[:], Res("pb%d" % i)) for i in range(8)]

        def rs(bufs):
            return [b.res for b in bufs]

        def mm(out, lhsT, rhs, start, stop, rd, wr):
            s.op("pe", lambda e: e.matmul(out, lhsT, rhs, start=start, stop=stop), rs(rd), rs(wr))

        def act(out, in_, func, rd, wr, bias=None, scale=None, accum=None, eng="act"):
            kw = {}
            if bias is not None:
                kw["bias"] = bias
            if scale is not None:
                kw["scale"] = scale
            if accum is not None:
                kw["accum_out"] = accum
            s.op(eng, lambda e: e.activation(out=out, in_=in_, func=func, **kw), rs(rd), rs(wr))

        def vop(eng, meth, rd, wr, **kw):
            s.op(eng, lambda e: getattr(e, meth)(**kw), rs(rd), rs(wr))

        def cp(eng, out, in_, rd, wr):
            if eng == "act":
                s.op("act", lambda e: e.copy(out=out, in_=in_), rs(rd), rs(wr))
            else:
                s.op(eng, lambda e: e.tensor_copy(out=out, in_=in_), rs(rd), rs(wr))

        def dma(eng, out, in_, buf, rd=(), wr=()):
            assert (eng == "pool") == (buf.sem[0] == "W"), (eng, buf.sem)
            s.dma(eng, out, in_, buf.sem, rs(rd), rs(wr))

        def run_pipeline(N, stages):
            ns = len(stages)
            for t in range(N + ns - 1):
                for si in range(ns):
                    it = t - si
                    if 0 <= it < N:
                        stages[si](it)

        ident = T([128, 128], BF16, name="ident")
        Ustr = T([128, 128], F32, name="Ustr")
        ones = T([128, 128], F32, name="ones")
        mC01 = T([128, 128], F32, name="mC01")
        mC01b = T([128, 128], BF16, name="mC01b")
        UstrB = T([128, 128], BF16, name="UstrB")
        PenC = T([128, 128], F32, name="PenC")
        McD = T([128, 128], F32, name="McD")
        RA = T([128, 256], F32, name="RA")
        RB = T([128, 256], F32, name="RB")
        sinkt = T([128, 16], F32, dma=True, name="sink")
        sink8 = T([128, 16], F32, name="sink8")
        persist_end[0] = aoff[0]
        tmpi = T([128, 256], I32, name="tmpi")
        tmpf = T([128, 256], F32, name="tmpf")
        tmpg = T([128, 256], F32, name="tmpg")
        tmph = T([128, 256], F32, name="tmph")
        vop("pool", "iota", [], [tmpi], out=tmpi.ap[:, 0:128], pattern=[[-1, 128]], base=0, channel_multiplier=1)
        cp("dve", tmpf.ap[:, 0:128], tmpi.ap[:, 0:128], [tmpi], [tmpf])
        vop("dve", "tensor_single_scalar", [tmpf], [ident], out=ident.ap, in_=tmpf.ap[:, 0:128], scalar=0.0, op=ALU.is_equal)
        vop("dve", "tensor_single_scalar", [tmpf], [Ustr], out=Ustr.ap, in_=tmpf.ap[:, 0:128], scalar=0.0, op=ALU.is_gt)
        vop("dve", "tensor_single_scalar", [tmpf], [mC01], out=mC01.ap, in_=tmpf.ap[:, 0:128], scalar=0.0, op=ALU.is_lt)
        cp("dve", mC01b.ap, mC01.ap, [mC01], [mC01b])
        cp("dve", UstrB.ap, Ustr.ap, [Ustr], [UstrB])
        vop("dve", "tensor_scalar", [Ustr], [PenC], out=PenC.ap, in0=Ustr.ap, scalar1=1.0, scalar2=-BIG, op0=ALU.subtract, op1=ALU.mult)
        vop("dve", "memset", [], [ones], ap=ones.ap, constant=1.0)
        vop("dve", "tensor_scalar", [tmpf], [McD], out=McD.ap, in0=tmpf.ap[:, 0:128], scalar1=0.0, scalar2=1.0,
            op0=ALU.is_ge, op1=ALU.subtract)
        vop("dve", "tensor_single_scalar", [McD], [McD], out=McD.ap, in_=McD.ap, scalar=BIG, op=ALU.mult)
        vop("pool", "iota", [tmpf], [tmpi], out=tmpi.ap, pattern=[[-1, 256]], base=128, channel_multiplier=1)
        cp("dve", tmpf.ap, tmpi.ap, [tmpi], [tmpf])
        for Rt, nb in ((RA, 127.0), (RB, 128.0)):
            vop("dve", "tensor_single_scalar", [tmpf], [tmpg], out=tmpg.ap, in_=tmpf.ap, scalar=0.0, op=ALU.is_ge)
            vop("dve", "tensor_single_scalar", [tmpf], [tmph], out=tmph.ap, in_=tmpf.ap, scalar=nb, op=ALU.is_le)
            vop("dve", "tensor_tensor", [tmpg, tmph], [tmpg], out=tmpg.ap, in0=tmpg.ap, in1=tmph.ap, op=ALU.mult)
            vop("dve", "tensor_tensor", [tmpg, tmpf], [tmph], out=tmph.ap, in0=tmpg.ap, in1=tmpf.ap, op=ALU.mult)
            vop("dve", "tensor_scalar", [tmpg], [tmpg], out=tmpg.ap, in0=tmpg.ap, scalar1=1.0, scalar2=BIG,
                op0=ALU.subtract, op1=ALU.mult)
            vop("dve", "tensor_tensor", [tmpg, tmph], [Rt], out=Rt.ap, in0=tmpg.ap, in1=tmph.ap, op=ALU.subtract)
        dma("sp", sinkt.ap, e_sinks.partition_broadcast(128), sinkt, [], [sinkt])
        vop("dve", "tensor_single_scalar", [sinkt], [sink8], out=sink8.ap, in_=sinkt.ap, scalar=8.0, op=ALU.mult)

        wcast = [Buf(None, Res("wc%d" % l), s.newsem("WC%d" % l)) for l in range(2)]
        bgq = []

        def bg_fill(l):
            jobs = [(w1b[l, r0:r0 + 128, :], w1_in[l, r0:r0 + 128, :]) for r0 in range(0, D, 128)]
            jobs += [(w2b[l, r0:r0 + 512, :], w2_in[l, r0:r0 + 512, :]) for r0 in range(0, DFF, 512)]
            for ji, (o_, i_) in enumerate(jobs):
                bgq.append((o_, i_, l, ji == len(jobs) - 1))

        def bg_tick(k=1):
            for _ in range(k):
                if not bgq:
                    return
                o_, i_, l, last = bgq.pop(0)
                dma("pool", o_, i_, wcast[l], [], [wcast[l]] if last else [])

        def bg_flush():
            bg_tick(len(bgq))
        phase_end()

        evq = [0]

        def ev_eng():
            evq[0] += 1
            return "act" if evq[0] % 2 else "dve"

        def load_T(src, ncol, dst, is_f32, keep=None):
            kc = ncol // 128
            ld = TR(2, [128, ncol], F32 if is_f32 else BF16, dma=True, name="ldT")
            cb = TR(2, [128, ncol], BF16, name="cbT") if is_f32 else None
            pbr = Ring([PB[6], PB[7]])
            for t in range(NT):
                lt = ld.next()
                dma("sp", lt.ap, src[t * 128:(t + 1) * 128, :], lt, [], [lt])
                if is_f32:
                    ct = cb.next()
                    cp("pool", ct.ap, lt.ap, [lt], [ct])
                else:
                    ct = lt
                for k0 in range(0, kc, 8):
                    kn = min(8, kc - k0)
                    pb = pbr.next()
                    pv = pb.ap.bitcast(BF16)
                    for j in range(kn):
                        k = k0 + j
                        s.op("pe", (lambda pv=pv, j=j, ct=ct, k=k: (lambda e: e.transpose(pv[:, j * 128:(j + 1) * 128], ct.ap[:, k * 128:(k + 1) * 128], ident.ap)))(),
                             rs([ct, ident]), rs([pb]))
                    cp(ev_eng(), dst.ap[:, k0:k0 + kn, t * 128:(t + 1) * 128],
                       pv[:, 0:kn * 128].rearrange("p (k t) -> p k t", k=kn), [pb], [dst])

        def wload(dst, wsrc, kc, ncol):
            dma("pool", dst.ap[:, 0:kc, 0:ncol], wsrc.rearrange("(k p) e -> p k e", p=128), dst, [], [dst])

        def proj_F(xT, kc, wsrc, ncol, dst, dil, wring, stg, oscale=None):
            wt = wring.next()
            wload(wt, wsrc, kc, ncol)
            pbr = Ring(PB[0:6])
            for c in range(ncol // 128):
                st = stg.next()
                for tg in range(4):
                    pb = pbr.next()
                    for k in range(kc):
                        mm(pb.ap, wt.ap[:, k, c * 128:(c + 1) * 128], xT.ap[:, k, tg * 512:(tg + 1) * 512],
                           k == 0, k == kc - 1, [wt, xT], [pb])
                    if oscale is not None:
                        if ev_eng() == "act":
                            s.op("act", (lambda o=st.ap[:, tg * 512:(tg + 1) * 512], i=pb.ap: (lambda e: e.mul(out=o, in_=i, mul=oscale)))(), rs([pb]), rs([st]))
                        else:
                            vop("dve", "tensor_single_scalar", [pb], [st], out=st.ap[:, tg * 512:(tg + 1) * 512], in_=pb.ap, scalar=oscale, op=ALU.mult)
                    elif dil == 1:
                        cp(ev_eng(), st.ap[:, tg * 512:(tg + 1) * 512], pb.ap, [pb], [st])
                    else:
                        na = 512 // dil
                        cp(ev_eng(), st.ap.rearrange("p (r a) -> p a r", r=dil)[:, tg * na:(tg + 1) * na, :],
                           pb.ap.rearrange("p (a r) -> p a r", r=dil), [pb], [st])
                dma("sp", dst[c * 128:(c + 1) * 128, :], st.ap, st, [st], [])

        def proj_T(xT, kc, wsrc, ncol, dst, dst_dt, wring, stg):
            wt = wring.next()
            wload(wt, wsrc, kc, ncol)
            pbr = Ring(PB[0:6])
            for t in range(NT):
                pb = pbr.next()
                for k in range(kc):
                    mm(pb.ap[:, 0:ncol], xT.ap[:, k, t * 128:(t + 1) * 128], wt.ap[:, k, 0:ncol],
                       k == 0, k == kc - 1, [wt, xT], [pb])
                st = stg.next()
                cp(ev_eng(), st.ap[:, 0:ncol], pb.ap[:, 0:ncol], [pb], [st])
                dma("sp", dst[t * 128:(t + 1) * 128, :], st.ap[:, 0:ncol], st, [st], [])

        def layer_norm_tile(z, gt, bt, stat):
            st6 = stat.ap[:, 0:24].rearrange("p (c s) -> p c s", c=4)
            for c in range(4):
                vop("dve", "bn_stats", [z], [stat], out=st6[:, c, :], in_=z.ap[:, c * 512:(c + 1) * 512])
            mv = stat.ap[:, 24:26]
            vop("dve", "bn_aggr", [stat], [stat], out=mv, in_=st6)
            vop("dve", "tensor_single_scalar", [stat], [stat], out=stat.ap[:, 26:27], in_=stat.ap[:, 25:26], scalar=LN_EPS, op=ALU.add)
            act(stat.ap[:, 27:28], stat.ap[:, 26:27], AF.Ln, [stat], [stat])
            act(stat.ap[:, 28:29], stat.ap[:, 27:28], AF.Exp, [stat], [stat], scale=-0.5)
            vop("dve", "tensor_scalar", [stat], [stat], out=stat.ap[:, 29:30], in0=stat.ap[:, 24:25], scalar1=stat.ap[:, 28:29], scalar2=-1.0,
                op0=ALU.mult, op1=ALU.mult)
            act(z.ap, z.ap, AF.Identity, [z, stat], [z], bias=stat.ap[:, 29:30], scale=stat.ap[:, 28:29])
            vop("dve", "tensor_tensor", [z, gt], [z], out=z.ap, in0=z.ap, in1=gt.ap, op=ALU.mult)
            vop("dve", "tensor_tensor", [z, bt], [z], out=z.ap, in0=z.ap, in1=bt.ap, op=ALU.add)

        def load_gb(g_src, b_src):
            gt = T([128, D], F32, dma=True, name="gam")
            bt = T([128, D], F32, dma=True, name="bet")
            dma("sp", gt.ap, g_src.partition_broadcast(128), gt, [], [gt])
            dma("sp", bt.ap, b_src.partition_broadcast(128), bt, [], [bt])
            return gt, bt

        def out_proj_ln(Yd, kc, wsrc, xres, g_src, b_src, dst, ncl=None, extra=False):
            ncl = ncl or kc * 128
            wt = T([128, kc, D], BF16, dma="sw", name="wout")
            for k0 in range(0, kc, 4):
                dma("pool", wt.ap[:, k0:k0 + 4, :], wsrc[k0 * 128:(k0 + 4) * 128, :].rearrange("(k p) e -> p k e", p=128), wt, [], [wt])
            gt, bt = load_gb(g_src, b_src)
            yl = TR(4, [128, kc * 128], BF16, dma=True, name="yl")
            yT = TR(3, [128, kc, 128], BF16, name="yT")
            zr = TR(4, [128, D], F32, dma=True, name="z")
            stat = TR(2, [128, 32], F32, name="stat")
            pbt = Ring([PB[4], PB[5], PB[6], PB[7]])
            if extra:
                ol = [TR(4, [128, 512], F32, dma=True, name="o%d" % g) for g in range(3)]
                ll = TR(4, [128, 3, 8], F32, dma=True, name="l")
                wk = TR(3, [128, 3, 8], F32, name="wk")
                sm = TR(3, [128, 16], F32, name="sm")

            def loads(t):
                y = yl.next()
                dma("sp", y.ap[:, 0:ncl], Yd[t * 128:(t + 1) * 128, 0:ncl], y, [], [y])
                og = lt = None
                if extra:
                    og = [ol[g].next() for g in range(3)]
                    lt = ll.next()
                    for g in range(3):
                        dma("sp", og[g].ap, OB_d[g][t * 128:(t + 1) * 128, :], og[g], [], [og[g]])
                        dma("sp", lt.ap[:, g, :], LSE_d[g][t * 128:(t + 1) * 128, :], lt, [], [lt])
                z = zr.next()
                dma("sp", z.ap, xres[t * 128:(t + 1) * 128, :], z, [], [z])
                return (t, y, og, lt, z)

            def mix_tile(t, y, og, lt):
                m = sm.next()
                vop("dve", "tensor_tensor", [lt], [m], out=m.ap[:, 0:8], in0=lt.ap[:, 0, :], in1=lt.ap[:, 1, :], op=ALU.max)
                vop("dve", "tensor_tensor", [lt, m], [m], out=m.ap[:, 0:8], in0=m.ap[:, 0:8], in1=lt.ap[:, 2, :], op=ALU.max)
                w = wk.next()
                vop("dve", "tensor_tensor", [lt, m], [w], out=w.ap, in0=lt.ap, in1=m.ap[:, 0:8].unsqueeze(1).broadcast_to([128, 3, 8]), op=ALU.subtract)
                act(w.ap, w.ap, AF.Exp, [w], [w])
                vop("dve", "tensor_tensor", [w], [m], out=m.ap[:, 8:16], in0=w.ap[:, 0, :], in1=w.ap[:, 1, :], op=ALU.add)
                vop("dve", "tensor_tensor", [w, m], [m], out=m.ap[:, 8:16], in0=m.ap[:, 8:16], in1=w.ap[:, 2, :], op=ALU.add)
                vop("dve", "reciprocal", [m], [m], out=m.ap[:, 8:16], in_=m.ap[:, 8:16])
                vop("dve", "tensor_tensor", [w, m], [w], out=w.ap, in0=w.ap, in1=m.ap[:, 8:16].unsqueeze(1).broadcast_to([128, 3, 8]), op=ALU.mult)
                for g in range(3):
                    eng = "dve" if g == 1 else "pool"
                    vop(eng, "tensor_tensor", [og[g], w], [og[g]], out=og[g].ap.rearrange("p (h d) -> p h d", h=8),
                        in0=og[g].ap.rearrange("p (h d) -> p h d", h=8), in1=w.ap[:, g, :].unsqueeze(2).broadcast_to([128, 8, 64]), op=ALU.mult)
                vop("pool", "tensor_tensor", [og[0], og[2]], [og[0]], out=og[0].ap, in0=og[0].ap, in1=og[2].ap, op=ALU.add)
                vop("dve", "tensor_tensor", [og[0], og[1]], [y], out=y.ap[:, 1024:1536], in0=og[0].ap, in1=og[1].ap, op=ALU.add)

            def prep(ld):
                t, y, og, lt, z = ld
                if extra:
                    mix_tile(t, y, og, lt)
                yt = yT.next()
                for k0 in range(0, kc, 8):
                    kn = min(8, kc - k0)
                    pb = pbt.next()
                    pv = pb.ap.bitcast(BF16)
                    for j in range(kn):
                        k = k0 + j
                        s.op("pe", (lambda pv=pv, j=j, y=y, k=k: (lambda e: e.transpose(pv[:, j * 128:(j + 1) * 128], y.ap[:, k * 128:(k + 1) * 128], ident.ap)))(),
                             rs([y, ident]), rs([pb]))
                    cp("act", yt.ap[:, k0:k0 + kn, :], pv[:, 0:kn * 128].rearrange("p (k t) -> p k t", k=kn), [pb], [yt])
                return yt, z

            lds = [loads(0), loads(1), loads(2)]
            pend = [prep(lds.pop(0)), prep(lds.pop(0))]
            for t in range(NT):
                if t + 3 < NT:
                    lds.append(loads(t + 3))
                if t + 2 < NT:
                    pend.append(prep(lds.pop(0)))
                yt, z = pend.pop(0)
                for dt in range(4):
                    pb = PB[dt]
                    for k in range(kc):
                        mm(pb.ap, yt.ap[:, k, :], wt.ap[:, k, dt * 512:(dt + 1) * 512], k == 0, k == kc - 1, [yt, wt], [pb])
                    vop("dve", "scalar_tensor_tensor", [z, pb], [z], out=z.ap[:, dt * 512:(dt + 1) * 512],
                        in0=z.ap[:, dt * 512:(dt + 1) * 512], scalar=ALPHA, in1=pb.ap, op0=ALU.mult, op1=ALU.add)
                layer_norm_tile(z, gt, bt, stat.next())
                dma("sp", dst[t * 128:(t + 1) * 128, :], z.ap, z, [z], [])

        def mlp(l, xsrc, g_src, b_src, dst):
            gt, bt = load_gb(g_src, b_src)
            zb = T([128, 4, D], F32, name="zb")
            zts = [Buf(zb.ap[:, tt, :], Res("z%d" % tt), s.dsem("H")) for tt in range(4)]
            xbs = [T([128, D], BF16, dma="sw", name="xb%d" % i) for i in range(2)]
            xTr = [T([128, 16, 512], BF16, name="xTm%d" % i) for i in range(2)]
            hT = T([128, 64, 512], BF16, name="hT")
            w1r = TR(2, [128, 16, 256], BF16, dma=True, name="w1")
            w2r = TR(3, [128, 8, 512], BF16, dma=True, name="w2")
            hr = TR(2, [128, 512], F32, name="hrelu")
            stat = TR(2, [128, 32], F32, name="stat")
            pbt = Ring([PB[6], PB[7]])
            pbh = Ring(PB[0:6])

            def issue_xload(G, tts):
                for tt in tts:
                    t = G * 4 + tt
                    xb = xbs[tt % 2]
                    dma("pool", xb.ap, xsrc[t * 128:(t + 1) * 128, :], xb, [], [xb])

            def prefetch_T(G, tts):
                xT = xTr[G % 2]
                for tt in tts:
                    xb = xbs[tt % 2]
                    for k0 in (0, 8):
                        pb = pbt.next()
                        pv = pb.ap.bitcast(BF16)
                        for j in range(8):
                            k = k0 + j
                            s.op("pe", (lambda pv=pv, j=j, xb=xb, k=k: (lambda e: e.transpose(pv[:, j * 128:(j + 1) * 128], xb.ap[:, k * 128:(k + 1) * 128], ident.ap)))(),
                                 rs([xb, ident]), rs([pb]))
                        cp("act", xT.ap[:, k0:k0 + 8, tt * 128:(tt + 1) * 128],
                           pv.rearrange("p (k t) -> p k t", k=8), [pb], [xT])

            issue_xload(0, [0, 1])
            prefetch_T(0, [0, 1])
            issue_xload(0, [2, 3])
            prefetch_T(0, [2, 3])
            for G in range(4):
                xT = xTr[G % 2]
                for f2 in range(32):
                    w1 = w1r.next()
                    dma("sp", w1.ap, w1b[l, :, f2 * 256:(f2 + 1) * 256].rearrange("(k p) f -> p k f", p=128), w1, [wcast[l]], [w1])
                    for fi in range(2):
                        f = f2 * 2 + fi
                        pb = pbh.next()
                        for k in range(16):
                            mm(pb.ap, w1.ap[:, k, fi * 128:(fi + 1) * 128], xT.ap[:, k, :], k == 0, k == 15, [w1, xT], [pb])
                        h = hr.next()
                        act(h.ap, pb.ap, AF.Relu, [pb], [h])
                        vop("pool" if f % 2 else "dve", "tensor_tensor", [h], [hT], out=hT.ap[:, f, :], in0=h.ap, in1=h.ap, op=ALU.mult)
                    if G > 0:
                        for tt in range(4):
                            if f2 == 1 + 4 * tt:
                                layer_norm_tile(zts[tt], gt, bt, stat.next())
                            if f2 == 4 + 4 * tt:
                                t = (G - 1) * 4 + tt
                                dma("sp", dst[t * 128:(t + 1) * 128, :], zts[tt].ap, zts[tt], [zts[tt]], [])
                    if f2 == 22:
                        for tt in range(4):
                            t = G * 4 + tt
                            dma("sp", zts[tt].ap, xsrc[t * 128:(t + 1) * 128, :], zts[tt], [], [zts[tt]])
                    if f2 == 26 and G < 3:
                        issue_xload(G + 1, [0, 1])
                for dt in range(4):
                    if G < 3 and dt == 0:
                        prefetch_T(G + 1, [0, 1])
                        issue_xload(G + 1, [2, 3])
                    if G < 3 and dt == 1:
                        prefetch_T(G + 1, [2, 3])
                    pbs = PB[0:4] if dt % 2 == 0 else PB[4:8]
                    for f8 in range(8):
                        w2 = w2r.next()
                        dma("sp", w2.ap, w2b[l, f8 * 1024:(f8 + 1) * 1024, dt * 512:(dt + 1) * 512].rearrange("(c p) d -> p c d", p=128),
                            w2, [wcast[l]], [w2])
                        for fi in range(8):
                            f = f8 * 8 + fi
                            for tt in range(4):
                                mm(pbs[tt].ap, hT.ap[:, f, tt * 128:(tt + 1) * 128], w2.ap[:, fi, :], f == 0, f == 63, [hT, w2], [pbs[tt]])
                    for tt in range(4):
                        zt = zts[tt]
                        vop("dve", "scalar_tensor_tensor", [zt, pbs[tt]], [zt], out=zt.ap[:, dt * 512:(dt + 1) * 512],
                            in0=zt.ap[:, dt * 512:(dt + 1) * 512], scalar=ALPHA, in1=pbs[tt].ap, op0=ALU.mult, op1=ALU.add)
            for tt in range(4):
                t = 12 + tt
                layer_norm_tile(zts[tt], gt, bt, stat.next())
                dma("sp", dst[t * 128:(t + 1) * 128, :], zts[tt].ap, zts[tt], [zts[tt]], [])

        def banded(Qd, Kd, kvmap, nh, Vd, nkv, dil, Rm, cvals, use_sink, out_mode, Yd=None, OBd=None, LSEd=None):
            L = S // dil
            nbpl = L // 128
            Vp = T([128, NT, nkv * 64], BF16, dma=True, name="Vp")
            for n in range(NT):
                r = (128 * n) // L
                a0 = (128 * n) % L
                st_ = r + dil * a0
                dma("sp", Vp.ap[:, n, :], Vd[st_:st_ + dil * 127 + 1:dil, :], Vp, [], [Vp])
            OF = T([128, NT, nh * 64], F32, dma=True, name="OF")
            OFr = [Res("OF%d" % n) for n in range(NT)]
            denAll = T([128, NT, nh], F32, name="denAll")
            nmAll = T([128, NT, nh], F32, name="nmAll")
            esAll = T([128, NT, nh], F32, name="esAll")
            denR = {}
            nmR = {}
            esR = {}
            for n_ in range(NT):
                for h_ in range(nh):
                    denR[(n_, h_)] = Buf(denAll.ap[:, n_, h_:h_ + 1], Res("den"))
                    nmR[(n_, h_)] = Buf(nmAll.ap[:, n_, h_:h_ + 1], Res("nm"))
                    esR[(n_, h_)] = Buf(esAll.ap[:, n_, h_:h_ + 1], Res("es"))
            if out_mode == "A":
                Oall = T([128, NT, nh * 64], BF16, dma=True, name="Oall")
            else:
                Oall = OF
                lse = T([128, NT, nh], F32, dma=True, name="lse")
            Qr = TR(4, [64, S], BF16, dma=True, name="Qh")
            Kr = TR(4, [64, S], BF16, dma=True, name="Kh")
            Tr = TR(3, [128, 256], F32, name="T")
            Pr = TR(4, [128, 256], BF16, name="P")
            PTr = TR(4, [128, 256], BF16, name="PT")
            str_ = TR(6, [128, 8], F32, name="st")
            pS = Ring([PB[0], PB[1], PB[6]])
            pT = Ring([PB[2], PB[3]])
            pO = Ring([PB[4], PB[5], PB[7]])
            heads = []
            lastkv = -1
            Kh = None
            for h in range(nh):
                Qh = Qr.next()
                kv = kvmap(h)
                newk = kv != lastkv
                if newk:
                    Kh = Kr.next()
                    lastkv = kv
                heads.append((Qh, Kh, kv, newk))

            def load_head(h):
                Qh, Kh, kv, newk = heads[h]
                dma("sp", Qh.ap, Qd[h * 64:(h + 1) * 64, :], Qh, [], [Qh])
                if newk:
                    dma("sp", Kh.ap, Kd[kv * 64:(kv + 1) * 64, :], Kh, [], [Kh])
            items = [(h, n) for h in range(nh) for n in range(NT)]
            ctx = [dict(ps=pS.next(), Tt=Tr.next(), st=str_.next(), Pt=Pr.next(), pt=pT.next(), PT=PTr.next(), po=pO.next())
                   for _ in items]
            load_head(0)
            if nh > 1:
                load_head(1)

            def geom(n):
                hasprev = (n % nbpl) != 0
                nk = 256 if hasprev else 128
                ks = (n - 1) * 128 if hasprev else n * 128
                return hasprev, nk, ks

            def stA(it):
                h, n = items[it]
                c = ctx[it]
                if n == 0 and h + 2 < nh:
                    load_head(h + 2)
                if it % 12 == 5:
                    bg_tick()
                Qh, Kh, kv, _ = heads[h]
                hasprev, nk, ks = geom(n)
                Rv = Rm.ap[:, 0:256] if hasprev else Rm.ap[:, 128:256]
                ps, Tt, st, Pt = c["ps"], c["Tt"], c["st"], c["Pt"]
                mm(ps.ap[:, 0:nk], Qh.ap[:, n * 128:(n + 1) * 128], Kh.ap[:, ks:ks + nk], True, True, [Qh, Kh], [ps])
                vop("dve", "scalar_tensor_tensor", [ps, Rm], [Tt], out=Tt.ap[:, 0:nk], in0=Rv, scalar=cvals[h], in1=ps.ap[:, 0:nk],
                    op0=ALU.mult, op1=ALU.add)
                vop("dve", "reduce_max", [Tt], [st], out=st.ap[:, 0:1], in_=Tt.ap[:, 0:nk], axis=AX.X)
                nmb, denb, esb = nmR[(n, h)], denR[(n, h)], esR[(n, h)]
                if use_sink:
                    vop("dve", "tensor_scalar", [st, sink8], [nmb], out=nmb.ap, in0=st.ap[:, 0:1], scalar1=sink8.ap[:, h:h + 1], scalar2=-0.125,
                        op0=ALU.max, op1=ALU.mult)
                else:
                    vop("dve", "tensor_single_scalar", [st], [nmb], out=nmb.ap, in_=st.ap[:, 0:1], scalar=-0.125, op=ALU.mult)
                act(Pt.ap[:, 0:nk], Tt.ap[:, 0:nk], AF.Exp, [Tt, nmb], [Pt, denb], bias=nmb.ap, scale=0.125, accum=denb.ap)
                if use_sink:
                    act(esb.ap, nmb.ap, AF.Exp, [nmb, sinkt], [esb], bias=sinkt.ap[:, h:h + 1], scale=1.0)

            def stB(it):
                h, n = items[it]
                c = ctx[it]
                hasprev, nk, ks = geom(n)
                Pt, pt, PT = c["Pt"], c["pt"], c["PT"]
                ptv = pt.ap.bitcast(BF16)
                for kb in range(nk // 128):
                    s.op("pe", (lambda ptv=ptv, kb=kb, Pt=Pt: (lambda e: e.transpose(ptv[:, kb * 128:(kb + 1) * 128], Pt.ap[:, kb * 128:(kb + 1) * 128], ident.ap)))(),
                         rs([Pt, ident]), rs([pt]))
                cp("act", PT.ap[:, 0:nk], ptv[:, 0:nk], [pt], [PT])

            def stC(it):
                h, n = items[it]
                c = ctx[it]
                Qh, Kh, kv, _ = heads[h]
                hasprev, nk, ks = geom(n)
                PT, po, st = c["PT"], c["po"], c["st"]
                nkb = nk // 128
                for kb in range(nkb):
                    blk = ks // 128 + kb
                    mm(po.ap[:, 0:64], PT.ap[:, kb * 128:(kb + 1) * 128], Vp.ap[:, blk, kv * 64:(kv + 1) * 64],
                       kb == 0, kb == nkb - 1, [PT, Vp], [po])
                ofb = Buf(OF.ap[:, n, h * 64:(h + 1) * 64], OFr[n])
                cp("act" if it % 2 else "dve", ofb.ap, po.ap[:, 0:64], [po], [ofb])

            run_pipeline(len(items), [stA, stB, stC])
            allden = list(denR.values())
            allnm = list(nmR.values())
            alles = list(esR.values())
            ofall = [Buf(None, r) for r in OFr]
            if use_sink:
                vop("dve", "tensor_tensor", allden + alles, [denAll], out=denAll.ap, in0=denAll.ap, in1=esAll.ap, op=ALU.add)
            if out_mode == "B":
                act(esAll.ap, denAll.ap, AF.Ln, allden + [denAll], [esAll])
                vop("dve", "tensor_tensor", [esAll] + allnm, [lse], out=lse.ap, in0=esAll.ap, in1=nmAll.ap, op=ALU.subtract)
            vop("dve", "reciprocal", allden + [denAll, esAll], [denAll], out=denAll.ap, in_=denAll.ap)
            for q4 in range(4):
                eng = "pool" if q4 % 2 else "dve"
                ns = slice(q4 * 4, (q4 + 1) * 4)
                vop(eng, "tensor_tensor", ofall + [denAll], [Oall],
                    out=Oall.ap[:, ns, :].rearrange("p n (h d) -> p n h d", h=nh),
                    in0=OF.ap[:, ns, :].rearrange("p n (h d) -> p n h d", h=nh),
                    in1=denAll.ap[:, ns, :].unsqueeze(3).broadcast_to([128, 4, nh, 64]), op=ALU.mult)
            if out_mode == "A":
                for n in range(NT):
                    dma("sp", Yd[n * 128:(n + 1) * 128, 0:nh * 64], Oall.ap[:, n, :], Oall, [Oall], [])
            else:
                for n in range(NT):
                    r = (128 * n) // L
                    a0 = (128 * n) % L
                    st_ = r + dil * a0
                    dma("sp", OBd[st_:st_ + dil * 127 + 1:dil, :], Oall.ap[:, n, :], Oall, [Oall], [])
                    dma("sp", LSEd[st_:st_ + dil * 127 + 1:dil, :], lse.ap[:, n, :], lse, [lse], [])

        def combine_B():
            ol = [TR(2, [128, 512], F32, dma=True, name="o%d" % g) for g in range(3)]
            ll = TR(2, [128, 3, 8], F32, dma=True, name="l")
            wk = TR(2, [128, 3, 8], F32, name="wk")
            sm = TR(2, [128, 16], F32, name="sm")
            yo = TR(2, [128, 512], BF16, dma=True, name="yo")
            for t in range(NT):
                og = [ol[g].next() for g in range(3)]
                lt = ll.next()
                for g in range(3):
                    dma("sp", og[g].ap, OB_d[g][t * 128:(t + 1) * 128, :], og[g], [], [og[g]])
                    dma("sp", lt.ap[:, g, :], LSE_d[g][t * 128:(t + 1) * 128, :], lt, [], [lt])
                m = sm.next()
                vop("dve", "tensor_tensor", [lt], [m], out=m.ap[:, 0:8], in0=lt.ap[:, 0, :], in1=lt.ap[:, 1, :], op=ALU.max)
                vop("dve", "tensor_tensor", [lt, m], [m], out=m.ap[:, 0:8], in0=m.ap[:, 0:8], in1=lt.ap[:, 2, :], op=ALU.max)
                w = wk.next()
                vop("dve", "tensor_tensor", [lt, m], [w], out=w.ap, in0=lt.ap, in1=m.ap[:, 0:8].unsqueeze(1).broadcast_to([128, 3, 8]), op=ALU.subtract)
                act(w.ap, w.ap, AF.Exp, [w], [w])
                vop("dve", "tensor_tensor", [w], [m], out=m.ap[:, 8:16], in0=w.ap[:, 0, :], in1=w.ap[:, 1, :], op=ALU.add)
                vop("dve", "tensor_tensor", [w, m], [m], out=m.ap[:, 8:16], in0=m.ap[:, 8:16], in1=w.ap[:, 2, :], op=ALU.add)
                vop("dve", "reciprocal", [m], [m], out=m.ap[:, 8:16], in_=m.ap[:, 8:16])
                vop("dve", "tensor_tensor", [w, m], [w], out=w.ap, in0=w.ap, in1=m.ap[:, 8:16].unsqueeze(1).broadcast_to([128, 3, 8]), op=ALU.mult)
                for g in range(3):
                    eng = "pool" if g == 1 else "dve"
                    vop(eng, "tensor_tensor", [og[g], w], [og[g]], out=og[g].ap.rearrange("p (h d) -> p h d", h=8),
                        in0=og[g].ap.rearrange("p (h d) -> p h d", h=8), in1=w.ap[:, g, :].unsqueeze(2).broadcast_to([128, 8, 64]), op=ALU.mult)
                vop("dve", "tensor_tensor", [og[0], og[1]], [og[0]], out=og[0].ap, in0=og[0].ap, in1=og[1].ap, op=ALU.add)
                y = yo.next()
                vop("dve", "tensor_tensor", [og[0], og[2]], [y], out=y.ap, in0=og[0].ap, in1=og[2].ap, op=ALU.add)
                dma("sp", Y0_d[t * 128:(t + 1) * 128, 1024:1536], y.ap, y, [y], [])

        xT = T([128, 16, S], BF16, name="xT")
        load_T(x_in, D, xT, True)
        wring = TR(2, [128, 16, 512], BF16, dma="sw", name="wring")
        stgF = TR(2, [128, S], BF16, dma=True, name="stgF")
        stgT = TR(2, [128, 512], BF16, dma=True, name="stgT")
        proj_F(xT, 16, e_win[:, 0:512], 512, QA_d[0:512, :], 1, wring, stgF)
        proj_F(xT, 16, e_win[:, 512:1024], 512, QA_d[512:1024, :], 1, wring, stgF)
        proj_F(xT, 16, e_win[:, 1024:1152], 128, KA_d, 1, wring, stgF)
        proj_T(xT, 16, e_win[:, 1152:1280], 128, VA_d, BF16, wring, stgT)
        for g, dil in enumerate((1, 4, 16)):
            base = 1280 + g * 1536
            proj_F(xT, 16, e_win[:, base:base + 512], 512, QB_d[g], dil, wring, stgF)
            proj_F(xT, 16, e_win[:, base + 512:base + 1024], 512, KB_d[g], dil, wring, stgF)
            proj_T(xT, 16, e_win[:, base + 1024:base + 1536], 512, VB_d[g], BF16, wring, stgT)
        phase_end()
        bg_fill(0)
        slA = alibi(16)
        banded(QA_d, KA_d, lambda h: h // 8, 16, VA_d, 2, 1, RA, [8.0 * sl for sl in slA], True, "A", Yd=Y0_d)
        phase_end()
        slB = alibi(8)
        for g, dil in enumerate((1, 4, 16)):
            banded(QB_d[g], KB_d[g], lambda h: h, 8, VB_d[g], 8, dil, RB, [8.0 * sl * dil for sl in slB], False, "B",
                   OBd=OB_d[g], LSEd=LSE_d[g])
            phase_end()
        out_proj_ln(Y0_d, 12, e_wout, x_in, ln1_g[0:1, :], ln1_b[0:1, :], xs1, ncl=1024, extra=True)
        phase_end()
        bg_flush()
        mlp(0, xs1, ln2_g[0:1, :], ln2_b[0:1, :], xs2 if "stop0" not in dbg else out_d)
        phase_end()

        if "stop0" not in dbg:

            base_persist = persist_end[0]
            bg_fill(1)
            cosT = T([128, S], F32, name="cosT")
            sinT = T([128, S], F32, name="sinT")
            persist_end[0] = aoff[0]
            pidx = T([128, 2], I32, name="pidx")
            pf = T([128, 2], F32, name="pf")
            vop("pool", "iota", [], [pidx], out=pidx.ap[:, 0:1], pattern=[[0, 1]], base=0, channel_multiplier=1)
            vop("dve", "tensor_single_scalar", [pidx], [pidx], out=pidx.ap[:, 1:2], in_=pidx.ap[:, 0:1], scalar=15, op=ALU.bitwise_and)
            cp("dve", pf.ap[:, 0:1], pidx.ap[:, 1:2], [pidx], [pf])
            act(pf.ap[:, 1:2], pf.ap[:, 0:1], AF.Exp, [pf], [pf], scale=-math.log(10000.0) / 16.0)
            vop("dve", "tensor_single_scalar", [pf], [pf], out=pf.ap[:, 1:2], in_=pf.ap[:, 1:2], scalar=1.0 / (2 * math.pi), op=ALU.mult)
            tpi = T([128, S], I32, name="tpi")
            tpf = T([128, S], F32, name="tpf")
            tq = T([128, S], F32, name="tq")
            vop("pool", "iota", [], [tpi], out=tpi.ap, pattern=[[1, S]], base=0, channel_multiplier=0)
            cp("dve", tpf.ap, tpi.ap, [tpi], [tpf])
            for tab, offs in ((sinT, 0.0), (cosT, 0.25)):
                vop("dve", "tensor_scalar", [tpf, pf], [tq], out=tq.ap, in0=tpf.ap, scalar1=pf.ap[:, 1:2], scalar2=offs,
                    op0=ALU.mult, op1=ALU.add)
                cp("dve", tpi.ap, tq.ap, [tq], [tpi])
                cp("dve", tab.ap, tpi.ap, [tpi], [tab])
                vop("dve", "tensor_tensor", [tq, tab], [tq], out=tq.ap, in0=tq.ap, in1=tab.ap, op=ALU.subtract)
                vop("dve", "scalar_tensor_tensor", [tq], [tq], out=tq.ap, in0=tq.ap, scalar=0.5, in1=tq.ap,
                    op0=ALU.is_gt, op1=ALU.subtract)
                act(tab.ap, tq.ap, AF.Sin, [tq], [tab], scale=-2.0 * math.pi)
            phase_end()

            def rope_evac(ps_m, ps_r, st, tg, tmpr):
                ta = tmpr.next()
                tb = tmpr.next()
                cs = slice(tg * 512, (tg + 1) * 512)
                vop("dve", "tensor_tensor", [ps_r, sinT], [ta], out=ta.ap[64:96, :], in0=ps_r.ap[64:96, :], in1=sinT.ap[64:96, cs], op=ALU.mult)
                vop("dve", "tensor_tensor", [ps_m, cosT], [tb], out=tb.ap[64:96, :], in0=ps_m.ap[64:96, :], in1=cosT.ap[64:96, cs], op=ALU.mult)
                vop("pool", "tensor_tensor", [ta, tb], [st], out=st.ap[64:96, cs], in0=ta.ap[64:96, :], in1=tb.ap[64:96, :], op=ALU.add)

            xT = T([128, 16, S], BF16, name="xT1")
            load_T(xs2, D, xT, True)
            wring = TR(2, [128, 16, 512], BF16, dma="sw", name="wring")
            stgF = TR(2, [128, S], BF16, dma=True, name="stgF")
            stgT = TR(2, [128, 512], BF16, dma=True, name="stgT")
            stgT32 = TR(2, [128, 512], F32, dma=True, name="stgT32")
            for i in range(2):
                proj_F(xT, 16, o_win[:, i * 512:(i + 1) * 512], 512, QC_d[i * 512:(i + 1) * 512, :], 1, wring, stgF)
                proj_F(xT, 16, o_win[:, 1024 + i * 512:1024 + (i + 1) * 512], 512, KC_d[i * 512:(i + 1) * 512, :], 1, wring, stgF)
                proj_T(xT, 16, o_win[:, 2048 + i * 512:2048 + (i + 1) * 512], 512, VC_d[:, i * 512:(i + 1) * 512], BF16, wring, stgT)
            proj_T(xT, 16, o_win[:, 3072:3584], 512, CQ_d[:, 0:512], F32, wring, stgT32)
            proj_T(xT, 16, o_win[:, 3584:3840], 256, CQ_d[:, 512:768], F32, wring, stgT32)
            wkr = T([128, 16, 96], BF16, dma="sw", name="wkr")
            wkrot = T([128, 16, 96], BF16, name="wkrot")
            vop("dve", "memset", [], [wkr], ap=wkr.ap, constant=0.0)
            vop("pool", "memset", [], [wkrot], ap=wkrot.ap, constant=0.0)
            dma("pool", wkr.ap[:, :, 64:96], o_win[:, 3840:3872].rearrange("(k p) e -> p k e", p=128), wkr, [], [wkr])
            vop("dve", "tensor_single_scalar", [wkr], [wkrot], out=wkrot.ap[:, :, 64:80], in_=wkr.ap[:, :, 80:96], scalar=-1.0, op=ALU.mult)
            cp("dve", wkrot.ap[:, :, 80:96], wkr.ap[:, :, 64:80], [wkr], [wkrot])
            tmpr = TR(4, [128, 512], F32, name="ropetmp")
            stK = T([128, S], BF16, dma=True, name="stKr")
            for tg in range(4):
                pm, pr = PB[0], PB[1]
                for k in range(16):
                    mm(pm.ap[0:96, :], wkr.ap[:, k, :], xT.ap[:, k, tg * 512:(tg + 1) * 512], k == 0, k == 15, [wkr, xT], [pm])
                for k in range(16):
                    mm(pr.ap[0:96, :], wkrot.ap[:, k, :], xT.ap[:, k, tg * 512:(tg + 1) * 512], k == 0, k == 15, [wkrot, xT], [pr])
                rope_evac(pm, pr, stK, tg, tmpr)
            for h in range(16):
                dma("sp", KD_d[h, 64:96, :], stK.ap[64:96, :], stK, [stK], [])
            phase_end()

            cT = T([128, 6, S], BF16, name="cT")
            gq = T([128, 768], F32, dma=True, name="gq")
            dma("sp", gq.ap[:, 0:512], o_qg.partition_broadcast(128), gq, [], [gq])
            dma("sp", gq.ap[:, 512:768], o_kvg.partition_broadcast(128), gq, [], [gq])
            cl = TR(2, [128, 768], F32, dma=True, name="cl")
            junk = TR(2, [128, 768], F32, name="junk")
            cbf = TR(2, [128, 768], BF16, name="cbf")
            str_ = TR(2, [128, 8], F32, name="st")
            pbt = Ring([PB[6], PB[7]])
            for t in range(NT):
                c = cl.next()
                dma("sp", c.ap, CQ_d[t * 128:(t + 1) * 128, :], c, [], [c])
                jk = junk.next()
                st = str_.next()
                act(jk.ap[:, 0:512], c.ap[:, 0:512], AF.Square, [c], [jk, st], accum=st.ap[:, 0:1])
                act(jk.ap[:, 512:768], c.ap[:, 512:768], AF.Square, [c], [jk, st], accum=st.ap[:, 1:2])
                vop("dve", "tensor_scalar", [st], [st], out=st.ap[:, 2:3], in0=st.ap[:, 0:1], scalar1=1.0 / 512.0, scalar2=RMS_EPS, op0=ALU.mult, op1=ALU.add)
                vop("dve", "tensor_scalar", [st], [st], out=st.ap[:, 3:4], in0=st.ap[:, 1:2], scalar1=1.0 / 256.0, scalar2=RMS_EPS, op0=ALU.mult, op1=ALU.add)
                act(st.ap[:, 4:6], st.ap[:, 2:4], AF.Sqrt, [st], [st])
                vop("dve", "reciprocal", [st], [st], out=st.ap[:, 6:8], in_=st.ap[:, 4:6])
                vop("pool", "tensor_tensor", [c, gq], [c], out=c.ap, in0=c.ap, in1=gq.ap, op=ALU.mult)
                cb = cbf.next()
                vop("dve", "tensor_scalar", [c, st], [cb], out=cb.ap[:, 0:512], in0=c.ap[:, 0:512], scalar1=st.ap[:, 6:7], scalar2=None, op0=ALU.mult)
                vop("dve", "tensor_scalar", [c, st], [cb], out=cb.ap[:, 512:768], in0=c.ap[:, 512:768], scalar1=st.ap[:, 7:8], scalar2=None, op0=ALU.mult)
                pb = pbt.next()
                pv = pb.ap.bitcast(BF16)
                for k in range(6):
                    s.op("pe", (lambda pv=pv, k=k, cb=cb: (lambda e: e.transpose(pv[:, k * 128:(k + 1) * 128], cb.ap[:, k * 128:(k + 1) * 128], ident.ap)))(),
                         rs([cb, ident]), rs([pb]))
                cp("act", cT.ap[:, :, t * 128:(t + 1) * 128], pv[:, 0:768].rearrange("p (k t) -> p k t", k=6), [pb], [cT])
            wq = T([128, 4, 1536], BF16, dma="sw", name="wq")
            wqr = T([128, 4, 1536], BF16, name="wqr")
            dma("pool", wq.ap, o_wuq.rearrange("(k p) e -> p k e", p=128), wq, [], [wq])
            wq4 = wq.ap.rearrange("p k (h e) -> p k h e", h=16)
            wqr4 = wqr.ap.rearrange("p k (h e) -> p k h e", h=16)
            cp("dve", wqr.ap, wq.ap, [wq], [wqr])
            for k in range(4):
                vop("dve", "tensor_single_scalar", [wq, wqr], [wqr], out=wqr4[:, k, :, 64:80], in_=wq4[:, k, :, 80:96], scalar=-1.0, op=ALU.mult)
                cp("dve", wqr4[:, k, :, 80:96], wq4[:, k, :, 64:80], [wq, wqr], [wqr])
            wkv = T([128, 2, 2048], BF16, dma="sw", name="wkv")
            dma("pool", wkv.ap, o_wukv.rearrange("(k p) e -> p k e", p=128), wkv, [], [wkv])
            stQ = TR(2, [128, S], BF16, dma=True, name="stQ")
            stKn = TR(2, [128, S], BF16, dma=True, name="stKn")
            stV = TR(2, [128, 1024], BF16, dma=True, name="stV")
            tmpr = TR(4, [128, 512], F32, name="ropetmp")
            pbq = Ring(PB[0:6])
            for h in range(16):
                sq = stQ.next()
                for tg in range(4):
                    pm = pbq.next()
                    pr = pbq.next()
                    for k in range(4):
                        mm(pm.ap[0:96, :], wq4[:, k, h, :], cT.ap[:, k, tg * 512:(tg + 1) * 512], k == 0, k == 3, [wq, cT], [pm])
                    for k in range(4):
                        mm(pr.ap[0:96, :], wqr4[:, k, h, :], cT.ap[:, k, tg * 512:(tg + 1) * 512], k == 0, k == 3, [wqr, cT], [pr])
                    cp("act", sq.ap[0:64, tg * 512:(tg + 1) * 512], pm.ap[0:64, :], [pm], [sq])
                    rope_evac(pm, pr, sq, tg, tmpr)
                dma("sp", QD_d[h], sq.ap[0:96, :], sq, [sq], [])
                sk = stKn.next()
                for tg in range(4):
                    pk = pbq.next()
                    for k in range(2):
                        mm(pk.ap[0:64, :], wkv.ap[:, k, h * 128:h * 128 + 64], cT.ap[:, 4 + k, tg * 512:(tg + 1) * 512], k == 0, k == 1, [wkv, cT], [pk])
                    cp(ev_eng(), sk.ap[0:64, tg * 512:(tg + 1) * 512], pk.ap[0:64, :], [pk], [sk])
                dma("sp", KD_d[h, 0:64, :], sk.ap[0:64, :], sk, [sk], [])
            wkv5 = wkv.ap.rearrange("p k (h two d) -> p k h two d", two=2, d=64)
            for t in range(NT):
                sv = stV.next()
                for hf in range(2):
                    pb = pbq.next()
                    for k in range(2):
                        mm(pb.ap.rearrange("p (h d) -> p h d", h=8), cT.ap[:, 4 + k, t * 128:(t + 1) * 128], wkv5[:, k, hf * 8:(hf + 1) * 8, 1, :],
                           k == 0, k == 1, [wkv, cT], [pb])
                    cp(ev_eng(), sv.ap[:, hf * 512:(hf + 1) * 512], pb.ap, [pb], [sv])
                dma("sp", VD_d[t * 128:(t + 1) * 128, :], sv.ap, sv, [sv], [])
            persist_end[0] = base_persist
            phase_end()

            def attn_C():
                VCp = T([128, NT, 1024], BF16, dma=True, name="VCp")
                dma("sp", VCp.ap, VC_d.rearrange("(n p) c -> p n c", p=128), VCp, [], [VCp])
                OC = T([128, NT, 1024], BF16, dma=True, name="OC")
                Qr = TR(3, [64, S], BF16, dma=True, name="Qh")
                Kr = TR(3, [64, S], BF16, dma=True, name="Kh")
                Er = TR(4, [128, 512], F32, name="E")
                SPr = TR(4, [128, 512], F32, name="SP")
                LKr = TR(4, [128, 512], F32, name="LK")
                Wr = TR(4, [128, 512], BF16, name="W")
                Srun = TR(2, [128, 512], F32, name="Srun")
                pZ = Ring([PB[0], PB[1]])
                pA = Ring([PB[2], PB[3]])
                pO = PB[4:8]
                heads = [(Qr.next(), Kr.next()) for _ in range(16)]

                def load_head(h):
                    Qh, Kh = heads[h]
                    dma("sp", Qh.ap, QC_d[h * 64:(h + 1) * 64, :], Qh, [], [Qh])
                    dma("sp", Kh.ap, KC_d[h * 64:(h + 1) * 64, :], Kh, [], [Kh])
                items = []
                for h in range(16):
                    for G in range(4):
                        Sr = Srun.next()
                        for j in range(4 * G + 3, -1, -1):
                            items.append((h, G, j, Sr))
                ctx = [dict(pz=pZ.next(), pa=pA.next(), E=Er.next(), SP=SPr.next(), LK=LKr.next(), W=Wr.next()) for _ in items]
                load_head(0)

                def s1(it):
                    h, G, j, Sr = items[it]
                    c = ctx[it]
                    if G == 0 and j == 3 and h + 1 < 16:
                        load_head(h + 1)
                    if it % 16 == 7:
                        bg_tick()
                    Qh, Kh = heads[h]
                    c0 = max(j - 4 * G, 0) * 128
                    pz, E, SP, LK = c["pz"], c["E"], c["SP"], c["LK"]
                    mm(pz.ap[:, c0:512], Kh.ap[:, j * 128:(j + 1) * 128], Qh.ap[:, G * 512 + c0:(G + 1) * 512], True, True, [Kh, Qh], [pz])
                    act(E.ap[:, c0:512], pz.ap[:, c0:512], AF.Exp, [pz], [E], scale=-0.125)
                    act(SP.ap[:, c0:512], E.ap[:, c0:512], AF.Ln, [E], [SP], bias=1.0, scale=1.0)
                    vop("dve", "scalar_tensor_tensor", [pz, SP], [LK], out=LK.ap[:, c0:512], in0=pz.ap[:, c0:512], scalar=-0.125,
                        in1=SP.ap[:, c0:512], op0=ALU.mult, op1=ALU.subtract)
                    if j >= 4 * G:
                        vop("pool", "tensor_tensor", [LK, mC01], [LK], out=LK.ap[:, c0:c0 + 128], in0=LK.ap[:, c0:c0 + 128], in1=mC01.ap, op=ALU.mult)

                def s2(it):
                    h, G, j, Sr = items[it]
                    c = ctx[it]
                    c0 = max(j - 4 * G, 0) * 128
                    pa, E, SP, LK, W = c["pa"], c["E"], c["SP"], c["LK"], c["W"]
                    first = j == 4 * G + 3
                    if first:
                        vop("pool", "memset", [], [Sr], ap=Sr.ap, constant=0.0)
                    mm(pa.ap[:, c0:512], Ustr.ap, LK.ap[:, c0:512], True, first, [Ustr, LK], [pa])
                    if not first:
                        mm(pa.ap[:, c0:512], ones.ap, Sr.ap[:, c0:512], False, True, [ones, Sr], [pa])
                    vop("dve", "tensor_tensor", [pa, SP], [E], out=E.ap[:, c0:512], in0=pa.ap[:, c0:512], in1=SP.ap[:, c0:512], op=ALU.subtract)
                    act(W.ap[:, c0:512], E.ap[:, c0:512], AF.Exp, [E], [W])
                    if j >= 4 * G:
                        vop("pool", "tensor_tensor", [W, mC01b], [W], out=W.ap[:, c0:c0 + 128], in0=W.ap[:, c0:c0 + 128], in1=mC01b.ap, op=ALU.mult)
                    if j > 0:
                        vop("pool", "tensor_tensor", [Sr, LK], [Sr], out=Sr.ap[:, c0:512], in0=Sr.ap[:, c0:512], in1=LK.ap[:, c0:512], op=ALU.add)

                def s3(it):
                    h, G, j, Sr = items[it]
                    c = ctx[it]
                    q0 = max(j - 4 * G, 0)
                    W = c["W"]
                    for qt in range(q0, 4):
                        mm(pO[qt].ap[:, 0:64], W.ap[:, qt * 128:(qt + 1) * 128], VCp.ap[:, j, h * 64:(h + 1) * 64],
                           j == 4 * G + qt, j == 0, [W, VCp], [pO[qt]])
                    if j == 0:
                        for qt in range(4):
                            cp("act" if qt % 2 else "dve", OC.ap[:, 4 * G + qt, h * 64:(h + 1) * 64], pO[qt].ap[:, 0:64], [pO[qt]], [OC])

                run_pipeline(len(items), [s1, s2, s3])
                for n in range(NT):
                    dma("sp", Y1_d[n * 128:(n + 1) * 128, 0:1024], OC.ap[:, n, :], OC, [OC], [])

            def attn_D():
                sc = 1.0 / math.sqrt(96.0)
                VDp = T([128, NT, 1024], BF16, dma=True, name="VDp")
                dma("sp", VDp.ap, VD_d.rearrange("(n p) c -> p n c", p=128), VDp, [], [VDp])
                OD = T([128, NT, 1024], BF16, dma=True, name="OD")
                ODF = T([128, NT, 1024], F32, name="ODF")
                ODFr = [Res("ODF%d" % n) for n in range(NT)]
                denP = T([128, NT * 16, 8], F32, name="denP")
                denS = T([128, NT, 16], F32, name="denS")
                vop("pool", "memset", [], [denP], ap=denP.ap, constant=0.0)
                denB = {}
                for i_ in range(NT):
                    for h_ in range(16):
                        denB[(h_, i_)] = Buf(denP.ap[:, i_ * 16 + h_, :], Res("dp"))
                Qr = TR(3, [96, S], BF16, dma=True, name="Qh")
                Kr = TR(3, [96, S], BF16, dma=True, name="Kh")
                Pr = TR(4, [128, S], BF16, name="P")
                PTr = TR(4, [128, S], BF16, name="PT")
                Sdr = TR(3, [128, 128], F32, name="Sd")
                str_ = TR(6, [128, 16], F32, name="st")
                pSa = Ring([PB[0:2], PB[2:4]])
                pT = Ring([PB[4], PB[5]])
                pO = Ring([PB[6], PB[7]])
                heads = [(Qr.next(), Kr.next()) for _ in range(16)]

                def load_head(h):
                    Qh, Kh = heads[h]
                    dma("sp", Qh.ap, QD_d[h], Qh, [], [Qh])
                    dma("sp", Kh.ap, KD_d[h], Kh, [], [Kh])
                items = [(h, i) for h in range(16) for i in range(NT)]
                ctx = []
                for (h, i) in items:
                    nbank = (i + 4) // 4
                    pS = pSa.next() if nbank <= 2 else PB[0:4]
                    ctx.append(dict(pS=pS, Sd=Sdr.next(), st=str_.next(), Pt=Pr.next(), PT=PTr.next(), po=pO.next()))
                load_head(0)

                def geom(i):
                    nkb = i + 1
                    nbank = (nkb + 3) // 4
                    widths = [min(512, nkb * 128 - bk * 512) for bk in range(nbank)]
                    return nkb, nbank, widths

                def sA(it):
                    h, i = items[it]
                    c = ctx[it]
                    if i == 0 and h + 1 < 16:
                        load_head(h + 1)
                    Qh, Kh = heads[h]
                    nkb, nbank, widths = geom(i)
                    pS, Sd, st, Pt = c["pS"], c["Sd"], c["st"], c["Pt"]
                    for bk in range(nbank):
                        mm(pS[bk].ap[:, 0:widths[bk]], Qh.ap[:, i * 128:(i + 1) * 128], Kh.ap[:, bk * 512:bk * 512 + widths[bk]],
                           True, True, [Qh, Kh], [pS[bk]])
                    bd = nbank - 1
                    dc = widths[bd] - 128
                    vop("dve", "tensor_tensor", [pS[bd], McD], [Sd], out=Sd.ap, in0=pS[bd].ap[:, dc:dc + 128], in1=McD.ap, op=ALU.add)
                    vop("dve", "reduce_max", [Sd], [st], out=st.ap[:, 0:1], in_=Sd.ap, axis=AX.X)
                    ncol = 1
                    for bk in range(nbank):
                        wv = widths[bk] - (128 if bk == bd else 0)
                        if wv > 0:
                            vop("dve", "reduce_max", [pS[bk]], [st], out=st.ap[:, ncol:ncol + 1], in_=pS[bk].ap[:, 0:wv], axis=AX.X)
                            ncol += 1
                    if ncol > 1:
                        vop("dve", "reduce_max", [st], [st], out=st.ap[:, 5:6], in_=st.ap[:, 0:ncol], axis=AX.X)
                        mxc = st.ap[:, 5:6]
                    else:
                        mxc = st.ap[:, 0:1]
                    vop("dve", "tensor_single_scalar", [st], [st], out=st.ap[:, 6:7], in_=mxc, scalar=-sc, op=ALU.mult)
                    dpb = denB[(h, i)]
                    act(Pt.ap[:, i * 128:(i + 1) * 128], Sd.ap, AF.Exp, [Sd, st, denP], [Pt, dpb], bias=st.ap[:, 6:7], scale=sc, accum=dpb.ap[:, 0:1])
                    ncol = 1
                    for bk in range(nbank):
                        wv = widths[bk] - (128 if bk == bd else 0)
                        if wv > 0:
                            act(Pt.ap[:, bk * 512:bk * 512 + wv], pS[bk].ap[:, 0:wv], AF.Exp, [pS[bk], st], [Pt, dpb],
                                bias=st.ap[:, 6:7], scale=sc, accum=dpb.ap[:, ncol:ncol + 1])
                            ncol += 1
                    c["ncol"] = ncol

                def sB(it):
                    h, i = items[it]
                    c = ctx[it]
                    nkb, nbank, widths = geom(i)
                    Pt, PT = c["Pt"], c["PT"]
                    for k0 in range(0, nkb, 8):
                        kn = min(8, nkb - k0)
                        pt = pT.next()
                        ptv = pt.ap.bitcast(BF16)
                        for jj in range(kn):
                            kb = k0 + jj
                            s.op("pe", (lambda ptv=ptv, jj=jj, Pt=Pt, kb=kb: (lambda e: e.transpose(ptv[:, jj * 128:(jj + 1) * 128], Pt.ap[:, kb * 128:(kb + 1) * 128], ident.ap)))(),
                                 rs([Pt, ident]), rs([pt]))
                        cp("act" if (k0 // 8) % 2 == 0 else "dve", PT.ap[:, k0 * 128:(k0 + kn) * 128], ptv[:, 0:kn * 128], [pt], [PT])

                def sC(it):
                    h, i = items[it]
                    c = ctx[it]
                    nkb, nbank, widths = geom(i)
                    PT, po, st = c["PT"], c["po"], c["st"]
                    ncol = c["ncol"]
                    for kb in range(nkb):
                        mm(po.ap[:, 0:64], PT.ap[:, kb * 128:(kb + 1) * 128], VDp.ap[:, kb, h * 64:(h + 1) * 64], kb == 0, kb == nkb - 1, [PT, VDp], [po])
                    ofb = Buf(ODF.ap[:, i, h * 64:(h + 1) * 64], ODFr[i])
                    cp("act" if it % 2 else "dve", ofb.ap, po.ap[:, 0:64], [po], [ofb])

                run_pipeline(len(items), [sA, sB, sC])
                alld = list(denB.values())
                vop("dve", "reduce_sum", alld + [denP], [denS], out=denS.ap.rearrange("p n h -> p (n h)"), in_=denP.ap, axis=AX.X)
                vop("dve", "reciprocal", [denS], [denS], out=denS.ap, in_=denS.ap)
                ofall = [Buf(None, r) for r in ODFr]
                for q4 in range(4):
                    eng = "pool" if q4 % 2 else "dve"
                    ns = slice(q4 * 4, (q4 + 1) * 4)
                    vop(eng, "tensor_tensor", ofall + [denS], [OD],
                        out=OD.ap[:, ns, :].rearrange("p n (h d) -> p n h d", h=16),
                        in0=ODF.ap[:, ns, :].rearrange("p n (h d) -> p n h d", h=16),
                        in1=denS.ap[:, ns, :].unsqueeze(3).broadcast_to([128, 4, 16, 64]), op=ALU.mult)
                for n in range(NT):
                    dma("sp", Y1_d[n * 128:(n + 1) * 128, 1024:2048], OD.ap[:, n, :], OD, [OD], [])

            if "skipC" not in dbg:
                attn_C()
                phase_end()
            if "skipD" not in dbg:
                attn_D()
                phase_end()
            out_proj_ln(Y1_d, 16, o_wout, xs2, ln1_g[1:2, :], ln1_b[1:2, :], xs1)
            phase_end()
            bg_flush()
            mlp(1, xs1, ln2_g[1:2, :], ln2_b[1:2, :], out_d)

        s.barrier(final=True)
        s.emit()
    return nc


_NC_CACHE = {}


def kernel(**inputs):
    B = inputs["x"].shape[0]
    if "nc" not in _NC_CACHE:
        _NC_CACHE["nc"] = build()
    nc = _NC_CACHE["nc"]
    f = lambda a: np.ascontiguousarray(np.asarray(a, dtype=np.float32))
    shared = {
        "even_w_in": f(inputs["even_w_in"][0]),
        "even_sinks": f(inputs["even_sinks"][0]).reshape(1, 16),
        "even_w_out": f(inputs["even_w_out"][0]),
        "odd_w_in": f(inputs["odd_w_in"][0]),
        "odd_q_norm_g": f(inputs["odd_q_norm_g"][0]).reshape(1, 512),
        "odd_kv_norm_g": f(inputs["odd_kv_norm_g"][0]).reshape(1, 256),
        "odd_w_uq": f(inputs["odd_w_uq"][0]),
        "odd_w_ukv": f(inputs["odd_w_ukv"][0]),
        "odd_w_out": f(inputs["odd_w_out"][0]),
        "ln1_g": f(inputs["ln1_g"]), "ln1_b": f(inputs["ln1_b"]),
        "ln2_g": f(inputs["ln2_g"]), "ln2_b": f(inputs["ln2_b"]),
        "mlp_w1": f(inputs["mlp_w1"]), "mlp_w2": f(inputs["mlp_w2"]),
    }
    x = f(inputs["x"])
    in_maps = [dict(shared, x=x[b]) for b in range(B)]
    res = run_bass_kernel_spmd(nc, in_maps, core_ids=list(range(B)))
    return np.stack([r["out"] for r in res.results], axis=0)
```
